# Optimizing a Trainium2 kernel written in Bass

```python
import math
import jax, jax.numpy as jnp
from jax import lax
import numpy as np

D_MODEL = 1024
BATCH = 4
SEQ = 4096
DEPTH = 1
DEC_BATCH = 32
DEC_SEQ = 4
PAST_LEN = 8192
PAGE_SIZE = 128

HEAD_DIM = 64
NSA_HEADS = 8
NSA_KV_HEADS = 2
NSA_GROUP = NSA_HEADS // NSA_KV_HEADS
CMP_LEN = 32
CMP_STRIDE = 16
CMP_RATIO = CMP_LEN // CMP_STRIDE
CMP_HIDDEN = 64
SLC_BLOCK = 64
SLC_RATIO = SLC_BLOCK // CMP_STRIDE
N_SELECT = 16
WINDOW = 512
Q_BLOCK = 128
N_KV_SLOTS = 6
GLA_HEADS = 4
GLA_DK = 64
GLA_DV = 128
GLA_LOWRANK = 16
GLA_NORMALIZER = 16.0
GLA_CHUNK = 32
N_BUCKETS = 32
MAX_DISTANCE = 128
D_FF = ((8 * D_MODEL + 767) // 768) * 256
D_PLE = 256
EPS = 1e-6
NEG_INF = -1e30
FORCED_SCORE = 1e9

NSA_Q_DIM = NSA_HEADS * HEAD_DIM
NSA_KV_DIM = N_KV_SLOTS * NSA_KV_HEADS * HEAD_DIM
NSA_GATE_DIM = 3 * NSA_HEADS
GLA_QK_DIM = GLA_HEADS * GLA_DK
GLA_V_DIM = GLA_HEADS * GLA_DV
MIX_WIDTH = NSA_Q_DIM + GLA_V_DIM
IN_SPLITS = (NSA_Q_DIM, NSA_KV_DIM, NSA_GATE_DIM, GLA_QK_DIM, GLA_QK_DIM, GLA_V_DIM, GLA_LOWRANK, GLA_V_DIM)
IN_DIM = sum(IN_SPLITS)

kernel_name = "nsa_gla_hybrid_step"


def rmsnorm(x, g):
    xf = x.astype(jnp.float32)
    xf = xf * lax.rsqrt(jnp.mean(xf * xf, axis=-1, keepdims=True) + EPS)
    return xf.astype(x.dtype) * g


def t5_bucket(dist):
    n = jnp.maximum(dist, 0)
    max_exact = N_BUCKETS // 2
    nf = jnp.maximum(n, 1).astype(jnp.float32)
    large = max_exact + (jnp.log(nf / max_exact) / math.log(MAX_DISTANCE / max_exact)
                         * (N_BUCKETS - max_exact)).astype(jnp.int32)
    large = jnp.minimum(large, N_BUCKETS - 1)
    return jnp.where(n < max_exact, n, large)


def split_projection(xn, w_in):
    B, T, _ = xn.shape
    offs = np.cumsum(IN_SPLITS)[:-1].tolist()
    q_n, kv, gt, q_g, k_g, v_g, lr, g_g = jnp.split(xn @ w_in, offs, axis=-1)
    q_n = q_n.reshape(B, T, NSA_KV_HEADS, NSA_GROUP, HEAD_DIM)
    kv = kv.reshape(B, T, N_KV_SLOTS, NSA_KV_HEADS, HEAD_DIM)
    gt = jax.nn.sigmoid(gt).reshape(B, T, 3, NSA_KV_HEADS, NSA_GROUP, 1)
    q_g = q_g.reshape(B, T, GLA_HEADS, GLA_DK) * (GLA_DK ** -0.5)
    k_g = k_g.reshape(B, T, GLA_HEADS, GLA_DK)
    v_g = v_g.reshape(B, T, GLA_HEADS, GLA_DV)
    g_g = g_g.reshape(B, T, GLA_HEADS, GLA_DV)
    return q_n, kv, gt, q_g, k_g, v_g, lr, g_g


def compress_rows(rows, pe, w1, w2):
    B, T, G, HD = rows.shape
    nb = T // CMP_STRIDE
    sub = rows[:, :nb * CMP_STRIDE].reshape(B, nb, CMP_STRIDE, G, HD)
    pe_r = pe.reshape(CMP_RATIO, CMP_STRIDE, HD)
    w1_r = w1.reshape(CMP_RATIO, CMP_STRIDE, HD, CMP_HIDDEN)
    n_cmp = nb - CMP_RATIO + 1
    h = sum(jnp.einsum('bnsgd,sdh->bngh', sub[:, j:j + n_cmp] + pe_r[j][:, None, :], w1_r[j])
            for j in range(CMP_RATIO))
    return jnp.einsum('bngh,hd->bngd', jax.nn.gelu(h), w2)


def nsa_query_block(q, pos, k_cmp, v_cmp, k_slc, v_slc, rel_bias):
    B, Q, G, R, _ = q.shape
    scale = HEAD_DIM ** -0.5
    n_cmp = k_cmp.shape[1]
    n_slc = k_slc.shape[2]
    blk_end = jnp.arange(n_cmp, dtype=jnp.int32) * CMP_STRIDE + (CMP_LEN - 1)
    dist_c = pos[:, None] - blk_end[None, :]
    vis_c = (dist_c >= 0)[None, :, :, None, None]
    bias_c = rel_bias[t5_bucket(dist_c)].reshape(Q, n_cmp, G, R)
    s_c = jnp.einsum('bqgrd,bngd->bqngr', q, k_cmp).astype(jnp.float32) * scale + bias_c
    p_c = jax.nn.softmax(jnp.where(vis_c, s_c, NEG_INF), axis=2) * vis_c
    o_cmp = jnp.einsum('bqngr,bngd->bqgrd', p_c.astype(v_cmp.dtype), v_cmp)
    imp = jnp.swapaxes(p_c.sum(-1), 2, 3)
    lead = CMP_RATIO - 1
    total = SLC_RATIO * n_slc + lead
    imp = jnp.pad(imp, ((0, 0), (0, 0), (0, 0), (lead, total - lead - n_cmp)))
    span = SLC_RATIO * (n_slc - 1) + 1
    imp_slc = sum(imp[..., m - n + lead: m - n + lead + span: SLC_RATIO]
                  for m in range(SLC_RATIO) for n in range(CMP_RATIO))
    blk = jnp.arange(n_slc, dtype=jnp.int32)[None, :]
    cur = (pos // SLC_BLOCK)[:, None]
    vis_s = blk * SLC_BLOCK <= pos[:, None]
    forced = (blk == 0) | (blk == cur) | (blk == cur - 1)
    score = jnp.where(forced[None, :, None], FORCED_SCORE,
                      jnp.where(vis_s[None, :, None], imp_slc, -1.0))
    n_sel = min(N_SELECT, n_slc)
    _, idx = lax.top_k(score, n_sel)
    b_idx = jnp.arange(B)[:, None, None, None]
    g_idx = jnp.arange(G)[None, None, :, None]
    n_keys = n_sel * SLC_BLOCK
    ks = k_slc[b_idx, g_idx, idx].reshape(B, Q, G, n_keys, HEAD_DIM)
    vs = v_slc[b_idx, g_idx, idx].reshape(B, Q, G, n_keys, HEAD_DIM)
    key_pos = (idx[..., None] * SLC_BLOCK + jnp.arange(SLC_BLOCK, dtype=jnp.int32)).reshape(B, Q, G, n_keys)
    dist_s = pos[None, :, None, None] - key_pos
    table_g = rel_bias.reshape(N_BUCKETS, G, R).transpose(1, 0, 2)
    bias_s = table_g[g_idx, t5_bucket(dist_s)]
    s_s = jnp.einsum('bqgrd,bqgkd->bqgkr', q, ks).astype(jnp.float32) * scale + bias_s
    p_s = jax.nn.softmax(jnp.where((dist_s >= 0)[..., None], s_s, NEG_INF), axis=3)
    o_slc = jnp.einsum('bqgkr,bqgkd->bqgrd', p_s.astype(vs.dtype), vs)
    return o_cmp, o_slc


def nsa_global(q, pos, rows, cmp_pe, cmp_w1, cmp_w2, rel_bias, q_block):
    B, T = rows.shape[:2]
    k_cmp = compress_rows(rows[:, :, 0], cmp_pe[0], cmp_w1[0], cmp_w2[0])
    v_cmp = compress_rows(rows[:, :, 1], cmp_pe[1], cmp_w1[1], cmp_w2[1])
    n_slc = -(-T // SLC_BLOCK)
    slc = jnp.pad(rows[:, :, 2:4], ((0, 0), (0, n_slc * SLC_BLOCK - T), (0, 0), (0, 0), (0, 0)))
    slc = slc.reshape(B, n_slc, SLC_BLOCK, 2, NSA_KV_HEADS, HEAD_DIM).transpose(3, 0, 4, 1, 2, 5)
    Q = q.shape[1]
    nqb = Q // q_block
    qb = jnp.swapaxes(q.reshape(B, nqb, q_block, NSA_KV_HEADS, NSA_GROUP, HEAD_DIM), 0, 1)
    pb = pos.reshape(nqb, q_block)
    o_cmp, o_slc = lax.map(
        lambda a: nsa_query_block(a[0], a[1], k_cmp, v_cmp, slc[0], slc[1], rel_bias), (qb, pb))
    o_cmp = jnp.swapaxes(o_cmp, 0, 1).reshape(B, Q, NSA_KV_HEADS, NSA_GROUP, HEAD_DIM)
    o_slc = jnp.swapaxes(o_slc, 0, 1).reshape(B, Q, NSA_KV_HEADS, NSA_GROUP, HEAD_DIM)
    return o_cmp, o_slc


def window_attend(q, k, v, q_pos, k_pos, rel_bias):
    B, N, Q, G, R, _ = q.shape
    K = k.shape[2]
    dist = q_pos[:, :, None] - k_pos[:, None, :]
    valid = ((dist >= 0) & (dist < WINDOW) & (k_pos[:, None, :] >= 0))[None, :, :, :, None, None]
    bias = rel_bias[t5_bucket(dist)].reshape(N, Q, K, G, R)
    s = jnp.einsum('bnqgrd,bnkgd->bnqkgr', q, k).astype(jnp.float32) * (HEAD_DIM ** -0.5) + bias
    p = jax.nn.softmax(jnp.where(valid, s, NEG_INF), axis=3)
    return jnp.einsum('bnqkgr,bnkgd->bnqgrd', p.astype(v.dtype), v)


def gla_chunked(q, k, v, log_a, s0):
    B, T, H, _ = q.shape
    DV = v.shape[-1]
    C = min(GLA_CHUNK, T)
    nc = -(-T // C)
    pad = nc * C - T

    def blocks(a):
        a = jnp.pad(a.astype(jnp.float32), ((0, 0), (0, pad), (0, 0), (0, 0)))
        return jnp.swapaxes(a.reshape(B, nc, C, H, a.shape[-1]), 0, 1)

    causal = jnp.tril(jnp.ones((C, C), dtype=bool))

    def step(S, blk):
        qc, kc, vc, ac = blk
        b = jnp.cumsum(ac, axis=1)
        qe = qc * jnp.exp(b)
        ke = kc * jnp.exp(-b)
        att = jnp.where(causal, jnp.einsum('bihd,bjhd->bhij', qe, ke), 0.0)
        o = jnp.einsum('bhij,bjhv->bihv', att, vc) + jnp.einsum('bihd,bhdv->bihv', qe, S)
        b_last = b[:, -1]
        kd = kc * jnp.exp(b_last[:, None] - b)
        S = S * jnp.exp(b_last)[..., None] + jnp.einsum('bjhd,bjhv->bhdv', kd, vc)
        return S, o

    S, o = lax.scan(step, s0.astype(jnp.float32), (blocks(q), blocks(k), blocks(v), blocks(log_a)))
    o = jnp.swapaxes(o, 0, 1).reshape(B, nc * C, H, DV)[:, :T]
    return o, S


def gla_branch(q_g, k_g, v_g, lr, w_gk, b_gk, s0):
    B, T = q_g.shape[:2]
    log_a = jax.nn.log_sigmoid((lr @ w_gk + b_gk).astype(jnp.float32)) / GLA_NORMALIZER
    o, S = gla_chunked(q_g, k_g, v_g, log_a.reshape(B, T, GLA_HEADS, GLA_DK), s0)
    return o.astype(q_g.dtype), S


def mixer_output(o_cmp, o_slc, o_win, gates, o_gla, g_gla, gla_norm, w_o):
    B, T = o_cmp.shape[:2]
    o_nsa = (gates[:, :, 0] * o_cmp + gates[:, :, 1] * o_slc + gates[:, :, 2] * o_win).reshape(B, T, NSA_Q_DIM)
    o_g = (rmsnorm(o_gla, gla_norm) * jax.nn.silu(g_gla)).reshape(B, T, GLA_V_DIM)
    return jnp.concatenate([o_nsa, o_g], axis=-1) @ w_o


def mix_prompt(xn, w_in, cmp_pe, cmp_w1, cmp_w2, w_gk, b_gk, gla_norm, w_o, rel_bias):
    B, T, _ = xn.shape
    q_n, kv, gates, q_g, k_g, v_g, lr, g_g = split_projection(xn, w_in)
    pos = jnp.arange(T, dtype=jnp.int32)
    o_cmp, o_slc = nsa_global(q_n, pos, kv[:, :, :4], cmp_pe, cmp_w1, cmp_w2, rel_bias, min(Q_BLOCK, T))
    nb = T // Q_BLOCK
    nwb = WINDOW // Q_BLOCK
    wk = jnp.pad(kv[:, :, 4:], ((0, 0), (nwb * Q_BLOCK, 0), (0, 0), (0, 0), (0, 0)))
    wk = wk.reshape(B, nb + nwb, Q_BLOCK, 2, NSA_KV_HEADS, HEAD_DIM)
    band = jnp.concatenate([wk[:, j:j + nb] for j in range(nwb + 1)], axis=2)
    k_pos = (jnp.arange(nb, dtype=jnp.int32)[:, None] - nwb) * Q_BLOCK + jnp.arange((nwb + 1) * Q_BLOCK, dtype=jnp.int32)[None, :]
    o_win = window_attend(q_n.reshape(B, nb, Q_BLOCK, NSA_KV_HEADS, NSA_GROUP, HEAD_DIM),
                          band[:, :, :, 0], band[:, :, :, 1], pos.reshape(nb, Q_BLOCK), k_pos, rel_bias)
    o_win = o_win.reshape(B, T, NSA_KV_HEADS, NSA_GROUP, HEAD_DIM)
    s0 = jnp.zeros((B, GLA_HEADS, GLA_DK, GLA_DV), jnp.float32)
    o_gla, s_new = gla_branch(q_g, k_g, v_g, lr, w_gk, b_gk, s0)
    out = mixer_output(o_cmp, o_slc, o_win, gates, o_gla, g_g, gla_norm, w_o)
    keep = min(WINDOW, T)
    return out, kv[:, :, :4], kv[:, T - keep:, 4:], s_new


def mix_sample(xn, cache_kv, page_table, win_buf, s0, w_in, cmp_pe, cmp_w1, cmp_w2, w_gk, b_gk,
               gla_norm, w_o, rel_bias):
    B, Tn, _ = xn.shape
    q_n, kv, gates, q_g, k_g, v_g, lr, g_g = split_projection(xn, w_in)
    n_pages = page_table.shape[1]
    past_len = n_pages * cache_kv.shape[1]
    past = cache_kv[page_table].reshape(B, past_len, 4, NSA_KV_HEADS, HEAD_DIM)
    rows_full = jnp.concatenate([past, kv[:, :, :4].astype(past.dtype)], axis=1)
    pos = past_len + jnp.arange(Tn, dtype=jnp.int32)
    o_cmp, o_slc = nsa_global(q_n, pos, rows_full, cmp_pe, cmp_w1, cmp_w2, rel_bias, Tn)
    w_buf = win_buf.shape[1]
    win_full = jnp.concatenate([win_buf, kv[:, :, 4:].astype(win_buf.dtype)], axis=1)
    k_pos = past_len - w_buf + jnp.arange(w_buf + Tn, dtype=jnp.int32)
    o_win = window_attend(q_n[:, None], win_full[:, None, :, 0], win_full[:, None, :, 1],
                          pos[None], k_pos[None], rel_bias)[:, 0]
    o_gla, s_new = gla_branch(q_g, k_g, v_g, lr, w_gk, b_gk, s0)
    out = mixer_output(o_cmp, o_slc, o_win, gates, o_gla, g_g, gla_norm, w_o)
    return out, kv[:, :, :4], win_full[:, Tn:], s_new


def ffn_ple(h, p_l, norm_ffn, w_gate, w_up, w_down, w_ple, norm_ple, w_ple_gate):
    xn = rmsnorm(h, norm_ffn)
    h = h + (jax.nn.silu(xn @ w_gate) * (xn @ w_up)) @ w_down
    gate = jax.nn.sigmoid(rmsnorm(h, norm_ple) @ w_ple_gate)
    return h + (p_l @ w_ple) * gate


def setup_inputs(seed: int = 0) -> dict:
    key = jax.random.key(seed)
    ks = iter(jax.random.split(key, 32))
    nrm = lambda shape, s=1.0: jax.random.normal(next(ks), shape, jnp.float32) * s
    n_pages = PAST_LEN // PAGE_SIZE
    used = DEC_BATCH * n_pages
    n_pool = used + max(1, used // 4)
    w_buf = min(WINDOW, PAST_LEN)
    page_table = jax.random.permutation(next(ks), n_pool)[:used].reshape(DEC_BATCH, n_pages).astype(jnp.int32)
    return {
        'x_prompt': nrm((BATCH, SEQ, D_MODEL)),
        'x_sample': nrm((DEC_BATCH, DEC_SEQ, D_MODEL)),
        'cache_nsa_kv': nrm((DEPTH, n_pool, PAGE_SIZE, 4, NSA_KV_HEADS, HEAD_DIM)),
        'cache_win_kv': nrm((DEPTH, DEC_BATCH, w_buf, 2, NSA_KV_HEADS, HEAD_DIM)),
        'state_gla': nrm((DEPTH, DEC_BATCH, GLA_HEADS, GLA_DK, GLA_DV)),
        'page_table': page_table,
        'p_prompt': nrm((DEPTH, BATCH, SEQ, D_PLE)),
        'p_sample': nrm((DEPTH, DEC_BATCH, DEC_SEQ, D_PLE)),
        'norm_mix': 1.0 + nrm((DEPTH, D_MODEL), 0.1),
        'w_in': nrm((DEPTH, D_MODEL, IN_DIM), D_MODEL ** -0.5),
        'cmp_pe': nrm((DEPTH, 2, CMP_LEN, HEAD_DIM), 0.1),
        'cmp_w1': nrm((DEPTH, 2, CMP_LEN * HEAD_DIM, CMP_HIDDEN), (CMP_LEN * HEAD_DIM) ** -0.5),
        'cmp_w2': nrm((DEPTH, 2, CMP_HIDDEN, HEAD_DIM), CMP_HIDDEN ** -0.5),
        'w_gk': nrm((DEPTH, GLA_LOWRANK, GLA_QK_DIM), GLA_LOWRANK ** -0.5),
        'b_gk': nrm((DEPTH, GLA_QK_DIM), 0.1),
        'gla_norm': 1.0 + nrm((DEPTH, GLA_DV), 0.1),
        'w_o': nrm((DEPTH, MIX_WIDTH, D_MODEL), MIX_WIDTH ** -0.5),
        'norm_ffn': 1.0 + nrm((DEPTH, D_MODEL), 0.1),
        'w_gate': nrm((DEPTH, D_MODEL, D_FF), D_MODEL ** -0.5),
        'w_up': nrm((DEPTH, D_MODEL, D_FF), D_MODEL ** -0.5),
        'w_down': nrm((DEPTH, D_FF, D_MODEL), D_FF ** -0.5),
        'w_ple': nrm((DEPTH, D_PLE, D_MODEL), D_PLE ** -0.5),
        'norm_ple': 1.0 + nrm((DEPTH, D_MODEL), 0.1),
        'w_ple_gate': nrm((DEPTH, D_MODEL, D_MODEL), D_MODEL ** -0.5),
        'rel_bias': nrm((N_BUCKETS, NSA_HEADS), 0.5),
        'norm_final': 1.0 + nrm((D_MODEL,), 0.1),
    }


def reference(x_prompt, x_sample, cache_nsa_kv, cache_win_kv, state_gla, page_table, p_prompt, p_sample,
              norm_mix, w_in, cmp_pe, cmp_w1, cmp_w2, w_gk, b_gk, gla_norm, w_o, norm_ffn, w_gate, w_up,
              w_down, w_ple, norm_ple, w_ple_gate, rel_bias, norm_final):
    kv_p, kv_s, win_p, win_s, st_p, st_s = [], [], [], [], [], []
    h_p, h_s = x_prompt, x_sample
    for l in range(DEPTH):
        lw = (w_in[l], cmp_pe[l], cmp_w1[l], cmp_w2[l], w_gk[l], b_gk[l], gla_norm[l], w_o[l], rel_bias)
        fw = (norm_ffn[l], w_gate[l], w_up[l], w_down[l], w_ple[l], norm_ple[l], w_ple_gate[l])
        mix, rows, win, st = mix_prompt(rmsnorm(h_p, norm_mix[l]), *lw)
        h_p = ffn_ple(h_p + mix, p_prompt[l], *fw)
        kv_p.append(rows)
        win_p.append(win)
        st_p.append(st)
        mix, rows, win, st = mix_sample(rmsnorm(h_s, norm_mix[l]), cache_nsa_kv[l], page_table,
                                        cache_win_kv[l], state_gla[l], *lw)
        h_s = ffn_ple(h_s + mix, p_sample[l], *fw)
        kv_s.append(rows)
        win_s.append(win)
        st_s.append(st)
    y_prompt = rmsnorm(h_p, norm_final)
    y_sample = rmsnorm(h_s, norm_final)
    new_kv_prompt = jnp.stack(kv_p)
    new_kv_sample = jnp.stack(kv_s)
    new_win_prompt = jnp.stack(win_p)
    new_win_sample = jnp.stack(win_s)
    new_gla_prompt = jnp.stack(st_p)
    new_gla_sample = jnp.stack(st_s)
    return (y_prompt, y_sample, new_kv_prompt, new_kv_sample, new_win_prompt, new_win_sample, new_gla_prompt, new_gla_sample)
```

```python
import math
import os
import contextlib
import numpy as np
import ml_dtypes
import concourse.bass as bass
import concourse.mybir as mybir
from concourse.bass_utils import run_bass_kernel_spmd

F32 = mybir.dt.float32
BF16 = mybir.dt.bfloat16
I32 = mybir.dt.int32
ALU = mybir.AluOpType
AF = mybir.ActivationFunctionType
AX = mybir.AxisListType

ENGS = ("pe", "act", "dve", "pool", "sp")
SAME_ENG_SYNC = True
RESCHED = True
SWDGE_DEPTH = 100000


class H:
    __slots__ = ("name", "writers", "readers", "gdeps", "excl")

    def __init__(self, name="", excl=False):
        self.name = name
        self.excl = excl
        self.writers = []
        self.readers = []
        self.gdeps = []


class Op:
    __slots__ = ("eng", "fn", "deps", "dma", "key", "ticket", "sig", "odeps", "dur", "idx")

    def __init__(self, eng, fn, dma, key):
        self.eng = eng
        self.fn = fn
        self.deps = []
        self.odeps = []
        self.dur = 0.3
        self.idx = 0
        self.dma = dma
        self.key = key
        self.ticket = None
        self.sig = False


class Sched:
    def __init__(self, nc):
        self.nc = nc
        self.ops = []
        self.q = {e: [] for e in ENGS}
        self.all_dma = []

    def add(self, eng, fn, reads=(), writes=(), joins=(), dma=False, key=None, dur=None):
        op = Op(eng, fn, dma, key)
        op.dur = dur if dur is not None else (2.5 if dma else 0.3)
        deps = []
        for h in reads:
            deps.extend(h.writers)
            if h.excl:
                deps.extend(r for r in h.readers if r.eng != eng)
        for h in writes:
            deps.extend(h.writers)
            deps.extend(h.readers)
        for h in joins:
            deps.extend(h.gdeps)
        for h in reads:
            h.readers.append(op)
        for h in writes:
            h.gdeps = list(h.writers) + list(h.readers)
            h.writers = [op]
            h.readers = []
        for h in joins:
            if h.excl and h.writers:
                op.odeps.append(h.writers[-1])
            h.writers.append(op)
        seen = set()
        for d in deps:
            if d is op or id(d) in seen:
                continue
            seen.add(id(d))
            op.odeps.append(d)
            if (not d.dma) and d.eng == eng and not dma:
                if eng == "pe" or not SAME_ENG_SYNC:
                    continue
            op.deps.append(d)
            d.sig = True
        if dma:
            assert key is not None
            self.all_dma.append(op)
            lk = self.q.setdefault("lastkey", {})
            if id(key) in lk:
                op.odeps.append(lk[id(key)])
            lk[id(key)] = op
            if eng == "pool":
                hist = self.q["pool_dma_hist"] if "pool_dma_hist" in self.q else self.q.setdefault("pool_dma_hist", [])
                if len(hist) >= SWDGE_DEPTH and hist[-SWDGE_DEPTH] not in op.deps and id(hist[-SWDGE_DEPTH].key) not in self.q.setdefault("group_keys", set()):
                    op.deps.append(hist[-SWDGE_DEPTH])
                    op.odeps.append(hist[-SWDGE_DEPTH])
                    hist[-SWDGE_DEPTH].sig = True
                if hist:
                    op.odeps.append(hist[-1])
                hist.append(op)
        self.ops.append(op)
        self.q[eng].append(op)
        return op

    def barrier(self):
        lasts = []
        for e in ENGS:
            for o in reversed(self.q[e]):
                if o.fn is not None and not o.dma:
                    lasts.append(o)
                    break
        dm = list(self.all_dma)
        for e in ENGS:
            op = Op(e, None, False, "barrier")
            for d in lasts + dm:
                if (not d.dma) and d.eng == e:
                    continue
                op.deps.append(d)
                d.sig = True
            self.ops.append(op)
            self.q[e].append(op)

    def finish(self, eng="sp"):
        op = Op(eng, None, False, None)
        for d in self.all_dma:
            op.deps.append(d)
            d.sig = True
        self.ops.append(op)
        self.q[eng].append(op)

    def reschedule(self):
        import heapq
        for i, op in enumerate(self.ops):
            op.idx = i
        segs, cur = [], []
        for op in self.ops:
            if op.fn is None:
                if cur:
                    segs.append(cur)
                    cur = []
                segs.append([op])
            else:
                cur.append(op)
        if cur:
            segs.append(cur)
        new_ops = []
        for seg in segs:
            if len(seg) == 1 and seg[0].fn is None:
                b = seg[0]
                if b.key == "barrier":
                    b.deps = []
                    for e2 in ENGS:
                        for o2 in reversed(new_ops):
                            if o2.eng == e2 and o2.fn is not None and not o2.dma:
                                if e2 != b.eng:
                                    b.deps.append(o2)
                                    o2.sig = True
                                break
                    for o2 in new_ops:
                        if o2.dma:
                            b.deps.append(o2)
                            o2.sig = True
                new_ops.append(b)
                continue
            inseg = {id(o) for o in seg}
            npred = {}
            succ = {}
            ready_t = {}
            for o in seg:
                ps = [d for d in o.odeps if id(d) in inseg]
                npred[id(o)] = len(ps)
                ready_t[id(o)] = 0.0
                for d in ps:
                    succ.setdefault(id(d), []).append(o)
            heaps = {e: [] for e in ENGS}
            for o in seg:
                if npred[id(o)] == 0:
                    heapq.heappush(heaps[o.eng], (0.0, o.idx, o))
            free = {e: 0.0 for e in ENGS}
            done = 0
            while done < len(seg):
                best = None
                for e in ENGS:
                    if heaps[e]:
                        rt, ix, o = heaps[e][0]
                        st = max(rt, free[e])
                        if best is None or (st, ix) < (best[0], best[1]):
                            best = (st, ix, e)
                st, ix, e = best
                rt, ix, o = heapq.heappop(heaps[e])
                issue = 0.05 if o.dma else o.dur
                free[e] = st + issue
                fin = st + o.dur + (0.25 if not o.dma else 0.0)
                new_ops.append(o)
                done += 1
                for sc in succ.get(id(o), ()):
                    ready_t[id(sc)] = max(ready_t[id(sc)], fin)
                    npred[id(sc)] -= 1
                    if npred[id(sc)] == 0:
                        heapq.heappush(heaps[sc.eng], (ready_t[id(sc)], sc.idx, sc))
        assert len(new_ops) == len(self.ops)
        self.ops = new_ops
        for e in ENGS:
            self.q[e] = [o for o in new_ops if o.eng == e]

    def emit(self):
        nc = self.nc
        if RESCHED:
            self.reschedule()
        cnt = {e: 0 for e in ENGS}
        dcnt = {}
        keys = []
        for op in self.ops:
            if op.dma:
                if op.key not in dcnt:
                    dcnt[op.key] = 0
                    keys.append(op.key)
                dcnt[op.key] += 16
                op.ticket = dcnt[op.key]
            elif op.sig:
                cnt[op.eng] += 1
                op.ticket = cnt[op.eng]
        with contextlib.ExitStack() as st:
            esem = {e: st.enter_context(nc.semaphore("s_" + e)) for e in ENGS if e != "sp"}
            dsem = {k: st.enter_context(nc.semaphore("d%d" % i)) for i, k in enumerate(keys)}
            block = st.enter_context(nc.Block())

            def run(engname):
                def body(e):
                    waited = {}
                    for op in self.q[engname]:
                        need = {}
                        for d in op.deps:
                            sem = dsem[d.key] if d.dma else esem[d.eng]
                            sid = id(sem)
                            if sid not in need or need[sid][1] < d.ticket:
                                need[sid] = (sem, d.ticket)
                        for sid, (sem, tk) in need.items():
                            if waited.get(sid, 0) >= tk:
                                continue
                            waited[sid] = tk
                            e.wait_ge(sem, tk)
                        if op.fn is None:
                            continue
                        ins = op.fn(e)
                        if op.dma:
                            ins.then_inc(dsem[op.key], 16)
                        elif op.sig:
                            ins.then_inc(esem[op.eng], 1)
                return body

            block.tensor(run("pe"))
            block.scalar(run("act"))
            block.vector(run("dve"))
            block.gpsimd(run("pool"))
            block.sync(run("sp"))
        return len(keys), cnt


D = 1024
KC = 8
SEQ = 4096
NSLOT = 33
TOK = NSLOT * 128
NOWN = 16
NT = 2048
NS = 16
NTS = NT + NS
DFF = 2816
NFC = 22
IN_DIM = 2856
C_Q, C_KV, C_GT, C_QG, C_KG, C_VG, C_LR, C_GG = 0, 512, 1280, 1304, 1560, 1816, 2328, 2344
EPS = 1e-6
BIG = 30000.0
LA = 4096
OFFA = 1936
LW = 768
OFFW = 128
LEXT = LA + LW
NPOOL = 2560
PAST = 8192
NPG = 64


def _bucket(n):
    n = np.asarray(n, np.int64)
    nf = np.maximum(n, 1).astype(np.float32)
    large = 16 + (np.log(nf / np.float32(16)) / np.float32(math.log(8.0)) * np.float32(16)).astype(np.int32)
    large = np.minimum(large, 31)
    return np.where(n < 16, n, large)


def _coef():
    c = np.zeros((33, LEXT), np.float32)
    m = np.arange(LA)
    n = m - OFFA
    for mi, ni in zip(m, n):
        if ni < 0:
            c[32, mi] = 1.0
        elif ni <= 112:
            c[_bucket(ni), mi] += 1.0
            c[31, mi] -= 1.0
    m = np.arange(LW)
    n = m - OFFW
    for mi, ni in zip(m, n):
        if ni < 0 or ni >= 512:
            c[32, LA + mi] = 1.0
        elif ni <= 112:
            c[_bucket(ni), LA + mi] += 1.0
            c[31, LA + mi] -= 1.0
    return c


def _static_consts():
    k = {}
    i = np.arange(128)
    k["cJ"] = (i[:, None] + i[None, :] == 127).astype(np.float32)
    k["cTri"] = (i[:, None] <= i[None, :]).astype(np.float32)
    k["cTriU"] = (i[:, None] > i[None, :]).astype(np.float32)
    k["cIdent"] = np.eye(128, dtype=np.float32)
    k["coef"] = _coef()
    E = np.zeros((128, TOK), np.float32)
    E[np.arange(TOK) // 64, np.arange(TOK)] = 1.0
    k["cE"] = E
    As = np.zeros((512, 128), np.float32)
    wts = {-1: 1.0, 0: 2.0, 1: 2.0, 2: 2.0, 3: 1.0}
    for j in range(128):
        for dd, w in wts.items():
            ii = 4 * j + dd
            if 0 <= ii <= 510:
                As[ii, j] = w
    k["cAs"] = As.reshape(4, 128, 128).transpose(1, 0, 2).copy()
    sel = np.zeros((4, 2, 128), np.float32)
    sel[:, 0, :] = 1.0
    sel[:, 0, 0] = 0.0
    sel[:, 0, 127] = 0.0
    sel[:, 1, 0] = 1e9
    sel[:, 1, 127] = 1e9
    k["cSels"] = sel
    blk = np.arange(128)
    k["cBm"] = (blk[:, None] // 4 == np.arange(64)[None, :] // 2).astype(np.float32)
    k["cHalf"] = ((blk[:, None] % 4) == (i[None, :] // 32)).astype(np.float32)
    t = np.arange(16)
    same = (t[:, None] // 4 == t[None, :] // 4)
    k["cTri16"] = (same & (t[:, None] <= t[None, :])).astype(np.float32)
    k["cTriU16"] = (same & (t[:, None] > t[None, :])).astype(np.float32)
    k["cSeqm"] = (t[:, None] // 4 == np.arange(4)[None, :]).astype(np.float32)
    vs_ = np.ones((128, 4), np.float32)
    vs_[127, 3] = 0.0
    k["cValS"] = vs_
    k["cPcol"] = (np.arange(128) % 64).astype(np.float32).reshape(128, 1)
    return k


def _core_consts(par):
    shift = 1 - par
    k = {}
    A = np.zeros((256, 128), np.float32)
    wts = {-1: 1.0, 0: 2.0, 1: 2.0, 2: 2.0, 3: 1.0}
    for bp in range(66):
        j = bp - 2 * shift
        if not (0 <= j <= 63):
            continue
        for dd, w in wts.items():
            ii = 4 * j + dd
            if 0 <= ii <= 254:
                c = ii + 8 * shift + 1
                if c < 256:
                    A[c, bp] = w
    k["cA"] = A.reshape(2, 128, 128).transpose(1, 0, 2).copy()
    sel = np.zeros((NOWN, 128, 2, 128), np.float32)
    sel[:, :, 1, :] = -2.0
    q = np.arange(128)
    for jo in range(NOWN):
        s = 2 * jo + 1
        pos = (s - shift) * 128 + q
        cur = pos // 64
        for bp in range(66):
            j = bp - 2 * shift
            if not (0 <= j <= 63):
                continue
            forced = (j == 0) | (j == cur) | (j == cur - 1)
            vis = (j * 64 <= pos)
            sel[jo, :, 0, bp] = np.where(vis & ~forced, 1.0, 0.0)
            sel[jo, :, 1, bp] = np.where(forced, 1e9, np.where(vis, 0.0, -1.0))
    k["cSel"] = sel
    valid = np.ones((128, NSLOT), np.float32)
    valid[:, 0 if shift == 1 else 32] = 0.0
    k["cValid"] = valid
    vc = np.zeros((256,), np.float32)
    for c in range(256):
        ii = c - 1 - 8 * shift
        vc[c] = 1.0 if 0 <= ii <= 254 else 0.0
    k["cValC"] = vc.reshape(2, 128).T.copy()
    return k


class Arena:
    def __init__(self, nc, base, end):
        self.nc, self.top, self.end = nc, base, end
        self.n = 0

    def alloc(self, shape, dt):
        sz = 4 if dt in (F32, I32) else 2
        nb = int(np.prod(shape[1:])) * sz
        nb = (nb + 63) // 64 * 64
        t = self.nc.alloc_sbuf_tensor_at("t%d" % self.n, list(shape), dt, offset=self.top)
        self.n += 1
        self.top += nb
        assert self.top <= self.end, ("SBUF overflow", self.top, self.end)
        return t


def build_program(dbg=False, nslots=NSLOT, cut=99, do_sample=True, nseq=4):
    nc = bass.Bass("TRN2", target_bir_lowering=False)
    S = Sched(nc)
    IN = {}
    OUT = {}

    def din(name, shape, dt=F32):
        IN[name] = (shape, dt)
        return nc.dram_tensor(name, list(shape), dt, kind="ExternalInput").ap()

    def dout(name, shape, dt=F32):
        OUT[name] = (shape, dt)
        return nc.dram_tensor(name, list(shape), dt, kind="ExternalOutput").ap()

    xT = din("xT", [128, KC, TOK])
    pT = din("pT", [128, 2, NTS])
    xsT = din("xsT", [128, KC, NS])
    w_in = din("w_in", [128, KC, IN_DIM])
    w_o_nsa = din("w_o_nsa", [64, 8, D])
    w_o_gla = din("w_o_gla", [128, 4, D])
    w_gate = din("w_gate", [128, KC, DFF])
    w_up = din("w_up", [128, KC, DFF])
    w_down = din("w_down", [128, NFC, D])
    w_ple = din("w_ple", [128, 2, D])
    w_pg = din("w_pg", [128, KC, D])
    norms = din("norms", [128, 4, KC])
    w1bd = din("w1bd", [128, 2, 32, 128])
    pecol = din("pecol", [128, 2, 32])
    w2bd = din("w2bd", [128, 2, 128])
    wgk = din("wgk", [33, 256])
    glan = din("glan", [128, 1])
    rb33 = din("rb33", [33, 8])
    coef = din("coef", [33, LEXT])
    cJ = din("cJ", [128, 128])
    cTri = din("cTri", [128, 128])
    cTriU = din("cTriU", [128, 128])
    cIdent = din("cIdent", [128, 128])
    cE = din("cE", [128, TOK])
    cA = din("cA", [128, 2, 128])
    cSel = din("cSel", [NOWN, 128, 2, 128])
    cValid = din("cValid", [128, NSLOT])
    cValC = din("cValC", [128, 2])
    cAs = din("cAs", [128, 4, 128])
    cSels = din("cSels", [4, 2, 128])
    cBm = din("cBm", [128, 64])
    cHalf = din("cHalf", [128, 128])
    cTri16 = din("cTri16", [16, 16])
    cTriU16 = din("cTriU16", [16, 16])
    cSeqm = din("cSeqm", [16, 4])
    cValS = din("cValS", [128, 4])
    cPcol = din("cPcol", [128, 1])
    cache = din("cache", [NPOOL, 128, 512])
    ptab = din("ptab", [1, 4 * NPG], I32)
    cwin = din("cwin", [4, 512, 256])
    sgla = din("sgla", [4, 4, 64, 128])

    yT = dout("yT", [128, KC, NTS])
    kvT_p = dout("kvT_p", [128, 4, NT])
    vtok_p = dout("vtok_p", [NT, 2, 128])
    gla_p = dout("gla_p", [128, 2, 128])
    kvT_s = dout("kvT_s", [128, 4, NS])
    vtok_s = dout("vtok_s", [NS, 2, 128])
    gla_s = dout("gla_s", [4, 128, 2, 128])
    win_s = dout("win_s", [4, 508, 256])
    DBG = {}

    ext_bf = nc.dram_tensor("ext_bf", [8, LEXT], BF16)
    h1_scr = nc.dram_tensor("h1_scr", [128, KC, NTS], F32).ap()
    H_ext = H("ext")
    gscr = [nc.dram_tensor("gscr%d" % i, [24, 128], F32).ap() for i in range(2)]
    gscr_s = nc.dram_tensor("gscr_s", [24, 16], F32).ap()
    H_gscr_s = H("gscr_s")
    H_gscr = [H("gscr0"), H("gscr1")]
    H_h1 = [H("h1s%d" % i) for i in range(NOWN + 1)]

    st = contextlib.ExitStack()
    arena = st.enter_context(nc.sbuf_tensor("arena", [128, 208000], mybir.dt.uint8))
    ABASE = 16512
    AEND = ABASE + 208000
    AR = Arena(nc, ABASE, AEND)

    PB = [st.enter_context(nc.psum_tensor("pb%d" % i, [128, 512], F32)) for i in range(7)]
    PBH = [H("pb%d" % i, excl=True) for i in range(7)]
    PT = st.enter_context(nc.psum_tensor("pbt", [128, 1024], BF16))
    H_PT = H("pbt", excl=True)
    rr_state = [0]

    def rr():
        i = 2 + (rr_state[0] % 5)
        rr_state[0] += 1
        return PB[i], PBH[i]

    def _fsz(ap):
        n = 1
        for d in ap.shape[1:]:
            n *= d
        return n

    def MM(out, lhsT, rhs, start=True, stop=True, R=(), W=(), J=()):
        S.add("pe", lambda e: e.matmul(out, lhsT=lhsT, rhs=rhs, start=start, stop=stop), reads=R, writes=W, joins=J,
              dur=max(_fsz(rhs), 128) / 2400.0 + 0.01)

    def TR(out, in_, ident, R=(), W=(), J=()):
        S.add("pe", lambda e: e.transpose(out=out, in_=in_, identity=ident), reads=R, writes=W, joins=J)

    def ACT(out, in_, func, bias=0.0, scale=1.0, R=(), W=(), J=()):
        S.add("act", lambda e: e.activation(out=out, in_=in_, func=func, bias=bias, scale=scale), reads=R, writes=W, joins=J,
              dur=_fsz(out) / 960.0 + 0.2)

    def CP(eng, out, in_, R=(), W=(), J=()):
        if eng == "act":
            S.add("act", lambda e: e.copy(out=out, in_=in_), reads=R, writes=W, joins=J)
        else:
            S.add(eng, lambda e: e.tensor_copy(out=out, in_=in_), reads=R, writes=W, joins=J)

    def TT(eng, out, in0, in1, op, R=(), W=(), J=()):
        S.add(eng, lambda e: e.tensor_tensor(out=out, in0=in0, in1=in1, op=op), reads=R, writes=W, joins=J, dur=_fsz(out) / 960.0 + 0.15)

    def TS(eng, out, in0, s1, s2, op0, op1=None, R=(), W=(), J=()):
        if op1 is None:
            S.add(eng, lambda e: e.tensor_scalar(out=out, in0=in0, scalar1=s1, scalar2=None, op0=op0), reads=R, writes=W, joins=J)
        else:
            S.add(eng, lambda e: e.tensor_scalar(out=out, in0=in0, scalar1=s1, scalar2=s2, op0=op0, op1=op1), reads=R, writes=W, joins=J)

    def STT(eng, out, in0, scalar, in1, op0, op1, R=(), W=(), J=()):
        S.add(eng, lambda e: e.scalar_tensor_tensor(out=out, in0=in0, scalar=scalar, in1=in1, op0=op0, op1=op1), reads=R, writes=W, joins=J)

    def MS(eng, ap, val, W=(), J=()):
        S.add(eng, lambda e: e.memset(ap, val), writes=W, joins=J)

    def RCP(eng, out, in_, R=(), W=(), J=()):
        S.add(eng, lambda e: e.reciprocal(out=out, in_=in_), reads=R, writes=W, joins=J, dur=_fsz(out) / 160.0 + 0.15)

    def DMA(eng, out, in_, key, R=(), W=(), J=(), **kw):
        S.add(eng, lambda e: e.dma_start(out=out, in_=in_, **kw), reads=R, writes=W, joins=J, dma=True, key=key)

    grp_started = set()
    cur_grp = [H("G0")]

    def gdma(eng, out, in_, grp, R=()):
        S.q.setdefault("group_keys", set()).add(id(grp))
        if id(grp) in grp_started:
            DMA(eng, out, in_, key=grp, R=R, J=[grp])
        else:
            grp_started.add(id(grp))
            DMA(eng, out, in_, key=grp, R=R, W=[grp])

    def load(shape, dt, src, eng=None, name=None, own=False):
        t = AR.alloc(shape, dt)
        if eng is None:
            eng = "pool" if dt == BF16 else "sp"
        if own or os.environ.get("OWNKEYS") == "1":
            h = H(name or "c")
            DMA(eng, t[:], src, key=h, W=[h])
            return t, h
        gdma(eng, t[:], src, cur_grp[0])
        return t, cur_grp[0]

    J_bf, hJ = load([128, 128], BF16, cJ)
    ident_bf, hId = load([128, 128], BF16, cIdent)
    tri_f, hTri = load([128, 128], F32, cTri)
    triu_f, hTriU = load([128, 128], F32, cTriU)
    norms_f, hNorms = load([128, 4, KC], F32, norms)
    glan_f, hGlan = load([128, 1], F32, glan)
    ones_bf = AR.alloc([128, 128], BF16); hOnes = H("ones")
    MS("dve", ones_bf[:], 1.0, W=[hOnes])
    ones_f = AR.alloc([128, 128], F32); hOnesF = H("onesf")
    MS("dve", ones_f[:], 1.0, W=[hOnesF])
    PERS_END = AR.top

    win_bf = AR.alloc([128, KC, 2880], BF16); hWin = H("win")
    first = True
    for kc in range(KC):
        for (a, b) in ((0, 1428), (1428, 2856)):
            DMA("pool", win_bf[:, kc, a:b], w_in[:, kc, a:b], key=hWin, W=[hWin] if first else (), J=() if first else [hWin])
            first = False
    wonsa_bf = AR.alloc([64, 8, D], BF16); hWon = H("wonsa")
    for hh in range(8):
        DMA("pool", wonsa_bf[:, hh, :], w_o_nsa[:, hh, :], key=hWon, W=[hWon] if hh == 0 else (), J=() if hh == 0 else [hWon])
    wogla_bf = AR.alloc([128, 4, D], BF16); hWog = H("wogla")
    for hh in range(4):
        DMA("pool", wogla_bf[:, hh, :], w_o_gla[:, hh, :], key=hWog, W=[hWog] if hh == 0 else (), J=() if hh == 0 else [hWog])
    w1_bf = AR.alloc([128, 2, 32, 128], BF16); hW1 = H("w1")
    for t in range(2):
        for s4 in range(2):
            DMA("pool", w1_bf[:, t, 16 * s4:16 * s4 + 16, :], w1bd[:, t, 16 * s4:16 * s4 + 16, :], key=hW1,
                W=[hW1] if (t == 0 and s4 == 0) else (), J=() if (t == 0 and s4 == 0) else [hW1])
    w2_bf, hW2 = load([128, 2, 128], BF16, w2bd)
    pe_bf, hPe = load([128, 2, 32], BF16, pecol, own=True)
    wgk_bf, hWgk = load([33, 256], BF16, wgk)
    As_bf, hAs = load([128, 4, 128], BF16, cAs)
    valid_f, hValid = load([128, NSLOT], F32, cValid)
    valc_f, hValC = load([128, 2], F32, cValC)
    bm_f, hBm = load([128, 64], F32, cBm)
    half_bf, hHalf = load([128, 128], BF16, cHalf)
    sels_f, hSels = load([4, 2, 128], F32, cSels)
    tri16_f, hTri16 = load([16, 16], F32, cTri16)
    triu16_f, hTriU16 = load([16, 16], F32, cTriU16)
    seqm_f, hSeqm = load([16, 4], F32, cSeqm)

    rb_f, hRb = load([33, 8], F32, rb33, own=True)
    _save_top = AR.top
    AR.top = AEND - 32768
    coef_f = AR.alloc([33, LEXT], F32); hCoef = H("coef")
    DMA("sp", coef_f[:], coef, key=hCoef, W=[hCoef])
    ext_sb = AR.alloc([8, LEXT], BF16); hExtSb = H("extsb")
    AR.top = _save_top
    nch = (LEXT + 511) // 512
    for i in range(nch):
        a = i * 512
        b = min(LEXT, a + 512)
        pb, ph = rr()
        MM(pb[0:8, 0:b - a], rb_f[:], coef_f[:, a:b], R=[hRb, hCoef], W=[ph])
        CP("act", ext_sb[:, a:b], pb[0:8, 0:b - a], R=[ph], W=[hExtSb] if i == 0 else (), )
        if i > 0:
            S.ops[-1]
    hExtSb.writers = [op for op in S.q["act"][-nch:]]
    DMA("sp", ext_bf.ap(), ext_sb[:], key=hExtSb, R=[hExtSb], W=[H_ext])

    def hankel(dst, g, base, pstride, nq, h, npart=128):
        src = bass.AP(ext_bf, 4 * g * LEXT + base, [[pstride, npart], [LEXT, 4], [1, nq]])
        if h in (GHS, GHSn, GHA):
            gdma("sp", dst, src, h, R=[H_ext])
        else:
            DMA("sp", dst, src, key=h, R=[H_ext], W=[h])

    GHS, GHSn, GHA = H("GHS"), H("GHSn"), H("GHA")

    HKS = AR.alloc([128, 4, 2, 16], BF16)
    hHKS = [[GHS for g in range(2)] for i in range(4)]
    MS("dve", HKS[:], 0.0, W=[GHS])
    HKC = AR.alloc([128, 2, 64], BF16)
    HKW = AR.alloc([128, 2, 64], BF16)
    HKS2 = AR.alloc([128, 2, 2, 16], BF16)
    MS("dve", HKC[:], 0.0, J=[GHS])
    MS("dve", HKW[:], 0.0, J=[GHS])
    for g in range(2):
        hankel(HKS[:, 0, g, :].rearrange("p (r q) -> p r q", r=4), g, OFFA - 15, 16, 4, hHKS[0][g])
        hankel(HKS[:, 1, g, :].rearrange("p (r q) -> p r q", r=4), g, OFFA + 1, 1, 4, hHKS[1][g])
        hankel(HKS[0:4, 2, g, :].rearrange("p (r q) -> p r q", r=4), g, OFFA - 3, 1, 4, hHKS[2][g], npart=4)
        hankel(HKS[:, 3, g, :].rearrange("p (r q) -> p r q", r=4), g, LA + OFFW + 385, 1, 4, hHKS[3][g])

    for g in range(2):
        for j in range(2):
            hankel(HKS2[:, j, g, :].rearrange("p (r q) -> p r q", r=4), g, OFFA + 2 - j, 2, 4, GHS)
        hankel(HKC[:, g, 48:64].rearrange("p (r q) -> p r q", r=4), g, OFFA - 15, 16, 4, GHS)
        hankel(HKW[:, g, 0:16].rearrange("p (r q) -> p r q", r=4), g, LA + OFFW + 385, 1, 4, GHS)
        hankel(HKW[:, g, 48:64].rearrange("p (r q) -> p r q", r=4), g, OFFA + 1, 1, 4, GHS)

    pew1 = AR.alloc([128, 2], F32); hPew1 = H("pew1")
    pb, ph = rr()
    for t in range(2):
        for sx in range(32):
            MM(pb[:, t * 8:t * 8 + 8], w1_bf[:, t, sx, :], pe_bf[:, t, sx:sx + 1].to_broadcast([128, 8]), start=(sx == 0), stop=(sx == 31),
               R=[hW1, hPe], **(dict(W=[ph]) if (t == 0 and sx == 0) else dict(J=[ph])))
    CP("dve", pew1[:], pb[:, 0:16:8], R=[ph], W=[hPew1])

    S.barrier()
    MIX_END = AR.top

    def WJ(h, first):
        return dict(W=[h]) if first else dict(J=[h])

    def phase_s():
        cur_grp[0] = H("GS")
        identf, hIdF = load([128, 128], F32, cIdent, name="identf")
        valS_f, hValS = load([128, 4], F32, cValS)
        xs_s = AR.alloc([128, KC, NS], F32); hxs_s = H("xs_s")
        DMA("sp", xs_s[:], xsT, key=hxs_s, W=[hxs_s])
        sq_s = AR.alloc([128, KC, NS], BF16); hsq_s = H("sq_s")
        rstd_s = AR.alloc([128, NS], F32); hrstd_s = H("rstd_s")
        xn_s = AR.alloc([128, KC, NS], BF16); hxn_s = H("xn_s")
        ACT(sq_s[:], xs_s[:], AF.Square, R=[hxs_s], W=[hsq_s])
        pb, ph = rr()
        for kc in range(KC):
            MM(pb[:, 0:NS], ones_bf[:], sq_s[:, kc, :], start=(kc == 0), stop=(kc == KC - 1), R=[hOnes, hsq_s], **WJ(ph, kc == 0))
        ACT(rstd_s[:], pb[:, 0:NS], AF.Ln, bias=EPS, scale=1.0 / D, R=[ph], W=[hrstd_s])
        ACT(rstd_s[:], rstd_s[:], AF.Exp, scale=-0.5, R=[hrstd_s], W=[hrstd_s])
        for kc in range(KC):
            STT("dve", xn_s[:, kc, :], xs_s[:, kc, :], norms_f[:, 0, kc:kc + 1], rstd_s[:], ALU.mult, ALU.mult,
                R=[hxs_s, hrstd_s, hNorms], **WJ(hxn_s, kc == 0))

        def pF(out, c0, M, fw):
            for kc in range(KC):
                MM(out, win_bf[:, kc, c0:c0 + M], xn_s[:, kc, :], start=(kc == 0), stop=(kc == KC - 1), R=[hWin, hxn_s], **fw(kc))

        def pTm(out, c0, N, fw):
            for kc in range(KC):
                MM(out, xn_s[:, kc, :], win_bf[:, kc, c0:c0 + N], start=(kc == 0), stop=(kc == KC - 1), R=[hWin, hxn_s], **fw(kc))

        if cut == 99.1:
            return
        QTs = AR.alloc([128, 64], BF16); hQTs = H("QTs")
        pq, phq = rr()
        for r in range(4):
            pF(pq[:, r * 16:(r + 1) * 16], r * 128, 128, lambda kc, r=r: WJ(phq, r == 0 and kc == 0))
        TS("dve", QTs[:].rearrange("p (b r q) -> p r b q", b=4, r=4), pq[:, 0:64].rearrange("p (r b q) -> p r b q", r=4, b=4),
           0.125, None, ALU.mult, R=[phq], W=[hQTs])
        kvo_s = AR.alloc([128, 4, NS], F32); hkvo_s = H("kvo_s")
        ksn = AR.alloc([128, NS], BF16); hksn = H("ksn")
        kwn = AR.alloc([128, NS], BF16); hkwn = H("kwn")
        pk, phk = rr()
        for i, c0 in enumerate((512, 640, 768, 1024)):
            pF(pk[:, i * 16:(i + 1) * 16], c0, 128, lambda kc, i=i: WJ(phk, i == 0 and kc == 0))
        CP("act", kvo_s[:].rearrange("p a b -> p (a b)"), pk[:, 0:64], R=[phk], W=[hkvo_s])
        CP("act", ksn[:], pk[:, 32:48], R=[phk], W=[hksn])
        CP("act", kwn[:], pk[:, 48:64], R=[phk], W=[hkwn])
        DMA("sp", kvT_s, kvo_s[:], key=hkvo_s, R=[hkvo_s])
        vto_s = AR.alloc([NS, 2, 128], F32); hvto_s = H("vto_s")
        Vsn = AR.alloc([NS, 2, 65], BF16); hVsn = H("Vsn")
        Vwn = AR.alloc([NS, 2, 65], BF16); hVwn = H("Vwn")
        Vsn_m = AR.alloc([NS, 4, 2, 65], BF16); hVsn_m = H("Vsn_m")
        Vwn_m = AR.alloc([NS, 4, 2, 65], BF16); hVwn_m = H("Vwn_m")
        pv, phv = rr()
        pTm(pv[0:NS, 0:128], 896, 128, lambda kc: WJ(phv, kc == 0))
        pTm(pv[0:NS, 128:256], 1152, 128, lambda kc: dict(J=[phv]))
        CP("act", vto_s[:].rearrange("p a b -> p (a b)"), pv[0:NS, 0:256], R=[phv], W=[hvto_s])
        DMA("sp", vtok_s, vto_s[:], key=hvto_s, R=[hvto_s])
        MS("pool", Vsn[:], 1.0, W=[hVsn])
        MS("pool", Vwn[:], 1.0, W=[hVwn])
        CP("act", Vsn[:, :, 0:64], pv[0:NS, 0:128].rearrange("p (g d) -> p g d", g=2), R=[phv], W=[hVsn])
        CP("act", Vwn[:, :, 0:64], pv[0:NS, 128:256].rearrange("p (g d) -> p g d", g=2), R=[phv], W=[hVwn])
        for bl in range(4):
            TS("dve", Vsn_m[:, bl, :, :], Vsn[:], seqm_f[:, bl:bl + 1], None, ALU.mult, R=[hVsn, hSeqm], **WJ(hVsn_m, bl == 0))
            TS("dve", Vwn_m[:, bl, :, :], Vwn[:], seqm_f[:, bl:bl + 1], None, ALU.mult, R=[hVwn, hSeqm], **WJ(hVwn_m, bl == 0))
        if cut == 99.2:
            return
        gate_s = AR.alloc([24, NS], F32); hgate_s = H("gate_s")
        grow_s = AR.alloc([128, 24 * NS], F32); hgrow_s = H("grow_s")
        pgt, phgt = rr()
        pF(pgt[0:24, 0:NS], C_GT, 24, lambda kc: WJ(phgt, kc == 0))
        ACT(gate_s[:], pgt[0:24, 0:NS], AF.Exp, scale=-1.0, R=[phgt], W=[hgate_s])
        TS("dve", gate_s[:], gate_s[:], 1.0, None, ALU.add, R=[hgate_s], W=[hgate_s])
        RCP("dve", gate_s[:], gate_s[:], R=[hgate_s], W=[hgate_s])
        DMA("sp", gscr_s, gate_s[:], key=hgate_s, R=[hgate_s], W=[H_gscr_s])
        DMA("sp", grow_s[64:65, :], gscr_s.rearrange("(o a) b -> o (a b)", o=1), key=hgrow_s, R=[H_gscr_s], W=[hgrow_s])
        if cut == 99.3:
            return
        HKSn = AR.alloc([NS, 4, 2, 16], BF16)
        hHKSn = [[GHSn for g in range(2)] for bl in range(4)]
        for bl in range(4):
            for g in range(2):
                hankel(HKSn[0:NS, bl, g, :].rearrange("p (r q) -> p r q", r=4), g, OFFA - 15 + 4 * bl, 1, 4, hHKSn[bl][g], npart=NS)
        if cut == 99.4:
            return
        lrT_s = AR.alloc([64, NS], BF16); hLr_s = H("lrT_s")
        MS("pool", lrT_s[:], 0.0, W=[hLr_s])
        MS("pool", lrT_s[32:33, :], 1.0, J=[hLr_s])
        la_s = AR.alloc([NS, 256], F32); hla_s = H("la_s")
        esuf_s = AR.alloc([NS, 256], F32); hesuf_s = H("esuf_s")
        ecum_s = AR.alloc([128, 2 * NS], F32); hecum_s = H("ecum_s")
        einv_s = AR.alloc([128, 2 * NS], F32); heinv_s = H("einv_s")
        keT_s = AR.alloc([128, 2 * NS], BF16); hke_s = H("keT_s")
        qeT_s = AR.alloc([128, 2 * NS], BF16); hqe_s = H("qeT_s")
        kd_s = AR.alloc([NS, 256], BF16); hkd_s = H("kd_s")
        KDM_off = [AR.top]
        kdm = AR.alloc([NS, 4, 256], BF16); hkdm = H("kdm")
        vg_s = AR.alloc([NS, 512], BF16); hvg_s = H("vg_s")
        attT_s = AR.alloc([NS, NS], BF16); hatt_s = H("attT_s")
        S0_off = [AR.top]
        S0 = AR.alloc([128, 4, 2, 128], F32); hS0 = H("S0")
        S0bf = AR.alloc([128, 4, 2, 128], BF16); hS0bf = H("S0bf")
        omixg_s = AR.alloc([128, 4, NS], BF16); homixg_s = H("omixg_s")
        osq_s = AR.alloc([128, NS], BF16); hosq_s = H("osq_s")
        grs_s = AR.alloc([128, NS], F32); hgrs_s = H("grs_s")
        sg_s = AR.alloc([128, NS], F32); hsg_s = H("sg_s")
        t1_s = AR.alloc([128, NS], F32); ht1_s = H("t1_s")
        first = True
        for bl in range(4):
            for hd in range(4):
                DMA("sp", S0[(hd % 2) * 64:(hd % 2) * 64 + 64, bl, hd // 2, :], sgla[bl, hd], key=hS0, **WJ(hS0, first))
                first = False
        CP("pool", S0bf[:], S0[:], R=[hS0], W=[hS0bf])
        pb, ph = rr()
        pF(pb[0:16, 0:NS], C_LR, 16, lambda kc: WJ(ph, kc == 0))
        CP("act", lrT_s[0:16, :], pb[0:16, 0:NS], R=[ph], W=[hLr_s])
        pz, phz = rr()
        MM(pz[0:NS, 0:256], lrT_s[0:33, :], wgk_bf[:], R=[hLr_s, hWgk], W=[phz])
        ACT(la_s[:], pz[0:NS, 0:256], AF.Exp, scale=-1.0, R=[phz], W=[hla_s])
        ACT(la_s[:], la_s[:], AF.Ln, bias=1.0, R=[hla_s], W=[hla_s])
        psf, phs = rr()
        MM(psf[0:NS, 0:256], triu16_f[:], la_s[:], R=[hTriU16, hla_s], W=[phs])
        ACT(esuf_s[:], psf[0:NS, 0:256], AF.Exp, scale=-1.0 / 16, R=[phs], W=[hesuf_s])
        pct, phc = rr()
        for ch in range(2):
            MM(pct[:, ch * NS:(ch + 1) * NS], la_s[:, ch * 128:(ch + 1) * 128], tri16_f[:], R=[hla_s, hTri16], **WJ(phc, ch == 0))
        ACT(ecum_s[:], pct[:, 0:2 * NS], AF.Exp, scale=-1.0 / 16, R=[phc], W=[hecum_s])
        ACT(einv_s[:], pct[:, 0:2 * NS], AF.Exp, scale=1.0 / 16, R=[phc], W=[heinv_s])
        pk, phk = rr()
        for ch in range(2):
            pF(pk[:, ch * NS:(ch + 1) * NS], C_KG + ch * 128, 128, lambda kc, ch=ch: WJ(phk, ch == 0 and kc == 0))
        TT("dve", keT_s[:], pk[:, 0:2 * NS], einv_s[:], ALU.mult, R=[phk, heinv_s], W=[hke_s])
        pq, phq = rr()
        for ch in range(2):
            pF(pq[:, ch * NS:(ch + 1) * NS], C_QG + ch * 128, 128, lambda kc, ch=ch: WJ(phq, ch == 0 and kc == 0))
        STT("dve", qeT_s[:], pq[:, 0:2 * NS], 0.125, ecum_s[:], ALU.mult, ALU.mult, R=[phq, hecum_s], W=[hqe_s])
        pkt, phkt = rr()
        pTm(pkt[0:NS, 0:256], C_KG, 256, lambda kc: WJ(phkt, kc == 0))
        TT("dve", kd_s[:], pkt[0:NS, 0:256], esuf_s[:], ALU.mult, R=[phkt, hesuf_s], W=[hkd_s])
        for bl in range(4):
            TS("dve", kdm[:, bl, :], kd_s[:], seqm_f[:, bl:bl + 1], None, ALU.mult, R=[hkd_s, hSeqm], **WJ(hkdm, bl == 0))
        pv2, phv2 = rr()
        pTm(pv2[0:NS, 0:512], C_VG, 512, lambda kc: WJ(phv2, kc == 0))
        CP("act", vg_s[:], pv2[0:NS, 0:512], R=[phv2], W=[hvg_s])
        if cut == 99.5:
            return
        for hd in range(4):
            ch = hd // 2
            pp = slice(64 * (hd % 2), 64 * (hd % 2) + 64)
            cs = slice(ch * NS, (ch + 1) * NS)
            pa, pha = rr()
            MM(pa[0:NS, 0:NS], keT_s[pp, cs], qeT_s[pp, cs], R=[hke_s, hqe_s], W=[pha])
            TT("dve", attT_s[:], pa[0:NS, 0:NS], tri16_f[:], ALU.mult, R=[pha, hTri16], W=[hatt_s])
            po, pho = rr()
            MM(po[:, 0:NS], vg_s[:, hd * 128:(hd + 1) * 128], attT_s[:], start=True, stop=False, R=[hvg_s, hatt_s], W=[pho])
            for bl in range(4):
                MM(po[:, bl * 4:bl * 4 + 4], S0bf[pp, bl, ch, :], qeT_s[pp, ch * NS + bl * 4:ch * NS + bl * 4 + 4],
                   start=False, stop=(bl == 3), R=[hS0bf, hqe_s], J=[pho])
            ACT(osq_s[:], po[:, 0:NS], AF.Square, R=[pho], W=[hosq_s])
            pn, phn = rr()
            MM(pn[:, 0:NS], ones_bf[:], osq_s[:], R=[hOnes, hosq_s], W=[phn])
            ACT(grs_s[:], pn[:, 0:NS], AF.Ln, bias=EPS, scale=1.0 / 128, R=[phn], W=[hgrs_s])
            ACT(grs_s[:], grs_s[:], AF.Exp, scale=-0.5, R=[hgrs_s], W=[hgrs_s])
            STT("dve", t1_s[:], po[:, 0:NS], glan_f[:, 0:1], grs_s[:], ALU.mult, ALU.mult, R=[pho, hGlan, hgrs_s], W=[ht1_s])
            pg, phg = rr()
            pF(pg[:, 0:NS], C_GG + hd * 128, 128, lambda kc: WJ(phg, kc == 0))
            ACT(sg_s[:], pg[:, 0:NS], AF.Exp, scale=-1.0, R=[phg], W=[hsg_s])
            TS("dve", sg_s[:], sg_s[:], 1.0, None, ALU.add, R=[hsg_s], W=[hsg_s])
            RCP("dve", sg_s[:], sg_s[:], R=[hsg_s], W=[hsg_s])
            TT("dve", sg_s[:], sg_s[:], pg[:, 0:NS], ALU.mult, R=[hsg_s, phg], W=[hsg_s])
            TT("dve", omixg_s[:, hd, :], t1_s[:], sg_s[:], ALU.mult, R=[ht1_s, hsg_s], **WJ(homixg_s, hd == 0))
        if cut == 99.6:
            return
        for bl in range(4):
            for ch in range(2):
                pu, phu = rr()
                MM(pu[:, 0:256], kdm[:, bl, ch * 128:(ch + 1) * 128], vg_s[:, ch * 256:(ch + 1) * 256], R=[hkdm, hvg_s], W=[phu])
                for hh in range(2):
                    pp = slice(64 * hh, 64 * hh + 64)
                    col = ch * NS + bl * 4 + 3
                    STT("dve", S0[pp, bl, ch, :], S0[pp, bl, ch, :], ecum_s[pp, col:col + 1], pu[pp, hh * 128:(hh + 1) * 128],
                        ALU.mult, ALU.add, R=[hS0, hecum_s, phu], W=[hS0])
        for bl in range(4):
            DMA("sp", gla_s[bl], S0[:, bl, :, :], key=hS0, R=[hS0])
        if cut == 99.7:
            return
        DMA("sp", win_s, cwin[:, 4:512, :], key=H("wincp"))

        if cut == 100:
            return
        KsT_s = AR.alloc([128, PAST], BF16); hKsT_s = H("KsT_s")
        Vs_s = AR.alloc([128, NPG, 2, 65], BF16); hVs_s = H("Vs_s")
        rawT_s = AR.alloc([128, 2, PAST], BF16); hraw_s = H("rawT_s")
        stg = [AR.alloc([128, 2, 512], F32)]; hstg = [H("stg0")]
        _s0base = KDM_off[0]
        assert S0_off[0] + 4096 + 2048 - _s0base >= 2 * 4096
        for i in range(2):
            stg.append(nc.alloc_sbuf_tensor_at("stgx%d" % i, [128, 2, 512], F32, offset=_s0base + i * 4096))
            hx_ = H("stgx%d" % i)
            for hd_ in (hkdm, hvg_s, hatt_s, hS0, hS0bf):
                hx_.writers += list(hd_.writers)
                hx_.readers += list(hd_.readers)
            hstg.append(hx_)
        NSTG = len(stg)
        KcT_s = AR.alloc([128, 512], BF16); hKcT_s = H("KcT_s")
        geV_s = AR.alloc([128, 512], BF16); hgeV_s = H("geV_s")
        ge_s = AR.alloc([128, 512], BF16); hge_s = H("ge_s")
        Vc_s = AR.alloc([128, 4, 2, 65], BF16); hVc_s = H("Vc_s")
        gx_s = AR.alloc([128, 512], F32); hgx_s = H("gx_s")
        gu_s = AR.alloc([128, 512], F32); hgu_s = H("gu_s")
        Pc_s = AR.alloc([128, 64], BF16); hPc_s = H("Pc_s")
        pn_s = AR.alloc([128, 64], F32); hpn_s = H("pn_s")
        PsT_s = AR.alloc([128, 4, 4], BF16); hPsT_s = H("PsT_s")
        P_s = AR.alloc([128, 1024], BF16); hP_s = H("P_s")
        R_s = AR.alloc([128, 1024], BF16); hR_s = H("R_s")
        KwT_s = AR.alloc([128, 512], BF16); hKwT_s = H("KwT_s")
        Vw_s = AR.alloc([128, 4, 2, 65], BF16); hVw_s = H("Vw_s")
        wstg = [AR.alloc([128, 256], F32) for _ in range(2)]; hwstg = [H("wstg0"), H("wstg1")]
        Pw_s = AR.alloc([128, 64], BF16); hPw_s = H("Pw_s")
        Pn16 = AR.alloc([NS, NS], BF16); hPn16 = H("Pn16")
        crow_s = AR.alloc([128, NS], F32); hcrow_s = H("crow_s")
        num_s = AR.alloc([64, NS], F32); hnum_s = H("num_s")
        onsa_s = AR.alloc([64, 2, NS], F32); honsa_s = H("onsa_s")
        onsab_s = AR.alloc([64, 2, 4, NS], BF16); honsab_s = H("onsab_s")
        score_s = AR.alloc([4, 128], F32); hscore_s = H("score_s")
        sc2_s = AR.alloc([4, 128], F32); hsc2_s = H("sc2_s")
        m8_s = AR.alloc([4, 8], F32); hm8_s = H("m8_s")
        m8b_s = AR.alloc([4, 8], F32); hm8b_s = H("m8b_s")
        nsel_s = AR.alloc([4, 128], BF16); hnsel_s = H("nsel_s")
        nselT_s = AR.alloc([128, 4], F32); hnselT_s = H("nselT_s")
        h1s = AR.alloc([128, KC, NS], F32); hh1s = H("h1s")
        MS("pool", Vs_s[:], 1.0, W=[hVs_s])
        MS("pool", Vw_s[:], 1.0, W=[hVw_s])
        MS("pool", KcT_s[:], 0.0, W=[hKcT_s])
        MS("pool", geV_s[:], 0.0, W=[hgeV_s])
        ACCs = [PB[0], PB[1]]
        hACCs = [PBH[0], PBH[1]]
        accs_rot = [0]

        def next_acc():
            i = accs_rot[0] % 2
            accs_rot[0] += 1
            return ACCs[i], hACCs[i]

        def finish_s(g, br, bl, acc, hacc, first_branch):
            TS("dve", crow_s[64:65, :], acc[64:65, 0:NS], 1e-30, None, ALU.max, R=[hacc], W=[hcrow_s])
            RCP("dve", crow_s[64:65, :], crow_s[64:65, :], R=[hcrow_s], W=[hcrow_s])
            off0 = (br * 8 + g * 4) * NS
            gv = grow_s[64:65, off0:off0 + 4 * NS].rearrange("o (r t) -> o r t", t=NS)[:, :, bl * 4:bl * 4 + 4]
            TT("dve", crow_s[64:65, :].rearrange("o (r q) -> o r q", r=4), crow_s[64:65, :].rearrange("o (r q) -> o r q", r=4), gv,
               ALU.mult, R=[hcrow_s, hgrow_s], W=[hcrow_s])
            pb, ph = rr()
            MM(pb[0:64, 0:NS], ones_f[64:65, 0:64], crow_s[64:65, :], R=[hOnesF, hcrow_s], W=[ph])
            CP("act", num_s[:], acc[0:64, 0:NS], R=[hacc], W=[hnum_s])
            if first_branch:
                TT("dve", onsa_s[:, g, :], num_s[:], pb[0:64, 0:NS], ALU.mult, R=[hnum_s, ph], W=[honsa_s])
            else:
                TT("dve", num_s[:], num_s[:], pb[0:64, 0:NS], ALU.mult, R=[hnum_s, ph], W=[hnum_s])
                TT("dve", onsa_s[:, g, :], onsa_s[:, g, :], num_s[:], ALU.add, R=[hnum_s, honsa_s], W=[honsa_s])
            if dbg and bl == 0:
                nm = "d_br%d%d" % (g, br)
                DBG[nm] = dout(nm, [64, NS])
                DMA("sp", DBG[nm], onsa_s[:, g, :], key=H("k" + nm), R=[honsa_s])

        def new_tile(g, bl, kn, hkn, Vm, hVm, acc, hacc):
            gp = slice(64 * g, 64 * g + 64)
            pn_, phn_ = rr()
            MM(pn_[0:NS, 0:NS], kn[gp, :], QTs[gp, bl * 16:(bl + 1) * 16], start=True, stop=False, R=[hkn, hQTs], W=[phn_])
            MM(pn_[0:NS, 0:NS], J_bf[0:NS, 128 - NS:128], HKSn[0:NS, bl, g, :], start=False, stop=True, R=[hJ, hHKSn[bl][g]], J=[phn_])
            ACT(Pn16[:], pn_[0:NS, 0:NS], AF.Exp, R=[phn_], W=[hPn16])
            if dbg and bl == 0 and g == 0 and kn is ksn:
                DBG["d_pn16"] = dout("d_pn16", [NS, NS])
                CP("act", crow_s[0:NS, 0:NS], pn_[0:NS, 0:NS], R=[phn_], W=[hcrow_s])
                DMA("sp", DBG["d_pn16"], crow_s[0:NS, 0:NS], key=H("kpn16"), R=[hcrow_s])
            MM(acc[0:65, 0:NS], Vm[:, bl, g, :], Pn16[:], start=False, stop=True, R=[hVm, hPn16], J=[hacc])

        pcol_f, hPcol = load([128, 1], F32, cPcol, own=True)
        ptab_i = AR.alloc([128, 4 * NPG], I32); hptab_i = H("ptab_i")
        DMA("sp", ptab_i[:], ptab.partition_broadcast(128), key=hptab_i, W=[hptab_i])
        idx_f = AR.alloc([128, 4 * NPG], F32); hidx_f = H("idx_f")
        idx_i = AR.alloc([128, 2 * NPG], I32); hidx_i = H("idx_i")
        CP("dve", idx_f[:], ptab_i[:], R=[hptab_i], W=[hidx_f])
        idx_f2 = AR.alloc([128, 2 * NPG], F32); hidx_f2 = H("idx_f2")
        CP("dve", idx_f2[0:64, :], idx_f[0:64, :].rearrange("p (a j) -> p a j", j=2)[:, :, 0], R=[hidx_f], W=[hidx_f2])
        CP("dve", idx_f2[64:128, :], idx_f[64:128, :].rearrange("p (a j) -> p a j", j=2)[:, :, 1], R=[hidx_f], J=[hidx_f2])
        TS("dve", idx_f2[:], idx_f2[:], 64.0, pcol_f[:, 0:1], ALU.mult, ALU.add, R=[hidx_f2, hPcol], W=[hidx_f2])
        CP("dve", idx_i[:, 0:2 * NPG], idx_f2[:], R=[hidx_f2], W=[hidx_i])
        cache_rows = cache.rearrange("n (p j) d -> (n p) (j d)", j=2)

        def gather_page(dst, idx):
            def f(e):
                return e.indirect_dma_start(out=dst, out_offset=None, in_=cache_rows,
                                            in_offset=bass.IndirectOffsetOnAxis(ap=idx_i[:, idx:idx + 1], axis=0))
            return f

        for bl in range(nseq):
            for pair in range(NPG // 2):
                sb_, hsb_ = stg[pair % NSTG], hstg[pair % NSTG]
                S.add("pool", gather_page(sb_[:].rearrange("p j d -> p (j d)"), bl * (NPG // 2) + pair), reads=[hidx_i], writes=[hsb_], dma=True, key=hsb_)
                for j in range(2):
                    tile_ = 2 * pair + j
                    ptx, phtx = rr()
                    for ci in range(3):
                        TR(ptx[:, ci * 128:(ci + 1) * 128], sb_[:, j, ci * 128:(ci + 1) * 128], identf[:], R=[hsb_, hIdF], **WJ(phtx, ci == 0))
                    CP("act", rawT_s[:, :, 256 * pair + j:256 * pair + j + 255:2], ptx[:, 0:256].rearrange("p (t n) -> p t n", t=2), R=[phtx],
                       **WJ(hraw_s, tile_ == 0))
                    CP("dve", KsT_s[:, tile_ * 128:(tile_ + 1) * 128], ptx[:, 256:384], R=[phtx], **WJ(hKsT_s, tile_ == 0))
                    CP("pool", Vs_s[:, tile_, :, 0:64], sb_[:, j, 384:512].rearrange("p (g d) -> p g d", g=2), R=[hsb_], **WJ(hVs_s, tile_ == 0))
                pg = 2 * pair + 1
                if (pg + 1) % 8 == 0:
                    gi = pg // 8
                    n0 = max(0, 64 * gi - 1)
                    cnt = 64 * gi + 62 - n0 + 1
                    for t in range(2):
                        pc, phc_ = rr()
                        for sx in range(32):
                            c0_ = 16 * n0 + sx
                            MM(pc[:, 0:cnt], w1_bf[:, t, sx, :], rawT_s[:, t, c0_:c0_ + 16 * (cnt - 1) + 1:16], start=(sx == 0), stop=(sx == 31),
                               R=[hW1, hraw_s], **WJ(phc_, sx == 0))
                        TS("dve", gx_s[:, 0:cnt], pc[:, 0:cnt], pew1[:, t:t + 1], None, ALU.add, R=[phc_, hPew1], W=[hgx_s])
                        TT("dve", gu_s[:, 0:cnt], gx_s[:, 0:cnt], gx_s[:, 0:cnt], ALU.mult, R=[hgx_s], W=[hgu_s])
                        TS("dve", gu_s[:, 0:cnt], gu_s[:, 0:cnt], 0.044715, 1.0, ALU.mult, ALU.add, R=[hgu_s], W=[hgu_s])
                        TT("dve", gu_s[:, 0:cnt], gu_s[:, 0:cnt], gx_s[:, 0:cnt], ALU.mult, R=[hgu_s, hgx_s], W=[hgu_s])
                        ACT(gu_s[:, 0:cnt], gu_s[:, 0:cnt], AF.Exp, scale=-1.5957691216057308, R=[hgu_s], W=[hgu_s])
                        TS("dve", gu_s[:, 0:cnt], gu_s[:, 0:cnt], 1.0, None, ALU.add, R=[hgu_s], W=[hgu_s])
                        RCP("dve", gu_s[:, 0:cnt], gu_s[:, 0:cnt], R=[hgu_s], W=[hgu_s])
                        if t == 0:
                            TT("dve", ge_s[:, 0:cnt], gx_s[:, 0:cnt], gu_s[:, 0:cnt], ALU.mult, R=[hgx_s, hgu_s], W=[hge_s])
                            pk2, phk2 = rr()
                            MM(pk2[:, 0:cnt], w2_bf[:, 0, :], ge_s[:, 0:cnt], R=[hW2, hge_s], W=[phk2])
                            CP("act", KcT_s[:, n0:n0 + cnt], pk2[:, 0:cnt], R=[phk2], **WJ(hKcT_s, gi == 0))
                        else:
                            TT("dve", geV_s[:, n0:n0 + cnt], gx_s[:, 0:cnt], gu_s[:, 0:cnt], ALU.mult, R=[hgx_s, hgu_s], **WJ(hgeV_s, gi == 0))
            if cut == 101:
                return
            for wt in range(4):
                wb_, hwb_ = wstg[wt % 2], hwstg[wt % 2]
                DMA("sp", wb_[:], cwin[bl, wt * 128:(wt + 1) * 128, :], key=hwb_, W=[hwb_])
                ptx, phtx = rr()
                TR(ptx[:, 0:128], wb_[:, 0:128], identf[:], R=[hwb_, hIdF], W=[phtx])
                CP("act", KwT_s[:, wt * 128:(wt + 1) * 128], ptx[:, 0:128], R=[phtx], **WJ(hKwT_s, wt == 0))
                CP("pool", Vw_s[:, wt, :, 0:64], wb_[:, 128:256].rearrange("p (g d) -> p g d", g=2), R=[hwb_], **WJ(hVw_s, wt == 0))
            for ct in range(4):
                pvc, phvc = rr()
                MM(pvc[:, 0:128], geV_s[:, ct * 128:(ct + 1) * 128], w2_bf[:, 1, :], R=[hgeV_s, hW2], W=[phvc])
                TS("dve", Vc_s[:, ct, :, 0:64], pvc[:, 0:128].rearrange("p (g d) -> p g d", g=2), valS_f[:, ct:ct + 1], None, ALU.mult,
                   R=[phvc, hValS], **WJ(hVc_s, ct == 0))
                for g in range(2):
                    CP("pool", Vc_s[:, ct, g, 64:65], valS_f[:, ct:ct + 1], R=[hValS], J=[hVc_s])
            if cut == 103:
                return
            for g in range(2):
                gp = slice(64 * g, 64 * g + 64)
                qs = QTs[gp, bl * 16:(bl + 1) * 16]
                pS, phS = rr()
                MM(pS[:, 0:64], J_bf[:], HKC[:, g, :], start=True, stop=False, R=[hJ, GHS], W=[phS])
                for ct in range(4):
                    MM(pS[:, ct * 16:(ct + 1) * 16], KcT_s[gp, ct * 128:(ct + 1) * 128], qs, start=False, stop=(ct == 3),
                       R=[hKcT_s, hQTs], J=[phS])
                ACT(Pc_s[:], pS[:, 0:64], AF.Exp, R=[phS], W=[hPc_s])
                acc, hacc = next_acc()
                for ct in range(4):
                    MM(acc[0:65, 0:NS], Vc_s[:, ct, g, :], Pc_s[:, ct * 16:(ct + 1) * 16], start=(ct == 0), stop=(ct == 3),
                       R=[hVc_s, hPc_s], **WJ(hacc, ct == 0))
                TS("dve", crow_s[64:65, :], acc[64:65, 0:NS], 1e-30, None, ALU.max, R=[hacc], W=[hcrow_s])
                RCP("dve", crow_s[64:65, :], crow_s[64:65, :], R=[hcrow_s], W=[hcrow_s])
                pbc, phbc = rr()
                MM(pbc[:, 0:NS], ones_f[64:65, :], crow_s[64:65, :], R=[hOnesF, hcrow_s], W=[phbc])
                CP("act", pn_s[:, 0:NS], pbc[:, 0:NS], R=[phbc], W=[hpn_s])
                for ct in range(4):
                    TT("dve", pn_s[:, 16 + 0:16 + NS] if False else gx_s[:, ct * 16:(ct + 1) * 16], Pc_s[:, ct * 16:(ct + 1) * 16], pn_s[:, 0:NS], ALU.mult,
                       R=[hPc_s, hpn_s], **WJ(hgx_s, ct == 0))

                for ct in range(4):
                    def _reds(e, ct=ct):
                        with nc.allow_low_precision("fp32 accumulate inside, bf16 store"):
                            return e.tensor_reduce(out=PsT_s[:, ct, :], in_=gx_s[:, ct * 16:(ct + 1) * 16].rearrange("p (r q) -> p q r", r=4),
                                                   axis=AX.X, op=ALU.add)
                    S.add("dve", _reds, reads=[hgx_s], **({"writes": [hPsT_s]} if ct == 0 else {"joins": [hPsT_s]}))
                pim, phim = rr()
                for ct in range(4):
                    MM(pim[0:4, 0:128], PsT_s[:, ct, :], As_bf[:, ct, :], start=(ct == 0), stop=(ct == 3), R=[hPsT_s, hAs], **WJ(phim, ct == 0))
                TT("dve", score_s[:], pim[0:4, 0:128], sels_f[:, 0, :], ALU.mult, R=[phim, hSels], W=[hscore_s])
                TT("dve", score_s[:], score_s[:], sels_f[:, 1, :], ALU.add, R=[hscore_s, hSels], W=[hscore_s])
                S.add("dve", lambda e: e.max(out=m8_s[:], in_=score_s[:]), reads=[hscore_s], writes=[hm8_s])
                S.add("dve", lambda e: e.match_replace(out=sc2_s[:], in_to_replace=m8_s[:], in_values=score_s[:], imm_value=-1e30),
                      reads=[hscore_s, hm8_s], writes=[hsc2_s])
                S.add("dve", lambda e: e.max(out=m8b_s[:], in_=sc2_s[:]), reads=[hsc2_s], writes=[hm8b_s])
                TS("dve", sc2_s[:], score_s[:], m8b_s[:, 6:7], None, ALU.is_ge, R=[hscore_s, hm8b_s], W=[hsc2_s])
                TS("dve", nsel_s[:], sc2_s[:], -1.0, BIG, ALU.add, ALU.mult, R=[hsc2_s], W=[hnsel_s])
                TR(PT[:, 0:4], nsel_s[:], ident_bf[0:4, 0:4], R=[hnsel_s, hId], W=[H_PT])
                CP("act", nselT_s[:], PT[:, 0:4], R=[H_PT], W=[hnselT_s])
                TT("dve", R_s[:].rearrange("p (k r q) -> p k r q", k=NPG, r=4),
                   nselT_s[:].unsqueeze(1).unsqueeze(1).to_broadcast([128, NPG, 4, 4]),
                   bm_f[:].unsqueeze(2).unsqueeze(3).to_broadcast([128, NPG, 4, 4]), ALU.mult,
                   R=[hnselT_s, hBm], W=[hR_s])
                finish_s(g, 0, bl, acc, hacc, True)
                if cut == 104:
                    return
                p1, ph1 = rr()
                p2, ph2 = rr()
                MM(p1[:, :], half_bf[:], R_s[:, 0:512], start=True, stop=False, R=[hHalf, hR_s], W=[ph1])
                MM(p2[:, :], half_bf[:], R_s[:, 512:1024], start=True, stop=False, R=[hHalf, hR_s], W=[ph2])
                for pg in range(NPG):
                    pp_, php_ = (p1, ph1) if pg < 32 else (p2, ph2)
                    c0 = (pg % 32) * 16
                    MM(pp_[:, c0:c0 + 16], KsT_s[gp, pg * 128:(pg + 1) * 128], qs, start=False, stop=(pg == 31),
                       R=[hKsT_s, hQTs], J=[php_])
                for j in range(2):
                    MM(p2[:, 480 + 16 * j:496 + 16 * j], J_bf[:], HKS2[:, j, g, :], start=False, stop=(j == 1), R=[hJ, GHS], J=[ph2])
                ACT(P_s[:, 0:512], p1[:, :], AF.Exp, R=[ph1], W=[hP_s])
                ACT(P_s[:, 512:1024], p2[:, :], AF.Exp, R=[ph2], J=[hP_s])
                if dbg and bl == 0 and g == 0:
                    DBG["d_S2"] = dout("d_S2", [128, 512])
                    CP("act", gx_s[:, 0:512], p2[:, :], R=[ph2], W=[hgx_s])
                    DMA("sp", DBG["d_S2"], gx_s[:, 0:512], key=H("kS2"), R=[hgx_s])
                    DBG["d_nselT"] = dout("d_nselT", [128, 4])
                    DMA("sp", DBG["d_nselT"], nselT_s[:], key=H("knselT"), R=[hnselT_s])
                acc, hacc = next_acc()
                for pg in range(NPG):
                    MM(acc[0:65, 0:NS], Vs_s[:, pg, g, :], P_s[:, pg * 16:(pg + 1) * 16], start=(pg == 0), stop=False,
                       R=[hVs_s, hP_s], **WJ(hacc, pg == 0))
                new_tile(g, bl, ksn, hksn, Vsn_m, hVsn_m, acc, hacc)
                finish_s(g, 1, bl, acc, hacc, False)
                if cut == 105:
                    return
                pW, phW = rr()
                MM(pW[:, 0:64], J_bf[:], HKW[:, g, :], start=True, stop=False, R=[hJ, GHS], W=[phW])
                for wt in range(4):
                    MM(pW[:, wt * 16:(wt + 1) * 16], KwT_s[gp, wt * 128:(wt + 1) * 128], qs, start=False, stop=(wt == 3),
                       R=[hKwT_s, hQTs], J=[phW])
                ACT(Pw_s[:], pW[:, 0:64], AF.Exp, R=[phW], W=[hPw_s])
                if dbg and bl == 0 and g == 0:
                    DBG["d_pW"] = dout("d_pW", [128, 64])
                    CP("act", gu_s[:, 0:64], pW[:, 0:64], R=[phW], W=[hgu_s])
                    DMA("sp", DBG["d_pW"], gu_s[:, 0:64], key=H("kpW"), R=[hgu_s])
                acc, hacc = next_acc()
                for wt in range(4):
                    MM(acc[0:65, 0:NS], Vw_s[:, wt, g, :], Pw_s[:, wt * 16:(wt + 1) * 16], start=(wt == 0), stop=False,
                       R=[hVw_s, hPw_s], **WJ(hacc, wt == 0))
                new_tile(g, bl, kwn, hkwn, Vwn_m, hVwn_m, acc, hacc)
                if dbg and bl == 0 and g == 0:
                    DBG["d_accW"] = dout("d_accW", [65, NS])
                    CP("act", gx_s[0:65, 0:NS], acc[0:65, 0:NS], R=[hacc], W=[hgx_s])
                    DMA("sp", DBG["d_accW"], gx_s[0:65, 0:NS], key=H("kaccW"), R=[hgx_s])
                finish_s(g, 2, bl, acc, hacc, False)
            CP("act", onsab_s[:, :, :, bl * 4:bl * 4 + 4], onsa_s[:].rearrange("p g (r q) -> p g r q", r=4), R=[honsa_s],
               **WJ(honsab_s, bl == 0))
        if nseq < 4:
            pass
        pw, phw = rr()
        for dc in range(KC):
            osl = pw[:, dc * NS:(dc + 1) * NS]
            n = 0
            for g in range(2):
                for r in range(4):
                    MM(osl, wonsa_bf[:, g * 4 + r, dc * 128:(dc + 1) * 128], onsab_s[:, g, r, :], start=(n == 0), stop=False,
                       R=[hWon, honsab_s], **WJ(phw, dc == 0 and n == 0))
                    n += 1
            for hd in range(4):
                MM(osl, wogla_bf[:, hd, dc * 128:(dc + 1) * 128], omixg_s[:, hd, :], start=False, stop=(hd == 3),
                   R=[hWog, homixg_s], J=[phw])
        TT("dve", h1s[:], xs_s[:], pw[:, 0:KC * NS].rearrange("p (c q) -> p c q", c=KC), ALU.add, R=[hxs_s, phw], W=[hh1s])
        DMA("sp", h1_scr[:, :, NT:NT + NS], h1s[:], key=hh1s, R=[hh1s], W=[H_h1[NOWN]])
        if dbg:
            DBG["d_h1s"] = dout("d_h1s", [128, KC, NS])
            DMA("sp", DBG["d_h1s"], h1s[:], key=H("kd_h1s"), R=[hh1s])
        DBG["S_END"] = AR.top

    if do_sample:
        phase_s()
        S.barrier()
        AR.top = MIX_END


    cur_grp[0] = H("GA")
    E_bf = AR.alloc([128, TOK], BF16); hE = H("E")
    for i in range(3):
        a, b = i * 1408, (i + 1) * 1408
        DMA("pool", E_bf[:, a:b], cE[:, a:b], key=hE, W=[hE] if i == 0 else (), J=() if i == 0 else [hE])
    A_bf, hA = load([128, 2, 128], BF16, cA)
    HK = AR.alloc([128, 3, 2, 512], BF16)
    hHK = [[GHA for g in range(2)] for i in range(3)]
    for g in range(2):
        hankel(HK[:, 0, g, :].rearrange("p (r q) -> p r q", r=4), g, OFFA - 127, 1, 128, hHK[0][g])
        hankel(HK[:, 1, g, :].rearrange("p (r q) -> p r q", r=4), g, OFFA + 1, 1, 128, hHK[1][g])
        hankel(HK[:, 2, g, :].rearrange("p (r q) -> p r q", r=4), g, LA + OFFW + 385, 1, 128, hHK[2][g])
    KsT = AR.alloc([128, TOK], BF16); hKs = [H("ks%d" % s) for s in range(NSLOT)]
    KwT = AR.alloc([128, 8 * 128], BF16); hKw = [H("kw%d" % s) for s in range(8)]
    Vs = AR.alloc([128, NSLOT, 2, 65], BF16); hVs = [H("vs%d" % s) for s in range(NSLOT)]
    Vw = AR.alloc([128, 8, 2, 65], BF16); hVw = [H("vw%d" % s) for s in range(8)]
    KcT = AR.alloc([128, 272], BF16); hKc = H("kc")
    geV = AR.alloc([128, 272], BF16); hGeV = H("gev")
    Vc = AR.alloc([128, 2, 2, 65], BF16); hVc = H("vc")
    rawT = AR.alloc([128, 2, 144], BF16); hRawP = H("rawp"); hRawC = H("rawc")
    Sst = AR.alloc([128, 2, 128], F32); hS = H("S")
    Sbf = AR.alloc([128, 2, 128], BF16); hSbf = H("Sbf")
    lrT = AR.alloc([64, 128], BF16); hLr = H("lrT")
    MS("pool", KsT[:], 0.0, W=hKs)
    MS("pool", KwT[:], 0.0, W=hKw)
    MS("pool", Vs[:], 0.0, W=hVs)
    MS("pool", Vw[:], 0.0, W=hVw)
    MS("pool", KcT[:], 0.0, W=[hKc])
    MS("pool", geV[:], 0.0, W=[hGeV])
    MS("pool", Vc[:], 0.0, W=[hVc])
    MS("pool", rawT[:], 0.0, W=[hRawP, hRawC])
    MS("pool", Sst[:], 0.0, W=[hS])
    MS("pool", Sbf[:], 0.0, W=[hSbf])
    MS("pool", lrT[:], 0.0, W=[hLr])
    MS("pool", lrT[32:33, :], 1.0, J=[hLr])

    xs = [AR.alloc([128, KC, 128], F32) for _ in range(2)]; hxs = [H("xs0"), H("xs1")]
    sq = AR.alloc([128, KC, 128], BF16); hsq = H("sq")
    rstd = AR.alloc([128, 128], F32); hrstd = H("rstd")
    xn = AR.alloc([128, KC, 128], BF16); hxn = H("xn")
    kvo = AR.alloc([128, 4, 128], F32); hkvo = H("kvo")
    vto = AR.alloc([128, 2, 128], F32); hvto = H("vto")
    la = AR.alloc([128, 256], F32); hla = H("la")
    esuf = AR.alloc([128, 256], F32); hesuf = H("esuf")
    ecum = AR.alloc([128, 256], F32); hecum = H("ecum")
    einv = AR.alloc([128, 256], F32); heinv = H("einv")
    keT = AR.alloc([128, 256], BF16); hke = H("keT")
    qeT = AR.alloc([128, 256], BF16); hqe = H("qeT")
    kd = AR.alloc([128, 256], BF16); hkd = H("kd")
    vg = AR.alloc([128, 512], BF16); hvg = H("vg")
    attT = AR.alloc([128, 128], BF16); hatt = H("attT")
    osq = AR.alloc([128, 128], BF16); hosq = H("osq")
    grs = AR.alloc([128, 128], F32); hgrs = H("grs")
    sg = AR.alloc([128, 128], F32); hsg = H("sg")
    t1 = AR.alloc([128, 128], F32); ht1 = H("t1")
    omixg = AR.alloc([128, 4, 128], BF16); homixg = H("omixg")
    gx = AR.alloc([128, 8], F32); hgx = H("gx")
    gu = AR.alloc([128, 8], F32); hgu = H("gu")
    ge = AR.alloc([128, 8], BF16); hge = H("ge")
    QT = AR.alloc([128, 512], BF16); hQT = H("QT")
    gate_sb = AR.alloc([24, 128], F32); hgate = H("gate")
    grow = AR.alloc([128, 24 * 128], F32); hgrow = H("grow")
    crow = AR.alloc([128, 512], F32); hcrow = H("crow")
    HKc = [AR.alloc([128, 2, 512], BF16) for _ in range(2)]; hHKc = [[H("hkc%d%d" % (i, g)) for g in range(2)] for i in range(2)]
    Pc = AR.alloc([128, 2, 512], BF16); hPc = [H("pc0"), H("pc1")]
    Pt = [AR.alloc([128, 512], BF16) for _ in range(3)]; hPt = [H("pt%d" % i) for i in range(3)]
    pn_f = AR.alloc([128, 512], F32); hpn = H("pn")
    PsT = AR.alloc([128, 2, 128], BF16); hPsT = H("PsT")
    selc = AR.alloc([128, 2, 128], F32); hselc = H("selc")
    score = AR.alloc([128, 128], F32); hscore = H("score")
    sc2 = AR.alloc([128, 128], F32); hsc2 = H("sc2")
    m8 = AR.alloc([128, 8], F32); hm8 = H("m8")
    m8b = AR.alloc([128, 8], F32); hm8b = H("m8b")
    nsel = AR.alloc([128, 128], BF16); hnsel = H("nsel")
    nselT = AR.alloc([128, 512], BF16); hnselT = H("nselT")
    numsb = AR.alloc([64, 512], F32); hnum = H("num")
    onsa = AR.alloc([64, 2, 512], F32); honsa = H("onsa")
    onsab = AR.alloc([64, 2, 512], BF16); honsab = H("onsab")
    h1t = AR.alloc([128, KC, 128], F32); hh1t = H("h1t")
    pt_rot = [0]

    def projF(out, c0, M, first_w):
        for kc in range(KC):
            MM(out, win_bf[:, kc, c0:c0 + M], xn[:, kc, :], start=(kc == 0), stop=(kc == KC - 1),
               R=[hWin, hxn], **first_w(kc))

    def projT(out, c0, N, first_w):
        for kc in range(KC):
            MM(out, xn[:, kc, :], win_bf[:, kc, c0:c0 + N], start=(kc == 0), stop=(kc == KC - 1),
               R=[hWin, hxn], **first_w(kc))

    def rms_rstd(src_ps, hsrc, n, scale):
        pass

    ACC = [PB[0], PB[1]]
    hACC = [PBH[0], PBH[1]]
    acc_rot = [0]

    def attn_tile(g, lhsK, hK, bias, mask, Vaug, hV, acc, hacc, first, last, keep=None, hkeep=None):
        gp = slice(64 * g, 64 * g + 64)
        pb, ph = rr()
        nmm = 1 + (bias is not None) + (mask is not None)
        k = 0
        MM(pb[:, :], lhsK, QT[gp, :], start=True, stop=(nmm == 1), R=[hK, hQT], W=[ph])
        k += 1
        if bias is not None:
            bap, bh = bias
            MM(pb[:, :], J_bf[:], bap, start=False, stop=(k == nmm - 1), R=[hJ, bh], J=[ph])
            k += 1
        if mask is not None:
            eap = mask
            MM(pb[:, :], eap, nselT[:, :], start=False, stop=True,
               R=[hE, hnselT], J=[ph])
        if keep is None:
            i = pt_rot[0] % 3
            pt_rot[0] += 1
            P, hP = Pt[i][:], hPt[i]
        else:
            P, hP = keep, hkeep
        ACT(P, pb[:, :], AF.Exp, R=[ph], W=[hP])
        MM(acc[0:65, :], Vaug, P, start=first, stop=last, R=[hV, hP], **(dict(W=[hacc]) if first else dict(J=[hacc])))

    def branch_finish(g, br, acc, hacc, first_branch):
        TS("dve", crow[64:65, :], acc[64:65, :], 1e-30, None, ALU.max, R=[hacc], W=[hcrow])
        RCP("dve", crow[64:65, :], crow[64:65, :], R=[hcrow], W=[hcrow])
        off = (br * 8 + g * 4) * 128
        TT("dve", crow[64:65, :], crow[64:65, :], grow[64:65, off:off + 512], ALU.mult, R=[hcrow, hgrow], W=[hcrow])
        pb, ph = rr()
        MM(pb[0:64, :], ones_f[64:65, 0:64], crow[64:65, :], R=[hOnesF, hcrow], W=[ph])
        CP("act", numsb[:, :], acc[0:64, :], R=[hacc], W=[hnum])
        if first_branch:
            TT("dve", onsa[:, g, :], numsb[:, :], pb[0:64, :], ALU.mult, R=[hnum, ph], W=[honsa])
        else:
            TT("dve", numsb[:, :], numsb[:, :], pb[0:64, :], ALU.mult, R=[hnum, ph], W=[hnum])
            TT("dve", onsa[:, g, :], onsa[:, g, :], numsb[:, :], ALU.add, R=[hnum, honsa], W=[honsa])

    for s in range(nslots):
        own = (s % 2 == 1)
        jo = s // 2
        xb, hx = xs[s % 2], hxs[s % 2]
        DMA("sp", xb[:], xT[:, :, s * 128:(s + 1) * 128], key=hx, W=[hx])
        ACT(sq[:], xb[:], AF.Square, R=[hx], W=[hsq])
        pb, ph = rr()
        for kc in range(KC):
            MM(pb[:, 0:128], ones_bf[:], sq[:, kc, :], start=(kc == 0), stop=(kc == KC - 1), R=[hOnes, hsq], **WJ(ph, kc == 0))
        ACT(rstd[:], pb[:, 0:128], AF.Ln, bias=EPS, scale=1.0 / D, R=[ph], W=[hrstd])
        ACT(rstd[:], rstd[:], AF.Exp, scale=-0.5, R=[hrstd], W=[hrstd])
        for kc in range(KC):
            STT("dve", xn[:, kc, :], xb[:, kc, :], norms_f[:, 0, kc:kc + 1], rstd[:], ALU.mult, ALU.mult,
                R=[hx, hrstd, hNorms], **WJ(hxn, kc == 0))
        if cut <= 1:
            break
        pb, ph = rr()
        for i, c0 in enumerate((512, 640, 768, 1024)):
            projF(pb[:, i * 128:(i + 1) * 128], c0, 128, lambda kc, i=i: WJ(ph, i == 0 and kc == 0))
        if cut == 1.1:
            break
        CP("act", rawT[:, :, 16:144], pb[:, 0:256].rearrange("p (t n) -> p t n", t=2), R=[ph], W=[hRawC])
        if cut == 1.2:
            break
        CP("act", KsT[:, s * 128:(s + 1) * 128], pb[:, 256:384], R=[ph], W=[hKs[s]])
        CP("act", KwT[:, (s % 8) * 128:(s % 8 + 1) * 128], pb[:, 384:512], R=[ph], W=[hKw[s % 8]])
        if own:
            CP("act", kvo[:].rearrange("p a b -> p (a b)"), pb[:, :], R=[ph], W=[hkvo])
            DMA("sp", kvT_p[:, :, jo * 128:(jo + 1) * 128], kvo[:], key=hkvo, R=[hkvo])
        if cut <= 2:
            break
        pb, ph = rr()
        projT(pb[:, 0:128], 896, 128, lambda kc: WJ(ph, kc == 0))
        projT(pb[:, 128:256], 1152, 128, lambda kc: dict(J=[ph]))
        CP("dve", Vs[:, s, :, 0:64], pb[:, 0:128].rearrange("p (g d) -> p g d", g=2), R=[ph], W=[hVs[s]])
        CP("dve", Vw[:, s % 8, :, 0:64], pb[:, 128:256].rearrange("p (g d) -> p g d", g=2), R=[ph], W=[hVw[s % 8]])
        for g in range(2):
            CP("pool", Vs[:, s, g, 64:65], valid_f[:, s:s + 1], R=[hValid], J=[hVs[s]])
            CP("pool", Vw[:, s % 8, g, 64:65], valid_f[:, s:s + 1], R=[hValid], J=[hVw[s % 8]])
        if own:
            CP("act", vto[:].rearrange("p a b -> p (a b)"), pb[:, 0:256], R=[ph], W=[hvto])
            DMA("sp", vtok_p[jo * 128:(jo + 1) * 128, :, :], vto[:], key=hvto, R=[hvto])
        if cut <= 3:
            break
        for t in range(2):
            pc, phc = rr()
            for sx in range(32):
                MM(pc[:, 0:8], w1_bf[:, t, sx, :], rawT[:, t, sx:sx + 113:16], start=(sx == 0), stop=(sx == 31),
                   R=[hW1, hRawP, hRawC], **WJ(phc, sx == 0))
            TS("dve", gx[:], pc[:, 0:8], pew1[:, t:t + 1], None, ALU.add, R=[phc, hPew1], W=[hgx])
            TT("dve", gu[:], gx[:], gx[:], ALU.mult, R=[hgx], W=[hgu])
            TS("dve", gu[:], gu[:], 0.044715, 1.0, ALU.mult, ALU.add, R=[hgu], W=[hgu])
            TT("dve", gu[:], gu[:], gx[:], ALU.mult, R=[hgu, hgx], W=[hgu])
            ACT(gu[:], gu[:], AF.Exp, scale=-1.5957691216057308, R=[hgu], W=[hgu])
            TS("dve", gu[:], gu[:], 1.0, None, ALU.add, R=[hgu], W=[hgu])
            RCP("dve", gu[:], gu[:], R=[hgu], W=[hgu])
            if t == 0:
                TT("dve", ge[:], gx[:], gu[:], ALU.mult, R=[hgx, hgu], W=[hge])
                pk2, phk2 = rr()
                MM(pk2[:, 0:8], w2_bf[:, 0, :], ge[:], R=[hW2, hge], W=[phk2])
                CP("act", KcT[:, 8 * s:8 * s + 8], pk2[:, 0:8], R=[phk2], W=[hKc])
            else:
                TT("dve", geV[:, 8 * s:8 * s + 8], gx[:], gu[:], ALU.mult, R=[hgx, hgu], W=[hGeV])
        CP("pool", rawT[:, :, 0:16], rawT[:, :, 128:144], R=[hRawC], W=[hRawP])
        if cut <= 4:
            break
        pb, ph = rr()
        projF(pb[0:16, 0:128], C_LR, 16, lambda kc: WJ(ph, kc == 0))
        CP("act", lrT[0:16, :], pb[0:16, 0:128], R=[ph], W=[hLr])
        pz, phz = rr()
        MM(pz[:, 0:256], lrT[0:33, :], wgk_bf[:], R=[hLr, hWgk], W=[phz])
        ACT(la[:], pz[:, 0:256], AF.Exp, scale=-1.0, R=[phz], W=[hla])
        ACT(la[:], la[:], AF.Ln, bias=1.0, R=[hla], W=[hla])
        psf, phs = rr()
        MM(psf[:, 0:256], triu_f[:], la[:], R=[hTriU, hla], W=[phs])
        ACT(esuf[:], psf[:, 0:256], AF.Exp, scale=-1.0 / 16, R=[phs], W=[hesuf])
        pct, phc = rr()
        for ch in range(2):
            MM(pct[:, ch * 128:(ch + 1) * 128], la[:, ch * 128:(ch + 1) * 128], tri_f[:], R=[hla, hTri], **WJ(phc, ch == 0))
        ACT(ecum[:], pct[:, 0:256], AF.Exp, scale=-1.0 / 16, R=[phc], W=[hecum])
        ACT(einv[:], pct[:, 0:256], AF.Exp, scale=1.0 / 16, R=[phc], W=[heinv])
        if cut <= 5:
            break
        pk, phk = rr()
        for ch in range(2):
            projF(pk[:, ch * 128:(ch + 1) * 128], C_KG + ch * 128, 128, lambda kc, ch=ch: WJ(phk, ch == 0 and kc == 0))
        TT("dve", keT[:], pk[:, 0:256], einv[:], ALU.mult, R=[phk, heinv], W=[hke])
        pkt, phkt = rr()
        projT(pkt[:, 0:256], C_KG, 256, lambda kc: WJ(phkt, kc == 0))
        TT("dve", kd[:], pkt[:, 0:256], esuf[:], ALU.mult, R=[phkt, hesuf], W=[hkd])
        pv, phv = rr()
        projT(pv[:, 0:512], C_VG, 512, lambda kc: WJ(phv, kc == 0))
        CP("act", vg[:], pv[:, 0:512], R=[phv], W=[hvg])
        if own:
            pq, phq = rr()
            for ch in range(2):
                projF(pq[:, ch * 128:(ch + 1) * 128], C_QG + ch * 128, 128, lambda kc, ch=ch: WJ(phq, ch == 0 and kc == 0))
            STT("dve", qeT[:], pq[:, 0:256], 0.125, ecum[:], ALU.mult, ALU.mult, R=[phq, hecum], W=[hqe])
            for hd in range(4):
                ch = hd // 2
                pp = slice(64 * (hd % 2), 64 * (hd % 2) + 64)
                cs = slice(ch * 128, (ch + 1) * 128)
                pa, pha = rr()
                MM(pa[:, 0:128], keT[pp, cs], qeT[pp, cs], R=[hke, hqe], W=[pha])
                TT("dve", attT[:], pa[:, 0:128], tri_f[:], ALU.mult, R=[pha, hTri], W=[hatt])
                po, pho = rr()
                MM(po[:, 0:128], vg[:, hd * 128:(hd + 1) * 128], attT[:], start=True, stop=False, R=[hvg, hatt], W=[pho])
                MM(po[:, 0:128], Sbf[pp, ch, :], qeT[pp, cs], start=False, stop=True, R=[hSbf, hqe], J=[pho])
                ACT(osq[:], po[:, 0:128], AF.Square, R=[pho], W=[hosq])
                pn, phn = rr()
                MM(pn[:, 0:128], ones_bf[:], osq[:], R=[hOnes, hosq], W=[phn])
                ACT(grs[:], pn[:, 0:128], AF.Ln, bias=EPS, scale=1.0 / 128, R=[phn], W=[hgrs])
                ACT(grs[:], grs[:], AF.Exp, scale=-0.5, R=[hgrs], W=[hgrs])
                STT("dve", t1[:], po[:, 0:128], glan_f[:, 0:1], grs[:], ALU.mult, ALU.mult, R=[pho, hGlan, hgrs], W=[ht1])
                pg, phg = rr()
                projF(pg[:, 0:128], C_GG + hd * 128, 128, lambda kc: WJ(phg, kc == 0))
                ACT(sg[:], pg[:, 0:128], AF.Exp, scale=-1.0, R=[phg], W=[hsg])
                TS("dve", sg[:], sg[:], 1.0, None, ALU.add, R=[hsg], W=[hsg])
                RCP("dve", sg[:], sg[:], R=[hsg], W=[hsg])
                TT("dve", sg[:], sg[:], pg[:, 0:128], ALU.mult, R=[hsg, phg], W=[hsg])
                TT("dve", omixg[:, hd, :], t1[:], sg[:], ALU.mult, R=[ht1, hsg], **WJ(homixg, hd == 0))
        if cut <= 6:
            break
        for ch in range(2):
            pu, phu = rr()
            MM(pu[:, 0:256], kd[:, ch * 128:(ch + 1) * 128], vg[:, ch * 256:(ch + 1) * 256], R=[hkd, hvg], W=[phu])
            for hh in range(2):
                pp = slice(64 * hh, 64 * hh + 64)
                STT("dve", Sst[pp, ch, :], Sst[pp, ch, :], ecum[pp, ch * 128 + 127:ch * 128 + 128], pu[pp, hh * 128:(hh + 1) * 128],
                    ALU.mult, ALU.add, R=[hS, hecum, phu], W=[hS])
        CP("pool", Sbf[:], Sst[:], R=[hS], W=[hSbf])
        if not own:
            continue
        pq, phq = rr()
        for r in range(4):
            projF(pq[:, r * 128:(r + 1) * 128], r * 128, 128, lambda kc, r=r: WJ(phq, r == 0 and kc == 0))
        TS("dve", QT[:], pq[:, :], 0.125, None, ALU.mult, R=[phq], W=[hQT])
        pgt, phgt = rr()
        projF(pgt[0:24, 0:128], C_GT, 24, lambda kc: WJ(phgt, kc == 0))
        ACT(gate_sb[:], pgt[0:24, 0:128], AF.Exp, scale=-1.0, R=[phgt], W=[hgate])
        TS("dve", gate_sb[:], gate_sb[:], 1.0, None, ALU.add, R=[hgate], W=[hgate])
        RCP("dve", gate_sb[:], gate_sb[:], R=[hgate], W=[hgate])
        DMA("sp", gscr[jo % 2], gate_sb[:], key=hgate, R=[hgate], W=[H_gscr[jo % 2]])
        DMA("sp", grow[64:65, :], gscr[jo % 2].rearrange("(o a) b -> o (a b)", o=1), key=hgrow, R=[H_gscr[jo % 2]], W=[hgrow])
        nct = 2 if s >= 17 else 1
        for ct in range(nct):
            pvc, phvc = rr()
            MM(pvc[:, 0:128], geV[:, ct * 128:(ct + 1) * 128], w2_bf[:, 1, :], R=[hGeV, hW2], W=[phvc])
            TS("dve", Vc[:, ct, :, 0:64], pvc[:, 0:128].rearrange("p (g d) -> p g d", g=2), valc_f[:, ct:ct + 1], None, ALU.mult,
               R=[phvc, hValC], **WJ(hVc, ct == 0))
            for g in range(2):
                CP("pool", Vc[:, ct, g, 64:65], valc_f[:, ct:ct + 1], R=[hValC], J=[hVc])
        DMA("sp", selc[:], cSel[jo], key=hselc, W=[hselc])
        for g in range(2):
            gp = slice(64 * g, 64 * g + 64)
            hk_i = jo % 2
            bias_ct = None
            if s <= 15:
                bias_ct, sprime = 0, s
            elif s >= 17:
                bias_ct, sprime = 1, s - 16
            if bias_ct is not None:
                hankel(HKc[hk_i][:, g, :].rearrange("p (r q) -> p r q", r=4), g, OFFA + 128 * sprime - 2047, 16, 128, hHKc[hk_i][g])
            acc, hacc = ACC[acc_rot[0] % 2], hACC[acc_rot[0] % 2]
            acc_rot[0] += 1
            for ct in range(nct):
                bias = (HKc[hk_i][:, g, :], hHKc[hk_i][g]) if ct == bias_ct else None
                attn_tile(g, KcT[gp, ct * 128:(ct + 1) * 128], hKc, bias, None, Vc[:, ct, g, :], hVc, acc, hacc,
                          ct == 0, ct == nct - 1, keep=Pc[:, ct, :], hkeep=hPc[ct])
            TS("dve", crow[64:65, :], acc[64:65, :], 1e-30, None, ALU.max, R=[hacc], W=[hcrow])
            RCP("dve", crow[64:65, :], crow[64:65, :], R=[hcrow], W=[hcrow])
            pbc, phbc = rr()
            MM(pbc[:, :], ones_f[64:65, :], crow[64:65, :], R=[hOnesF, hcrow], W=[phbc])
            for ct in range(nct):
                TT("dve", pn_f[:], Pc[:, ct, :], pbc[:, :], ALU.mult, R=[hPc[ct], phbc], W=[hpn])
                def _red(e, ct=ct):
                    with nc.allow_low_precision("fp32 accumulate inside, bf16 store"):
                        return e.tensor_reduce(out=PsT[:, ct, :], in_=pn_f[:].rearrange("p (r q) -> p q r", r=4), axis=AX.X, op=ALU.add)
                S.add("dve", _red, reads=[hpn], **({"writes": [hPsT]} if ct == 0 else {"joins": [hPsT]}))
            pim, phim = rr()
            for ct in range(nct):
                MM(pim[:, 0:128], PsT[:, ct, :], A_bf[:, ct, :], start=(ct == 0), stop=(ct == nct - 1), R=[hPsT, hA], **WJ(phim, ct == 0))
            TT("dve", score[:], pim[:, 0:128], selc[:, 0, :], ALU.mult, R=[phim, hselc], W=[hscore])
            TT("dve", score[:], score[:], selc[:, 1, :], ALU.add, R=[hscore, hselc], W=[hscore])
            S.add("dve", lambda e: e.max(out=m8[:], in_=score[:, 0:72]), reads=[hscore], writes=[hm8])
            S.add("dve", lambda e: e.match_replace(out=sc2[:, 0:72], in_to_replace=m8[:], in_values=score[:, 0:72], imm_value=-1e30),
                  reads=[hscore, hm8], writes=[hsc2])
            S.add("dve", lambda e: e.max(out=m8b[:], in_=sc2[:, 0:72]), reads=[hsc2], writes=[hm8b])
            TS("dve", sc2[:], score[:], m8b[:, 7:8], None, ALU.is_ge, R=[hscore, hm8b], W=[hsc2])
            TS("dve", nsel[:], sc2[:], -1.0, BIG, ALU.add, ALU.mult, R=[hsc2], W=[hnsel])
            TR(PT[:, 0:128], nsel[:], ident_bf[:], R=[hnsel, hId], W=[H_PT])
            CP("act", nselT[:].rearrange("p (r q) -> p r q", r=4), PT[:, 0:128].unsqueeze(1).to_broadcast([128, 4, 128]), R=[H_PT], W=[hnselT])
            branch_finish(g, 0, acc, hacc, True)
            acc, hacc = ACC[acc_rot[0] % 2], hACC[acc_rot[0] % 2]
            acc_rot[0] += 1
            for ks in range(s + 1):
                dl = s - ks
                bias = (HK[:, dl, g, :], hHK[dl][g]) if dl <= 1 else None
                attn_tile(g, KsT[gp, ks * 128:(ks + 1) * 128], hKs[ks], bias, E_bf[:, ks * 128:(ks + 1) * 128],
                          Vs[:, ks, g, :], hVs[ks], acc, hacc, ks == 0, ks == s)
            branch_finish(g, 1, acc, hacc, False)
            acc, hacc = ACC[acc_rot[0] % 2], hACC[acc_rot[0] % 2]
            acc_rot[0] += 1
            k0 = max(0, s - 4)
            for ks in range(k0, s + 1):
                dl = s - ks
                bias = None
                if dl <= 1:
                    bias = (HK[:, dl, g, :], hHK[dl][g])
                elif dl == 4:
                    bias = (HK[:, 2, g, :], hHK[2][g])
                attn_tile(g, KwT[gp, (ks % 8) * 128:(ks % 8 + 1) * 128], hKw[ks % 8], bias, None,
                          Vw[:, ks % 8, g, :], hVw[ks % 8], acc, hacc, ks == k0, ks == s)
            branch_finish(g, 2, acc, hacc, False)
        CP("act", onsab[:], onsa[:], R=[honsa], W=[honsab])
        for half in range(2):
            pw, phw = rr()
            for c4 in range(4):
                dc = half * 4 + c4
                osl = pw[:, c4 * 128:(c4 + 1) * 128]
                n = 0
                for g in range(2):
                    for r in range(4):
                        MM(osl, wonsa_bf[:, g * 4 + r, dc * 128:(dc + 1) * 128], onsab[:, g, r * 128:(r + 1) * 128],
                           start=(n == 0), stop=False, R=[hWon, honsab], **WJ(phw, c4 == 0 and n == 0))
                        n += 1
                for hd in range(4):
                    MM(osl, wogla_bf[:, hd, dc * 128:(dc + 1) * 128], omixg[:, hd, :], start=False, stop=(hd == 3),
                       R=[hWog, homixg], J=[phw])
            TT("dve", h1t[:, half * 4:half * 4 + 4, :], xb[:, half * 4:half * 4 + 4, :],
               pw[:, :].rearrange("p (c q) -> p c q", c=4), ALU.add, R=[hx, phw], **WJ(hh1t, half == 0))
        DMA("sp", h1_scr[:, :, jo * 128:(jo + 1) * 128], h1t[:], key=hh1t, R=[hh1t], W=[H_h1[jo]])
        if dbg and s == 31:
            DBG["d_h1L"] = dout("d_h1L", [128, KC, 128])
            DMA("sp", DBG["d_h1L"], h1t[:], key=hh1t, R=[hh1t])
        if dbg and s == 1:
            DBG["d_h1"] = dout("d_h1", [128, KC, 128])
            DMA("sp", DBG["d_h1"], h1t[:], key=hh1t, R=[hh1t])
            for nm, tl, hh, shp in (("d_onsa", onsa, honsa, [64, 2, 512]), ("d_grow", grow, hgrow, [128, 3072]), ("d_score", score, hscore, [128, 128]),
                                    ("d_t1", t1, ht1, [128, 128]), ("d_sg", sg, hsg, [128, 128]), ("d_grs", grs, hgrs, [128, 128]),
                                    ("d_la", la, hla, [128, 256]), ("d_ecum", ecum, hecum, [128, 256]), ("d_S", Sst, hS, [128, 2, 128]),
                                    ("d_rstd", rstd, hrstd, [128, 128]), ("d_pn", pn_f, hpn, [128, 512]), ("d_num", numsb, hnum, [64, 512]), ("d_gx", gx, hgx, [128, 8])):
                DBG[nm] = dout(nm, shp)
                DMA("sp", DBG[nm], tl[:], key=H("k" + nm), R=[hh])
    DMA("sp", gla_p, Sst[:], key=hS, R=[hS])
    A_END = AR.top

    S.barrier()
    AR.top = PERS_END
    wg_bf = AR.alloc([128, KC, DFF], BF16); hWg = H("wg")
    wu_bf = AR.alloc([128, KC, DFF], BF16); hWu = H("wu")
    wd_bf = AR.alloc([128, NFC, D], BF16); hWd = H("wd")
    wpg_bf = AR.alloc([128, KC, D], BF16); hWpg = H("wpg")
    wple_bf = AR.alloc([128, 2, D], BF16); hWple = H("wple")

    def wload(dst, src, n1, ncol, h, step):
        first = True
        for i in range(n1):
            for a in range(0, ncol, step):
                DMA("pool", dst[:, i, a:a + step], src[:, i, a:a + step], key=h, **(dict(W=[h]) if first else dict(J=[h])))
                first = False
    if os.environ.get("SKIPB") == "1":
        ctx = dict(locals())
        return ctx
    wload(wg_bf, w_gate, KC, DFF, hWg, 1408)
    wload(wu_bf, w_up, KC, DFF, hWu, 1408)
    wload(wd_bf, w_down, NFC, D, hWd, 1024)
    wload(wpg_bf, w_pg, KC, D, hWpg, 1024)
    wload(wple_bf, w_ple, 2, D, hWple, 1024)
    TBM = 256
    hb = AR.alloc([128, KC, TBM], F32); hhb = H("hb")
    sqb = AR.alloc([128, KC, TBM], BF16); hsqb = H("sqb")
    rsb = AR.alloc([128, TBM], F32); hrsb = H("rsb")
    xnb = AR.alloc([128, KC, TBM], BF16); hxnb = H("xnb")
    actb = AR.alloc([128, NFC, TBM], BF16); hactb = H("actb")
    egbs = [AR.alloc([128, TBM], F32) for _ in range(2)]; hegbs = [H("egb0"), H("egb1")]
    pTf = AR.alloc([128, 2, TBM], F32); hpTf = H("pTf")
    pTb = AR.alloc([128, 2, TBM], BF16); hpTb = H("pTb")

    def rmsnorm_fm(src, hsrc, dst, hdst, nidx, TB, out_f32=False):
        ACT(sqb[:, :, 0:TB], src[:, :, 0:TB], AF.Square, R=[hsrc], W=[hsqb])
        pb, ph = rr()
        for kc in range(KC):
            MM(pb[:, 0:TB], ones_bf[:], sqb[:, kc, 0:TB], start=(kc == 0), stop=(kc == KC - 1), R=[hOnes, hsqb], **WJ(ph, kc == 0))
        ACT(rsb[:, 0:TB], pb[:, 0:TB], AF.Ln, bias=EPS, scale=1.0 / D, R=[ph], W=[hrsb])
        ACT(rsb[:, 0:TB], rsb[:, 0:TB], AF.Exp, scale=-0.5, R=[hrsb], W=[hrsb])
        for kc in range(KC):
            STT("dve", dst[:, kc, 0:TB], src[:, kc, 0:TB], norms_f[:, nidx, kc:kc + 1], rsb[:, 0:TB], ALU.mult, ALU.mult,
                R=[hsrc, hrsb, hNorms], **(WJ(hdst, kc == 0) if hdst is not hsrc else dict(W=[hdst])))

    def ffn_block(t0, TB, hsrc_dram):
        DMA("sp", hb[:, :, 0:TB], h1_scr[:, :, t0:t0 + TB], key=hhb, R=[hsrc_dram], W=[hhb])
        DMA("sp", pTf[:, :, 0:TB], pT[:, :, t0:t0 + TB], key=hpTf, W=[hpTf])
        CP("pool", pTb[:, :, 0:TB], pTf[:, :, 0:TB], R=[hpTf], W=[hpTb])
        rmsnorm_fm(hb, hhb, xnb, hxnb, 1, TB)
        for fc in range(NFC):
            pg_, phg_ = rr()
            for kc in range(KC):
                MM(pg_[:, 0:TB], wg_bf[:, kc, fc * 128:(fc + 1) * 128], xnb[:, kc, 0:TB], start=(kc == 0), stop=(kc == KC - 1),
                   R=[hWg, hxnb], **WJ(phg_, kc == 0))
            pu_, phu_ = rr()
            for kc in range(KC):
                MM(pu_[:, 0:TB], wu_bf[:, kc, fc * 128:(fc + 1) * 128], xnb[:, kc, 0:TB], start=(kc == 0), stop=(kc == KC - 1),
                   R=[hWu, hxnb], **WJ(phu_, kc == 0))
            eg_, heg_ = egbs[fc % 2], hegbs[fc % 2]
            ACT(eg_[:, 0:TB], pg_[:, 0:TB], AF.Silu, R=[phg_], W=[heg_])
            TT("dve", actb[:, fc, 0:TB], eg_[:, 0:TB], pu_[:, 0:TB], ALU.mult, R=[heg_, phu_], **WJ(hactb, fc == 0))
        for dc in range(KC):
            pd_, phd_ = rr()
            for fc in range(NFC):
                MM(pd_[:, 0:TB], wd_bf[:, fc, dc * 128:(dc + 1) * 128], actb[:, fc, 0:TB], start=(fc == 0), stop=(fc == NFC - 1),
                   R=[hWd, hactb], **WJ(phd_, fc == 0))
            TT("dve", hb[:, dc, 0:TB], hb[:, dc, 0:TB], pd_[:, 0:TB], ALU.add, R=[hhb, phd_], W=[hhb])
        rmsnorm_fm(hb, hhb, xnb, hxnb, 2, TB)
        for dc in range(KC):
            pg_, phg_ = rr()
            for kc in range(KC):
                MM(pg_[:, 0:TB], wpg_bf[:, kc, dc * 128:(dc + 1) * 128], xnb[:, kc, 0:TB], start=(kc == 0), stop=(kc == KC - 1),
                   R=[hWpg, hxnb], **WJ(phg_, kc == 0))
            pu_, phu_ = rr()
            for k2 in range(2):
                MM(pu_[:, 0:TB], wple_bf[:, k2, dc * 128:(dc + 1) * 128], pTb[:, k2, 0:TB], start=(k2 == 0), stop=(k2 == 1),
                   R=[hWple, hpTb], **WJ(phu_, k2 == 0))
            eg_, heg_ = egbs[dc % 2], hegbs[dc % 2]
            ACT(eg_[:, 0:TB], pg_[:, 0:TB], AF.Sigmoid, R=[phg_], W=[heg_])
            TT("dve", eg_[:, 0:TB], eg_[:, 0:TB], pu_[:, 0:TB], ALU.mult, R=[heg_, phu_], W=[heg_])
            TT("dve", hb[:, dc, 0:TB], hb[:, dc, 0:TB], eg_[:, 0:TB], ALU.add, R=[hhb, heg_], W=[hhb])
        rmsnorm_fm(hb, hhb, hb, hhb, 3, TB)
        DMA("sp", yT[:, :, t0:t0 + TB], hb[:, :, 0:TB], key=hhb, R=[hhb])

    nblk = (min(nslots, NSLOT) // 2) * 128 // TBM
    for bi in range(nblk):
        ffn_block(bi * TBM, TBM, H_h1[(bi * TBM) // 128 + 1] if False else H_h1[min(NOWN - 1, (bi * TBM + TBM - 1) // 128)])
    if do_sample and not (99.05 <= cut < 107):
        ffn_block(NT, NS, H_h1[NOWN])
    B_END = AR.top
    ctx = dict(locals())
    return ctx


_STATIC = None


def _kc(a):
    K = a.shape[0] // 128
    return np.ascontiguousarray(a.reshape(K, 128, -1).transpose(1, 0, 2))


def prep_shared(inp):
    global _STATIC
    if _STATIC is None:
        _STATIC = _static_consts()
    sh = dict(_STATIC)
    l = 0
    wi = np.array(inp["w_in"][l])
    wi[:, 0:512] = wi[:, 0:512].reshape(D, 2, 4, 64).transpose(0, 2, 1, 3).reshape(D, 512)
    sh["w_in"] = _kc(wi)
    wo = inp["w_o"][l]
    sh["w_o_nsa"] = np.ascontiguousarray(wo[:512].reshape(8, 64, D).transpose(1, 0, 2))
    sh["w_o_gla"] = _kc(wo[512:])
    sh["w_gate"] = _kc(inp["w_gate"][l])
    sh["w_up"] = _kc(inp["w_up"][l])
    sh["w_down"] = _kc(inp["w_down"][l])
    sh["w_ple"] = _kc(inp["w_ple"][l])
    sh["w_pg"] = _kc(inp["w_ple_gate"][l])
    nm = np.stack([inp["norm_mix"][l], inp["norm_ffn"][l], inp["norm_ple"][l], inp["norm_final"]], 0)
    sh["norms"] = np.ascontiguousarray(nm.reshape(4, KC, 128).transpose(2, 0, 1))
    w1 = inp["cmp_w1"][l].reshape(2, 32, 64, 64)
    w1bd = np.zeros((128, 2, 32, 128), np.float32)
    for g in range(2):
        w1bd[g * 64:(g + 1) * 64, :, :, g * 64:(g + 1) * 64] = w1.transpose(2, 0, 1, 3)
    sh["w1bd"] = w1bd
    pe = inp["cmp_pe"][l]
    sh["pecol"] = np.ascontiguousarray(np.concatenate([pe.transpose(2, 0, 1)] * 2, 0))
    w2 = inp["cmp_w2"][l]
    w2bd = np.zeros((128, 2, 128), np.float32)
    for g in range(2):
        w2bd[g * 64:(g + 1) * 64, :, g * 64:(g + 1) * 64] = w2.transpose(1, 0, 2)
    sh["w2bd"] = w2bd
    wg = np.zeros((33, 256), np.float32)
    wg[:16] = inp["w_gk"][l]
    wg[32] = inp["b_gk"][l]
    sh["wgk"] = wg
    sh["glan"] = np.ascontiguousarray(inp["gla_norm"][l].reshape(128, 1))
    rb = np.zeros((33, 8), np.float32)
    rb[:32] = inp["rel_bias"]
    rb[32] = -BIG
    sh["rb33"] = rb
    sh["cache"] = np.ascontiguousarray(inp["cache_nsa_kv"][l].reshape(-1, 128, 512))
    return sh


def prep_core(inp, c, sh):
    b, par = c // 2, c % 2
    shift = 1 - par
    l = 0
    m = dict(sh)
    m.update(_core_consts(par))
    x = inp["x_prompt"][b]
    xs = np.zeros((TOK, D), np.float32)
    xs[shift * 128:shift * 128 + SEQ] = x
    m["xT"] = np.ascontiguousarray(xs.T.reshape(KC, 128, TOK).transpose(1, 0, 2))
    own_tok = np.concatenate([np.arange(128) + (2 * j + par) * 128 for j in range(NOWN)])
    bs = slice(4 * c, 4 * c + 4)
    p = np.concatenate([inp["p_prompt"][l, b][own_tok], inp["p_sample"][l, bs].reshape(NS, 256)], 0)
    m["pT"] = np.ascontiguousarray(p.T.reshape(2, 128, NTS).transpose(1, 0, 2))
    xsm = inp["x_sample"][bs].reshape(NS, D)
    m["xsT"] = np.ascontiguousarray(xsm.T.reshape(KC, 128, NS).transpose(1, 0, 2))
    m["ptab"] = np.ascontiguousarray(inp["page_table"][bs].reshape(1, 4 * NPG).astype(np.int32))
    m["cwin"] = np.ascontiguousarray(inp["cache_win_kv"][l, bs].reshape(4, 512, 256))
    m["sgla"] = np.ascontiguousarray(inp["state_gla"][l, bs])
    return m


_PROG = None


def _get_prog():
    global _PROG
    if _PROG is None:
        c = build_program(dbg=False)
        c["S"].finish()
        c["S"].emit()
        _PROG = c
    return _PROG


def kernel(**inputs):
    inp = {k: np.asarray(v) for k, v in inputs.items()}
    c = _get_prog()
    sh = prep_shared(inp)
    maps = []
    for ci in range(8):
        m = prep_core(inp, ci, sh)
        maps.append({k: m[k] for k in c["IN"]})
    res = run_bass_kernel_spmd(c["nc"], maps, core_ids=list(range(8)))
    R = res.results
    B, T = 4, SEQ
    y_prompt = np.zeros((B, T, D), np.float32)
    y_sample = np.zeros((32, 4, D), np.float32)
    new_kv_prompt = np.zeros((1, B, T, 4, 2, 64), np.float32)
    new_kv_sample = np.zeros((1, 32, 4, 4, 2, 64), np.float32)
    new_win_prompt = np.zeros((1, B, 512, 2, 2, 64), np.float32)
    new_win_sample = np.zeros((1, 32, 512, 2, 2, 64), np.float32)
    new_gla_prompt = np.zeros((1, B, 4, 64, 128), np.float32)
    new_gla_sample = np.zeros((1, 32, 4, 64, 128), np.float32)
    kwin = np.zeros((B, T, 2, 64), np.float32)
    vwin = np.zeros((B, T, 2, 64), np.float32)
    for ci in range(8):
        b, par = ci // 2, ci % 2
        r = R[ci]
        own_tok = np.concatenate([np.arange(128) + (2 * j + par) * 128 for j in range(NOWN)])
        kvT = r["kvT_p"]
        vt = r["vtok_p"]
        yT = r["yT"]
        y_prompt[b, own_tok] = yT[:, :, :NT].transpose(2, 1, 0).reshape(NT, D)
        y_sample[4 * ci:4 * ci + 4] = yT[:, :, NT:].transpose(2, 1, 0).reshape(4, 4, D)
        for i, sl in enumerate((0, 1, 2)):
            new_kv_prompt[0, b, own_tok, sl] = kvT[:, i, :].T.reshape(NT, 2, 64)
        new_kv_prompt[0, b, own_tok, 3] = vt[:, 0, :].reshape(NT, 2, 64)
        kwin[b, own_tok] = kvT[:, 3, :].T.reshape(NT, 2, 64)
        vwin[b, own_tok] = vt[:, 1, :].reshape(NT, 2, 64)
        if par == 0:
            gp = r["gla_p"]
            for h in range(4):
                new_gla_prompt[0, b, h] = gp[(h % 2) * 64:(h % 2) * 64 + 64, h // 2, :]
        ks = r["kvT_s"]
        vs = r["vtok_s"]
        for i, sl in enumerate((0, 1, 2)):
            new_kv_sample[0, 4 * ci:4 * ci + 4, :, sl] = ks[:, i, :].T.reshape(4, 4, 2, 64)
        new_kv_sample[0, 4 * ci:4 * ci + 4, :, 3] = vs[:, 0, :].reshape(4, 4, 2, 64)
        ws = r["win_s"].reshape(4, 508, 2, 2, 64)
        new_win_sample[0, 4 * ci:4 * ci + 4, :508] = ws
        new_win_sample[0, 4 * ci:4 * ci + 4, 508:, 0] = ks[:, 3, :].T.reshape(4, 4, 2, 64)
        new_win_sample[0, 4 * ci:4 * ci + 4, 508:, 1] = vs[:, 1, :].reshape(4, 4, 2, 64)
        new_gla_sample[0, 4 * ci:4 * ci + 4] = r["gla_s"].reshape(4, 2, 64, 2, 128).transpose(0, 3, 1, 2, 4).reshape(4, 4, 64, 128)
    new_win_prompt[0, :, :, 0] = kwin[:, T - 512:]
    new_win_prompt[0, :, :, 1] = vwin[:, T - 512:]
    return (y_prompt, y_sample, new_kv_prompt, new_kv_sample, new_win_prompt, new_win_sample,
            new_gla_prompt, new_gla_sample)
```

```python
import math
import os
import contextlib
import numpy as np
import ml_dtypes
import concourse.bass as bass
import concourse.mybir as mybir
from concourse.bass_utils import run_bass_kernel_spmd

F32 = mybir.dt.float32
BF16 = mybir.dt.bfloat16
I32 = mybir.dt.int32
ALU = mybir.AluOpType
AF = mybir.ActivationFunctionType
AX = mybir.AxisListType

ENGS = ("pe", "act", "dve", "pool", "sp")
SAME_ENG_SYNC = True
RESCHED = True
SWDGE_DEPTH = 100000


class H:
    __slots__ = ("name", "writers", "readers", "gdeps", "excl")

    def __init__(self, name="", excl=False):
        self.name = name
        self.excl = excl
        self.writers = []
        self.readers = []
        self.gdeps = []


class Op:
    __slots__ = ("eng", "fn", "deps", "dma", "key", "ticket", "sig", "odeps", "dur", "idx")

    def __init__(self, eng, fn, dma, key):
        self.eng = eng
        self.fn = fn
        self.deps = []
        self.odeps = []
        self.dur = 0.3
        self.idx = 0
        self.dma = dma
        self.key = key
        self.ticket = None
        self.sig = False


class Sched:
    def __init__(self, nc):
        self.nc = nc
        self.ops = []
        self.q = {e: [] for e in ENGS}
        self.all_dma = []

    def add(self, eng, fn, reads=(), writes=(), joins=(), dma=False, key=None, dur=None):
        op = Op(eng, fn, dma, key)
        op.dur = dur if dur is not None else (2.5 if dma else 0.3)
        deps = []
        for h in reads:
            deps.extend(h.writers)
            if h.excl:
                deps.extend(r for r in h.readers if r.eng != eng)
        for h in writes:
            deps.extend(h.writers)
            deps.extend(h.readers)
        for h in joins:
            deps.extend(h.gdeps)
        for h in reads:
            h.readers.append(op)
        for h in writes:
            h.gdeps = list(h.writers) + list(h.readers)
            h.writers = [op]
            h.readers = []
        for h in joins:
            if h.excl and h.writers:
                op.odeps.append(h.writers[-1])
            h.writers.append(op)
        seen = set()
        for d in deps:
            if d is op or id(d) in seen:
                continue
            seen.add(id(d))
            op.odeps.append(d)
            if (not d.dma) and d.eng == eng and not dma:
                if eng == "pe" or not SAME_ENG_SYNC:
                    continue
            op.deps.append(d)
            d.sig = True
        if dma:
            assert key is not None
            self.all_dma.append(op)
            lk = self.q.setdefault("lastkey", {})
            if id(key) in lk:
                op.odeps.append(lk[id(key)])
            lk[id(key)] = op
            if eng == "pool":
                hist = self.q["pool_dma_hist"] if "pool_dma_hist" in self.q else self.q.setdefault("pool_dma_hist", [])
                if len(hist) >= SWDGE_DEPTH and hist[-SWDGE_DEPTH] not in op.deps and id(hist[-SWDGE_DEPTH].key) not in self.q.setdefault("group_keys", set()):
                    op.deps.append(hist[-SWDGE_DEPTH])
                    op.odeps.append(hist[-SWDGE_DEPTH])
                    hist[-SWDGE_DEPTH].sig = True
                if hist:
                    op.odeps.append(hist[-1])
                hist.append(op)
        self.ops.append(op)
        self.q[eng].append(op)
        return op

    def barrier(self):
        lasts = []
        for e in ENGS:
            for o in reversed(self.q[e]):
                if o.fn is not None and not o.dma:
                    lasts.append(o)
                    break
        dm = list(self.all_dma)
        for e in ENGS:
            op = Op(e, None, False, "barrier")
            for d in lasts + dm:
                if (not d.dma) and d.eng == e:
                    continue
                op.deps.append(d)
                d.sig = True
            self.ops.append(op)
            self.q[e].append(op)

    def finish(self, eng="sp"):
        op = Op(eng, None, False, None)
        for d in self.all_dma:
            op.deps.append(d)
            d.sig = True
        self.ops.append(op)
        self.q[eng].append(op)

    def reschedule(self):
        import heapq
        for i, op in enumerate(self.ops):
            op.idx = i
        segs, cur = [], []
        for op in self.ops:
            if op.fn is None:
                if cur:
                    segs.append(cur)
                    cur = []
                segs.append([op])
            else:
                cur.append(op)
        if cur:
            segs.append(cur)
        new_ops = []
        for seg in segs:
            if len(seg) == 1 and seg[0].fn is None:
                b = seg[0]
                if b.key == "barrier":
                    b.deps = []
                    for e2 in ENGS:
                        for o2 in reversed(new_ops):
                            if o2.eng == e2 and o2.fn is not None and not o2.dma:
                                if e2 != b.eng:
                                    b.deps.append(o2)
                                    o2.sig = True
                                break
                    for o2 in new_ops:
                        if o2.dma:
                            b.deps.append(o2)
                            o2.sig = True
                new_ops.append(b)
                continue
            inseg = {id(o) for o in seg}
            npred = {}
            succ = {}
            ready_t = {}
            for o in seg:
                ps = [d for d in o.odeps if id(d) in inseg]
                npred[id(o)] = len(ps)
                ready_t[id(o)] = 0.0
                for d in ps:
                    succ.setdefault(id(d), []).append(o)
            heaps = {e: [] for e in ENGS}
            for o in seg:
                if npred[id(o)] == 0:
                    heapq.heappush(heaps[o.eng], (0.0, o.idx, o))
            free = {e: 0.0 for e in ENGS}
            done = 0
            while done < len(seg):
                best = None
                for e in ENGS:
                    if heaps[e]:
                        rt, ix, o = heaps[e][0]
                        st = max(rt, free[e])
                        if best is None or (st, ix) < (best[0], best[1]):
                            best = (st, ix, e)
                st, ix, e = best
                rt, ix, o = heapq.heappop(heaps[e])
                issue = 0.05 if o.dma else o.dur
                free[e] = st + issue
                fin = st + o.dur + (0.25 if not o.dma else 0.0)
                new_ops.append(o)
                done += 1
                for sc in succ.get(id(o), ()):
                    ready_t[id(sc)] = max(ready_t[id(sc)], fin)
                    npred[id(sc)] -= 1
                    if npred[id(sc)] == 0:
                        heapq.heappush(heaps[sc.eng], (ready_t[id(sc)], sc.idx, sc))
        assert len(new_ops) == len(self.ops)
        self.ops = new_ops
        for e in ENGS:
            self.q[e] = [o for o in new_ops if o.eng == e]

    def emit(self):
        nc = self.nc
        if RESCHED:
            self.reschedule()
        cnt = {e: 0 for e in ENGS}
        dcnt = {}
        keys = []
        for op in self.ops:
            if op.dma:
                if op.key not in dcnt:
                    dcnt[op.key] = 0
                    keys.append(op.key)
                dcnt[op.key] += 16
                op.ticket = dcnt[op.key]
            elif op.sig:
                cnt[op.eng] += 1
                op.ticket = cnt[op.eng]
        with contextlib.ExitStack() as st:
            esem = {e: st.enter_context(nc.semaphore("s_" + e)) for e in ENGS if e != "sp"}
            dsem = {k: st.enter_context(nc.semaphore("d%d" % i)) for i, k in enumerate(keys)}
            block = st.enter_context(nc.Block())

            def run(engname):
                def body(e):
                    waited = {}
                    for op in self.q[engname]:
                        need = {}
                        for d in op.deps:
                            sem = dsem[d.key] if d.dma else esem[d.eng]
                            sid = id(sem)
                            if sid not in need or need[sid][1] < d.ticket:
                                need[sid] = (sem, d.ticket)
                        for sid, (sem, tk) in need.items():
                            if waited.get(sid, 0) >= tk:
                                continue
                            waited[sid] = tk
                            e.wait_ge(sem, tk)
                        if op.fn is None:
                            continue
                        ins = op.fn(e)
                        if op.dma:
                            ins.then_inc(dsem[op.key], 16)
                        elif op.sig:
                            ins.then_inc(esem[op.eng], 1)
                return body

            block.tensor(run("pe"))
            block.scalar(run("act"))
            block.vector(run("dve"))
            block.gpsimd(run("pool"))
            block.sync(run("sp"))
        return len(keys), cnt


D = 1024
KC = 8
SEQ = 4096
NSLOT = 33
TOK = NSLOT * 128
NOWN = 16
NT = 2048
NS = 16
NTS = NT + NS
DFF = 2816
NFC = 22
IN_DIM = 2856
C_Q, C_KV, C_GT, C_QG, C_KG, C_VG, C_LR, C_GG = 0, 512, 1280, 1304, 1560, 1816, 2328, 2344
EPS = 1e-6
BIG = 30000.0
LA = 4096
OFFA = 1936
LW = 768
OFFW = 128
LEXT = LA + LW
NPOOL = 2560
PAST = 8192
NPG = 64


def _bucket(n):
    n = np.asarray(n, np.int64)
    nf = np.maximum(n, 1).astype(np.float32)
    large = 16 + (np.log(nf / np.float32(16)) / np.float32(math.log(8.0)) * np.float32(16)).astype(np.int32)
    large = np.minimum(large, 31)
    return np.where(n < 16, n, large)


def _coef():
    c = np.zeros((33, LEXT), np.float32)
    m = np.arange(LA)
    n = m - OFFA
    for mi, ni in zip(m, n):
        if ni < 0:
            c[32, mi] = 1.0
        elif ni <= 112:
            c[_bucket(ni), mi] += 1.0
            c[31, mi] -= 1.0
    m = np.arange(LW)
    n = m - OFFW
    for mi, ni in zip(m, n):
        if ni < 0 or ni >= 512:
            c[32, LA + mi] = 1.0
        elif ni <= 112:
            c[_bucket(ni), LA + mi] += 1.0
            c[31, LA + mi] -= 1.0
    return c


def _static_consts():
    k = {}
    i = np.arange(128)
    k["cJ"] = (i[:, None] + i[None, :] == 127).astype(np.float32)
    k["cTri"] = (i[:, None] <= i[None, :]).astype(np.float32)
    k["cTriU"] = (i[:, None] > i[None, :]).astype(np.float32)
    k["cIdent"] = np.eye(128, dtype=np.float32)
    k["coef"] = _coef()
    E = np.zeros((128, TOK), np.float32)
    E[np.arange(TOK) // 64, np.arange(TOK)] = 1.0
    k["cE"] = E
    As = np.zeros((512, 128), np.float32)
    wts = {-1: 1.0, 0: 2.0, 1: 2.0, 2: 2.0, 3: 1.0}
    for j in range(128):
        for dd, w in wts.items():
            ii = 4 * j + dd
            if 0 <= ii <= 510:
                As[ii, j] = w
    k["cAs"] = As.reshape(4, 128, 128).transpose(1, 0, 2).copy()
    sel = np.zeros((4, 2, 128), np.float32)
    sel[:, 0, :] = 1.0
    sel[:, 0, 0] = 0.0
    sel[:, 0, 127] = 0.0
    sel[:, 1, 0] = 1e9
    sel[:, 1, 127] = 1e9
    k["cSels"] = sel
    blk = np.arange(128)
    k["cBm"] = (blk[:, None] // 4 == np.arange(64)[None, :] // 2).astype(np.float32)
    k["cHalf"] = ((blk[:, None] % 4) == (i[None, :] // 32)).astype(np.float32)
    t = np.arange(16)
    same = (t[:, None] // 4 == t[None, :] // 4)
    k["cTri16"] = (same & (t[:, None] <= t[None, :])).astype(np.float32)
    k["cTriU16"] = (same & (t[:, None] > t[None, :])).astype(np.float32)
    k["cSeqm"] = (t[:, None] // 4 == np.arange(4)[None, :]).astype(np.float32)
    vs_ = np.ones((128, 4), np.float32)
    vs_[127, 3] = 0.0
    k["cValS"] = vs_
    k["cPcol"] = (np.arange(128) % 64).astype(np.float32).reshape(128, 1)
    return k


def _core_consts(par):
    shift = 1 - par
    k = {}
    A = np.zeros((256, 128), np.float32)
    wts = {-1: 1.0, 0: 2.0, 1: 2.0, 2: 2.0, 3: 1.0}
    for bp in range(66):
        j = bp - 2 * shift
        if not (0 <= j <= 63):
            continue
        for dd, w in wts.items():
            ii = 4 * j + dd
            if 0 <= ii <= 254:
                c = ii + 8 * shift + 1
                if c < 256:
                    A[c, bp] = w
    k["cA"] = A.reshape(2, 128, 128).transpose(1, 0, 2).copy()
    sel = np.zeros((NOWN, 128, 2, 128), np.float32)
    sel[:, :, 1, :] = -2.0
    q = np.arange(128)
    for jo in range(NOWN):
        s = 2 * jo + 1
        pos = (s - shift) * 128 + q
        cur = pos // 64
        for bp in range(66):
            j = bp - 2 * shift
            if not (0 <= j <= 63):
                continue
            forced = (j == 0) | (j == cur) | (j == cur - 1)
            vis = (j * 64 <= pos)
            sel[jo, :, 0, bp] = np.where(vis & ~forced, 1.0, 0.0)
            sel[jo, :, 1, bp] = np.where(forced, 1e9, np.where(vis, 0.0, -1.0))
    k["cSel"] = sel
    valid = np.ones((128, NSLOT), np.float32)
    valid[:, 0 if shift == 1 else 32] = 0.0
    k["cValid"] = valid
    vc = np.zeros((256,), np.float32)
    for c in range(256):
        ii = c - 1 - 8 * shift
        vc[c] = 1.0 if 0 <= ii <= 254 else 0.0
    k["cValC"] = vc.reshape(2, 128).T.copy()
    return k


class Arena:
    def __init__(self, nc, base, end):
        self.nc, self.top, self.end = nc, base, end
        self.n = 0

    def alloc(self, shape, dt):
        sz = 4 if dt in (F32, I32) else 2
        nb = int(np.prod(shape[1:])) * sz
        nb = (nb + 63) // 64 * 64
        t = self.nc.alloc_sbuf_tensor_at("t%d" % self.n, list(shape), dt, offset=self.top)
        self.n += 1
        self.top += nb
        assert self.top <= self.end, ("SBUF overflow", self.top, self.end)
        return t


def build_program(dbg=False, nslots=NSLOT, cut=99, do_sample=True, nseq=4):
    nc = bass.Bass("TRN2", target_bir_lowering=False)
    S = Sched(nc)
    IN = {}
    OUT = {}

    def din(name, shape, dt=F32):
        IN[name] = (shape, dt)
        return nc.dram_tensor(name, list(shape), dt, kind="ExternalInput").ap()

    def dout(name, shape, dt=F32):
        OUT[name] = (shape, dt)
        return nc.dram_tensor(name, list(shape), dt, kind="ExternalOutput").ap()

    xT = din("xT", [128, KC, TOK])
    pT = din("pT", [128, 2, NTS])
    xsT = din("xsT", [128, KC, NS])
    w_in = din("w_in", [128, KC, IN_DIM])
    w_o_nsa = din("w_o_nsa", [64, 8, D])
    w_o_gla = din("w_o_gla", [128, 4, D])
    w_gate = din("w_gate", [128, KC, DFF])
    w_up = din("w_up", [128, KC, DFF])
    w_down = din("w_down", [128, NFC, D])
    w_ple = din("w_ple", [128, 2, D])
    w_pg = din("w_pg", [128, KC, D])
    norms = din("norms", [128, 4, KC])
    w1bd = din("w1bd", [128, 2, 32, 128])
    pecol = din("pecol", [128, 2, 32])
    w2bd = din("w2bd", [128, 2, 128])
    wgk = din("wgk", [33, 256])
    glan = din("glan", [128, 1])
    rb33 = din("rb33", [33, 8])
    coef = din("coef", [33, LEXT])
    cJ = din("cJ", [128, 128])
    cTri = din("cTri", [128, 128])
    cTriU = din("cTriU", [128, 128])
    cIdent = din("cIdent", [128, 128])
    cE = din("cE", [128, TOK])
    cA = din("cA", [128, 2, 128])
    cSel = din("cSel", [NOWN, 128, 2, 128])
    cValid = din("cValid", [128, NSLOT])
    cValC = din("cValC", [128, 2])
    cAs = din("cAs", [128, 4, 128])
    cSels = din("cSels", [4, 2, 128])
    cBm = din("cBm", [128, 64])
    cHalf = din("cHalf", [128, 128])
    cTri16 = din("cTri16", [16, 16])
    cTriU16 = din("cTriU16", [16, 16])
    cSeqm = din("cSeqm", [16, 4])
    cValS = din("cValS", [128, 4])
    cPcol = din("cPcol", [128, 1])
    cache = din("cache", [NPOOL, 128, 512])
    ptab = din("ptab", [1, 4 * NPG], I32)
    cwin = din("cwin", [4, 512, 256])
    sgla = din("sgla", [4, 4, 64, 128])

    yT = dout("yT", [128, KC, NTS])
    kvT_p = dout("kvT_p", [128, 4, NT])
    vtok_p = dout("vtok_p", [NT, 2, 128])
    gla_p = dout("gla_p", [128, 2, 128])
    kvT_s = dout("kvT_s", [128, 4, NS])
    vtok_s = dout("vtok_s", [NS, 2, 128])
    gla_s = dout("gla_s", [4, 128, 2, 128])
    win_s = dout("win_s", [4, 508, 256])
    DBG = {}

    ext_bf = nc.dram_tensor("ext_bf", [8, LEXT], BF16)
    h1_scr = nc.dram_tensor("h1_scr", [128, KC, NTS], F32).ap()
    H_ext = H("ext")
    gscr = [nc.dram_tensor("gscr%d" % i, [24, 128], F32).ap() for i in range(2)]
    gscr_s = nc.dram_tensor("gscr_s", [24, 16], F32).ap()
    H_gscr_s = H("gscr_s")
    H_gscr = [H("gscr0"), H("gscr1")]
    H_h1 = [H("h1s%d" % i) for i in range(NOWN + 1)]

    st = contextlib.ExitStack()
    arena = st.enter_context(nc.sbuf_tensor("arena", [128, 208000], mybir.dt.uint8))
    ABASE = 16512
    AEND = ABASE + 208000
    AR = Arena(nc, ABASE, AEND)

    PB = [st.enter_context(nc.psum_tensor("pb%d" % i, [128, 512], F32)) for i in range(7)]
    PBH = [H("pb%d" % i, excl=True) for i in range(7)]
    PT = st.enter_context(nc.psum_tensor("pbt", [128, 1024], BF16))
    H_PT = H("pbt", excl=True)
    rr_state = [0]

    def rr():
        i = 2 + (rr_state[0] % 5)
        rr_state[0] += 1
        return PB[i], PBH[i]

    def _fsz(ap):
        n = 1
        for d in ap.shape[1:]:
            n *= d
        return n

    def MM(out, lhsT, rhs, start=True, stop=True, R=(), W=(), J=()):
        S.add("pe", lambda e: e.matmul(out, lhsT=lhsT, rhs=rhs, start=start, stop=stop), reads=R, writes=W, joins=J,
              dur=max(_fsz(rhs), 128) / 2400.0 + 0.01)

    def TR(out, in_, ident, R=(), W=(), J=()):
        S.add("pe", lambda e: e.transpose(out=out, in_=in_, identity=ident), reads=R, writes=W, joins=J)

    def ACT(out, in_, func, bias=0.0, scale=1.0, R=(), W=(), J=()):
        S.add("act", lambda e: e.activation(out=out, in_=in_, func=func, bias=bias, scale=scale), reads=R, writes=W, joins=J,
              dur=_fsz(out) / 960.0 + 0.2)

    def CP(eng, out, in_, R=(), W=(), J=()):
        if eng == "act":
            S.add("act", lambda e: e.copy(out=out, in_=in_), reads=R, writes=W, joins=J)
        else:
            S.add(eng, lambda e: e.tensor_copy(out=out, in_=in_), reads=R, writes=W, joins=J)

    def TT(eng, out, in0, in1, op, R=(), W=(), J=()):
        S.add(eng, lambda e: e.tensor_tensor(out=out, in0=in0, in1=in1, op=op), reads=R, writes=W, joins=J, dur=_fsz(out) / 960.0 + 0.15)

    def TS(eng, out, in0, s1, s2, op0, op1=None, R=(), W=(), J=()):
        if op1 is None:
            S.add(eng, lambda e: e.tensor_scalar(out=out, in0=in0, scalar1=s1, scalar2=None, op0=op0), reads=R, writes=W, joins=J)
        else:
            S.add(eng, lambda e: e.tensor_scalar(out=out, in0=in0, scalar1=s1, scalar2=s2, op0=op0, op1=op1), reads=R, writes=W, joins=J)

    def STT(eng, out, in0, scalar, in1, op0, op1, R=(), W=(), J=()):
        S.add(eng, lambda e: e.scalar_tensor_tensor(out=out, in0=in0, scalar=scalar, in1=in1, op0=op0, op1=op1), reads=R, writes=W, joins=J)

    def MS(eng, ap, val, W=(), J=()):
        S.add(eng, lambda e: e.memset(ap, val), writes=W, joins=J)

    def RCP(eng, out, in_, R=(), W=(), J=()):
        S.add(eng, lambda e: e.reciprocal(out=out, in_=in_), reads=R, writes=W, joins=J, dur=_fsz(out) / 160.0 + 0.15)

    def DMA(eng, out, in_, key, R=(), W=(), J=(), **kw):
        S.add(eng, lambda e: e.dma_start(out=out, in_=in_, **kw), reads=R, writes=W, joins=J, dma=True, key=key)

    grp_started = set()
    cur_grp = [H("G0")]

    grp_qkeys = {}

    def gdma(eng, out, in_, grp, R=()):
        kobj = grp_qkeys.setdefault((id(grp), eng), H(grp.name + "_" + eng))
        S.q.setdefault("group_keys", set()).add(id(kobj))
        if id(grp) in grp_started:
            DMA(eng, out, in_, key=kobj, R=R, J=[grp])
        else:
            grp_started.add(id(grp))
            DMA(eng, out, in_, key=kobj, R=R, W=[grp])

    def load(shape, dt, src, eng=None, name=None, own=False):
        t = AR.alloc(shape, dt)
        if eng is None:
            eng = "pool" if dt == BF16 else "sp"
        if own or os.environ.get("OWNKEYS") == "1":
            h = H(name or "c")
            DMA(eng, t[:], src, key=h, W=[h])
            return t, h
        gdma(eng, t[:], src, cur_grp[0])
        return t, cur_grp[0]

    J_bf, hJ = load([128, 128], BF16, cJ)
    ident_bf, hId = load([128, 128], BF16, cIdent)
    tri_f, hTri = load([128, 128], F32, cTri)
    triu_f, hTriU = load([128, 128], F32, cTriU)
    norms_f, hNorms = load([128, 4, KC], F32, norms)
    glan_f, hGlan = load([128, 1], F32, glan)
    ones_bf = AR.alloc([128, 128], BF16); hOnes = H("ones")
    MS("dve", ones_bf[:], 1.0, W=[hOnes])
    ones_f = AR.alloc([128, 128], F32); hOnesF = H("onesf")
    MS("dve", ones_f[:], 1.0, W=[hOnesF])
    PERS_END = AR.top

    win_bf = AR.alloc([128, KC, 2880], BF16); hWin = H("win")
    first = True
    for kc in range(KC):
        for (a, b) in ((0, 1428), (1428, 2856)):
            DMA("pool", win_bf[:, kc, a:b], w_in[:, kc, a:b], key=hWin, W=[hWin] if first else (), J=() if first else [hWin])
            first = False
    wonsa_bf = AR.alloc([64, 8, D], BF16); hWon = H("wonsa")
    for hh in range(8):
        DMA("pool", wonsa_bf[:, hh, :], w_o_nsa[:, hh, :], key=hWon, W=[hWon] if hh == 0 else (), J=() if hh == 0 else [hWon])
    wogla_bf = AR.alloc([128, 4, D], BF16); hWog = H("wogla")
    for hh in range(4):
        DMA("pool", wogla_bf[:, hh, :], w_o_gla[:, hh, :], key=hWog, W=[hWog] if hh == 0 else (), J=() if hh == 0 else [hWog])
    w1_bf = AR.alloc([128, 2, 32, 128], BF16); hW1 = H("w1")
    for t in range(2):
        for s4 in range(2):
            DMA("pool", w1_bf[:, t, 16 * s4:16 * s4 + 16, :], w1bd[:, t, 16 * s4:16 * s4 + 16, :], key=hW1,
                W=[hW1] if (t == 0 and s4 == 0) else (), J=() if (t == 0 and s4 == 0) else [hW1])
    w2_bf, hW2 = load([128, 2, 128], BF16, w2bd)
    pe_bf, hPe = load([128, 2, 32], BF16, pecol, own=True)
    wgk_bf, hWgk = load([33, 256], BF16, wgk)
    As_bf, hAs = load([128, 4, 128], BF16, cAs)
    valid_f, hValid = load([128, NSLOT], F32, cValid)
    valc_f, hValC = load([128, 2], F32, cValC)
    bm_f, hBm = load([128, 64], F32, cBm)
    half_bf, hHalf = load([128, 128], BF16, cHalf)
    sels_f, hSels = load([4, 2, 128], F32, cSels)
    tri16_f, hTri16 = load([16, 16], F32, cTri16)
    triu16_f, hTriU16 = load([16, 16], F32, cTriU16)
    seqm_f, hSeqm = load([16, 4], F32, cSeqm)

    rb_f, hRb = load([33, 8], F32, rb33, own=True)
    _save_top = AR.top
    AR.top = AEND - 32768
    coef_f = AR.alloc([33, LEXT], F32); hCoef = H("coef")
    DMA("sp", coef_f[:], coef, key=hCoef, W=[hCoef])
    ext_sb = AR.alloc([8, LEXT], BF16); hExtSb = H("extsb")
    AR.top = _save_top
    nch = (LEXT + 511) // 512
    for i in range(nch):
        a = i * 512
        b = min(LEXT, a + 512)
        pb, ph = rr()
        MM(pb[0:8, 0:b - a], rb_f[:], coef_f[:, a:b], R=[hRb, hCoef], W=[ph])
        CP("act", ext_sb[:, a:b], pb[0:8, 0:b - a], R=[ph], W=[hExtSb] if i == 0 else (), )
        if i > 0:
            S.ops[-1]
    hExtSb.writers = [op for op in S.q["act"][-nch:]]
    DMA("sp", ext_bf.ap(), ext_sb[:], key=hExtSb, R=[hExtSb], W=[H_ext])

    def hankel(dst, g, base, pstride, nq, h, npart=128):
        src = bass.AP(ext_bf, 4 * g * LEXT + base, [[pstride, npart], [LEXT, 4], [1, nq]])
        if h in (GHS, GHSn, GHA):
            gdma("sp", dst, src, h, R=[H_ext])
        else:
            DMA("sp", dst, src, key=h, R=[H_ext], W=[h])

    GHS, GHSn, GHA = H("GHS"), H("GHSn"), H("GHA")

    HKS = AR.alloc([128, 4, 2, 16], BF16)
    hHKS = [[GHS for g in range(2)] for i in range(4)]
    MS("dve", HKS[:], 0.0, W=[GHS])
    HKC = AR.alloc([128, 2, 64], BF16)
    HKW = AR.alloc([128, 2, 64], BF16)
    HKS2 = AR.alloc([128, 2, 2, 16], BF16)
    MS("dve", HKC[:], 0.0, J=[GHS])
    MS("dve", HKW[:], 0.0, J=[GHS])
    for g in range(2):
        hankel(HKS[:, 0, g, :].rearrange("p (r q) -> p r q", r=4), g, OFFA - 15, 16, 4, hHKS[0][g])
        hankel(HKS[:, 1, g, :].rearrange("p (r q) -> p r q", r=4), g, OFFA + 1, 1, 4, hHKS[1][g])
        hankel(HKS[0:4, 2, g, :].rearrange("p (r q) -> p r q", r=4), g, OFFA - 3, 1, 4, hHKS[2][g], npart=4)
        hankel(HKS[:, 3, g, :].rearrange("p (r q) -> p r q", r=4), g, LA + OFFW + 385, 1, 4, hHKS[3][g])

    for g in range(2):
        for j in range(2):
            hankel(HKS2[:, j, g, :].rearrange("p (r q) -> p r q", r=4), g, OFFA + 2 - j, 2, 4, GHS)
        hankel(HKC[:, g, 48:64].rearrange("p (r q) -> p r q", r=4), g, OFFA - 15, 16, 4, GHS)
        hankel(HKW[:, g, 0:16].rearrange("p (r q) -> p r q", r=4), g, LA + OFFW + 385, 1, 4, GHS)
        hankel(HKW[:, g, 48:64].rearrange("p (r q) -> p r q", r=4), g, OFFA + 1, 1, 4, GHS)

    pew1 = AR.alloc([128, 2], F32); hPew1 = H("pew1")
    pb, ph = rr()
    for t in range(2):
        for sx in range(32):
            MM(pb[:, t * 8:t * 8 + 8], w1_bf[:, t, sx, :], pe_bf[:, t, sx:sx + 1].to_broadcast([128, 8]), start=(sx == 0), stop=(sx == 31),
               R=[hW1, hPe], **(dict(W=[ph]) if (t == 0 and sx == 0) else dict(J=[ph])))
    CP("dve", pew1[:], pb[:, 0:16:8], R=[ph], W=[hPew1])

    S.barrier()
    MIX_END = AR.top

    def WJ(h, first):
        return dict(W=[h]) if first else dict(J=[h])

    def phase_s():
        cur_grp[0] = H("GS")
        identf, hIdF = load([128, 128], F32, cIdent, name="identf")
        valS_f, hValS = load([128, 4], F32, cValS)
        xs_s = AR.alloc([128, KC, NS], F32); hxs_s = H("xs_s")
        DMA("sp", xs_s[:], xsT, key=hxs_s, W=[hxs_s])
        sq_s = AR.alloc([128, KC, NS], BF16); hsq_s = H("sq_s")
        rstd_s = AR.alloc([128, NS], F32); hrstd_s = H("rstd_s")
        xn_s = AR.alloc([128, KC, NS], BF16); hxn_s = H("xn_s")
        ACT(sq_s[:], xs_s[:], AF.Square, R=[hxs_s], W=[hsq_s])
        pb, ph = rr()
        for kc in range(KC):
            MM(pb[:, 0:NS], ones_bf[:], sq_s[:, kc, :], start=(kc == 0), stop=(kc == KC - 1), R=[hOnes, hsq_s], **WJ(ph, kc == 0))
        ACT(rstd_s[:], pb[:, 0:NS], AF.Ln, bias=EPS, scale=1.0 / D, R=[ph], W=[hrstd_s])
        ACT(rstd_s[:], rstd_s[:], AF.Exp, scale=-0.5, R=[hrstd_s], W=[hrstd_s])
        for kc in range(KC):
            STT("dve", xn_s[:, kc, :], xs_s[:, kc, :], norms_f[:, 0, kc:kc + 1], rstd_s[:], ALU.mult, ALU.mult,
                R=[hxs_s, hrstd_s, hNorms], **WJ(hxn_s, kc == 0))

        def pF(out, c0, M, fw):
            for kc in range(KC):
                MM(out, win_bf[:, kc, c0:c0 + M], xn_s[:, kc, :], start=(kc == 0), stop=(kc == KC - 1), R=[hWin, hxn_s], **fw(kc))

        def pTm(out, c0, N, fw):
            for kc in range(KC):
                MM(out, xn_s[:, kc, :], win_bf[:, kc, c0:c0 + N], start=(kc == 0), stop=(kc == KC - 1), R=[hWin, hxn_s], **fw(kc))

        if cut == 99.1:
            return
        QTs = AR.alloc([128, 64], BF16); hQTs = H("QTs")
        pq, phq = rr()
        for r in range(4):
            pF(pq[:, r * 16:(r + 1) * 16], r * 128, 128, lambda kc, r=r: WJ(phq, r == 0 and kc == 0))
        TS("dve", QTs[:].rearrange("p (b r q) -> p r b q", b=4, r=4), pq[:, 0:64].rearrange("p (r b q) -> p r b q", r=4, b=4),
           0.125, None, ALU.mult, R=[phq], W=[hQTs])
        kvo_s = AR.alloc([128, 4, NS], F32); hkvo_s = H("kvo_s")
        ksn = AR.alloc([128, NS], BF16); hksn = H("ksn")
        kwn = AR.alloc([128, NS], BF16); hkwn = H("kwn")
        pk, phk = rr()
        for i, c0 in enumerate((512, 640, 768, 1024)):
            pF(pk[:, i * 16:(i + 1) * 16], c0, 128, lambda kc, i=i: WJ(phk, i == 0 and kc == 0))
        CP("act", kvo_s[:].rearrange("p a b -> p (a b)"), pk[:, 0:64], R=[phk], W=[hkvo_s])
        CP("act", ksn[:], pk[:, 32:48], R=[phk], W=[hksn])
        CP("act", kwn[:], pk[:, 48:64], R=[phk], W=[hkwn])
        DMA("sp", kvT_s, kvo_s[:], key=hkvo_s, R=[hkvo_s])
        vto_s = AR.alloc([NS, 2, 128], F32); hvto_s = H("vto_s")
        Vsn = AR.alloc([NS, 2, 65], BF16); hVsn = H("Vsn")
        Vwn = AR.alloc([NS, 2, 65], BF16); hVwn = H("Vwn")
        Vsn_m = AR.alloc([NS, 4, 2, 65], BF16); hVsn_m = H("Vsn_m")
        Vwn_m = AR.alloc([NS, 4, 2, 65], BF16); hVwn_m = H("Vwn_m")
        pv, phv = rr()
        pTm(pv[0:NS, 0:128], 896, 128, lambda kc: WJ(phv, kc == 0))
        pTm(pv[0:NS, 128:256], 1152, 128, lambda kc: dict(J=[phv]))
        CP("act", vto_s[:].rearrange("p a b -> p (a b)"), pv[0:NS, 0:256], R=[phv], W=[hvto_s])
        DMA("sp", vtok_s, vto_s[:], key=hvto_s, R=[hvto_s])
        MS("pool", Vsn[:], 1.0, W=[hVsn])
        MS("pool", Vwn[:], 1.0, W=[hVwn])
        CP("act", Vsn[:, :, 0:64], pv[0:NS, 0:128].rearrange("p (g d) -> p g d", g=2), R=[phv], W=[hVsn])
        CP("act", Vwn[:, :, 0:64], pv[0:NS, 128:256].rearrange("p (g d) -> p g d", g=2), R=[phv], W=[hVwn])
        for bl in range(4):
            TS("dve", Vsn_m[:, bl, :, :], Vsn[:], seqm_f[:, bl:bl + 1], None, ALU.mult, R=[hVsn, hSeqm], **WJ(hVsn_m, bl == 0))
            TS("dve", Vwn_m[:, bl, :, :], Vwn[:], seqm_f[:, bl:bl + 1], None, ALU.mult, R=[hVwn, hSeqm], **WJ(hVwn_m, bl == 0))
        if cut == 99.2:
            return
        gate_s = AR.alloc([24, NS], F32); hgate_s = H("gate_s")
        grow_s = AR.alloc([128, 24 * NS], F32); hgrow_s = H("grow_s")
        pgt, phgt = rr()
        pF(pgt[0:24, 0:NS], C_GT, 24, lambda kc: WJ(phgt, kc == 0))
        ACT(gate_s[:], pgt[0:24, 0:NS], AF.Exp, scale=-1.0, R=[phgt], W=[hgate_s])
        TS("dve", gate_s[:], gate_s[:], 1.0, None, ALU.add, R=[hgate_s], W=[hgate_s])
        RCP("dve", gate_s[:], gate_s[:], R=[hgate_s], W=[hgate_s])
        DMA("sp", gscr_s, gate_s[:], key=hgate_s, R=[hgate_s], W=[H_gscr_s])
        DMA("sp", grow_s[64:65, :], gscr_s.rearrange("(o a) b -> o (a b)", o=1), key=hgrow_s, R=[H_gscr_s], W=[hgrow_s])
        if cut == 99.3:
            return
        HKSn = AR.alloc([NS, 4, 2, 16], BF16)
        hHKSn = [[GHSn for g in range(2)] for bl in range(4)]
        for bl in range(4):
            for g in range(2):
                hankel(HKSn[0:NS, bl, g, :].rearrange("p (r q) -> p r q", r=4), g, OFFA - 15 + 4 * bl, 1, 4, hHKSn[bl][g], npart=NS)
        if cut == 99.4:
            return
        lrT_s = AR.alloc([64, NS], BF16); hLr_s = H("lrT_s")
        MS("pool", lrT_s[:], 0.0, W=[hLr_s])
        MS("pool", lrT_s[32:33, :], 1.0, J=[hLr_s])
        la_s = AR.alloc([NS, 256], F32); hla_s = H("la_s")
        esuf_s = AR.alloc([NS, 256], F32); hesuf_s = H("esuf_s")
        ecum_s = AR.alloc([128, 2 * NS], F32); hecum_s = H("ecum_s")
        einv_s = AR.alloc([128, 2 * NS], F32); heinv_s = H("einv_s")
        keT_s = AR.alloc([128, 2 * NS], BF16); hke_s = H("keT_s")
        qeT_s = AR.alloc([128, 2 * NS], BF16); hqe_s = H("qeT_s")
        kd_s = AR.alloc([NS, 256], BF16); hkd_s = H("kd_s")
        KDM_off = [AR.top]
        kdm = AR.alloc([NS, 4, 256], BF16); hkdm = H("kdm")
        vg_s = AR.alloc([NS, 512], BF16); hvg_s = H("vg_s")
        attT_s = AR.alloc([NS, NS], BF16); hatt_s = H("attT_s")
        S0_off = [AR.top]
        S0 = AR.alloc([128, 4, 2, 128], F32); hS0 = H("S0")
        S0bf = AR.alloc([128, 4, 2, 128], BF16); hS0bf = H("S0bf")
        omixg_s = AR.alloc([128, 4, NS], BF16); homixg_s = H("omixg_s")
        osq_s = AR.alloc([128, NS], BF16); hosq_s = H("osq_s")
        grs_s = AR.alloc([128, NS], F32); hgrs_s = H("grs_s")
        sg_s = AR.alloc([128, NS], F32); hsg_s = H("sg_s")
        t1_s = AR.alloc([128, NS], F32); ht1_s = H("t1_s")
        first = True
        for bl in range(4):
            for hd in range(4):
                DMA("sp", S0[(hd % 2) * 64:(hd % 2) * 64 + 64, bl, hd // 2, :], sgla[bl, hd], key=hS0, **WJ(hS0, first))
                first = False
        CP("pool", S0bf[:], S0[:], R=[hS0], W=[hS0bf])
        pb, ph = rr()
        pF(pb[0:16, 0:NS], C_LR, 16, lambda kc: WJ(ph, kc == 0))
        CP("act", lrT_s[0:16, :], pb[0:16, 0:NS], R=[ph], W=[hLr_s])
        pz, phz = rr()
        MM(pz[0:NS, 0:256], lrT_s[0:33, :], wgk_bf[:], R=[hLr_s, hWgk], W=[phz])
        ACT(la_s[:], pz[0:NS, 0:256], AF.Exp, scale=-1.0, R=[phz], W=[hla_s])
        ACT(la_s[:], la_s[:], AF.Ln, bias=1.0, R=[hla_s], W=[hla_s])
        psf, phs = rr()
        MM(psf[0:NS, 0:256], triu16_f[:], la_s[:], R=[hTriU16, hla_s], W=[phs])
        ACT(esuf_s[:], psf[0:NS, 0:256], AF.Exp, scale=-1.0 / 16, R=[phs], W=[hesuf_s])
        pct, phc = rr()
        for ch in range(2):
            MM(pct[:, ch * NS:(ch + 1) * NS], la_s[:, ch * 128:(ch + 1) * 128], tri16_f[:], R=[hla_s, hTri16], **WJ(phc, ch == 0))
        ACT(ecum_s[:], pct[:, 0:2 * NS], AF.Exp, scale=-1.0 / 16, R=[phc], W=[hecum_s])
        ACT(einv_s[:], pct[:, 0:2 * NS], AF.Exp, scale=1.0 / 16, R=[phc], W=[heinv_s])
        pk, phk = rr()
        for ch in range(2):
            pF(pk[:, ch * NS:(ch + 1) * NS], C_KG + ch * 128, 128, lambda kc, ch=ch: WJ(phk, ch == 0 and kc == 0))
        TT("dve", keT_s[:], pk[:, 0:2 * NS], einv_s[:], ALU.mult, R=[phk, heinv_s], W=[hke_s])
        pq, phq = rr()
        for ch in range(2):
            pF(pq[:, ch * NS:(ch + 1) * NS], C_QG + ch * 128, 128, lambda kc, ch=ch: WJ(phq, ch == 0 and kc == 0))
        STT("dve", qeT_s[:], pq[:, 0:2 * NS], 0.125, ecum_s[:], ALU.mult, ALU.mult, R=[phq, hecum_s], W=[hqe_s])
        pkt, phkt = rr()
        pTm(pkt[0:NS, 0:256], C_KG, 256, lambda kc: WJ(phkt, kc == 0))
        TT("dve", kd_s[:], pkt[0:NS, 0:256], esuf_s[:], ALU.mult, R=[phkt, hesuf_s], W=[hkd_s])
        for bl in range(4):
            TS("dve", kdm[:, bl, :], kd_s[:], seqm_f[:, bl:bl + 1], None, ALU.mult, R=[hkd_s, hSeqm], **WJ(hkdm, bl == 0))
        pv2, phv2 = rr()
        pTm(pv2[0:NS, 0:512], C_VG, 512, lambda kc: WJ(phv2, kc == 0))
        CP("act", vg_s[:], pv2[0:NS, 0:512], R=[phv2], W=[hvg_s])
        if cut == 99.5:
            return
        for hd in range(4):
            ch = hd // 2
            pp = slice(64 * (hd % 2), 64 * (hd % 2) + 64)
            cs = slice(ch * NS, (ch + 1) * NS)
            pa, pha = rr()
            MM(pa[0:NS, 0:NS], keT_s[pp, cs], qeT_s[pp, cs], R=[hke_s, hqe_s], W=[pha])
            TT("dve", attT_s[:], pa[0:NS, 0:NS], tri16_f[:], ALU.mult, R=[pha, hTri16], W=[hatt_s])
            po, pho = rr()
            MM(po[:, 0:NS], vg_s[:, hd * 128:(hd + 1) * 128], attT_s[:], start=True, stop=False, R=[hvg_s, hatt_s], W=[pho])
            for bl in range(4):
                MM(po[:, bl * 4:bl * 4 + 4], S0bf[pp, bl, ch, :], qeT_s[pp, ch * NS + bl * 4:ch * NS + bl * 4 + 4],
                   start=False, stop=(bl == 3), R=[hS0bf, hqe_s], J=[pho])
            ACT(osq_s[:], po[:, 0:NS], AF.Square, R=[pho], W=[hosq_s])
            pn, phn = rr()
            MM(pn[:, 0:NS], ones_bf[:], osq_s[:], R=[hOnes, hosq_s], W=[phn])
            ACT(grs_s[:], pn[:, 0:NS], AF.Ln, bias=EPS, scale=1.0 / 128, R=[phn], W=[hgrs_s])
            ACT(grs_s[:], grs_s[:], AF.Exp, scale=-0.5, R=[hgrs_s], W=[hgrs_s])
            STT("dve", t1_s[:], po[:, 0:NS], glan_f[:, 0:1], grs_s[:], ALU.mult, ALU.mult, R=[pho, hGlan, hgrs_s], W=[ht1_s])
            pg, phg = rr()
            pF(pg[:, 0:NS], C_GG + hd * 128, 128, lambda kc: WJ(phg, kc == 0))
            ACT(sg_s[:], pg[:, 0:NS], AF.Exp, scale=-1.0, R=[phg], W=[hsg_s])
            TS("dve", sg_s[:], sg_s[:], 1.0, None, ALU.add, R=[hsg_s], W=[hsg_s])
            RCP("dve", sg_s[:], sg_s[:], R=[hsg_s], W=[hsg_s])
            TT("dve", sg_s[:], sg_s[:], pg[:, 0:NS], ALU.mult, R=[hsg_s, phg], W=[hsg_s])
            TT("dve", omixg_s[:, hd, :], t1_s[:], sg_s[:], ALU.mult, R=[ht1_s, hsg_s], **WJ(homixg_s, hd == 0))
        if cut == 99.6:
            return
        for bl in range(4):
            for ch in range(2):
                pu, phu = rr()
                MM(pu[:, 0:256], kdm[:, bl, ch * 128:(ch + 1) * 128], vg_s[:, ch * 256:(ch + 1) * 256], R=[hkdm, hvg_s], W=[phu])
                for hh in range(2):
                    pp = slice(64 * hh, 64 * hh + 64)
                    col = ch * NS + bl * 4 + 3
                    STT("dve", S0[pp, bl, ch, :], S0[pp, bl, ch, :], ecum_s[pp, col:col + 1], pu[pp, hh * 128:(hh + 1) * 128],
                        ALU.mult, ALU.add, R=[hS0, hecum_s, phu], W=[hS0])
        for bl in range(4):
            DMA("sp", gla_s[bl], S0[:, bl, :, :], key=hS0, R=[hS0])
        if cut == 99.7:
            return
        DMA("sp", win_s, cwin[:, 4:512, :], key=H("wincp"))

        if cut == 100:
            return
        KsT_s = AR.alloc([128, PAST], BF16); hKsT_s = H("KsT_s")
        Vs_s = AR.alloc([128, NPG, 2, 65], BF16); hVs_s = H("Vs_s")
        rawT_s = AR.alloc([128, 2, PAST], BF16); hraw_s = H("rawT_s")
        stg = [AR.alloc([128, 2, 512], F32)]; hstg = [H("stg0")]
        _s0base = KDM_off[0]
        assert S0_off[0] + 4096 + 2048 - _s0base >= 2 * 4096
        for i in range(2):
            stg.append(nc.alloc_sbuf_tensor_at("stgx%d" % i, [128, 2, 512], F32, offset=_s0base + i * 4096))
            hx_ = H("stgx%d" % i)
            for hd_ in (hkdm, hvg_s, hatt_s, hS0, hS0bf):
                hx_.writers += list(hd_.writers)
                hx_.readers += list(hd_.readers)
            hstg.append(hx_)
        NSTG = len(stg)
        KcT_s = AR.alloc([128, 512], BF16); hKcT_s = H("KcT_s")
        geV_s = AR.alloc([128, 512], BF16); hgeV_s = H("geV_s")
        ge_s = AR.alloc([128, 512], BF16); hge_s = H("ge_s")
        Vc_s = AR.alloc([128, 4, 2, 65], BF16); hVc_s = H("Vc_s")
        gx_s = AR.alloc([128, 512], F32); hgx_s = H("gx_s")
        gu_s = AR.alloc([128, 512], F32); hgu_s = H("gu_s")
        Pc_s = AR.alloc([128, 64], BF16); hPc_s = H("Pc_s")
        pn_s = AR.alloc([128, 64], F32); hpn_s = H("pn_s")
        PsT_s = AR.alloc([128, 4, 4], BF16); hPsT_s = H("PsT_s")
        P_s = AR.alloc([128, 1024], BF16); hP_s = H("P_s")
        R_s = AR.alloc([128, 1024], BF16); hR_s = H("R_s")
        KwT_s = AR.alloc([128, 512], BF16); hKwT_s = H("KwT_s")
        Vw_s = AR.alloc([128, 4, 2, 65], BF16); hVw_s = H("Vw_s")
        wstg = [AR.alloc([128, 256], F32) for _ in range(2)]; hwstg = [H("wstg0"), H("wstg1")]
        Pw_s = AR.alloc([128, 64], BF16); hPw_s = H("Pw_s")
        Pn16 = AR.alloc([NS, NS], BF16); hPn16 = H("Pn16")
        crow_s = AR.alloc([128, NS], F32); hcrow_s = H("crow_s")
        num_s = AR.alloc([64, NS], F32); hnum_s = H("num_s")
        onsa_s = AR.alloc([64, 2, NS], F32); honsa_s = H("onsa_s")
        onsab_s = AR.alloc([64, 2, 4, NS], BF16); honsab_s = H("onsab_s")
        score_s = AR.alloc([4, 128], F32); hscore_s = H("score_s")
        sc2_s = AR.alloc([4, 128], F32); hsc2_s = H("sc2_s")
        m8_s = AR.alloc([4, 8], F32); hm8_s = H("m8_s")
        m8b_s = AR.alloc([4, 8], F32); hm8b_s = H("m8b_s")
        nsel_s = AR.alloc([4, 128], BF16); hnsel_s = H("nsel_s")
        nselT_s = AR.alloc([128, 4], F32); hnselT_s = H("nselT_s")
        h1s = AR.alloc([128, KC, NS], F32); hh1s = H("h1s")
        MS("pool", Vs_s[:], 1.0, W=[hVs_s])
        MS("pool", Vw_s[:], 1.0, W=[hVw_s])
        MS("pool", KcT_s[:], 0.0, W=[hKcT_s])
        MS("pool", geV_s[:], 0.0, W=[hgeV_s])
        ACCs = [PB[0], PB[1]]
        hACCs = [PBH[0], PBH[1]]
        accs_rot = [0]

        def next_acc():
            i = accs_rot[0] % 2
            accs_rot[0] += 1
            return ACCs[i], hACCs[i]

        def finish_s(g, br, bl, acc, hacc, first_branch):
            TS("dve", crow_s[64:65, :], acc[64:65, 0:NS], 1e-30, None, ALU.max, R=[hacc], W=[hcrow_s])
            RCP("dve", crow_s[64:65, :], crow_s[64:65, :], R=[hcrow_s], W=[hcrow_s])
            off0 = (br * 8 + g * 4) * NS
            gv = grow_s[64:65, off0:off0 + 4 * NS].rearrange("o (r t) -> o r t", t=NS)[:, :, bl * 4:bl * 4 + 4]
            TT("dve", crow_s[64:65, :].rearrange("o (r q) -> o r q", r=4), crow_s[64:65, :].rearrange("o (r q) -> o r q", r=4), gv,
               ALU.mult, R=[hcrow_s, hgrow_s], W=[hcrow_s])
            pb, ph = rr()
            MM(pb[0:64, 0:NS], ones_f[64:65, 0:64], crow_s[64:65, :], R=[hOnesF, hcrow_s], W=[ph])
            CP("act", num_s[:], acc[0:64, 0:NS], R=[hacc], W=[hnum_s])
            if first_branch:
                TT("dve", onsa_s[:, g, :], num_s[:], pb[0:64, 0:NS], ALU.mult, R=[hnum_s, ph], W=[honsa_s])
            else:
                TT("dve", num_s[:], num_s[:], pb[0:64, 0:NS], ALU.mult, R=[hnum_s, ph], W=[hnum_s])
                TT("dve", onsa_s[:, g, :], onsa_s[:, g, :], num_s[:], ALU.add, R=[hnum_s, honsa_s], W=[honsa_s])
            if dbg and bl == 0:
                nm = "d_br%d%d" % (g, br)
                DBG[nm] = dout(nm, [64, NS])
                DMA("sp", DBG[nm], onsa_s[:, g, :], key=H("k" + nm), R=[honsa_s])

        def new_tile(g, bl, kn, hkn, Vm, hVm, acc, hacc):
            gp = slice(64 * g, 64 * g + 64)
            pn_, phn_ = rr()
            MM(pn_[0:NS, 0:NS], kn[gp, :], QTs[gp, bl * 16:(bl + 1) * 16], start=True, stop=False, R=[hkn, hQTs], W=[phn_])
            MM(pn_[0:NS, 0:NS], J_bf[0:NS, 128 - NS:128], HKSn[0:NS, bl, g, :], start=False, stop=True, R=[hJ, hHKSn[bl][g]], J=[phn_])
            ACT(Pn16[:], pn_[0:NS, 0:NS], AF.Exp, R=[phn_], W=[hPn16])
            if dbg and bl == 0 and g == 0 and kn is ksn:
                DBG["d_pn16"] = dout("d_pn16", [NS, NS])
                CP("act", crow_s[0:NS, 0:NS], pn_[0:NS, 0:NS], R=[phn_], W=[hcrow_s])
                DMA("sp", DBG["d_pn16"], crow_s[0:NS, 0:NS], key=H("kpn16"), R=[hcrow_s])
            MM(acc[0:65, 0:NS], Vm[:, bl, g, :], Pn16[:], start=False, stop=True, R=[hVm, hPn16], J=[hacc])

        pcol_f, hPcol = load([128, 1], F32, cPcol, own=True)
        ptab_i = AR.alloc([128, 4 * NPG], I32); hptab_i = H("ptab_i")
        DMA("sp", ptab_i[:], ptab.partition_broadcast(128), key=hptab_i, W=[hptab_i])
        idx_f = AR.alloc([128, 4 * NPG], F32); hidx_f = H("idx_f")
        idx_i = AR.alloc([128, 2 * NPG], I32); hidx_i = H("idx_i")
        CP("dve", idx_f[:], ptab_i[:], R=[hptab_i], W=[hidx_f])
        idx_f2 = AR.alloc([128, 2 * NPG], F32); hidx_f2 = H("idx_f2")
        CP("dve", idx_f2[0:64, :], idx_f[0:64, :].rearrange("p (a j) -> p a j", j=2)[:, :, 0], R=[hidx_f], W=[hidx_f2])
        CP("dve", idx_f2[64:128, :], idx_f[64:128, :].rearrange("p (a j) -> p a j", j=2)[:, :, 1], R=[hidx_f], J=[hidx_f2])
        TS("dve", idx_f2[:], idx_f2[:], 64.0, pcol_f[:, 0:1], ALU.mult, ALU.add, R=[hidx_f2, hPcol], W=[hidx_f2])
        CP("dve", idx_i[:, 0:2 * NPG], idx_f2[:], R=[hidx_f2], W=[hidx_i])
        cache_rows = cache.rearrange("n (p j) d -> (n p) (j d)", j=2)

        def gather_page(dst, idx):
            def f(e):
                return e.indirect_dma_start(out=dst, out_offset=None, in_=cache_rows,
                                            in_offset=bass.IndirectOffsetOnAxis(ap=idx_i[:, idx:idx + 1], axis=0))
            return f

        for bl in range(nseq):
            for pair in range(NPG // 2):
                sb_, hsb_ = stg[pair % NSTG], hstg[pair % NSTG]
                S.add("pool", gather_page(sb_[:].rearrange("p j d -> p (j d)"), bl * (NPG // 2) + pair), reads=[hidx_i], writes=[hsb_], dma=True, key=hsb_)
                for j in range(2):
                    tile_ = 2 * pair + j
                    ptx, phtx = rr()
                    for ci in range(3):
                        TR(ptx[:, ci * 128:(ci + 1) * 128], sb_[:, j, ci * 128:(ci + 1) * 128], identf[:], R=[hsb_, hIdF], **WJ(phtx, ci == 0))
                    CP("act", rawT_s[:, :, 256 * pair + j:256 * pair + j + 255:2], ptx[:, 0:256].rearrange("p (t n) -> p t n", t=2), R=[phtx],
                       **WJ(hraw_s, tile_ == 0))
                    CP("dve", KsT_s[:, tile_ * 128:(tile_ + 1) * 128], ptx[:, 256:384], R=[phtx], **WJ(hKsT_s, tile_ == 0))
                    CP("pool", Vs_s[:, tile_, :, 0:64], sb_[:, j, 384:512].rearrange("p (g d) -> p g d", g=2), R=[hsb_], **WJ(hVs_s, tile_ == 0))
                pg = 2 * pair + 1
                if (pg + 1) % 8 == 0:
                    gi = pg // 8
                    n0 = max(0, 64 * gi - 1)
                    cnt = 64 * gi + 62 - n0 + 1
                    for t in range(2):
                        pc, phc_ = rr()
                        for sx in range(32):
                            c0_ = 16 * n0 + sx
                            MM(pc[:, 0:cnt], w1_bf[:, t, sx, :], rawT_s[:, t, c0_:c0_ + 16 * (cnt - 1) + 1:16], start=(sx == 0), stop=(sx == 31),
                               R=[hW1, hraw_s], **WJ(phc_, sx == 0))
                        TS("dve", gx_s[:, 0:cnt], pc[:, 0:cnt], pew1[:, t:t + 1], None, ALU.add, R=[phc_, hPew1], W=[hgx_s])
                        TT("dve", gu_s[:, 0:cnt], gx_s[:, 0:cnt], gx_s[:, 0:cnt], ALU.mult, R=[hgx_s], W=[hgu_s])
                        TS("dve", gu_s[:, 0:cnt], gu_s[:, 0:cnt], 0.044715, 1.0, ALU.mult, ALU.add, R=[hgu_s], W=[hgu_s])
                        TT("dve", gu_s[:, 0:cnt], gu_s[:, 0:cnt], gx_s[:, 0:cnt], ALU.mult, R=[hgu_s, hgx_s], W=[hgu_s])
                        ACT(gu_s[:, 0:cnt], gu_s[:, 0:cnt], AF.Exp, scale=-1.5957691216057308, R=[hgu_s], W=[hgu_s])
                        TS("dve", gu_s[:, 0:cnt], gu_s[:, 0:cnt], 1.0, None, ALU.add, R=[hgu_s], W=[hgu_s])
                        RCP("dve", gu_s[:, 0:cnt], gu_s[:, 0:cnt], R=[hgu_s], W=[hgu_s])
                        if t == 0:
                            TT("dve", ge_s[:, 0:cnt], gx_s[:, 0:cnt], gu_s[:, 0:cnt], ALU.mult, R=[hgx_s, hgu_s], W=[hge_s])
                            pk2, phk2 = rr()
                            MM(pk2[:, 0:cnt], w2_bf[:, 0, :], ge_s[:, 0:cnt], R=[hW2, hge_s], W=[phk2])
                            CP("act", KcT_s[:, n0:n0 + cnt], pk2[:, 0:cnt], R=[phk2], **WJ(hKcT_s, gi == 0))
                        else:
                            TT("dve", geV_s[:, n0:n0 + cnt], gx_s[:, 0:cnt], gu_s[:, 0:cnt], ALU.mult, R=[hgx_s, hgu_s], **WJ(hgeV_s, gi == 0))
            if cut == 101:
                return
            for wt in range(4):
                wb_, hwb_ = wstg[wt % 2], hwstg[wt % 2]
                DMA("sp", wb_[:], cwin[bl, wt * 128:(wt + 1) * 128, :], key=hwb_, W=[hwb_])
                ptx, phtx = rr()
                TR(ptx[:, 0:128], wb_[:, 0:128], identf[:], R=[hwb_, hIdF], W=[phtx])
                CP("act", KwT_s[:, wt * 128:(wt + 1) * 128], ptx[:, 0:128], R=[phtx], **WJ(hKwT_s, wt == 0))
                CP("pool", Vw_s[:, wt, :, 0:64], wb_[:, 128:256].rearrange("p (g d) -> p g d", g=2), R=[hwb_], **WJ(hVw_s, wt == 0))
            for ct in range(4):
                pvc, phvc = rr()
                MM(pvc[:, 0:128], geV_s[:, ct * 128:(ct + 1) * 128], w2_bf[:, 1, :], R=[hgeV_s, hW2], W=[phvc])
                TS("dve", Vc_s[:, ct, :, 0:64], pvc[:, 0:128].rearrange("p (g d) -> p g d", g=2), valS_f[:, ct:ct + 1], None, ALU.mult,
                   R=[phvc, hValS], **WJ(hVc_s, ct == 0))
                for g in range(2):
                    CP("pool", Vc_s[:, ct, g, 64:65], valS_f[:, ct:ct + 1], R=[hValS], J=[hVc_s])
            if cut == 103:
                return
            for g in range(2):
                gp = slice(64 * g, 64 * g + 64)
                qs = QTs[gp, bl * 16:(bl + 1) * 16]
                pS, phS = rr()
                MM(pS[:, 0:64], J_bf[:], HKC[:, g, :], start=True, stop=False, R=[hJ, GHS], W=[phS])
                for ct in range(4):
                    MM(pS[:, ct * 16:(ct + 1) * 16], KcT_s[gp, ct * 128:(ct + 1) * 128], qs, start=False, stop=(ct == 3),
                       R=[hKcT_s, hQTs], J=[phS])
                ACT(Pc_s[:], pS[:, 0:64], AF.Exp, R=[phS], W=[hPc_s])
                acc, hacc = next_acc()
                for ct in range(4):
                    MM(acc[0:65, 0:NS], Vc_s[:, ct, g, :], Pc_s[:, ct * 16:(ct + 1) * 16], start=(ct == 0), stop=(ct == 3),
                       R=[hVc_s, hPc_s], **WJ(hacc, ct == 0))
                TS("dve", crow_s[64:65, :], acc[64:65, 0:NS], 1e-30, None, ALU.max, R=[hacc], W=[hcrow_s])
                RCP("dve", crow_s[64:65, :], crow_s[64:65, :], R=[hcrow_s], W=[hcrow_s])
                pbc, phbc = rr()
                MM(pbc[:, 0:NS], ones_f[64:65, :], crow_s[64:65, :], R=[hOnesF, hcrow_s], W=[phbc])
                CP("act", pn_s[:, 0:NS], pbc[:, 0:NS], R=[phbc], W=[hpn_s])
                for ct in range(4):
                    TT("dve", pn_s[:, 16 + 0:16 + NS] if False else gx_s[:, ct * 16:(ct + 1) * 16], Pc_s[:, ct * 16:(ct + 1) * 16], pn_s[:, 0:NS], ALU.mult,
                       R=[hPc_s, hpn_s], **WJ(hgx_s, ct == 0))

                for ct in range(4):
                    def _reds(e, ct=ct):
                        with nc.allow_low_precision("fp32 accumulate inside, bf16 store"):
                            return e.tensor_reduce(out=PsT_s[:, ct, :], in_=gx_s[:, ct * 16:(ct + 1) * 16].rearrange("p (r q) -> p q r", r=4),
                                                   axis=AX.X, op=ALU.add)
                    S.add("dve", _reds, reads=[hgx_s], **({"writes": [hPsT_s]} if ct == 0 else {"joins": [hPsT_s]}))
                pim, phim = rr()
                for ct in range(4):
                    MM(pim[0:4, 0:128], PsT_s[:, ct, :], As_bf[:, ct, :], start=(ct == 0), stop=(ct == 3), R=[hPsT_s, hAs], **WJ(phim, ct == 0))
                TT("dve", score_s[:], pim[0:4, 0:128], sels_f[:, 0, :], ALU.mult, R=[phim, hSels], W=[hscore_s])
                TT("dve", score_s[:], score_s[:], sels_f[:, 1, :], ALU.add, R=[hscore_s, hSels], W=[hscore_s])
                S.add("dve", lambda e: e.max(out=m8_s[:], in_=score_s[:]), reads=[hscore_s], writes=[hm8_s])
                S.add("dve", lambda e: e.match_replace(out=sc2_s[:], in_to_replace=m8_s[:], in_values=score_s[:], imm_value=-1e30),
                      reads=[hscore_s, hm8_s], writes=[hsc2_s])
                S.add("dve", lambda e: e.max(out=m8b_s[:], in_=sc2_s[:]), reads=[hsc2_s], writes=[hm8b_s])
                TS("dve", sc2_s[:], score_s[:], m8b_s[:, 6:7], None, ALU.is_ge, R=[hscore_s, hm8b_s], W=[hsc2_s])
                TS("dve", nsel_s[:], sc2_s[:], -1.0, BIG, ALU.add, ALU.mult, R=[hsc2_s], W=[hnsel_s])
                TR(PT[:, 0:4], nsel_s[:], ident_bf[0:4, 0:4], R=[hnsel_s, hId], W=[H_PT])
                CP("act", nselT_s[:], PT[:, 0:4], R=[H_PT], W=[hnselT_s])
                TT("dve", R_s[:].rearrange("p (k r q) -> p k r q", k=NPG, r=4),
                   nselT_s[:].unsqueeze(1).unsqueeze(1).to_broadcast([128, NPG, 4, 4]),
                   bm_f[:].unsqueeze(2).unsqueeze(3).to_broadcast([128, NPG, 4, 4]), ALU.mult,
                   R=[hnselT_s, hBm], W=[hR_s])
                finish_s(g, 0, bl, acc, hacc, True)
                if cut == 104:
                    return
                p1, ph1 = rr()
                p2, ph2 = rr()
                MM(p1[:, :], half_bf[:], R_s[:, 0:512], start=True, stop=False, R=[hHalf, hR_s], W=[ph1])
                MM(p2[:, :], half_bf[:], R_s[:, 512:1024], start=True, stop=False, R=[hHalf, hR_s], W=[ph2])
                for pg in range(NPG):
                    pp_, php_ = (p1, ph1) if pg < 32 else (p2, ph2)
                    c0 = (pg % 32) * 16
                    MM(pp_[:, c0:c0 + 16], KsT_s[gp, pg * 128:(pg + 1) * 128], qs, start=False, stop=(pg == 31),
                       R=[hKsT_s, hQTs], J=[php_])
                for j in range(2):
                    MM(p2[:, 480 + 16 * j:496 + 16 * j], J_bf[:], HKS2[:, j, g, :], start=False, stop=(j == 1), R=[hJ, GHS], J=[ph2])
                ACT(P_s[:, 0:512], p1[:, :], AF.Exp, R=[ph1], W=[hP_s])
                ACT(P_s[:, 512:1024], p2[:, :], AF.Exp, R=[ph2], J=[hP_s])
                if dbg and bl == 0 and g == 0:
                    DBG["d_S2"] = dout("d_S2", [128, 512])
                    CP("act", gx_s[:, 0:512], p2[:, :], R=[ph2], W=[hgx_s])
                    DMA("sp", DBG["d_S2"], gx_s[:, 0:512], key=H("kS2"), R=[hgx_s])
                    DBG["d_nselT"] = dout("d_nselT", [128, 4])
                    DMA("sp", DBG["d_nselT"], nselT_s[:], key=H("knselT"), R=[hnselT_s])
                acc, hacc = next_acc()
                for pg in range(NPG):
                    MM(acc[0:65, 0:NS], Vs_s[:, pg, g, :], P_s[:, pg * 16:(pg + 1) * 16], start=(pg == 0), stop=False,
                       R=[hVs_s, hP_s], **WJ(hacc, pg == 0))
                new_tile(g, bl, ksn, hksn, Vsn_m, hVsn_m, acc, hacc)
                finish_s(g, 1, bl, acc, hacc, False)
                if cut == 105:
                    return
                pW, phW = rr()
                MM(pW[:, 0:64], J_bf[:], HKW[:, g, :], start=True, stop=False, R=[hJ, GHS], W=[phW])
                for wt in range(4):
                    MM(pW[:, wt * 16:(wt + 1) * 16], KwT_s[gp, wt * 128:(wt + 1) * 128], qs, start=False, stop=(wt == 3),
                       R=[hKwT_s, hQTs], J=[phW])
                ACT(Pw_s[:], pW[:, 0:64], AF.Exp, R=[phW], W=[hPw_s])
                if dbg and bl == 0 and g == 0:
                    DBG["d_pW"] = dout("d_pW", [128, 64])
                    CP("act", gu_s[:, 0:64], pW[:, 0:64], R=[phW], W=[hgu_s])
                    DMA("sp", DBG["d_pW"], gu_s[:, 0:64], key=H("kpW"), R=[hgu_s])
                acc, hacc = next_acc()
                for wt in range(4):
                    MM(acc[0:65, 0:NS], Vw_s[:, wt, g, :], Pw_s[:, wt * 16:(wt + 1) * 16], start=(wt == 0), stop=False,
                       R=[hVw_s, hPw_s], **WJ(hacc, wt == 0))
                new_tile(g, bl, kwn, hkwn, Vwn_m, hVwn_m, acc, hacc)
                if dbg and bl == 0 and g == 0:
                    DBG["d_accW"] = dout("d_accW", [65, NS])
                    CP("act", gx_s[0:65, 0:NS], acc[0:65, 0:NS], R=[hacc], W=[hgx_s])
                    DMA("sp", DBG["d_accW"], gx_s[0:65, 0:NS], key=H("kaccW"), R=[hgx_s])
                finish_s(g, 2, bl, acc, hacc, False)
            CP("act", onsab_s[:, :, :, bl * 4:bl * 4 + 4], onsa_s[:].rearrange("p g (r q) -> p g r q", r=4), R=[honsa_s],
               **WJ(honsab_s, bl == 0))
        if nseq < 4:
            pass
        pw, phw = rr()
        for dc in range(KC):
            osl = pw[:, dc * NS:(dc + 1) * NS]
            n = 0
            for g in range(2):
                for r in range(4):
                    MM(osl, wonsa_bf[:, g * 4 + r, dc * 128:(dc + 1) * 128], onsab_s[:, g, r, :], start=(n == 0), stop=False,
                       R=[hWon, honsab_s], **WJ(phw, dc == 0 and n == 0))
                    n += 1
            for hd in range(4):
                MM(osl, wogla_bf[:, hd, dc * 128:(dc + 1) * 128], omixg_s[:, hd, :], start=False, stop=(hd == 3),
                   R=[hWog, homixg_s], J=[phw])
        TT("dve", h1s[:], xs_s[:], pw[:, 0:KC * NS].rearrange("p (c q) -> p c q", c=KC), ALU.add, R=[hxs_s, phw], W=[hh1s])
        DMA("sp", h1_scr[:, :, NT:NT + NS], h1s[:], key=hh1s, R=[hh1s], W=[H_h1[NOWN]])
        if dbg:
            DBG["d_h1s"] = dout("d_h1s", [128, KC, NS])
            DMA("sp", DBG["d_h1s"], h1s[:], key=H("kd_h1s"), R=[hh1s])
        DBG["S_END"] = AR.top

    if do_sample:
        phase_s()
        S.barrier()
        AR.top = MIX_END


    cur_grp[0] = H("GA")
    E_bf = AR.alloc([128, TOK], BF16); hE = H("E")
    for i in range(3):
        a, b = i * 1408, (i + 1) * 1408
        DMA("pool", E_bf[:, a:b], cE[:, a:b], key=hE, W=[hE] if i == 0 else (), J=() if i == 0 else [hE])
    A_bf, hA = load([128, 2, 128], BF16, cA)
    HK = AR.alloc([128, 3, 2, 512], BF16)
    hHK = [[GHA for g in range(2)] for i in range(3)]
    for g in range(2):
        hankel(HK[:, 0, g, :].rearrange("p (r q) -> p r q", r=4), g, OFFA - 127, 1, 128, hHK[0][g])
        hankel(HK[:, 1, g, :].rearrange("p (r q) -> p r q", r=4), g, OFFA + 1, 1, 128, hHK[1][g])
        hankel(HK[:, 2, g, :].rearrange("p (r q) -> p r q", r=4), g, LA + OFFW + 385, 1, 128, hHK[2][g])
    KsT = AR.alloc([128, TOK], BF16); hKs = [H("ks%d" % s) for s in range(NSLOT)]
    KwT = AR.alloc([128, 8 * 128], BF16); hKw = [H("kw%d" % s) for s in range(8)]
    Vs = AR.alloc([128, NSLOT, 2, 65], BF16); hVs = [H("vs%d" % s) for s in range(NSLOT)]
    Vw = AR.alloc([128, 8, 2, 65], BF16); hVw = [H("vw%d" % s) for s in range(8)]
    KcT = AR.alloc([128, 272], BF16); hKc = H("kc")
    geV = AR.alloc([128, 272], BF16); hGeV = H("gev")
    Vc = AR.alloc([128, 2, 2, 65], BF16); hVc = H("vc")
    rawT = AR.alloc([128, 2, 144], BF16); hRawP = H("rawp"); hRawC = H("rawc")
    Sst = AR.alloc([128, 2, 128], F32); hS = H("S")
    Sbf = AR.alloc([128, 2, 128], BF16); hSbf = H("Sbf")
    lrT = AR.alloc([64, 128], BF16); hLr = H("lrT")
    MS("pool", KsT[:], 0.0, W=hKs)
    MS("pool", KwT[:], 0.0, W=hKw)
    MS("pool", Vs[:], 0.0, W=hVs)
    MS("pool", Vw[:], 0.0, W=hVw)
    MS("pool", KcT[:], 0.0, W=[hKc])
    MS("pool", geV[:], 0.0, W=[hGeV])
    MS("pool", Vc[:], 0.0, W=[hVc])
    MS("pool", rawT[:], 0.0, W=[hRawP, hRawC])
    MS("pool", Sst[:], 0.0, W=[hS])
    MS("pool", Sbf[:], 0.0, W=[hSbf])
    MS("pool", lrT[:], 0.0, W=[hLr])
    MS("pool", lrT[32:33, :], 1.0, J=[hLr])

    xs = [AR.alloc([128, KC, 128], F32) for _ in range(2)]; hxs = [H("xs0"), H("xs1")]
    sq = AR.alloc([128, KC, 128], BF16); hsq = H("sq")
    rstd = AR.alloc([128, 128], F32); hrstd = H("rstd")
    xn = AR.alloc([128, KC, 128], BF16); hxn = H("xn")
    kvo = AR.alloc([128, 4, 128], F32); hkvo = H("kvo")
    vto = AR.alloc([128, 2, 128], F32); hvto = H("vto")
    la = AR.alloc([128, 256], F32); hla = H("la")
    esuf = AR.alloc([128, 256], F32); hesuf = H("esuf")
    ecum = AR.alloc([128, 256], F32); hecum = H("ecum")
    einv = AR.alloc([128, 256], F32); heinv = H("einv")
    keT = AR.alloc([128, 256], BF16); hke = H("keT")
    qeT = AR.alloc([128, 256], BF16); hqe = H("qeT")
    kd = AR.alloc([128, 256], BF16); hkd = H("kd")
    vg = AR.alloc([128, 512], BF16); hvg = H("vg")
    attT = AR.alloc([128, 128], BF16); hatt = H("attT")
    osq = AR.alloc([128, 128], BF16); hosq = H("osq")
    grs = AR.alloc([128, 128], F32); hgrs = H("grs")
    sg = AR.alloc([128, 128], F32); hsg = H("sg")
    t1 = AR.alloc([128, 128], F32); ht1 = H("t1")
    omixg = AR.alloc([128, 4, 128], BF16); homixg = H("omixg")
    gx = AR.alloc([128, 8], F32); hgx = H("gx")
    gu = AR.alloc([128, 8], F32); hgu = H("gu")
    ge = AR.alloc([128, 8], BF16); hge = H("ge")
    QT = AR.alloc([128, 512], BF16); hQT = H("QT")
    gate_sb = AR.alloc([24, 128], F32); hgate = H("gate")
    grow = AR.alloc([128, 24 * 128], F32); hgrow = H("grow")
    crow = AR.alloc([128, 512], F32); hcrow = H("crow")
    HKc = [AR.alloc([128, 2, 512], BF16) for _ in range(2)]; hHKc = [[H("hkc%d%d" % (i, g)) for g in range(2)] for i in range(2)]
    Pc = AR.alloc([128, 2, 512], BF16); hPc = [H("pc0"), H("pc1")]
    Pt = [AR.alloc([128, 512], BF16) for _ in range(3)]; hPt = [H("pt%d" % i) for i in range(3)]
    pn_f = AR.alloc([128, 512], F32); hpn = H("pn")
    PsT = AR.alloc([128, 2, 128], BF16); hPsT = H("PsT")
    selc = AR.alloc([128, 2, 128], F32); hselc = H("selc")
    score = AR.alloc([128, 128], F32); hscore = H("score")
    sc2 = AR.alloc([128, 128], F32); hsc2 = H("sc2")
    m8 = AR.alloc([128, 8], F32); hm8 = H("m8")
    m8b = AR.alloc([128, 8], F32); hm8b = H("m8b")
    nsel = AR.alloc([128, 128], BF16); hnsel = H("nsel")
    nselT = AR.alloc([128, 512], BF16); hnselT = H("nselT")
    numsb = AR.alloc([64, 512], F32); hnum = H("num")
    onsa = AR.alloc([64, 2, 512], F32); honsa = H("onsa")
    onsab = AR.alloc([64, 2, 512], BF16); honsab = H("onsab")
    h1t = AR.alloc([128, KC, 128], F32); hh1t = H("h1t")
    pt_rot = [0]

    def projF(out, c0, M, first_w):
        for kc in range(KC):
            MM(out, win_bf[:, kc, c0:c0 + M], xn[:, kc, :], start=(kc == 0), stop=(kc == KC - 1),
               R=[hWin, hxn], **first_w(kc))

    def projT(out, c0, N, first_w):
        for kc in range(KC):
            MM(out, xn[:, kc, :], win_bf[:, kc, c0:c0 + N], start=(kc == 0), stop=(kc == KC - 1),
               R=[hWin, hxn], **first_w(kc))

    def rms_rstd(src_ps, hsrc, n, scale):
        pass

    ACC = [PB[0], PB[1]]
    hACC = [PBH[0], PBH[1]]
    acc_rot = [0]

    def attn_tile(g, lhsK, hK, bias, mask, Vaug, hV, acc, hacc, first, last, keep=None, hkeep=None):
        gp = slice(64 * g, 64 * g + 64)
        pb, ph = rr()
        nmm = 1 + (bias is not None) + (mask is not None)
        k = 0
        MM(pb[:, :], lhsK, QT[gp, :], start=True, stop=(nmm == 1), R=[hK, hQT], W=[ph])
        k += 1
        if bias is not None:
            bap, bh = bias
            MM(pb[:, :], J_bf[:], bap, start=False, stop=(k == nmm - 1), R=[hJ, bh], J=[ph])
            k += 1
        if mask is not None:
            eap = mask
            MM(pb[:, :], eap, nselT[:, :], start=False, stop=True,
               R=[hE, hnselT], J=[ph])
        if keep is None:
            i = pt_rot[0] % 3
            pt_rot[0] += 1
            P, hP = Pt[i][:], hPt[i]
        else:
            P, hP = keep, hkeep
        ACT(P, pb[:, :], AF.Exp, R=[ph], W=[hP])
        MM(acc[0:65, :], Vaug, P, start=first, stop=last, R=[hV, hP], **(dict(W=[hacc]) if first else dict(J=[hacc])))

    def branch_finish(g, br, acc, hacc, first_branch):
        TS("dve", crow[64:65, :], acc[64:65, :], 1e-30, None, ALU.max, R=[hacc], W=[hcrow])
        RCP("dve", crow[64:65, :], crow[64:65, :], R=[hcrow], W=[hcrow])
        off = (br * 8 + g * 4) * 128
        TT("dve", crow[64:65, :], crow[64:65, :], grow[64:65, off:off + 512], ALU.mult, R=[hcrow, hgrow], W=[hcrow])
        pb, ph = rr()
        MM(pb[0:64, :], ones_f[64:65, 0:64], crow[64:65, :], R=[hOnesF, hcrow], W=[ph])
        CP("act", numsb[:, :], acc[0:64, :], R=[hacc], W=[hnum])
        if first_branch:
            TT("dve", onsa[:, g, :], numsb[:, :], pb[0:64, :], ALU.mult, R=[hnum, ph], W=[honsa])
        else:
            TT("dve", numsb[:, :], numsb[:, :], pb[0:64, :], ALU.mult, R=[hnum, ph], W=[hnum])
            TT("dve", onsa[:, g, :], onsa[:, g, :], numsb[:, :], ALU.add, R=[hnum, honsa], W=[honsa])

    for s in range(nslots):
        own = (s % 2 == 1)
        jo = s // 2
        xb, hx = xs[s % 2], hxs[s % 2]
        DMA("sp", xb[:], xT[:, :, s * 128:(s + 1) * 128], key=hx, W=[hx])
        ACT(sq[:], xb[:], AF.Square, R=[hx], W=[hsq])
        pb, ph = rr()
        for kc in range(KC):
            MM(pb[:, 0:128], ones_bf[:], sq[:, kc, :], start=(kc == 0), stop=(kc == KC - 1), R=[hOnes, hsq], **WJ(ph, kc == 0))
        ACT(rstd[:], pb[:, 0:128], AF.Ln, bias=EPS, scale=1.0 / D, R=[ph], W=[hrstd])
        ACT(rstd[:], rstd[:], AF.Exp, scale=-0.5, R=[hrstd], W=[hrstd])
        for kc in range(KC):
            STT("dve", xn[:, kc, :], xb[:, kc, :], norms_f[:, 0, kc:kc + 1], rstd[:], ALU.mult, ALU.mult,
                R=[hx, hrstd, hNorms], **WJ(hxn, kc == 0))
        if cut <= 1:
            break
        pb, ph = rr()
        for i, c0 in enumerate((512, 640, 768, 1024)):
            projF(pb[:, i * 128:(i + 1) * 128], c0, 128, lambda kc, i=i: WJ(ph, i == 0 and kc == 0))
        if cut == 1.1:
            break
        CP("act", rawT[:, :, 16:144], pb[:, 0:256].rearrange("p (t n) -> p t n", t=2), R=[ph], W=[hRawC])
        if cut == 1.2:
            break
        CP("act", KsT[:, s * 128:(s + 1) * 128], pb[:, 256:384], R=[ph], W=[hKs[s]])
        CP("act", KwT[:, (s % 8) * 128:(s % 8 + 1) * 128], pb[:, 384:512], R=[ph], W=[hKw[s % 8]])
        if own:
            CP("act", kvo[:].rearrange("p a b -> p (a b)"), pb[:, :], R=[ph], W=[hkvo])
            DMA("sp", kvT_p[:, :, jo * 128:(jo + 1) * 128], kvo[:], key=hkvo, R=[hkvo])
        if cut <= 2:
            break
        pb, ph = rr()
        projT(pb[:, 0:128], 896, 128, lambda kc: WJ(ph, kc == 0))
        projT(pb[:, 128:256], 1152, 128, lambda kc: dict(J=[ph]))
        CP("dve", Vs[:, s, :, 0:64], pb[:, 0:128].rearrange("p (g d) -> p g d", g=2), R=[ph], W=[hVs[s]])
        CP("dve", Vw[:, s % 8, :, 0:64], pb[:, 128:256].rearrange("p (g d) -> p g d", g=2), R=[ph], W=[hVw[s % 8]])
        for g in range(2):
            CP("pool", Vs[:, s, g, 64:65], valid_f[:, s:s + 1], R=[hValid], J=[hVs[s]])
            CP("pool", Vw[:, s % 8, g, 64:65], valid_f[:, s:s + 1], R=[hValid], J=[hVw[s % 8]])
        if own:
            CP("act", vto[:].rearrange("p a b -> p (a b)"), pb[:, 0:256], R=[ph], W=[hvto])
            DMA("sp", vtok_p[jo * 128:(jo + 1) * 128, :, :], vto[:], key=hvto, R=[hvto])
        if cut <= 3:
            break
        for t in range(2):
            pc, phc = rr()
            for sx in range(32):
                MM(pc[:, 0:8], w1_bf[:, t, sx, :], rawT[:, t, sx:sx + 113:16], start=(sx == 0), stop=(sx == 31),
                   R=[hW1, hRawP, hRawC], **WJ(phc, sx == 0))
            TS("dve", gx[:], pc[:, 0:8], pew1[:, t:t + 1], None, ALU.add, R=[phc, hPew1], W=[hgx])
            TT("dve", gu[:], gx[:], gx[:], ALU.mult, R=[hgx], W=[hgu])
            TS("dve", gu[:], gu[:], 0.044715, 1.0, ALU.mult, ALU.add, R=[hgu], W=[hgu])
            TT("dve", gu[:], gu[:], gx[:], ALU.mult, R=[hgu, hgx], W=[hgu])
            ACT(gu[:], gu[:], AF.Exp, scale=-1.5957691216057308, R=[hgu], W=[hgu])
            TS("dve", gu[:], gu[:], 1.0, None, ALU.add, R=[hgu], W=[hgu])
            RCP("dve", gu[:], gu[:], R=[hgu], W=[hgu])
            if t == 0:
                TT("dve", ge[:], gx[:], gu[:], ALU.mult, R=[hgx, hgu], W=[hge])
                pk2, phk2 = rr()
                MM(pk2[:, 0:8], w2_bf[:, 0, :], ge[:], R=[hW2, hge], W=[phk2])
                CP("act", KcT[:, 8 * s:8 * s + 8], pk2[:, 0:8], R=[phk2], W=[hKc])
            else:
                TT("dve", geV[:, 8 * s:8 * s + 8], gx[:], gu[:], ALU.mult, R=[hgx, hgu], W=[hGeV])
        CP("pool", rawT[:, :, 0:16], rawT[:, :, 128:144], R=[hRawC], W=[hRawP])
        if cut <= 4:
            break
        pb, ph = rr()
        projF(pb[0:16, 0:128], C_LR, 16, lambda kc: WJ(ph, kc == 0))
        CP("act", lrT[0:16, :], pb[0:16, 0:128], R=[ph], W=[hLr])
        pz, phz = rr()
        MM(pz[:, 0:256], lrT[0:33, :], wgk_bf[:], R=[hLr, hWgk], W=[phz])
        ACT(la[:], pz[:, 0:256], AF.Exp, scale=-1.0, R=[phz], W=[hla])
        ACT(la[:], la[:], AF.Ln, bias=1.0, R=[hla], W=[hla])
        psf, phs = rr()
        MM(psf[:, 0:256], triu_f[:], la[:], R=[hTriU, hla], W=[phs])
        ACT(esuf[:], psf[:, 0:256], AF.Exp, scale=-1.0 / 16, R=[phs], W=[hesuf])
        pct, phc = rr()
        for ch in range(2):
            MM(pct[:, ch * 128:(ch + 1) * 128], la[:, ch * 128:(ch + 1) * 128], tri_f[:], R=[hla, hTri], **WJ(phc, ch == 0))
        ACT(ecum[:], pct[:, 0:256], AF.Exp, scale=-1.0 / 16, R=[phc], W=[hecum])
        ACT(einv[:], pct[:, 0:256], AF.Exp, scale=1.0 / 16, R=[phc], W=[heinv])
        if cut <= 5:
            break
        pk, phk = rr()
        for ch in range(2):
            projF(pk[:, ch * 128:(ch + 1) * 128], C_KG + ch * 128, 128, lambda kc, ch=ch: WJ(phk, ch == 0 and kc == 0))
        TT("dve", keT[:], pk[:, 0:256], einv[:], ALU.mult, R=[phk, heinv], W=[hke])
        pkt, phkt = rr()
        projT(pkt[:, 0:256], C_KG, 256, lambda kc: WJ(phkt, kc == 0))
        TT("dve", kd[:], pkt[:, 0:256], esuf[:], ALU.mult, R=[phkt, hesuf], W=[hkd])
        pv, phv = rr()
        projT(pv[:, 0:512], C_VG, 512, lambda kc: WJ(phv, kc == 0))
        CP("act", vg[:], pv[:, 0:512], R=[phv], W=[hvg])
        if own:
            pq, phq = rr()
            for ch in range(2):
                projF(pq[:, ch * 128:(ch + 1) * 128], C_QG + ch * 128, 128, lambda kc, ch=ch: WJ(phq, ch == 0 and kc == 0))
            STT("dve", qeT[:], pq[:, 0:256], 0.125, ecum[:], ALU.mult, ALU.mult, R=[phq, hecum], W=[hqe])
            for hd in range(4):
                ch = hd // 2
                pp = slice(64 * (hd % 2), 64 * (hd % 2) + 64)
                cs = slice(ch * 128, (ch + 1) * 128)
                pa, pha = rr()
                MM(pa[:, 0:128], keT[pp, cs], qeT[pp, cs], R=[hke, hqe], W=[pha])
                TT("dve", attT[:], pa[:, 0:128], tri_f[:], ALU.mult, R=[pha, hTri], W=[hatt])
                po, pho = rr()
                MM(po[:, 0:128], vg[:, hd * 128:(hd + 1) * 128], attT[:], start=True, stop=False, R=[hvg, hatt], W=[pho])
                MM(po[:, 0:128], Sbf[pp, ch, :], qeT[pp, cs], start=False, stop=True, R=[hSbf, hqe], J=[pho])
                ACT(osq[:], po[:, 0:128], AF.Square, R=[pho], W=[hosq])
                pn, phn = rr()
                MM(pn[:, 0:128], ones_bf[:], osq[:], R=[hOnes, hosq], W=[phn])
                ACT(grs[:], pn[:, 0:128], AF.Ln, bias=EPS, scale=1.0 / 128, R=[phn], W=[hgrs])
                ACT(grs[:], grs[:], AF.Exp, scale=-0.5, R=[hgrs], W=[hgrs])
                STT("dve", t1[:], po[:, 0:128], glan_f[:, 0:1], grs[:], ALU.mult, ALU.mult, R=[pho, hGlan, hgrs], W=[ht1])
                pg, phg = rr()
                projF(pg[:, 0:128], C_GG + hd * 128, 128, lambda kc: WJ(phg, kc == 0))
                ACT(sg[:], pg[:, 0:128], AF.Exp, scale=-1.0, R=[phg], W=[hsg])
                TS("dve", sg[:], sg[:], 1.0, None, ALU.add, R=[hsg], W=[hsg])
                RCP("dve", sg[:], sg[:], R=[hsg], W=[hsg])
                TT("dve", sg[:], sg[:], pg[:, 0:128], ALU.mult, R=[hsg, phg], W=[hsg])
                TT("dve", omixg[:, hd, :], t1[:], sg[:], ALU.mult, R=[ht1, hsg], **WJ(homixg, hd == 0))
        if cut <= 6:
            break
        for ch in range(2):
            pu, phu = rr()
            MM(pu[:, 0:256], kd[:, ch * 128:(ch + 1) * 128], vg[:, ch * 256:(ch + 1) * 256], R=[hkd, hvg], W=[phu])
            for hh in range(2):
                pp = slice(64 * hh, 64 * hh + 64)
                STT("dve", Sst[pp, ch, :], Sst[pp, ch, :], ecum[pp, ch * 128 + 127:ch * 128 + 128], pu[pp, hh * 128:(hh + 1) * 128],
                    ALU.mult, ALU.add, R=[hS, hecum, phu], W=[hS])
        CP("pool", Sbf[:], Sst[:], R=[hS], W=[hSbf])
        if not own:
            continue
        pq, phq = rr()
        for r in range(4):
            projF(pq[:, r * 128:(r + 1) * 128], r * 128, 128, lambda kc, r=r: WJ(phq, r == 0 and kc == 0))
        TS("dve", QT[:], pq[:, :], 0.125, None, ALU.mult, R=[phq], W=[hQT])
        pgt, phgt = rr()
        projF(pgt[0:24, 0:128], C_GT, 24, lambda kc: WJ(phgt, kc == 0))
        ACT(gate_sb[:], pgt[0:24, 0:128], AF.Exp, scale=-1.0, R=[phgt], W=[hgate])
        TS("dve", gate_sb[:], gate_sb[:], 1.0, None, ALU.add, R=[hgate], W=[hgate])
        RCP("dve", gate_sb[:], gate_sb[:], R=[hgate], W=[hgate])
        DMA("sp", gscr[jo % 2], gate_sb[:], key=hgate, R=[hgate], W=[H_gscr[jo % 2]])
        DMA("sp", grow[64:65, :], gscr[jo % 2].rearrange("(o a) b -> o (a b)", o=1), key=hgrow, R=[H_gscr[jo % 2]], W=[hgrow])
        nct = 2 if s >= 17 else 1
        for ct in range(nct):
            pvc, phvc = rr()
            MM(pvc[:, 0:128], geV[:, ct * 128:(ct + 1) * 128], w2_bf[:, 1, :], R=[hGeV, hW2], W=[phvc])
            TS("dve", Vc[:, ct, :, 0:64], pvc[:, 0:128].rearrange("p (g d) -> p g d", g=2), valc_f[:, ct:ct + 1], None, ALU.mult,
               R=[phvc, hValC], **WJ(hVc, ct == 0))
            for g in range(2):
                CP("pool", Vc[:, ct, g, 64:65], valc_f[:, ct:ct + 1], R=[hValC], J=[hVc])
        DMA("sp", selc[:], cSel[jo], key=hselc, W=[hselc])
        for g in range(2):
            gp = slice(64 * g, 64 * g + 64)
            hk_i = jo % 2
            bias_ct = None
            if s <= 15:
                bias_ct, sprime = 0, s
            elif s >= 17:
                bias_ct, sprime = 1, s - 16
            if bias_ct is not None:
                hankel(HKc[hk_i][:, g, :].rearrange("p (r q) -> p r q", r=4), g, OFFA + 128 * sprime - 2047, 16, 128, hHKc[hk_i][g])
            acc, hacc = ACC[acc_rot[0] % 2], hACC[acc_rot[0] % 2]
            acc_rot[0] += 1
            for ct in range(nct):
                bias = (HKc[hk_i][:, g, :], hHKc[hk_i][g]) if ct == bias_ct else None
                attn_tile(g, KcT[gp, ct * 128:(ct + 1) * 128], hKc, bias, None, Vc[:, ct, g, :], hVc, acc, hacc,
                          ct == 0, ct == nct - 1, keep=Pc[:, ct, :], hkeep=hPc[ct])
            TS("dve", crow[64:65, :], acc[64:65, :], 1e-30, None, ALU.max, R=[hacc], W=[hcrow])
            RCP("dve", crow[64:65, :], crow[64:65, :], R=[hcrow], W=[hcrow])
            pbc, phbc = rr()
            MM(pbc[:, :], ones_f[64:65, :], crow[64:65, :], R=[hOnesF, hcrow], W=[phbc])
            for ct in range(nct):
                TT("dve", pn_f[:], Pc[:, ct, :], pbc[:, :], ALU.mult, R=[hPc[ct], phbc], W=[hpn])
                def _red(e, ct=ct):
                    with nc.allow_low_precision("fp32 accumulate inside, bf16 store"):
                        return e.tensor_reduce(out=PsT[:, ct, :], in_=pn_f[:].rearrange("p (r q) -> p q r", r=4), axis=AX.X, op=ALU.add)
                S.add("dve", _red, reads=[hpn], **({"writes": [hPsT]} if ct == 0 else {"joins": [hPsT]}))
            pim, phim = rr()
            for ct in range(nct):
                MM(pim[:, 0:128], PsT[:, ct, :], A_bf[:, ct, :], start=(ct == 0), stop=(ct == nct - 1), R=[hPsT, hA], **WJ(phim, ct == 0))
            TT("dve", score[:], pim[:, 0:128], selc[:, 0, :], ALU.mult, R=[phim, hselc], W=[hscore])
            TT("dve", score[:], score[:], selc[:, 1, :], ALU.add, R=[hscore, hselc], W=[hscore])
            S.add("dve", lambda e: e.max(out=m8[:], in_=score[:, 0:72]), reads=[hscore], writes=[hm8])
            S.add("dve", lambda e: e.match_replace(out=sc2[:, 0:72], in_to_replace=m8[:], in_values=score[:, 0:72], imm_value=-1e30),
                  reads=[hscore, hm8], writes=[hsc2])
            S.add("dve", lambda e: e.max(out=m8b[:], in_=sc2[:, 0:72]), reads=[hsc2], writes=[hm8b])
            TS("dve", sc2[:], score[:], m8b[:, 7:8], None, ALU.is_ge, R=[hscore, hm8b], W=[hsc2])
            TS("dve", nsel[:], sc2[:], -1.0, BIG, ALU.add, ALU.mult, R=[hsc2], W=[hnsel])
            TR(PT[:, 0:128], nsel[:], ident_bf[:], R=[hnsel, hId], W=[H_PT])
            CP("act", nselT[:].rearrange("p (r q) -> p r q", r=4), PT[:, 0:128].unsqueeze(1).to_broadcast([128, 4, 128]), R=[H_PT], W=[hnselT])
            branch_finish(g, 0, acc, hacc, True)
            acc, hacc = ACC[acc_rot[0] % 2], hACC[acc_rot[0] % 2]
            acc_rot[0] += 1
            for ks in range(s + 1):
                dl = s - ks
                bias = (HK[:, dl, g, :], hHK[dl][g]) if dl <= 1 else None
                attn_tile(g, KsT[gp, ks * 128:(ks + 1) * 128], hKs[ks], bias, E_bf[:, ks * 128:(ks + 1) * 128],
                          Vs[:, ks, g, :], hVs[ks], acc, hacc, ks == 0, ks == s)
            branch_finish(g, 1, acc, hacc, False)
            acc, hacc = ACC[acc_rot[0] % 2], hACC[acc_rot[0] % 2]
            acc_rot[0] += 1
            k0 = max(0, s - 4)
            for ks in range(k0, s + 1):
                dl = s - ks
                bias = None
                if dl <= 1:
                    bias = (HK[:, dl, g, :], hHK[dl][g])
                elif dl == 4:
                    bias = (HK[:, 2, g, :], hHK[2][g])
                attn_tile(g, KwT[gp, (ks % 8) * 128:(ks % 8 + 1) * 128], hKw[ks % 8], bias, None,
                          Vw[:, ks % 8, g, :], hVw[ks % 8], acc, hacc, ks == k0, ks == s)
            branch_finish(g, 2, acc, hacc, False)
        CP("act", onsab[:], onsa[:], R=[honsa], W=[honsab])
        for half in range(2):
            pw, phw = rr()
            for c4 in range(4):
                dc = half * 4 + c4
                osl = pw[:, c4 * 128:(c4 + 1) * 128]
                n = 0
                for g in range(2):
                    for r in range(4):
                        MM(osl, wonsa_bf[:, g * 4 + r, dc * 128:(dc + 1) * 128], onsab[:, g, r * 128:(r + 1) * 128],
                           start=(n == 0), stop=False, R=[hWon, honsab], **WJ(phw, c4 == 0 and n == 0))
                        n += 1
                for hd in range(4):
                    MM(osl, wogla_bf[:, hd, dc * 128:(dc + 1) * 128], omixg[:, hd, :], start=False, stop=(hd == 3),
                       R=[hWog, homixg], J=[phw])
            TT("dve", h1t[:, half * 4:half * 4 + 4, :], xb[:, half * 4:half * 4 + 4, :],
               pw[:, :].rearrange("p (c q) -> p c q", c=4), ALU.add, R=[hx, phw], **WJ(hh1t, half == 0))
        DMA("sp", h1_scr[:, :, jo * 128:(jo + 1) * 128], h1t[:], key=hh1t, R=[hh1t], W=[H_h1[jo]])
        if dbg and s == 31:
            DBG["d_h1L"] = dout("d_h1L", [128, KC, 128])
            DMA("sp", DBG["d_h1L"], h1t[:], key=hh1t, R=[hh1t])
        if dbg and s == 1:
            DBG["d_h1"] = dout("d_h1", [128, KC, 128])
            DMA("sp", DBG["d_h1"], h1t[:], key=hh1t, R=[hh1t])
            for nm, tl, hh, shp in (("d_onsa", onsa, honsa, [64, 2, 512]), ("d_grow", grow, hgrow, [128, 3072]), ("d_score", score, hscore, [128, 128]),
                                    ("d_t1", t1, ht1, [128, 128]), ("d_sg", sg, hsg, [128, 128]), ("d_grs", grs, hgrs, [128, 128]),
                                    ("d_la", la, hla, [128, 256]), ("d_ecum", ecum, hecum, [128, 256]), ("d_S", Sst, hS, [128, 2, 128]),
                                    ("d_rstd", rstd, hrstd, [128, 128]), ("d_pn", pn_f, hpn, [128, 512]), ("d_num", numsb, hnum, [64, 512]), ("d_gx", gx, hgx, [128, 8])):
                DBG[nm] = dout(nm, shp)
                DMA("sp", DBG[nm], tl[:], key=H("k" + nm), R=[hh])
    DMA("sp", gla_p, Sst[:], key=hS, R=[hS])
    A_END = AR.top

    S.barrier()
    AR.top = PERS_END
    wg_bf = AR.alloc([128, KC, DFF], BF16); hWg = H("wg")
    wu_bf = AR.alloc([128, KC, DFF], BF16); hWu = H("wu")
    wd_bf = AR.alloc([128, NFC, D], BF16); hWd = H("wd")
    wpg_bf = AR.alloc([128, KC, D], BF16); hWpg = H("wpg")
    wple_bf = AR.alloc([128, 2, D], BF16); hWple = H("wple")

    def wload(dst, src, n1, ncol, h, step):
        first = True
        for i in range(n1):
            for a in range(0, ncol, step):
                DMA("pool", dst[:, i, a:a + step], src[:, i, a:a + step], key=h, **(dict(W=[h]) if first else dict(J=[h])))
                first = False
    if os.environ.get("SKIPB") == "1":
        ctx = dict(locals())
        return ctx
    wload(wg_bf, w_gate, KC, DFF, hWg, 1408)
    wload(wu_bf, w_up, KC, DFF, hWu, 1408)
    wload(wd_bf, w_down, NFC, D, hWd, 1024)
    wload(wpg_bf, w_pg, KC, D, hWpg, 1024)
    wload(wple_bf, w_ple, 2, D, hWple, 1024)
    TBM = 256
    hb = AR.alloc([128, KC, TBM], F32); hhb = H("hb")
    sqb = AR.alloc([128, KC, TBM], BF16); hsqb = H("sqb")
    rsb = AR.alloc([128, TBM], F32); hrsb = H("rsb")
    xnb = AR.alloc([128, KC, TBM], BF16); hxnb = H("xnb")
    actb = AR.alloc([128, NFC, TBM], BF16); hactb = H("actb")
    egbs = [AR.alloc([128, TBM], F32) for _ in range(2)]; hegbs = [H("egb0"), H("egb1")]
    pTf = AR.alloc([128, 2, TBM], F32); hpTf = H("pTf")
    pTb = AR.alloc([128, 2, TBM], BF16); hpTb = H("pTb")

    def rmsnorm_fm(src, hsrc, dst, hdst, nidx, TB, out_f32=False):
        ACT(sqb[:, :, 0:TB], src[:, :, 0:TB], AF.Square, R=[hsrc], W=[hsqb])
        pb, ph = rr()
        for kc in range(KC):
            MM(pb[:, 0:TB], ones_bf[:], sqb[:, kc, 0:TB], start=(kc == 0), stop=(kc == KC - 1), R=[hOnes, hsqb], **WJ(ph, kc == 0))
        ACT(rsb[:, 0:TB], pb[:, 0:TB], AF.Ln, bias=EPS, scale=1.0 / D, R=[ph], W=[hrsb])
        ACT(rsb[:, 0:TB], rsb[:, 0:TB], AF.Exp, scale=-0.5, R=[hrsb], W=[hrsb])
        for kc in range(KC):
            STT("dve", dst[:, kc, 0:TB], src[:, kc, 0:TB], norms_f[:, nidx, kc:kc + 1], rsb[:, 0:TB], ALU.mult, ALU.mult,
                R=[hsrc, hrsb, hNorms], **(WJ(hdst, kc == 0) if hdst is not hsrc else dict(W=[hdst])))

    def ffn_block(t0, TB, hsrc_dram):
        DMA("sp", hb[:, :, 0:TB], h1_scr[:, :, t0:t0 + TB], key=hhb, R=[hsrc_dram], W=[hhb])
        DMA("sp", pTf[:, :, 0:TB], pT[:, :, t0:t0 + TB], key=hpTf, W=[hpTf])
        CP("pool", pTb[:, :, 0:TB], pTf[:, :, 0:TB], R=[hpTf], W=[hpTb])
        rmsnorm_fm(hb, hhb, xnb, hxnb, 1, TB)
        for fc in range(NFC):
            pg_, phg_ = rr()
            for kc in range(KC):
                MM(pg_[:, 0:TB], wg_bf[:, kc, fc * 128:(fc + 1) * 128], xnb[:, kc, 0:TB], start=(kc == 0), stop=(kc == KC - 1),
                   R=[hWg, hxnb], **WJ(phg_, kc == 0))
            pu_, phu_ = rr()
            for kc in range(KC):
                MM(pu_[:, 0:TB], wu_bf[:, kc, fc * 128:(fc + 1) * 128], xnb[:, kc, 0:TB], start=(kc == 0), stop=(kc == KC - 1),
                   R=[hWu, hxnb], **WJ(phu_, kc == 0))
            eg_, heg_ = egbs[fc % 2], hegbs[fc % 2]
            ACT(eg_[:, 0:TB], pg_[:, 0:TB], AF.Silu, R=[phg_], W=[heg_])
            TT("dve", actb[:, fc, 0:TB], eg_[:, 0:TB], pu_[:, 0:TB], ALU.mult, R=[heg_, phu_], **WJ(hactb, fc == 0))
        for dc in range(KC):
            pd_, phd_ = rr()
            for fc in range(NFC):
                MM(pd_[:, 0:TB], wd_bf[:, fc, dc * 128:(dc + 1) * 128], actb[:, fc, 0:TB], start=(fc == 0), stop=(fc == NFC - 1),
                   R=[hWd, hactb], **WJ(phd_, fc == 0))
            TT("dve", hb[:, dc, 0:TB], hb[:, dc, 0:TB], pd_[:, 0:TB], ALU.add, R=[hhb, phd_], W=[hhb])
        rmsnorm_fm(hb, hhb, xnb, hxnb, 2, TB)
        for dc in range(KC):
            pg_, phg_ = rr()
            for kc in range(KC):
                MM(pg_[:, 0:TB], wpg_bf[:, kc, dc * 128:(dc + 1) * 128], xnb[:, kc, 0:TB], start=(kc == 0), stop=(kc == KC - 1),
                   R=[hWpg, hxnb], **WJ(phg_, kc == 0))
            pu_, phu_ = rr()
            for k2 in range(2):
                MM(pu_[:, 0:TB], wple_bf[:, k2, dc * 128:(dc + 1) * 128], pTb[:, k2, 0:TB], start=(k2 == 0), stop=(k2 == 1),
                   R=[hWple, hpTb], **WJ(phu_, k2 == 0))
            eg_, heg_ = egbs[dc % 2], hegbs[dc % 2]
            ACT(eg_[:, 0:TB], pg_[:, 0:TB], AF.Sigmoid, R=[phg_], W=[heg_])
            TT("dve", eg_[:, 0:TB], eg_[:, 0:TB], pu_[:, 0:TB], ALU.mult, R=[heg_, phu_], W=[heg_])
            TT("dve", hb[:, dc, 0:TB], hb[:, dc, 0:TB], eg_[:, 0:TB], ALU.add, R=[hhb, heg_], W=[hhb])
        rmsnorm_fm(hb, hhb, hb, hhb, 3, TB)
        DMA("sp", yT[:, :, t0:t0 + TB], hb[:, :, 0:TB], key=hhb, R=[hhb])

    nblk = (min(nslots, NSLOT) // 2) * 128 // TBM
    for bi in range(nblk):
        ffn_block(bi * TBM, TBM, H_h1[(bi * TBM) // 128 + 1] if False else H_h1[min(NOWN - 1, (bi * TBM + TBM - 1) // 128)])
    if do_sample and not (99.05 <= cut < 107):
        ffn_block(NT, NS, H_h1[NOWN])
    B_END = AR.top
    ctx = dict(locals())
    return ctx


_STATIC = None


def _kc(a):
    K = a.shape[0] // 128
    return np.ascontiguousarray(a.reshape(K, 128, -1).transpose(1, 0, 2))


def prep_shared(inp):
    global _STATIC
    if _STATIC is None:
        _STATIC = _static_consts()
    sh = dict(_STATIC)
    l = 0
    wi = np.array(inp["w_in"][l])
    wi[:, 0:512] = wi[:, 0:512].reshape(D, 2, 4, 64).transpose(0, 2, 1, 3).reshape(D, 512)
    sh["w_in"] = _kc(wi)
    wo = inp["w_o"][l]
    sh["w_o_nsa"] = np.ascontiguousarray(wo[:512].reshape(8, 64, D).transpose(1, 0, 2))
    sh["w_o_gla"] = _kc(wo[512:])
    sh["w_gate"] = _kc(inp["w_gate"][l])
    sh["w_up"] = _kc(inp["w_up"][l])
    sh["w_down"] = _kc(inp["w_down"][l])
    sh["w_ple"] = _kc(inp["w_ple"][l])
    sh["w_pg"] = _kc(inp["w_ple_gate"][l])
    nm = np.stack([inp["norm_mix"][l], inp["norm_ffn"][l], inp["norm_ple"][l], inp["norm_final"]], 0)
    sh["norms"] = np.ascontiguousarray(nm.reshape(4, KC, 128).transpose(2, 0, 1))
    w1 = inp["cmp_w1"][l].reshape(2, 32, 64, 64)
    w1bd = np.zeros((128, 2, 32, 128), np.float32)
    for g in range(2):
        w1bd[g * 64:(g + 1) * 64, :, :, g * 64:(g + 1) * 64] = w1.transpose(2, 0, 1, 3)
    sh["w1bd"] = w1bd
    pe = inp["cmp_pe"][l]
    sh["pecol"] = np.ascontiguousarray(np.concatenate([pe.transpose(2, 0, 1)] * 2, 0))
    w2 = inp["cmp_w2"][l]
    w2bd = np.zeros((128, 2, 128), np.float32)
    for g in range(2):
        w2bd[g * 64:(g + 1) * 64, :, g * 64:(g + 1) * 64] = w2.transpose(1, 0, 2)
    sh["w2bd"] = w2bd
    wg = np.zeros((33, 256), np.float32)
    wg[:16] = inp["w_gk"][l]
    wg[32] = inp["b_gk"][l]
    sh["wgk"] = wg
    sh["glan"] = np.ascontiguousarray(inp["gla_norm"][l].reshape(128, 1))
    rb = np.zeros((33, 8), np.float32)
    rb[:32] = inp["rel_bias"]
    rb[32] = -BIG
    sh["rb33"] = rb
    sh["cache"] = np.ascontiguousarray(inp["cache_nsa_kv"][l].reshape(-1, 128, 512))
    return sh


def prep_core(inp, c, sh):
    b, par = c // 2, c % 2
    shift = 1 - par
    l = 0
    m = dict(sh)
    m.update(_core_consts(par))
    x = inp["x_prompt"][b]
    xs = np.zeros((TOK, D), np.float32)
    xs[shift * 128:shift * 128 + SEQ] = x
    m["xT"] = np.ascontiguousarray(xs.T.reshape(KC, 128, TOK).transpose(1, 0, 2))
    own_tok = np.concatenate([np.arange(128) + (2 * j + par) * 128 for j in range(NOWN)])
    bs = slice(4 * c, 4 * c + 4)
    p = np.concatenate([inp["p_prompt"][l, b][own_tok], inp["p_sample"][l, bs].reshape(NS, 256)], 0)
    m["pT"] = np.ascontiguousarray(p.T.reshape(2, 128, NTS).transpose(1, 0, 2))
    xsm = inp["x_sample"][bs].reshape(NS, D)
    m["xsT"] = np.ascontiguousarray(xsm.T.reshape(KC, 128, NS).transpose(1, 0, 2))
    m["ptab"] = np.ascontiguousarray(inp["page_table"][bs].reshape(1, 4 * NPG).astype(np.int32))
    m["cwin"] = np.ascontiguousarray(inp["cache_win_kv"][l, bs].reshape(4, 512, 256))
    m["sgla"] = np.ascontiguousarray(inp["state_gla"][l, bs])
    return m


_PROG = None


def _get_prog():
    global _PROG
    if _PROG is None:
        c = build_program(dbg=False)
        c["S"].finish()
        c["S"].emit()
        _PROG = c
    return _PROG


def kernel(**inputs):
    inp = {k: np.asarray(v) for k, v in inputs.items()}
    c = _get_prog()
    sh = prep_shared(inp)
    maps = []
    for ci in range(8):
        m = prep_core(inp, ci, sh)
        maps.append({k: m[k] for k in c["IN"]})
    res = run_bass_kernel_spmd(c["nc"], maps, core_ids=list(range(8)))
    R = res.results
    B, T = 4, SEQ
    y_prompt = np.zeros((B, T, D), np.float32)
    y_sample = np.zeros((32, 4, D), np.float32)
    new_kv_prompt = np.zeros((1, B, T, 4, 2, 64), np.float32)
    new_kv_sample = np.zeros((1, 32, 4, 4, 2, 64), np.float32)
    new_win_prompt = np.zeros((1, B, 512, 2, 2, 64), np.float32)
    new_win_sample = np.zeros((1, 32, 512, 2, 2, 64), np.float32)
    new_gla_prompt = np.zeros((1, B, 4, 64, 128), np.float32)
    new_gla_sample = np.zeros((1, 32, 4, 64, 128), np.float32)
    kwin = np.zeros((B, T, 2, 64), np.float32)
    vwin = np.zeros((B, T, 2, 64), np.float32)
    for ci in range(8):
        b, par = ci // 2, ci % 2
        r = R[ci]
        own_tok = np.concatenate([np.arange(128) + (2 * j + par) * 128 for j in range(NOWN)])
        kvT = r["kvT_p"]
        vt = r["vtok_p"]
        yT = r["yT"]
        y_prompt[b, own_tok] = yT[:, :, :NT].transpose(2, 1, 0).reshape(NT, D)
        y_sample[4 * ci:4 * ci + 4] = yT[:, :, NT:].transpose(2, 1, 0).reshape(4, 4, D)
        for i, sl in enumerate((0, 1, 2)):
            new_kv_prompt[0, b, own_tok, sl] = kvT[:, i, :].T.reshape(NT, 2, 64)
        new_kv_prompt[0, b, own_tok, 3] = vt[:, 0, :].reshape(NT, 2, 64)
        kwin[b, own_tok] = kvT[:, 3, :].T.reshape(NT, 2, 64)
        vwin[b, own_tok] = vt[:, 1, :].reshape(NT, 2, 64)
        if par == 0:
            gp = r["gla_p"]
            for h in range(4):
                new_gla_prompt[0, b, h] = gp[(h % 2) * 64:(h % 2) * 64 + 64, h // 2, :]
        ks = r["kvT_s"]
        vs = r["vtok_s"]
        for i, sl in enumerate((0, 1, 2)):
            new_kv_sample[0, 4 * ci:4 * ci + 4, :, sl] = ks[:, i, :].T.reshape(4, 4, 2, 64)
        new_kv_sample[0, 4 * ci:4 * ci + 4, :, 3] = vs[:, 0, :].reshape(4, 4, 2, 64)
        ws = r["win_s"].reshape(4, 508, 2, 2, 64)
        new_win_sample[0, 4 * ci:4 * ci + 4, :508] = ws
        new_win_sample[0, 4 * ci:4 * ci + 4, 508:, 0] = ks[:, 3, :].T.reshape(4, 4, 2, 64)
        new_win_sample[0, 4 * ci:4 * ci + 4, 508:, 1] = vs[:, 1, :].reshape(4, 4, 2, 64)
        new_gla_sample[0, 4 * ci:4 * ci + 4] = r["gla_s"].reshape(4, 2, 64, 2, 128).transpose(0, 3, 1, 2, 4).reshape(4, 4, 64, 128)
    new_win_prompt[0, :, :, 0] = kwin[:, T - 512:]
    new_win_prompt[0, :, :, 1] = vwin[:, T - 512:]
    return (y_prompt, y_sample, new_kv_prompt, new_kv_sample, new_win_prompt, new_win_sample,
            new_gla_prompt, new_gla_sample)
```

```python
import math
import os
import contextlib
import numpy as np
import ml_dtypes
import concourse.bass as bass
import concourse.mybir as mybir
from concourse.bass_utils import run_bass_kernel_spmd

F32 = mybir.dt.float32
BF16 = mybir.dt.bfloat16
I32 = mybir.dt.int32
ALU = mybir.AluOpType
AF = mybir.ActivationFunctionType
AX = mybir.AxisListType

ENGS = ("pe", "act", "dve", "pool", "sp")
SAME_ENG_SYNC = True
RESCHED = True
SWDGE_DEPTH = 100000


class H:
    __slots__ = ("name", "writers", "readers", "gdeps", "excl")

    def __init__(self, name="", excl=False):
        self.name = name
        self.excl = excl
        self.writers = []
        self.readers = []
        self.gdeps = []


class Op:
    __slots__ = ("eng", "fn", "deps", "dma", "key", "ticket", "sig", "odeps", "dur", "idx")

    def __init__(self, eng, fn, dma, key):
        self.eng = eng
        self.fn = fn
        self.deps = []
        self.odeps = []
        self.dur = 0.3
        self.idx = 0
        self.dma = dma
        self.key = key
        self.ticket = None
        self.sig = False


class Sched:
    def __init__(self, nc):
        self.nc = nc
        self.ops = []
        self.q = {e: [] for e in ENGS}
        self.all_dma = []

    def add(self, eng, fn, reads=(), writes=(), joins=(), dma=False, key=None, dur=None):
        op = Op(eng, fn, dma, key)
        op.dur = dur if dur is not None else (2.5 if dma else 0.3)
        deps = []
        for h in reads:
            deps.extend(h.writers)
            if h.excl:
                deps.extend(r for r in h.readers if r.eng != eng)
        for h in writes:
            deps.extend(h.writers)
            deps.extend(h.readers)
        for h in joins:
            deps.extend(h.gdeps)
        for h in reads:
            h.readers.append(op)
        for h in writes:
            h.gdeps = list(h.writers) + list(h.readers)
            h.writers = [op]
            h.readers = []
        for h in joins:
            if h.excl and h.writers:
                op.odeps.append(h.writers[-1])
            h.writers.append(op)
        seen = set()
        for d in deps:
            if d is op or id(d) in seen:
                continue
            seen.add(id(d))
            op.odeps.append(d)
            if (not d.dma) and d.eng == eng and not dma:
                if eng == "pe" or not SAME_ENG_SYNC:
                    continue
            op.deps.append(d)
            d.sig = True
        if dma:
            assert key is not None
            self.all_dma.append(op)
            lk = self.q.setdefault("lastkey", {})
            if id(key) in lk:
                op.odeps.append(lk[id(key)])
            lk[id(key)] = op
            if eng == "pool":
                hist = self.q["pool_dma_hist"] if "pool_dma_hist" in self.q else self.q.setdefault("pool_dma_hist", [])
                if len(hist) >= SWDGE_DEPTH and hist[-SWDGE_DEPTH] not in op.deps and id(hist[-SWDGE_DEPTH].key) not in self.q.setdefault("group_keys", set()):
                    op.deps.append(hist[-SWDGE_DEPTH])
                    op.odeps.append(hist[-SWDGE_DEPTH])
                    hist[-SWDGE_DEPTH].sig = True
                if hist:
                    op.odeps.append(hist[-1])
                hist.append(op)
        self.ops.append(op)
        self.q[eng].append(op)
        return op

    def barrier(self):
        lasts = []
        for e in ENGS:
            for o in reversed(self.q[e]):
                if o.fn is not None and not o.dma:
                    lasts.append(o)
                    break
        dm = list(self.all_dma)
        for e in ENGS:
            op = Op(e, None, False, "barrier")
            for d in lasts + dm:
                if (not d.dma) and d.eng == e:
                    continue
                op.deps.append(d)
                d.sig = True
            self.ops.append(op)
            self.q[e].append(op)

    def finish(self, eng="sp"):
        op = Op(eng, None, False, None)
        for d in self.all_dma:
            op.deps.append(d)
            d.sig = True
        self.ops.append(op)
        self.q[eng].append(op)

    def reschedule(self):
        import heapq
        for i, op in enumerate(self.ops):
            op.idx = i
        segs, cur = [], []
        for op in self.ops:
            if op.fn is None:
                if cur:
                    segs.append(cur)
                    cur = []
                segs.append([op])
            else:
                cur.append(op)
        if cur:
            segs.append(cur)
        new_ops = []
        for seg in segs:
            if len(seg) == 1 and seg[0].fn is None:
                b = seg[0]
                if b.key == "barrier":
                    b.deps = []
                    for e2 in ENGS:
                        for o2 in reversed(new_ops):
                            if o2.eng == e2 and o2.fn is not None and not o2.dma:
                                if e2 != b.eng:
                                    b.deps.append(o2)
                                    o2.sig = True
                                break
                    for o2 in new_ops:
                        if o2.dma:
                            b.deps.append(o2)
                            o2.sig = True
                new_ops.append(b)
                continue
            inseg = {id(o) for o in seg}
            npred = {}
            succ = {}
            ready_t = {}
            for o in seg:
                ps = [d for d in o.odeps if id(d) in inseg]
                npred[id(o)] = len(ps)
                ready_t[id(o)] = 0.0
                for d in ps:
                    succ.setdefault(id(d), []).append(o)
            heaps = {e: [] for e in ENGS}
            for o in seg:
                if npred[id(o)] == 0:
                    heapq.heappush(heaps[o.eng], (0.0, o.idx, o))
            free = {e: 0.0 for e in ENGS}
            done = 0
            while done < len(seg):
                best = None
                for e in ENGS:
                    if heaps[e]:
                        rt, ix, o = heaps[e][0]
                        st = max(rt, free[e])
                        if best is None or (st, ix) < (best[0], best[1]):
                            best = (st, ix, e)
                st, ix, e = best
                rt, ix, o = heapq.heappop(heaps[e])
                issue = 0.05 if o.dma else o.dur
                free[e] = st + issue
                fin = st + o.dur + (0.25 if not o.dma else 0.0)
                new_ops.append(o)
                done += 1
                for sc in succ.get(id(o), ()):
                    ready_t[id(sc)] = max(ready_t[id(sc)], fin)
                    npred[id(sc)] -= 1
                    if npred[id(sc)] == 0:
                        heapq.heappush(heaps[sc.eng], (ready_t[id(sc)], sc.idx, sc))
        assert len(new_ops) == len(self.ops)
        self.ops = new_ops
        for e in ENGS:
            self.q[e] = [o for o in new_ops if o.eng == e]

    def emit(self):
        nc = self.nc
        if RESCHED:
            self.reschedule()
        cnt = {e: 0 for e in ENGS}
        dcnt = {}
        keys = []
        for op in self.ops:
            if op.dma:
                if op.key not in dcnt:
                    dcnt[op.key] = 0
                    keys.append(op.key)
                dcnt[op.key] += 16
                op.ticket = dcnt[op.key]
            elif op.sig:
                cnt[op.eng] += 1
                op.ticket = cnt[op.eng]
        with contextlib.ExitStack() as st:
            esem = {e: st.enter_context(nc.semaphore("s_" + e)) for e in ENGS if e != "sp"}
            dsem = {k: st.enter_context(nc.semaphore("d%d" % i)) for i, k in enumerate(keys)}
            block = st.enter_context(nc.Block())

            def run(engname):
                def body(e):
                    waited = {}
                    for op in self.q[engname]:
                        need = {}
                        for d in op.deps:
                            sem = dsem[d.key] if d.dma else esem[d.eng]
                            sid = id(sem)
                            if sid not in need or need[sid][1] < d.ticket:
                                need[sid] = (sem, d.ticket)
                        for sid, (sem, tk) in need.items():
                            if waited.get(sid, 0) >= tk:
                                continue
                            waited[sid] = tk
                            e.wait_ge(sem, tk)
                        if op.fn is None:
                            continue
                        ins = op.fn(e)
                        if op.dma:
                            ins.then_inc(dsem[op.key], 16)
                        elif op.sig:
                            ins.then_inc(esem[op.eng], 1)
                return body

            block.tensor(run("pe"))
            block.scalar(run("act"))
            block.vector(run("dve"))
            block.gpsimd(run("pool"))
            block.sync(run("sp"))
        return len(keys), cnt


D = 1024
KC = 8
SEQ = 4096
NSLOT = 33
TOK = NSLOT * 128
NOWN = 16
NT = 2048
NS = 16
NTS = NT + NS
DFF = 2816
NFC = 22
IN_DIM = 2856
C_Q, C_KV, C_GT, C_QG, C_KG, C_VG, C_LR, C_GG = 0, 512, 1280, 1304, 1560, 1816, 2328, 2344
EPS = 1e-6
BIG = 30000.0
LA = 4096
OFFA = 1936
LW = 768
OFFW = 128
LEXT = LA + LW
NPOOL = 2560
PAST = 8192
NPG = 64


def _bucket(n):
    n = np.asarray(n, np.int64)
    nf = np.maximum(n, 1).astype(np.float32)
    large = 16 + (np.log(nf / np.float32(16)) / np.float32(math.log(8.0)) * np.float32(16)).astype(np.int32)
    large = np.minimum(large, 31)
    return np.where(n < 16, n, large)


def _coef():
    c = np.zeros((33, LEXT), np.float32)
    m = np.arange(LA)
    n = m - OFFA
    for mi, ni in zip(m, n):
        if ni < 0:
            c[32, mi] = 1.0
        elif ni <= 112:
            c[_bucket(ni), mi] += 1.0
            c[31, mi] -= 1.0
    m = np.arange(LW)
    n = m - OFFW
    for mi, ni in zip(m, n):
        if ni < 0 or ni >= 512:
            c[32, LA + mi] = 1.0
        elif ni <= 112:
            c[_bucket(ni), LA + mi] += 1.0
            c[31, LA + mi] -= 1.0
    return c


def _static_consts():
    k = {}
    i = np.arange(128)
    k["cJ"] = (i[:, None] + i[None, :] == 127).astype(np.float32)
    k["cTri"] = (i[:, None] <= i[None, :]).astype(np.float32)
    k["cTriU"] = (i[:, None] > i[None, :]).astype(np.float32)
    k["cIdent"] = np.eye(128, dtype=np.float32)
    k["coef"] = _coef()
    E = np.zeros((128, TOK), np.float32)
    E[np.arange(TOK) // 64, np.arange(TOK)] = 1.0
    k["cE"] = E
    As = np.zeros((512, 128), np.float32)
    wts = {-1: 1.0, 0: 2.0, 1: 2.0, 2: 2.0, 3: 1.0}
    for j in range(128):
        for dd, w in wts.items():
            ii = 4 * j + dd
            if 0 <= ii <= 510:
                As[ii, j] = w
    k["cAs"] = As.reshape(4, 128, 128).transpose(1, 0, 2).copy()
    sel = np.zeros((4, 2, 128), np.float32)
    sel[:, 0, :] = 1.0
    sel[:, 0, 0] = 0.0
    sel[:, 0, 127] = 0.0
    sel[:, 1, 0] = 1e9
    sel[:, 1, 127] = 1e9
    k["cSels"] = sel
    blk = np.arange(128)
    k["cBm"] = (blk[:, None] // 4 == np.arange(64)[None, :] // 2).astype(np.float32)
    k["cHalf"] = ((blk[:, None] % 4) == (i[None, :] // 32)).astype(np.float32)
    t = np.arange(16)
    same = (t[:, None] // 4 == t[None, :] // 4)
    k["cTri16"] = (same & (t[:, None] <= t[None, :])).astype(np.float32)
    k["cTriU16"] = (same & (t[:, None] > t[None, :])).astype(np.float32)
    k["cSeqm"] = (t[:, None] // 4 == np.arange(4)[None, :]).astype(np.float32)
    vs_ = np.ones((128, 4), np.float32)
    vs_[127, 3] = 0.0
    k["cValS"] = vs_
    k["cPcol"] = (np.arange(128) % 64).astype(np.float32).reshape(128, 1)
    return k


def _core_consts(par):
    shift = 1 - par
    k = {}
    A = np.zeros((256, 128), np.float32)
    wts = {-1: 1.0, 0: 2.0, 1: 2.0, 2: 2.0, 3: 1.0}
    for bp in range(66):
        j = bp - 2 * shift
        if not (0 <= j <= 63):
            continue
        for dd, w in wts.items():
            ii = 4 * j + dd
            if 0 <= ii <= 254:
                c = ii + 8 * shift + 1
                if c < 256:
                    A[c, bp] = w
    k["cA"] = A.reshape(2, 128, 128).transpose(1, 0, 2).copy()
    sel = np.zeros((NOWN, 128, 2, 128), np.float32)
    sel[:, :, 1, :] = -2.0
    q = np.arange(128)
    for jo in range(NOWN):
        s = 2 * jo + 1
        pos = (s - shift) * 128 + q
        cur = pos // 64
        for bp in range(66):
            j = bp - 2 * shift
            if not (0 <= j <= 63):
                continue
            forced = (j == 0) | (j == cur) | (j == cur - 1)
            vis = (j * 64 <= pos)
            sel[jo, :, 0, bp] = np.where(vis & ~forced, 1.0, 0.0)
            sel[jo, :, 1, bp] = np.where(forced, 1e9, np.where(vis, 0.0, -1.0))
    k["cSel"] = sel
    valid = np.ones((128, NSLOT), np.float32)
    valid[:, 0 if shift == 1 else 32] = 0.0
    k["cValid"] = valid
    vc = np.zeros((256,), np.float32)
    for c in range(256):
        ii = c - 1 - 8 * shift
        vc[c] = 1.0 if 0 <= ii <= 254 else 0.0
    k["cValC"] = vc.reshape(2, 128).T.copy()
    return k


class Arena:
    def __init__(self, nc, base, end):
        self.nc, self.top, self.end = nc, base, end
        self.n = 0

    def alloc(self, shape, dt):
        sz = 4 if dt in (F32, I32) else 2
        nb = int(np.prod(shape[1:])) * sz
        nb = (nb + 63) // 64 * 64
        t = self.nc.alloc_sbuf_tensor_at("t%d" % self.n, list(shape), dt, offset=self.top)
        self.n += 1
        self.top += nb
        assert self.top <= self.end, ("SBUF overflow", self.top, self.end)
        return t


def build_program(dbg=False, nslots=NSLOT, cut=99, do_sample=True, nseq=4):
    nc = bass.Bass("TRN2", target_bir_lowering=False)
    S = Sched(nc)
    IN = {}
    OUT = {}

    def din(name, shape, dt=F32):
        IN[name] = (shape, dt)
        return nc.dram_tensor(name, list(shape), dt, kind="ExternalInput").ap()

    def dout(name, shape, dt=F32):
        OUT[name] = (shape, dt)
        return nc.dram_tensor(name, list(shape), dt, kind="ExternalOutput").ap()

    xT = din("xT", [128, KC, TOK])
    pT = din("pT", [128, 2, NTS])
    xsT = din("xsT", [128, KC, NS])
    w_in = din("w_in", [128, KC, IN_DIM])
    w_o_nsa = din("w_o_nsa", [64, 8, D])
    w_o_gla = din("w_o_gla", [128, 4, D])
    w_gate = din("w_gate", [128, KC, DFF])
    w_up = din("w_up", [128, KC, DFF])
    w_down = din("w_down", [128, NFC, D])
    w_ple = din("w_ple", [128, 2, D])
    w_pg = din("w_pg", [128, KC, D])
    norms = din("norms", [128, 4, KC])
    w1bd = din("w1bd", [128, 2, 32, 128])
    pecol = din("pecol", [128, 2, 32])
    w2bd = din("w2bd", [128, 2, 128])
    wgk = din("wgk", [33, 256])
    glan = din("glan", [128, 1])
    rb33 = din("rb33", [33, 8])
    coef = din("coef", [33, LEXT])
    cJ = din("cJ", [128, 128])
    cTri = din("cTri", [128, 128])
    cTriU = din("cTriU", [128, 128])
    cIdent = din("cIdent", [128, 128])
    cE = din("cE", [128, TOK])
    cA = din("cA", [128, 2, 128])
    cSel = din("cSel", [NOWN, 128, 2, 128])
    cValid = din("cValid", [128, NSLOT])
    cValC = din("cValC", [128, 2])
    cAs = din("cAs", [128, 4, 128])
    cSels = din("cSels", [4, 2, 128])
    cBm = din("cBm", [128, 64])
    cHalf = din("cHalf", [128, 128])
    cTri16 = din("cTri16", [16, 16])
    cTriU16 = din("cTriU16", [16, 16])
    cSeqm = din("cSeqm", [16, 4])
    cValS = din("cValS", [128, 4])
    cPcol = din("cPcol", [128, 1])
    cache = din("cache", [NPOOL, 128, 512])
    ptab = din("ptab", [1, 4 * NPG], I32)
    cwin = din("cwin", [4, 512, 256])
    sgla = din("sgla", [4, 4, 64, 128])

    yT = dout("yT", [128, KC, NTS])
    kvT_p = dout("kvT_p", [128, 4, NT])
    vtok_p = dout("vtok_p", [NT, 2, 128])
    gla_p = dout("gla_p", [128, 2, 128])
    kvT_s = dout("kvT_s", [128, 4, NS])
    vtok_s = dout("vtok_s", [NS, 2, 128])
    gla_s = dout("gla_s", [4, 128, 2, 128])
    win_s = dout("win_s", [4, 508, 256])
    DBG = {}

    ext_bf = nc.dram_tensor("ext_bf", [8, LEXT], BF16)
    h1_scr = nc.dram_tensor("h1_scr", [128, KC, NTS], F32).ap()
    H_ext = H("ext")
    gscr = [nc.dram_tensor("gscr%d" % i, [24, 128], F32).ap() for i in range(2)]
    gscr_s = nc.dram_tensor("gscr_s", [24, 16], F32).ap()
    H_gscr_s = H("gscr_s")
    H_gscr = [H("gscr0"), H("gscr1")]
    H_h1 = [H("h1s%d" % i) for i in range(NOWN + 1)]

    st = contextlib.ExitStack()
    arena = st.enter_context(nc.sbuf_tensor("arena", [128, 208000], mybir.dt.uint8))
    ABASE = 16512
    AEND = ABASE + 208000
    AR = Arena(nc, ABASE, AEND)

    PB = [st.enter_context(nc.psum_tensor("pb%d" % i, [128, 512], F32)) for i in range(7)]
    PBH = [H("pb%d" % i, excl=True) for i in range(7)]
    PT = st.enter_context(nc.psum_tensor("pbt", [128, 1024], BF16))
    H_PT = H("pbt", excl=True)
    rr_state = [0]

    def rr():
        i = 2 + (rr_state[0] % 5)
        rr_state[0] += 1
        return PB[i], PBH[i]

    def _fsz(ap):
        n = 1
        for d in ap.shape[1:]:
            n *= d
        return n

    def MM(out, lhsT, rhs, start=True, stop=True, R=(), W=(), J=()):
        S.add("pe", lambda e: e.matmul(out, lhsT=lhsT, rhs=rhs, start=start, stop=stop), reads=R, writes=W, joins=J,
              dur=max(_fsz(rhs), 128) / 2400.0 + 0.01)

    def TR(out, in_, ident, R=(), W=(), J=()):
        S.add("pe", lambda e: e.transpose(out=out, in_=in_, identity=ident), reads=R, writes=W, joins=J)

    def ACT(out, in_, func, bias=0.0, scale=1.0, R=(), W=(), J=()):
        S.add("act", lambda e: e.activation(out=out, in_=in_, func=func, bias=bias, scale=scale), reads=R, writes=W, joins=J,
              dur=_fsz(out) / 960.0 + 0.2)

    def CP(eng, out, in_, R=(), W=(), J=()):
        if eng == "act":
            S.add("act", lambda e: e.copy(out=out, in_=in_), reads=R, writes=W, joins=J)
        else:
            S.add(eng, lambda e: e.tensor_copy(out=out, in_=in_), reads=R, writes=W, joins=J)

    def TT(eng, out, in0, in1, op, R=(), W=(), J=()):
        S.add(eng, lambda e: e.tensor_tensor(out=out, in0=in0, in1=in1, op=op), reads=R, writes=W, joins=J, dur=_fsz(out) / 960.0 + 0.15)

    def TS(eng, out, in0, s1, s2, op0, op1=None, R=(), W=(), J=()):
        if op1 is None:
            S.add(eng, lambda e: e.tensor_scalar(out=out, in0=in0, scalar1=s1, scalar2=None, op0=op0), reads=R, writes=W, joins=J)
        else:
            S.add(eng, lambda e: e.tensor_scalar(out=out, in0=in0, scalar1=s1, scalar2=s2, op0=op0, op1=op1), reads=R, writes=W, joins=J)

    def STT(eng, out, in0, scalar, in1, op0, op1, R=(), W=(), J=()):
        S.add(eng, lambda e: e.scalar_tensor_tensor(out=out, in0=in0, scalar=scalar, in1=in1, op0=op0, op1=op1), reads=R, writes=W, joins=J)

    def MS(eng, ap, val, W=(), J=()):
        S.add(eng, lambda e: e.memset(ap, val), writes=W, joins=J)

    def RCP(eng, out, in_, R=(), W=(), J=()):
        S.add(eng, lambda e: e.reciprocal(out=out, in_=in_), reads=R, writes=W, joins=J, dur=_fsz(out) / 160.0 + 0.15)

    def DMA(eng, out, in_, key, R=(), W=(), J=(), **kw):
        S.add(eng, lambda e: e.dma_start(out=out, in_=in_, **kw), reads=R, writes=W, joins=J, dma=True, key=key)

    grp_started = set()
    cur_grp = [H("G0")]

    grp_qkeys = {}

    def gdma(eng, out, in_, grp, R=()):
        kobj = grp_qkeys.setdefault((id(grp), eng), H(grp.name + "_" + eng))
        S.q.setdefault("group_keys", set()).add(id(kobj))
        if id(grp) in grp_started:
            DMA(eng, out, in_, key=kobj, R=R, J=[grp])
        else:
            grp_started.add(id(grp))
            DMA(eng, out, in_, key=kobj, R=R, W=[grp])

    def load(shape, dt, src, eng=None, name=None, own=False):
        t = AR.alloc(shape, dt)
        if eng is None:
            eng = "pool" if dt == BF16 else "sp"
        if own or os.environ.get("OWNKEYS") == "1":
            h = H(name or "c")
            DMA(eng, t[:], src, key=h, W=[h])
            return t, h
        gdma(eng, t[:], src, cur_grp[0])
        return t, cur_grp[0]

    J_bf, hJ = load([128, 128], BF16, cJ)
    ident_bf, hId = load([128, 128], BF16, cIdent)
    tri_f, hTri = load([128, 128], F32, cTri)
    triu_f, hTriU = load([128, 128], F32, cTriU)
    norms_f, hNorms = load([128, 4, KC], F32, norms)
    glan_f, hGlan = load([128, 1], F32, glan)
    ones_bf = AR.alloc([128, 128], BF16); hOnes = H("ones")
    MS("dve", ones_bf[:], 1.0, W=[hOnes])
    ones_f = AR.alloc([128, 128], F32); hOnesF = H("onesf")
    MS("dve", ones_f[:], 1.0, W=[hOnesF])
    PERS_END = AR.top

    win_bf = AR.alloc([128, KC, 2880], BF16); hWin = H("win")
    first = True
    for kc in range(KC):
        for (a, b) in ((0, 1428), (1428, 2856)):
            DMA("pool", win_bf[:, kc, a:b], w_in[:, kc, a:b], key=hWin, W=[hWin] if first else (), J=() if first else [hWin])
            first = False
    wonsa_bf = AR.alloc([64, 8, D], BF16); hWon = H("wonsa")
    for hh in range(8):
        DMA("pool", wonsa_bf[:, hh, :], w_o_nsa[:, hh, :], key=hWon, W=[hWon] if hh == 0 else (), J=() if hh == 0 else [hWon])
    wogla_bf = AR.alloc([128, 4, D], BF16); hWog = H("wogla")
    for hh in range(4):
        DMA("pool", wogla_bf[:, hh, :], w_o_gla[:, hh, :], key=hWog, W=[hWog] if hh == 0 else (), J=() if hh == 0 else [hWog])
    w1_bf = AR.alloc([128, 2, 32, 128], BF16); hW1 = H("w1")
    for t in range(2):
        for s4 in range(2):
            DMA("pool", w1_bf[:, t, 16 * s4:16 * s4 + 16, :], w1bd[:, t, 16 * s4:16 * s4 + 16, :], key=hW1,
                W=[hW1] if (t == 0 and s4 == 0) else (), J=() if (t == 0 and s4 == 0) else [hW1])
    w2_bf, hW2 = load([128, 2, 128], BF16, w2bd)
    pe_bf, hPe = load([128, 2, 32], BF16, pecol, own=True)
    wgk_bf, hWgk = load([33, 256], BF16, wgk)
    As_bf, hAs = load([128, 4, 128], BF16, cAs)
    valid_f, hValid = load([128, NSLOT], F32, cValid)
    valc_f, hValC = load([128, 2], F32, cValC)
    bm_f, hBm = load([128, 64], F32, cBm)
    half_bf, hHalf = load([128, 128], BF16, cHalf)
    sels_f, hSels = load([4, 2, 128], F32, cSels)
    tri16_f, hTri16 = load([16, 16], F32, cTri16)
    triu16_f, hTriU16 = load([16, 16], F32, cTriU16)
    seqm_f, hSeqm = load([16, 4], F32, cSeqm)

    rb_f, hRb = load([33, 8], F32, rb33, own=True)
    _save_top = AR.top
    AR.top = AEND - 32768
    coef_f = AR.alloc([33, LEXT], F32); hCoef = H("coef")
    DMA("sp", coef_f[:], coef, key=hCoef, W=[hCoef])
    ext_sb = AR.alloc([8, LEXT], BF16); hExtSb = H("extsb")
    AR.top = _save_top
    nch = (LEXT + 511) // 512
    for i in range(nch):
        a = i * 512
        b = min(LEXT, a + 512)
        pb, ph = rr()
        MM(pb[0:8, 0:b - a], rb_f[:], coef_f[:, a:b], R=[hRb, hCoef], W=[ph])
        CP("act", ext_sb[:, a:b], pb[0:8, 0:b - a], R=[ph], W=[hExtSb] if i == 0 else (), )
        if i > 0:
            S.ops[-1]
    hExtSb.writers = [op for op in S.q["act"][-nch:]]
    DMA("sp", ext_bf.ap(), ext_sb[:], key=hExtSb, R=[hExtSb], W=[H_ext])

    def hankel(dst, g, base, pstride, nq, h, npart=128):
        src = bass.AP(ext_bf, 4 * g * LEXT + base, [[pstride, npart], [LEXT, 4], [1, nq]])
        if h in (GHS, GHSn, GHA):
            gdma("sp", dst, src, h, R=[H_ext])
        else:
            DMA("sp", dst, src, key=h, R=[H_ext], W=[h])

    GHS, GHSn, GHA = H("GHS"), H("GHSn"), H("GHA")

    HKS = AR.alloc([128, 4, 2, 16], BF16)
    hHKS = [[GHS for g in range(2)] for i in range(4)]
    MS("dve", HKS[:], 0.0, W=[GHS])
    HKC = AR.alloc([128, 2, 64], BF16)
    HKW = AR.alloc([128, 2, 64], BF16)
    HKS2 = AR.alloc([128, 2, 2, 16], BF16)
    MS("dve", HKC[:], 0.0, J=[GHS])
    MS("dve", HKW[:], 0.0, J=[GHS])
    for g in range(2):
        hankel(HKS[:, 0, g, :].rearrange("p (r q) -> p r q", r=4), g, OFFA - 15, 16, 4, hHKS[0][g])
        hankel(HKS[:, 1, g, :].rearrange("p (r q) -> p r q", r=4), g, OFFA + 1, 1, 4, hHKS[1][g])
        hankel(HKS[0:4, 2, g, :].rearrange("p (r q) -> p r q", r=4), g, OFFA - 3, 1, 4, hHKS[2][g], npart=4)
        hankel(HKS[:, 3, g, :].rearrange("p (r q) -> p r q", r=4), g, LA + OFFW + 385, 1, 4, hHKS[3][g])

    for g in range(2):
        for j in range(2):
            hankel(HKS2[:, j, g, :].rearrange("p (r q) -> p r q", r=4), g, OFFA + 2 - j, 2, 4, GHS)
        hankel(HKC[:, g, 48:64].rearrange("p (r q) -> p r q", r=4), g, OFFA - 15, 16, 4, GHS)
        hankel(HKW[:, g, 0:16].rearrange("p (r q) -> p r q", r=4), g, LA + OFFW + 385, 1, 4, GHS)
        hankel(HKW[:, g, 48:64].rearrange("p (r q) -> p r q", r=4), g, OFFA + 1, 1, 4, GHS)

    pew1 = AR.alloc([128, 2], F32); hPew1 = H("pew1")
    pb, ph = rr()
    for t in range(2):
        for sx in range(32):
            MM(pb[:, t * 8:t * 8 + 8], w1_bf[:, t, sx, :], pe_bf[:, t, sx:sx + 1].to_broadcast([128, 8]), start=(sx == 0), stop=(sx == 31),
               R=[hW1, hPe], **(dict(W=[ph]) if (t == 0 and sx == 0) else dict(J=[ph])))
    CP("dve", pew1[:], pb[:, 0:16:8], R=[ph], W=[hPew1])

    S.barrier()
    MIX_END = AR.top

    def WJ(h, first):
        return dict(W=[h]) if first else dict(J=[h])

    def phase_s():
        cur_grp[0] = H("GS")
        identf, hIdF = load([128, 128], F32, cIdent, name="identf")
        valS_f, hValS = load([128, 4], F32, cValS)
        xs_s = AR.alloc([128, KC, NS], F32); hxs_s = H("xs_s")
        DMA("sp", xs_s[:], xsT, key=hxs_s, W=[hxs_s])
        sq_s = AR.alloc([128, KC, NS], BF16); hsq_s = H("sq_s")
        rstd_s = AR.alloc([128, NS], F32); hrstd_s = H("rstd_s")
        xn_s = AR.alloc([128, KC, NS], BF16); hxn_s = H("xn_s")
        ACT(sq_s[:], xs_s[:], AF.Square, R=[hxs_s], W=[hsq_s])
        pb, ph = rr()
        for kc in range(KC):
            MM(pb[:, 0:NS], ones_bf[:], sq_s[:, kc, :], start=(kc == 0), stop=(kc == KC - 1), R=[hOnes, hsq_s], **WJ(ph, kc == 0))
        ACT(rstd_s[:], pb[:, 0:NS], AF.Ln, bias=EPS, scale=1.0 / D, R=[ph], W=[hrstd_s])
        ACT(rstd_s[:], rstd_s[:], AF.Exp, scale=-0.5, R=[hrstd_s], W=[hrstd_s])
        for kc in range(KC):
            STT("dve", xn_s[:, kc, :], xs_s[:, kc, :], norms_f[:, 0, kc:kc + 1], rstd_s[:], ALU.mult, ALU.mult,
                R=[hxs_s, hrstd_s, hNorms], **WJ(hxn_s, kc == 0))

        def pF(out, c0, M, fw):
            for kc in range(KC):
                MM(out, win_bf[:, kc, c0:c0 + M], xn_s[:, kc, :], start=(kc == 0), stop=(kc == KC - 1), R=[hWin, hxn_s], **fw(kc))

        def pTm(out, c0, N, fw):
            for kc in range(KC):
                MM(out, xn_s[:, kc, :], win_bf[:, kc, c0:c0 + N], start=(kc == 0), stop=(kc == KC - 1), R=[hWin, hxn_s], **fw(kc))

        if cut == 99.1:
            return
        QTs = AR.alloc([128, 64], BF16); hQTs = H("QTs")
        pq, phq = rr()
        for r in range(4):
            pF(pq[:, r * 16:(r + 1) * 16], r * 128, 128, lambda kc, r=r: WJ(phq, r == 0 and kc == 0))
        TS("dve", QTs[:].rearrange("p (b r q) -> p r b q", b=4, r=4), pq[:, 0:64].rearrange("p (r b q) -> p r b q", r=4, b=4),
           0.125, None, ALU.mult, R=[phq], W=[hQTs])
        kvo_s = AR.alloc([128, 4, NS], F32); hkvo_s = H("kvo_s")
        ksn = AR.alloc([128, NS], BF16); hksn = H("ksn")
        kwn = AR.alloc([128, NS], BF16); hkwn = H("kwn")
        pk, phk = rr()
        for i, c0 in enumerate((512, 640, 768, 1024)):
            pF(pk[:, i * 16:(i + 1) * 16], c0, 128, lambda kc, i=i: WJ(phk, i == 0 and kc == 0))
        CP("act", kvo_s[:].rearrange("p a b -> p (a b)"), pk[:, 0:64], R=[phk], W=[hkvo_s])
        CP("act", ksn[:], pk[:, 32:48], R=[phk], W=[hksn])
        CP("act", kwn[:], pk[:, 48:64], R=[phk], W=[hkwn])
        DMA("sp", kvT_s, kvo_s[:], key=hkvo_s, R=[hkvo_s])
        vto_s = AR.alloc([NS, 2, 128], F32); hvto_s = H("vto_s")
        Vsn = AR.alloc([NS, 2, 65], BF16); hVsn = H("Vsn")
        Vwn = AR.alloc([NS, 2, 65], BF16); hVwn = H("Vwn")
        Vsn_m = AR.alloc([NS, 4, 2, 65], BF16); hVsn_m = H("Vsn_m")
        Vwn_m = AR.alloc([NS, 4, 2, 65], BF16); hVwn_m = H("Vwn_m")
        pv, phv = rr()
        pTm(pv[0:NS, 0:128], 896, 128, lambda kc: WJ(phv, kc == 0))
        pTm(pv[0:NS, 128:256], 1152, 128, lambda kc: dict(J=[phv]))
        CP("act", vto_s[:].rearrange("p a b -> p (a b)"), pv[0:NS, 0:256], R=[phv], W=[hvto_s])
        DMA("sp", vtok_s, vto_s[:], key=hvto_s, R=[hvto_s])
        MS("pool", Vsn[:], 1.0, W=[hVsn])
        MS("pool", Vwn[:], 1.0, W=[hVwn])
        CP("act", Vsn[:, :, 0:64], pv[0:NS, 0:128].rearrange("p (g d) -> p g d", g=2), R=[phv], W=[hVsn])
        CP("act", Vwn[:, :, 0:64], pv[0:NS, 128:256].rearrange("p (g d) -> p g d", g=2), R=[phv], W=[hVwn])
        for bl in range(4):
            TS("dve", Vsn_m[:, bl, :, :], Vsn[:], seqm_f[:, bl:bl + 1], None, ALU.mult, R=[hVsn, hSeqm], **WJ(hVsn_m, bl == 0))
            TS("dve", Vwn_m[:, bl, :, :], Vwn[:], seqm_f[:, bl:bl + 1], None, ALU.mult, R=[hVwn, hSeqm], **WJ(hVwn_m, bl == 0))
        if cut == 99.2:
            return
        gate_s = AR.alloc([24, NS], F32); hgate_s = H("gate_s")
        grow_s = AR.alloc([128, 24 * NS], F32); hgrow_s = H("grow_s")
        pgt, phgt = rr()
        pF(pgt[0:24, 0:NS], C_GT, 24, lambda kc: WJ(phgt, kc == 0))
        ACT(gate_s[:], pgt[0:24, 0:NS], AF.Exp, scale=-1.0, R=[phgt], W=[hgate_s])
        TS("dve", gate_s[:], gate_s[:], 1.0, None, ALU.add, R=[hgate_s], W=[hgate_s])
        RCP("dve", gate_s[:], gate_s[:], R=[hgate_s], W=[hgate_s])
        DMA("sp", gscr_s, gate_s[:], key=hgate_s, R=[hgate_s], W=[H_gscr_s])
        DMA("sp", grow_s[64:65, :], gscr_s.rearrange("(o a) b -> o (a b)", o=1), key=hgrow_s, R=[H_gscr_s], W=[hgrow_s])
        if cut == 99.3:
            return
        HKSn = AR.alloc([NS, 4, 2, 16], BF16)
        hHKSn = [[GHSn for g in range(2)] for bl in range(4)]
        for bl in range(4):
            for g in range(2):
                hankel(HKSn[0:NS, bl, g, :].rearrange("p (r q) -> p r q", r=4), g, OFFA - 15 + 4 * bl, 1, 4, hHKSn[bl][g], npart=NS)
        if cut == 99.4:
            return
        lrT_s = AR.alloc([64, NS], BF16); hLr_s = H("lrT_s")
        MS("pool", lrT_s[:], 0.0, W=[hLr_s])
        MS("pool", lrT_s[32:33, :], 1.0, J=[hLr_s])
        la_s = AR.alloc([NS, 256], F32); hla_s = H("la_s")
        esuf_s = AR.alloc([NS, 256], F32); hesuf_s = H("esuf_s")
        ecum_s = AR.alloc([128, 2 * NS], F32); hecum_s = H("ecum_s")
        einv_s = AR.alloc([128, 2 * NS], F32); heinv_s = H("einv_s")
        keT_s = AR.alloc([128, 2 * NS], BF16); hke_s = H("keT_s")
        qeT_s = AR.alloc([128, 2 * NS], BF16); hqe_s = H("qeT_s")
        kd_s = AR.alloc([NS, 256], BF16); hkd_s = H("kd_s")
        KDM_off = [AR.top]
        kdm = AR.alloc([NS, 4, 256], BF16); hkdm = H("kdm")
        vg_s = AR.alloc([NS, 512], BF16); hvg_s = H("vg_s")
        attT_s = AR.alloc([NS, NS], BF16); hatt_s = H("attT_s")
        S0_off = [AR.top]
        S0 = AR.alloc([128, 4, 2, 128], F32); hS0 = H("S0")
        S0bf = AR.alloc([128, 4, 2, 128], BF16); hS0bf = H("S0bf")
        omixg_s = AR.alloc([128, 4, NS], BF16); homixg_s = H("omixg_s")
        osq_s = AR.alloc([128, NS], BF16); hosq_s = H("osq_s")
        grs_s = AR.alloc([128, NS], F32); hgrs_s = H("grs_s")
        sg_s = AR.alloc([128, NS], F32); hsg_s = H("sg_s")
        t1_s = AR.alloc([128, NS], F32); ht1_s = H("t1_s")
        first = True
        for bl in range(4):
            for hd in range(4):
                DMA("sp", S0[(hd % 2) * 64:(hd % 2) * 64 + 64, bl, hd // 2, :], sgla[bl, hd], key=hS0, **WJ(hS0, first))
                first = False
        CP("pool", S0bf[:], S0[:], R=[hS0], W=[hS0bf])
        pb, ph = rr()
        pF(pb[0:16, 0:NS], C_LR, 16, lambda kc: WJ(ph, kc == 0))
        CP("act", lrT_s[0:16, :], pb[0:16, 0:NS], R=[ph], W=[hLr_s])
        pz, phz = rr()
        MM(pz[0:NS, 0:256], lrT_s[0:33, :], wgk_bf[:], R=[hLr_s, hWgk], W=[phz])
        ACT(la_s[:], pz[0:NS, 0:256], AF.Exp, scale=-1.0, R=[phz], W=[hla_s])
        ACT(la_s[:], la_s[:], AF.Ln, bias=1.0, R=[hla_s], W=[hla_s])
        psf, phs = rr()
        MM(psf[0:NS, 0:256], triu16_f[:], la_s[:], R=[hTriU16, hla_s], W=[phs])
        ACT(esuf_s[:], psf[0:NS, 0:256], AF.Exp, scale=-1.0 / 16, R=[phs], W=[hesuf_s])
        pct, phc = rr()
        for ch in range(2):
            MM(pct[:, ch * NS:(ch + 1) * NS], la_s[:, ch * 128:(ch + 1) * 128], tri16_f[:], R=[hla_s, hTri16], **WJ(phc, ch == 0))
        ACT(ecum_s[:], pct[:, 0:2 * NS], AF.Exp, scale=-1.0 / 16, R=[phc], W=[hecum_s])
        ACT(einv_s[:], pct[:, 0:2 * NS], AF.Exp, scale=1.0 / 16, R=[phc], W=[heinv_s])
        pk, phk = rr()
        for ch in range(2):
            pF(pk[:, ch * NS:(ch + 1) * NS], C_KG + ch * 128, 128, lambda kc, ch=ch: WJ(phk, ch == 0 and kc == 0))
        TT("dve", keT_s[:], pk[:, 0:2 * NS], einv_s[:], ALU.mult, R=[phk, heinv_s], W=[hke_s])
        pq, phq = rr()
        for ch in range(2):
            pF(pq[:, ch * NS:(ch + 1) * NS], C_QG + ch * 128, 128, lambda kc, ch=ch: WJ(phq, ch == 0 and kc == 0))
        STT("dve", qeT_s[:], pq[:, 0:2 * NS], 0.125, ecum_s[:], ALU.mult, ALU.mult, R=[phq, hecum_s], W=[hqe_s])
        pkt, phkt = rr()
        pTm(pkt[0:NS, 0:256], C_KG, 256, lambda kc: WJ(phkt, kc == 0))
        TT("dve", kd_s[:], pkt[0:NS, 0:256], esuf_s[:], ALU.mult, R=[phkt, hesuf_s], W=[hkd_s])
        for bl in range(4):
            TS("dve", kdm[:, bl, :], kd_s[:], seqm_f[:, bl:bl + 1], None, ALU.mult, R=[hkd_s, hSeqm], **WJ(hkdm, bl == 0))
        pv2, phv2 = rr()
        pTm(pv2[0:NS, 0:512], C_VG, 512, lambda kc: WJ(phv2, kc == 0))
        CP("act", vg_s[:], pv2[0:NS, 0:512], R=[phv2], W=[hvg_s])
        if cut == 99.5:
            return
        for hd in range(4):
            ch = hd // 2
            pp = slice(64 * (hd % 2), 64 * (hd % 2) + 64)
            cs = slice(ch * NS, (ch + 1) * NS)
            pa, pha = rr()
            MM(pa[0:NS, 0:NS], keT_s[pp, cs], qeT_s[pp, cs], R=[hke_s, hqe_s], W=[pha])
            TT("dve", attT_s[:], pa[0:NS, 0:NS], tri16_f[:], ALU.mult, R=[pha, hTri16], W=[hatt_s])
            po, pho = rr()
            MM(po[:, 0:NS], vg_s[:, hd * 128:(hd + 1) * 128], attT_s[:], start=True, stop=False, R=[hvg_s, hatt_s], W=[pho])
            for bl in range(4):
                MM(po[:, bl * 4:bl * 4 + 4], S0bf[pp, bl, ch, :], qeT_s[pp, ch * NS + bl * 4:ch * NS + bl * 4 + 4],
                   start=False, stop=(bl == 3), R=[hS0bf, hqe_s], J=[pho])
            ACT(osq_s[:], po[:, 0:NS], AF.Square, R=[pho], W=[hosq_s])
            pn, phn = rr()
            MM(pn[:, 0:NS], ones_bf[:], osq_s[:], R=[hOnes, hosq_s], W=[phn])
            ACT(grs_s[:], pn[:, 0:NS], AF.Ln, bias=EPS, scale=1.0 / 128, R=[phn], W=[hgrs_s])
            ACT(grs_s[:], grs_s[:], AF.Exp, scale=-0.5, R=[hgrs_s], W=[hgrs_s])
            STT("dve", t1_s[:], po[:, 0:NS], glan_f[:, 0:1], grs_s[:], ALU.mult, ALU.mult, R=[pho, hGlan, hgrs_s], W=[ht1_s])
            pg, phg = rr()
            pF(pg[:, 0:NS], C_GG + hd * 128, 128, lambda kc: WJ(phg, kc == 0))
            ACT(sg_s[:], pg[:, 0:NS], AF.Exp, scale=-1.0, R=[phg], W=[hsg_s])
            TS("dve", sg_s[:], sg_s[:], 1.0, None, ALU.add, R=[hsg_s], W=[hsg_s])
            RCP("dve", sg_s[:], sg_s[:], R=[hsg_s], W=[hsg_s])
            TT("dve", sg_s[:], sg_s[:], pg[:, 0:NS], ALU.mult, R=[hsg_s, phg], W=[hsg_s])
            TT("dve", omixg_s[:, hd, :], t1_s[:], sg_s[:], ALU.mult, R=[ht1_s, hsg_s], **WJ(homixg_s, hd == 0))
        if cut == 99.6:
            return
        for bl in range(4):
            for ch in range(2):
                pu, phu = rr()
                MM(pu[:, 0:256], kdm[:, bl, ch * 128:(ch + 1) * 128], vg_s[:, ch * 256:(ch + 1) * 256], R=[hkdm, hvg_s], W=[phu])
                for hh in range(2):
                    pp = slice(64 * hh, 64 * hh + 64)
                    col = ch * NS + bl * 4 + 3
                    STT("dve", S0[pp, bl, ch, :], S0[pp, bl, ch, :], ecum_s[pp, col:col + 1], pu[pp, hh * 128:(hh + 1) * 128],
                        ALU.mult, ALU.add, R=[hS0, hecum_s, phu], W=[hS0])
        for bl in range(4):
            DMA("sp", gla_s[bl], S0[:, bl, :, :], key=hS0, R=[hS0])
        if cut == 99.7:
            return
        DMA("sp", win_s, cwin[:, 4:512, :], key=H("wincp"))

        if cut == 100:
            return
        KsT_s = AR.alloc([128, PAST], BF16); hKsT_s = H("KsT_s")
        Vs_s = AR.alloc([128, NPG, 2, 65], BF16); hVs_s = H("Vs_s")
        rawT_s = AR.alloc([128, 2, PAST], BF16); hraw_s = H("rawT_s")
        stg = [AR.alloc([128, 2, 512], F32)]; hstg = [H("stg0")]
        _s0base = KDM_off[0]
        assert S0_off[0] + 4096 + 2048 - _s0base >= 2 * 4096
        for i in range(2):
            stg.append(nc.alloc_sbuf_tensor_at("stgx%d" % i, [128, 2, 512], F32, offset=_s0base + i * 4096))
            hx_ = H("stgx%d" % i)
            for hd_ in (hkdm, hvg_s, hatt_s, hS0, hS0bf):
                hx_.writers += list(hd_.writers)
                hx_.readers += list(hd_.readers)
            hstg.append(hx_)
        NSTG = len(stg)
        KcT_s = AR.alloc([128, 512], BF16); hKcT_s = H("KcT_s")
        geV_s = AR.alloc([128, 512], BF16); hgeV_s = H("geV_s")
        ge_s = AR.alloc([128, 512], BF16); hge_s = H("ge_s")
        Vc_s = AR.alloc([128, 4, 2, 65], BF16); hVc_s = H("Vc_s")
        gx_s = AR.alloc([128, 512], F32); hgx_s = H("gx_s")
        gu_s = AR.alloc([128, 512], F32); hgu_s = H("gu_s")
        Pc_s = AR.alloc([128, 64], BF16); hPc_s = H("Pc_s")
        pn_s = AR.alloc([128, 64], F32); hpn_s = H("pn_s")
        PsT_s = AR.alloc([128, 4, 4], BF16); hPsT_s = H("PsT_s")
        P_s = AR.alloc([128, 1024], BF16); hP_s = H("P_s")
        R_s = AR.alloc([128, 1024], BF16); hR_s = H("R_s")
        KwT_s = AR.alloc([128, 512], BF16); hKwT_s = H("KwT_s")
        Vw_s = AR.alloc([128, 4, 2, 65], BF16); hVw_s = H("Vw_s")
        wstg = [AR.alloc([128, 256], F32) for _ in range(2)]; hwstg = [H("wstg0"), H("wstg1")]
        Pw_s = AR.alloc([128, 64], BF16); hPw_s = H("Pw_s")
        Pn16 = AR.alloc([NS, NS], BF16); hPn16 = H("Pn16")
        crow_s = AR.alloc([128, NS], F32); hcrow_s = H("crow_s")
        num_s = AR.alloc([64, NS], F32); hnum_s = H("num_s")
        onsa_s = AR.alloc([64, 2, NS], F32); honsa_s = H("onsa_s")
        onsab_s = AR.alloc([64, 2, 4, NS], BF16); honsab_s = H("onsab_s")
        score_s = AR.alloc([4, 128], F32); hscore_s = H("score_s")
        sc2_s = AR.alloc([4, 128], F32); hsc2_s = H("sc2_s")
        m8_s = AR.alloc([4, 8], F32); hm8_s = H("m8_s")
        m8b_s = AR.alloc([4, 8], F32); hm8b_s = H("m8b_s")
        nsel_s = AR.alloc([4, 128], BF16); hnsel_s = H("nsel_s")
        nselT_s = AR.alloc([128, 4], F32); hnselT_s = H("nselT_s")
        h1s = AR.alloc([128, KC, NS], F32); hh1s = H("h1s")
        MS("pool", Vs_s[:], 1.0, W=[hVs_s])
        MS("pool", Vw_s[:], 1.0, W=[hVw_s])
        MS("pool", KcT_s[:], 0.0, W=[hKcT_s])
        MS("pool", geV_s[:], 0.0, W=[hgeV_s])
        ACCs = [PB[0], PB[1]]
        hACCs = [PBH[0], PBH[1]]
        accs_rot = [0]

        def next_acc():
            i = accs_rot[0] % 2
            accs_rot[0] += 1
            return ACCs[i], hACCs[i]

        def finish_s(g, br, bl, acc, hacc, first_branch):
            TS("dve", crow_s[64:65, :], acc[64:65, 0:NS], 1e-30, None, ALU.max, R=[hacc], W=[hcrow_s])
            RCP("dve", crow_s[64:65, :], crow_s[64:65, :], R=[hcrow_s], W=[hcrow_s])
            off0 = (br * 8 + g * 4) * NS
            gv = grow_s[64:65, off0:off0 + 4 * NS].rearrange("o (r t) -> o r t", t=NS)[:, :, bl * 4:bl * 4 + 4]
            TT("dve", crow_s[64:65, :].rearrange("o (r q) -> o r q", r=4), crow_s[64:65, :].rearrange("o (r q) -> o r q", r=4), gv,
               ALU.mult, R=[hcrow_s, hgrow_s], W=[hcrow_s])
            pb, ph = rr()
            MM(pb[0:64, 0:NS], ones_f[64:65, 0:64], crow_s[64:65, :], R=[hOnesF, hcrow_s], W=[ph])
            CP("act", num_s[:], acc[0:64, 0:NS], R=[hacc], W=[hnum_s])
            if first_branch:
                TT("dve", onsa_s[:, g, :], num_s[:], pb[0:64, 0:NS], ALU.mult, R=[hnum_s, ph], W=[honsa_s])
            else:
                TT("dve", num_s[:], num_s[:], pb[0:64, 0:NS], ALU.mult, R=[hnum_s, ph], W=[hnum_s])
                TT("dve", onsa_s[:, g, :], onsa_s[:, g, :], num_s[:], ALU.add, R=[hnum_s, honsa_s], W=[honsa_s])
            if dbg and bl == 0:
                nm = "d_br%d%d" % (g, br)
                DBG[nm] = dout(nm, [64, NS])
                DMA("sp", DBG[nm], onsa_s[:, g, :], key=H("k" + nm), R=[honsa_s])

        def new_tile(g, bl, kn, hkn, Vm, hVm, acc, hacc):
            gp = slice(64 * g, 64 * g + 64)
            pn_, phn_ = rr()
            MM(pn_[0:NS, 0:NS], kn[gp, :], QTs[gp, bl * 16:(bl + 1) * 16], start=True, stop=False, R=[hkn, hQTs], W=[phn_])
            MM(pn_[0:NS, 0:NS], J_bf[0:NS, 128 - NS:128], HKSn[0:NS, bl, g, :], start=False, stop=True, R=[hJ, hHKSn[bl][g]], J=[phn_])
            ACT(Pn16[:], pn_[0:NS, 0:NS], AF.Exp, R=[phn_], W=[hPn16])
            if dbg and bl == 0 and g == 0 and kn is ksn:
                DBG["d_pn16"] = dout("d_pn16", [NS, NS])
                CP("act", crow_s[0:NS, 0:NS], pn_[0:NS, 0:NS], R=[phn_], W=[hcrow_s])
                DMA("sp", DBG["d_pn16"], crow_s[0:NS, 0:NS], key=H("kpn16"), R=[hcrow_s])
            MM(acc[0:65, 0:NS], Vm[:, bl, g, :], Pn16[:], start=False, stop=True, R=[hVm, hPn16], J=[hacc])

        pcol_f, hPcol = load([128, 1], F32, cPcol, own=True)
        ptab_i = AR.alloc([128, 4 * NPG], I32); hptab_i = H("ptab_i")
        DMA("sp", ptab_i[:], ptab.partition_broadcast(128), key=hptab_i, W=[hptab_i])
        idx_f = AR.alloc([128, 4 * NPG], F32); hidx_f = H("idx_f")
        idx_i = AR.alloc([128, 2 * NPG], I32); hidx_i = H("idx_i")
        CP("dve", idx_f[:], ptab_i[:], R=[hptab_i], W=[hidx_f])
        idx_f2 = AR.alloc([128, 2 * NPG], F32); hidx_f2 = H("idx_f2")
        CP("dve", idx_f2[0:64, :], idx_f[0:64, :].rearrange("p (a j) -> p a j", j=2)[:, :, 0], R=[hidx_f], W=[hidx_f2])
        CP("dve", idx_f2[64:128, :], idx_f[64:128, :].rearrange("p (a j) -> p a j", j=2)[:, :, 1], R=[hidx_f], J=[hidx_f2])
        TS("dve", idx_f2[:], idx_f2[:], 64.0, pcol_f[:, 0:1], ALU.mult, ALU.add, R=[hidx_f2, hPcol], W=[hidx_f2])
        CP("dve", idx_i[:, 0:2 * NPG], idx_f2[:], R=[hidx_f2], W=[hidx_i])
        cache_rows = cache.rearrange("n (p j) d -> (n p) (j d)", j=2)

        def gather_page(dst, idx):
            def f(e):
                return e.indirect_dma_start(out=dst, out_offset=None, in_=cache_rows,
                                            in_offset=bass.IndirectOffsetOnAxis(ap=idx_i[:, idx:idx + 1], axis=0))
            return f

        for bl in range(nseq):
            for pair in range(NPG // 2):
                sb_, hsb_ = stg[pair % NSTG], hstg[pair % NSTG]
                S.add("pool", gather_page(sb_[:].rearrange("p j d -> p (j d)"), bl * (NPG // 2) + pair), reads=[hidx_i], writes=[hsb_], dma=True, key=hsb_)
                for j in range(2):
                    tile_ = 2 * pair + j
                    ptx, phtx = rr()
                    for ci in range(3):
                        TR(ptx[:, ci * 128:(ci + 1) * 128], sb_[:, j, ci * 128:(ci + 1) * 128], identf[:], R=[hsb_, hIdF], **WJ(phtx, ci == 0))
                    CP("act", rawT_s[:, :, 256 * pair + j:256 * pair + j + 255:2], ptx[:, 0:256].rearrange("p (t n) -> p t n", t=2), R=[phtx],
                       **WJ(hraw_s, tile_ == 0))
                    CP("dve", KsT_s[:, tile_ * 128:(tile_ + 1) * 128], ptx[:, 256:384], R=[phtx], **WJ(hKsT_s, tile_ == 0))
                    CP("pool", Vs_s[:, tile_, :, 0:64], sb_[:, j, 384:512].rearrange("p (g d) -> p g d", g=2), R=[hsb_], **WJ(hVs_s, tile_ == 0))
                pg = 2 * pair + 1
                if (pg + 1) % 8 == 0:
                    gi = pg // 8
                    n0 = max(0, 64 * gi - 1)
                    cnt = 64 * gi + 62 - n0 + 1
                    for t in range(2):
                        pc, phc_ = rr()
                        for sx in range(32):
                            c0_ = 16 * n0 + sx
                            MM(pc[:, 0:cnt], w1_bf[:, t, sx, :], rawT_s[:, t, c0_:c0_ + 16 * (cnt - 1) + 1:16], start=(sx == 0), stop=(sx == 31),
                               R=[hW1, hraw_s], **WJ(phc_, sx == 0))
                        TS("dve", gx_s[:, 0:cnt], pc[:, 0:cnt], pew1[:, t:t + 1], None, ALU.add, R=[phc_, hPew1], W=[hgx_s])
                        TT("dve", gu_s[:, 0:cnt], gx_s[:, 0:cnt], gx_s[:, 0:cnt], ALU.mult, R=[hgx_s], W=[hgu_s])
                        TS("dve", gu_s[:, 0:cnt], gu_s[:, 0:cnt], 0.044715, 1.0, ALU.mult, ALU.add, R=[hgu_s], W=[hgu_s])
                        TT("dve", gu_s[:, 0:cnt], gu_s[:, 0:cnt], gx_s[:, 0:cnt], ALU.mult, R=[hgu_s, hgx_s], W=[hgu_s])
                        ACT(gu_s[:, 0:cnt], gu_s[:, 0:cnt], AF.Exp, scale=-1.5957691216057308, R=[hgu_s], W=[hgu_s])
                        TS("dve", gu_s[:, 0:cnt], gu_s[:, 0:cnt], 1.0, None, ALU.add, R=[hgu_s], W=[hgu_s])
                        RCP("dve", gu_s[:, 0:cnt], gu_s[:, 0:cnt], R=[hgu_s], W=[hgu_s])
                        if t == 0:
                            TT("dve", ge_s[:, 0:cnt], gx_s[:, 0:cnt], gu_s[:, 0:cnt], ALU.mult, R=[hgx_s, hgu_s], W=[hge_s])
                            pk2, phk2 = rr()
                            MM(pk2[:, 0:cnt], w2_bf[:, 0, :], ge_s[:, 0:cnt], R=[hW2, hge_s], W=[phk2])
                            CP("act", KcT_s[:, n0:n0 + cnt], pk2[:, 0:cnt], R=[phk2], **WJ(hKcT_s, gi == 0))
                        else:
                            TT("dve", geV_s[:, n0:n0 + cnt], gx_s[:, 0:cnt], gu_s[:, 0:cnt], ALU.mult, R=[hgx_s, hgu_s], **WJ(hgeV_s, gi == 0))
            if cut == 101:
                return
            for wt in range(4):
                wb_, hwb_ = wstg[wt % 2], hwstg[wt % 2]
                DMA("sp", wb_[:], cwin[bl, wt * 128:(wt + 1) * 128, :], key=hwb_, W=[hwb_])
                ptx, phtx = rr()
                TR(ptx[:, 0:128], wb_[:, 0:128], identf[:], R=[hwb_, hIdF], W=[phtx])
                CP("act", KwT_s[:, wt * 128:(wt + 1) * 128], ptx[:, 0:128], R=[phtx], **WJ(hKwT_s, wt == 0))
                CP("pool", Vw_s[:, wt, :, 0:64], wb_[:, 128:256].rearrange("p (g d) -> p g d", g=2), R=[hwb_], **WJ(hVw_s, wt == 0))
            for ct in range(4):
                pvc, phvc = rr()
                MM(pvc[:, 0:128], geV_s[:, ct * 128:(ct + 1) * 128], w2_bf[:, 1, :], R=[hgeV_s, hW2], W=[phvc])
                TS("dve", Vc_s[:, ct, :, 0:64], pvc[:, 0:128].rearrange("p (g d) -> p g d", g=2), valS_f[:, ct:ct + 1], None, ALU.mult,
                   R=[phvc, hValS], **WJ(hVc_s, ct == 0))
                for g in range(2):
                    CP("pool", Vc_s[:, ct, g, 64:65], valS_f[:, ct:ct + 1], R=[hValS], J=[hVc_s])
            if cut == 103:
                return
            for g in range(2):
                gp = slice(64 * g, 64 * g + 64)
                qs = QTs[gp, bl * 16:(bl + 1) * 16]
                pS, phS = rr()
                MM(pS[:, 0:64], J_bf[:], HKC[:, g, :], start=True, stop=False, R=[hJ, GHS], W=[phS])
                for ct in range(4):
                    MM(pS[:, ct * 16:(ct + 1) * 16], KcT_s[gp, ct * 128:(ct + 1) * 128], qs, start=False, stop=(ct == 3),
                       R=[hKcT_s, hQTs], J=[phS])
                ACT(Pc_s[:], pS[:, 0:64], AF.Exp, R=[phS], W=[hPc_s])
                acc, hacc = next_acc()
                for ct in range(4):
                    MM(acc[0:65, 0:NS], Vc_s[:, ct, g, :], Pc_s[:, ct * 16:(ct + 1) * 16], start=(ct == 0), stop=(ct == 3),
                       R=[hVc_s, hPc_s], **WJ(hacc, ct == 0))
                TS("dve", crow_s[64:65, :], acc[64:65, 0:NS], 1e-30, None, ALU.max, R=[hacc], W=[hcrow_s])
                RCP("dve", crow_s[64:65, :], crow_s[64:65, :], R=[hcrow_s], W=[hcrow_s])
                pbc, phbc = rr()
                MM(pbc[:, 0:NS], ones_f[64:65, :], crow_s[64:65, :], R=[hOnesF, hcrow_s], W=[phbc])
                CP("act", pn_s[:, 0:NS], pbc[:, 0:NS], R=[phbc], W=[hpn_s])
                for ct in range(4):
                    TT("dve", pn_s[:, 16 + 0:16 + NS] if False else gx_s[:, ct * 16:(ct + 1) * 16], Pc_s[:, ct * 16:(ct + 1) * 16], pn_s[:, 0:NS], ALU.mult,
                       R=[hPc_s, hpn_s], **WJ(hgx_s, ct == 0))

                for ct in range(4):
                    def _reds(e, ct=ct):
                        with nc.allow_low_precision("fp32 accumulate inside, bf16 store"):
                            return e.tensor_reduce(out=PsT_s[:, ct, :], in_=gx_s[:, ct * 16:(ct + 1) * 16].rearrange("p (r q) -> p q r", r=4),
                                                   axis=AX.X, op=ALU.add)
                    S.add("dve", _reds, reads=[hgx_s], **({"writes": [hPsT_s]} if ct == 0 else {"joins": [hPsT_s]}))
                pim, phim = rr()
                for ct in range(4):
                    MM(pim[0:4, 0:128], PsT_s[:, ct, :], As_bf[:, ct, :], start=(ct == 0), stop=(ct == 3), R=[hPsT_s, hAs], **WJ(phim, ct == 0))
                TT("dve", score_s[:], pim[0:4, 0:128], sels_f[:, 0, :], ALU.mult, R=[phim, hSels], W=[hscore_s])
                TT("dve", score_s[:], score_s[:], sels_f[:, 1, :], ALU.add, R=[hscore_s, hSels], W=[hscore_s])
                S.add("dve", lambda e: e.max(out=m8_s[:], in_=score_s[:]), reads=[hscore_s], writes=[hm8_s])
                S.add("dve", lambda e: e.match_replace(out=sc2_s[:], in_to_replace=m8_s[:], in_values=score_s[:], imm_value=-1e30),
                      reads=[hscore_s, hm8_s], writes=[hsc2_s])
                S.add("dve", lambda e: e.max(out=m8b_s[:], in_=sc2_s[:]), reads=[hsc2_s], writes=[hm8b_s])
                TS("dve", sc2_s[:], score_s[:], m8b_s[:, 6:7], None, ALU.is_ge, R=[hscore_s, hm8b_s], W=[hsc2_s])
                TS("dve", nsel_s[:], sc2_s[:], -1.0, BIG, ALU.add, ALU.mult, R=[hsc2_s], W=[hnsel_s])
                TR(PT[:, 0:4], nsel_s[:], ident_bf[0:4, 0:4], R=[hnsel_s, hId], W=[H_PT])
                CP("act", nselT_s[:], PT[:, 0:4], R=[H_PT], W=[hnselT_s])
                TT("dve", R_s[:].rearrange("p (k r q) -> p k r q", k=NPG, r=4),
                   nselT_s[:].unsqueeze(1).unsqueeze(1).to_broadcast([128, NPG, 4, 4]),
                   bm_f[:].unsqueeze(2).unsqueeze(3).to_broadcast([128, NPG, 4, 4]), ALU.mult,
                   R=[hnselT_s, hBm], W=[hR_s])
                finish_s(g, 0, bl, acc, hacc, True)
                if cut == 104:
                    return
                p1, ph1 = rr()
                p2, ph2 = rr()
                MM(p1[:, :], half_bf[:], R_s[:, 0:512], start=True, stop=False, R=[hHalf, hR_s], W=[ph1])
                MM(p2[:, :], half_bf[:], R_s[:, 512:1024], start=True, stop=False, R=[hHalf, hR_s], W=[ph2])
                for pg in range(NPG):
                    pp_, php_ = (p1, ph1) if pg < 32 else (p2, ph2)
                    c0 = (pg % 32) * 16
                    MM(pp_[:, c0:c0 + 16], KsT_s[gp, pg * 128:(pg + 1) * 128], qs, start=False, stop=(pg == 31),
                       R=[hKsT_s, hQTs], J=[php_])
                for j in range(2):
                    MM(p2[:, 480 + 16 * j:496 + 16 * j], J_bf[:], HKS2[:, j, g, :], start=False, stop=(j == 1), R=[hJ, GHS], J=[ph2])
                ACT(P_s[:, 0:512], p1[:, :], AF.Exp, R=[ph1], W=[hP_s])
                ACT(P_s[:, 512:1024], p2[:, :], AF.Exp, R=[ph2], J=[hP_s])
                if dbg and bl == 0 and g == 0:
                    DBG["d_S2"] = dout("d_S2", [128, 512])
                    CP("act", gx_s[:, 0:512], p2[:, :], R=[ph2], W=[hgx_s])
                    DMA("sp", DBG["d_S2"], gx_s[:, 0:512], key=H("kS2"), R=[hgx_s])
                    DBG["d_nselT"] = dout("d_nselT", [128, 4])
                    DMA("sp", DBG["d_nselT"], nselT_s[:], key=H("knselT"), R=[hnselT_s])
                acc, hacc = next_acc()
                for pg in range(NPG):
                    MM(acc[0:65, 0:NS], Vs_s[:, pg, g, :], P_s[:, pg * 16:(pg + 1) * 16], start=(pg == 0), stop=False,
                       R=[hVs_s, hP_s], **WJ(hacc, pg == 0))
                new_tile(g, bl, ksn, hksn, Vsn_m, hVsn_m, acc, hacc)
                finish_s(g, 1, bl, acc, hacc, False)
                if cut == 105:
                    return
                pW, phW = rr()
                MM(pW[:, 0:64], J_bf[:], HKW[:, g, :], start=True, stop=False, R=[hJ, GHS], W=[phW])
                for wt in range(4):
                    MM(pW[:, wt * 16:(wt + 1) * 16], KwT_s[gp, wt * 128:(wt + 1) * 128], qs, start=False, stop=(wt == 3),
                       R=[hKwT_s, hQTs], J=[phW])
                ACT(Pw_s[:], pW[:, 0:64], AF.Exp, R=[phW], W=[hPw_s])
                if dbg and bl == 0 and g == 0:
                    DBG["d_pW"] = dout("d_pW", [128, 64])
                    CP("act", gu_s[:, 0:64], pW[:, 0:64], R=[phW], W=[hgu_s])
                    DMA("sp", DBG["d_pW"], gu_s[:, 0:64], key=H("kpW"), R=[hgu_s])
                acc, hacc = next_acc()
                for wt in range(4):
                    MM(acc[0:65, 0:NS], Vw_s[:, wt, g, :], Pw_s[:, wt * 16:(wt + 1) * 16], start=(wt == 0), stop=False,
                       R=[hVw_s, hPw_s], **WJ(hacc, wt == 0))
                new_tile(g, bl, kwn, hkwn, Vwn_m, hVwn_m, acc, hacc)
                if dbg and bl == 0 and g == 0:
                    DBG["d_accW"] = dout("d_accW", [65, NS])
                    CP("act", gx_s[0:65, 0:NS], acc[0:65, 0:NS], R=[hacc], W=[hgx_s])
                    DMA("sp", DBG["d_accW"], gx_s[0:65, 0:NS], key=H("kaccW"), R=[hgx_s])
                finish_s(g, 2, bl, acc, hacc, False)
            CP("act", onsab_s[:, :, :, bl * 4:bl * 4 + 4], onsa_s[:].rearrange("p g (r q) -> p g r q", r=4), R=[honsa_s],
               **WJ(honsab_s, bl == 0))
        if nseq < 4:
            pass
        pw, phw = rr()
        for dc in range(KC):
            osl = pw[:, dc * NS:(dc + 1) * NS]
            n = 0
            for g in range(2):
                for r in range(4):
                    MM(osl, wonsa_bf[:, g * 4 + r, dc * 128:(dc + 1) * 128], onsab_s[:, g, r, :], start=(n == 0), stop=False,
                       R=[hWon, honsab_s], **WJ(phw, dc == 0 and n == 0))
                    n += 1
            for hd in range(4):
                MM(osl, wogla_bf[:, hd, dc * 128:(dc + 1) * 128], omixg_s[:, hd, :], start=False, stop=(hd == 3),
                   R=[hWog, homixg_s], J=[phw])
        TT("dve", h1s[:], xs_s[:], pw[:, 0:KC * NS].rearrange("p (c q) -> p c q", c=KC), ALU.add, R=[hxs_s, phw], W=[hh1s])
        DMA("sp", h1_scr[:, :, NT:NT + NS], h1s[:], key=hh1s, R=[hh1s], W=[H_h1[NOWN]])
        if dbg:
            DBG["d_h1s"] = dout("d_h1s", [128, KC, NS])
            DMA("sp", DBG["d_h1s"], h1s[:], key=H("kd_h1s"), R=[hh1s])
        DBG["S_END"] = AR.top

    if do_sample:
        phase_s()
        S.barrier()
        AR.top = MIX_END


    cur_grp[0] = H("GA")
    E_bf = AR.alloc([128, TOK], BF16); hE = H("E")
    for i in range(3):
        a, b = i * 1408, (i + 1) * 1408
        DMA("pool", E_bf[:, a:b], cE[:, a:b], key=hE, W=[hE] if i == 0 else (), J=() if i == 0 else [hE])
    A_bf, hA = load([128, 2, 128], BF16, cA)
    HK = AR.alloc([128, 3, 2, 512], BF16)
    hHK = [[GHA for g in range(2)] for i in range(3)]
    for g in range(2):
        hankel(HK[:, 0, g, :].rearrange("p (r q) -> p r q", r=4), g, OFFA - 127, 1, 128, hHK[0][g])
        hankel(HK[:, 1, g, :].rearrange("p (r q) -> p r q", r=4), g, OFFA + 1, 1, 128, hHK[1][g])
        hankel(HK[:, 2, g, :].rearrange("p (r q) -> p r q", r=4), g, LA + OFFW + 385, 1, 128, hHK[2][g])
    KsT = AR.alloc([128, TOK], BF16); hKs = [H("ks%d" % s) for s in range(NSLOT)]
    KwT = AR.alloc([128, 8 * 128], BF16); hKw = [H("kw%d" % s) for s in range(8)]
    Vs = AR.alloc([128, NSLOT, 2, 65], BF16); hVs = [H("vs%d" % s) for s in range(NSLOT)]
    Vw = AR.alloc([128, 8, 2, 65], BF16); hVw = [H("vw%d" % s) for s in range(8)]
    KcT = AR.alloc([128, 272], BF16); hKc = H("kc")
    geV = AR.alloc([128, 272], BF16); hGeV = H("gev")
    Vc = AR.alloc([128, 2, 2, 65], BF16); hVc = H("vc")
    rawT = AR.alloc([128, 2, 144], BF16); hRawP = H("rawp"); hRawC = H("rawc")
    Sst = AR.alloc([128, 2, 128], F32); hS = H("S")
    Sbf = AR.alloc([128, 2, 128], BF16); hSbf = H("Sbf")
    lrT = AR.alloc([64, 128], BF16); hLr = H("lrT")
    MS("pool", KsT[:], 0.0, W=hKs)
    MS("pool", KwT[:], 0.0, W=hKw)
    MS("pool", Vs[:], 0.0, W=hVs)
    MS("pool", Vw[:], 0.0, W=hVw)
    MS("pool", KcT[:], 0.0, W=[hKc])
    MS("pool", geV[:], 0.0, W=[hGeV])
    MS("pool", Vc[:], 0.0, W=[hVc])
    MS("pool", rawT[:], 0.0, W=[hRawP, hRawC])
    MS("pool", Sst[:], 0.0, W=[hS])
    MS("pool", Sbf[:], 0.0, W=[hSbf])
    MS("pool", lrT[:], 0.0, W=[hLr])
    MS("pool", lrT[32:33, :], 1.0, J=[hLr])

    xs = [AR.alloc([128, KC, 128], F32) for _ in range(2)]; hxs = [H("xs0"), H("xs1")]
    sq = AR.alloc([128, KC, 128], BF16); hsq = H("sq")
    rstd = AR.alloc([128, 128], F32); hrstd = H("rstd")
    xn = AR.alloc([128, KC, 128], BF16); hxn = H("xn")
    kvo = AR.alloc([128, 4, 128], F32); hkvo = H("kvo")
    vto = AR.alloc([128, 2, 128], F32); hvto = H("vto")
    la = AR.alloc([128, 256], F32); hla = H("la")
    esuf = AR.alloc([128, 256], F32); hesuf = H("esuf")
    ecum = AR.alloc([128, 256], F32); hecum = H("ecum")
    einv = AR.alloc([128, 256], F32); heinv = H("einv")
    keT = AR.alloc([128, 256], BF16); hke = H("keT")
    qeT = AR.alloc([128, 256], BF16); hqe = H("qeT")
    kd = AR.alloc([128, 256], BF16); hkd = H("kd")
    vg = AR.alloc([128, 512], BF16); hvg = H("vg")
    attT = AR.alloc([128, 128], BF16); hatt = H("attT")
    osq = AR.alloc([128, 128], BF16); hosq = H("osq")
    grs = AR.alloc([128, 128], F32); hgrs = H("grs")
    sg = AR.alloc([128, 128], F32); hsg = H("sg")
    t1 = AR.alloc([128, 128], F32); ht1 = H("t1")
    omixg = AR.alloc([128, 4, 128], BF16); homixg = H("omixg")
    gx = AR.alloc([128, 8], F32); hgx = H("gx")
    gu = AR.alloc([128, 8], F32); hgu = H("gu")
    ge = AR.alloc([128, 8], BF16); hge = H("ge")
    QT = AR.alloc([128, 512], BF16); hQT = H("QT")
    gate_sb = AR.alloc([24, 128], F32); hgate = H("gate")
    grow = AR.alloc([128, 24 * 128], F32); hgrow = H("grow")
    crow = AR.alloc([128, 512], F32); hcrow = H("crow")
    HKc = [AR.alloc([128, 2, 512], BF16) for _ in range(2)]; hHKc = [[H("hkc%d%d" % (i, g)) for g in range(2)] for i in range(2)]
    Pc = AR.alloc([128, 2, 512], BF16); hPc = [H("pc0"), H("pc1")]
    Pt = [AR.alloc([128, 512], BF16) for _ in range(6)]; hPt = [H("pt%d" % i) for i in range(6)]
    pn_f = AR.alloc([128, 512], F32); hpn = H("pn")
    PsT = AR.alloc([128, 2, 128], BF16); hPsT = H("PsT")
    selc = AR.alloc([128, 2, 128], F32); hselc = H("selc")
    score = AR.alloc([128, 128], F32); hscore = H("score")
    sc2 = AR.alloc([128, 128], F32); hsc2 = H("sc2")
    m8 = AR.alloc([128, 8], F32); hm8 = H("m8")
    m8b = AR.alloc([128, 8], F32); hm8b = H("m8b")
    nsel = AR.alloc([128, 128], BF16); hnsel = H("nsel")
    nselT = AR.alloc([128, 512], BF16); hnselT = H("nselT")
    numsb = AR.alloc([64, 512], F32); hnum = H("num")
    onsa = AR.alloc([64, 2, 512], F32); honsa = H("onsa")
    onsab = AR.alloc([64, 2, 512], BF16); honsab = H("onsab")
    h1t = AR.alloc([128, KC, 128], F32); hh1t = H("h1t")
    pt_rot = [0]

    def projF(out, c0, M, first_w):
        for kc in range(KC):
            MM(out, win_bf[:, kc, c0:c0 + M], xn[:, kc, :], start=(kc == 0), stop=(kc == KC - 1),
               R=[hWin, hxn], **first_w(kc))

    def projT(out, c0, N, first_w):
        for kc in range(KC):
            MM(out, xn[:, kc, :], win_bf[:, kc, c0:c0 + N], start=(kc == 0), stop=(kc == KC - 1),
               R=[hWin, hxn], **first_w(kc))

    def rms_rstd(src_ps, hsrc, n, scale):
        pass

    ACC = [PB[0], PB[1]]
    hACC = [PBH[0], PBH[1]]
    acc_rot = [0]

    def attn_tile(g, lhsK, hK, bias, mask, Vaug, hV, acc, hacc, first, last, keep=None, hkeep=None):
        gp = slice(64 * g, 64 * g + 64)
        pb, ph = rr()
        nmm = 1 + (bias is not None) + (mask is not None)
        k = 0
        MM(pb[:, :], lhsK, QT[gp, :], start=True, stop=(nmm == 1), R=[hK, hQT], W=[ph])
        k += 1
        if bias is not None:
            bap, bh = bias
            MM(pb[:, :], J_bf[:], bap, start=False, stop=(k == nmm - 1), R=[hJ, bh], J=[ph])
            k += 1
        if mask is not None:
            eap = mask
            MM(pb[:, :], eap, nselT[:, :], start=False, stop=True,
               R=[hE, hnselT], J=[ph])
        if keep is None:
            i = pt_rot[0] % 6
            pt_rot[0] += 1
            P, hP = Pt[i][:], hPt[i]
        else:
            P, hP = keep, hkeep
        ACT(P, pb[:, :], AF.Exp, R=[ph], W=[hP])
        MM(acc[0:65, :], Vaug, P, start=first, stop=last, R=[hV, hP], **(dict(W=[hacc]) if first else dict(J=[hacc])))

    def branch_finish(g, br, acc, hacc, first_branch):
        TS("dve", crow[64:65, :], acc[64:65, :], 1e-30, None, ALU.max, R=[hacc], W=[hcrow])
        RCP("dve", crow[64:65, :], crow[64:65, :], R=[hcrow], W=[hcrow])
        off = (br * 8 + g * 4) * 128
        TT("dve", crow[64:65, :], crow[64:65, :], grow[64:65, off:off + 512], ALU.mult, R=[hcrow, hgrow], W=[hcrow])
        pb, ph = rr()
        MM(pb[0:64, :], ones_f[64:65, 0:64], crow[64:65, :], R=[hOnesF, hcrow], W=[ph])
        CP("act", numsb[:, :], acc[0:64, :], R=[hacc], W=[hnum])
        if first_branch:
            TT("dve", onsa[:, g, :], numsb[:, :], pb[0:64, :], ALU.mult, R=[hnum, ph], W=[honsa])
        else:
            TT("dve", numsb[:, :], numsb[:, :], pb[0:64, :], ALU.mult, R=[hnum, ph], W=[hnum])
            TT("dve", onsa[:, g, :], onsa[:, g, :], numsb[:, :], ALU.add, R=[hnum, honsa], W=[honsa])

    for s in range(nslots):
        own = (s % 2 == 1)
        jo = s // 2
        xb, hx = xs[s % 2], hxs[s % 2]
        DMA("sp", xb[:], xT[:, :, s * 128:(s + 1) * 128], key=hx, W=[hx])
        ACT(sq[:], xb[:], AF.Square, R=[hx], W=[hsq])
        pb, ph = rr()
        for kc in range(KC):
            MM(pb[:, 0:128], ones_bf[:], sq[:, kc, :], start=(kc == 0), stop=(kc == KC - 1), R=[hOnes, hsq], **WJ(ph, kc == 0))
        ACT(rstd[:], pb[:, 0:128], AF.Ln, bias=EPS, scale=1.0 / D, R=[ph], W=[hrstd])
        ACT(rstd[:], rstd[:], AF.Exp, scale=-0.5, R=[hrstd], W=[hrstd])
        for kc in range(KC):
            STT("dve", xn[:, kc, :], xb[:, kc, :], norms_f[:, 0, kc:kc + 1], rstd[:], ALU.mult, ALU.mult,
                R=[hx, hrstd, hNorms], **WJ(hxn, kc == 0))
        if cut <= 1:
            break
        pb, ph = rr()
        for i, c0 in enumerate((512, 640, 768, 1024)):
            projF(pb[:, i * 128:(i + 1) * 128], c0, 128, lambda kc, i=i: WJ(ph, i == 0 and kc == 0))
        if cut == 1.1:
            break
        CP("act", rawT[:, :, 16:144], pb[:, 0:256].rearrange("p (t n) -> p t n", t=2), R=[ph], W=[hRawC])
        if cut == 1.2:
            break
        CP("act", KsT[:, s * 128:(s + 1) * 128], pb[:, 256:384], R=[ph], W=[hKs[s]])
        CP("act", KwT[:, (s % 8) * 128:(s % 8 + 1) * 128], pb[:, 384:512], R=[ph], W=[hKw[s % 8]])
        if own:
            CP("act", kvo[:].rearrange("p a b -> p (a b)"), pb[:, :], R=[ph], W=[hkvo])
            DMA("sp", kvT_p[:, :, jo * 128:(jo + 1) * 128], kvo[:], key=hkvo, R=[hkvo])
        if cut <= 2:
            break
        pb, ph = rr()
        projT(pb[:, 0:128], 896, 128, lambda kc: WJ(ph, kc == 0))
        projT(pb[:, 128:256], 1152, 128, lambda kc: dict(J=[ph]))
        CP("dve", Vs[:, s, :, 0:64], pb[:, 0:128].rearrange("p (g d) -> p g d", g=2), R=[ph], W=[hVs[s]])
        CP("dve", Vw[:, s % 8, :, 0:64], pb[:, 128:256].rearrange("p (g d) -> p g d", g=2), R=[ph], W=[hVw[s % 8]])
        for g in range(2):
            CP("pool", Vs[:, s, g, 64:65], valid_f[:, s:s + 1], R=[hValid], J=[hVs[s]])
            CP("pool", Vw[:, s % 8, g, 64:65], valid_f[:, s:s + 1], R=[hValid], J=[hVw[s % 8]])
        if own:
            CP("act", vto[:].rearrange("p a b -> p (a b)"), pb[:, 0:256], R=[ph], W=[hvto])
            DMA("sp", vtok_p[jo * 128:(jo + 1) * 128, :, :], vto[:], key=hvto, R=[hvto])
        if cut <= 3:
            break
        for t in range(2):
            pc, phc = rr()
            for sx in range(32):
                MM(pc[:, 0:8], w1_bf[:, t, sx, :], rawT[:, t, sx:sx + 113:16], start=(sx == 0), stop=(sx == 31),
                   R=[hW1, hRawP, hRawC], **WJ(phc, sx == 0))
            TS("dve", gx[:], pc[:, 0:8], pew1[:, t:t + 1], None, ALU.add, R=[phc, hPew1], W=[hgx])
            TT("dve", gu[:], gx[:], gx[:], ALU.mult, R=[hgx], W=[hgu])
            TS("dve", gu[:], gu[:], 0.044715, 1.0, ALU.mult, ALU.add, R=[hgu], W=[hgu])
            TT("dve", gu[:], gu[:], gx[:], ALU.mult, R=[hgu, hgx], W=[hgu])
            ACT(gu[:], gu[:], AF.Exp, scale=-1.5957691216057308, R=[hgu], W=[hgu])
            TS("dve", gu[:], gu[:], 1.0, None, ALU.add, R=[hgu], W=[hgu])
            RCP("dve", gu[:], gu[:], R=[hgu], W=[hgu])
            if t == 0:
                TT("dve", ge[:], gx[:], gu[:], ALU.mult, R=[hgx, hgu], W=[hge])
                pk2, phk2 = rr()
                MM(pk2[:, 0:8], w2_bf[:, 0, :], ge[:], R=[hW2, hge], W=[phk2])
                CP("act", KcT[:, 8 * s:8 * s + 8], pk2[:, 0:8], R=[phk2], W=[hKc])
            else:
                TT("dve", geV[:, 8 * s:8 * s + 8], gx[:], gu[:], ALU.mult, R=[hgx, hgu], W=[hGeV])
        CP("pool", rawT[:, :, 0:16], rawT[:, :, 128:144], R=[hRawC], W=[hRawP])
        if cut <= 4:
            break
        pb, ph = rr()
        projF(pb[0:16, 0:128], C_LR, 16, lambda kc: WJ(ph, kc == 0))
        CP("act", lrT[0:16, :], pb[0:16, 0:128], R=[ph], W=[hLr])
        pz, phz = rr()
        MM(pz[:, 0:256], lrT[0:33, :], wgk_bf[:], R=[hLr, hWgk], W=[phz])
        ACT(la[:], pz[:, 0:256], AF.Exp, scale=-1.0, R=[phz], W=[hla])
        ACT(la[:], la[:], AF.Ln, bias=1.0, R=[hla], W=[hla])
        psf, phs = rr()
        MM(psf[:, 0:256], triu_f[:], la[:], R=[hTriU, hla], W=[phs])
        ACT(esuf[:], psf[:, 0:256], AF.Exp, scale=-1.0 / 16, R=[phs], W=[hesuf])
        pct, phc = rr()
        for ch in range(2):
            MM(pct[:, ch * 128:(ch + 1) * 128], la[:, ch * 128:(ch + 1) * 128], tri_f[:], R=[hla, hTri], **WJ(phc, ch == 0))
        ACT(ecum[:], pct[:, 0:256], AF.Exp, scale=-1.0 / 16, R=[phc], W=[hecum])
        ACT(einv[:], pct[:, 0:256], AF.Exp, scale=1.0 / 16, R=[phc], W=[heinv])
        if cut <= 5:
            break
        pk, phk = rr()
        for ch in range(2):
            projF(pk[:, ch * 128:(ch + 1) * 128], C_KG + ch * 128, 128, lambda kc, ch=ch: WJ(phk, ch == 0 and kc == 0))
        TT("dve", keT[:], pk[:, 0:256], einv[:], ALU.mult, R=[phk, heinv], W=[hke])
        pkt, phkt = rr()
        projT(pkt[:, 0:256], C_KG, 256, lambda kc: WJ(phkt, kc == 0))
        TT("dve", kd[:], pkt[:, 0:256], esuf[:], ALU.mult, R=[phkt, hesuf], W=[hkd])
        pv, phv = rr()
        projT(pv[:, 0:512], C_VG, 512, lambda kc: WJ(phv, kc == 0))
        CP("act", vg[:], pv[:, 0:512], R=[phv], W=[hvg])
        if own:
            pq, phq = rr()
            for ch in range(2):
                projF(pq[:, ch * 128:(ch + 1) * 128], C_QG + ch * 128, 128, lambda kc, ch=ch: WJ(phq, ch == 0 and kc == 0))
            STT("dve", qeT[:], pq[:, 0:256], 0.125, ecum[:], ALU.mult, ALU.mult, R=[phq, hecum], W=[hqe])
            for hd in range(4):
                ch = hd // 2
                pp = slice(64 * (hd % 2), 64 * (hd % 2) + 64)
                cs = slice(ch * 128, (ch + 1) * 128)
                pa, pha = rr()
                MM(pa[:, 0:128], keT[pp, cs], qeT[pp, cs], R=[hke, hqe], W=[pha])
                TT("dve", attT[:], pa[:, 0:128], tri_f[:], ALU.mult, R=[pha, hTri], W=[hatt])
                po, pho = rr()
                MM(po[:, 0:128], vg[:, hd * 128:(hd + 1) * 128], attT[:], start=True, stop=False, R=[hvg, hatt], W=[pho])
                MM(po[:, 0:128], Sbf[pp, ch, :], qeT[pp, cs], start=False, stop=True, R=[hSbf, hqe], J=[pho])
                ACT(osq[:], po[:, 0:128], AF.Square, R=[pho], W=[hosq])
                pn, phn = rr()
                MM(pn[:, 0:128], ones_bf[:], osq[:], R=[hOnes, hosq], W=[phn])
                ACT(grs[:], pn[:, 0:128], AF.Ln, bias=EPS, scale=1.0 / 128, R=[phn], W=[hgrs])
                ACT(grs[:], grs[:], AF.Exp, scale=-0.5, R=[hgrs], W=[hgrs])
                STT("dve", t1[:], po[:, 0:128], glan_f[:, 0:1], grs[:], ALU.mult, ALU.mult, R=[pho, hGlan, hgrs], W=[ht1])
                pg, phg = rr()
                projF(pg[:, 0:128], C_GG + hd * 128, 128, lambda kc: WJ(phg, kc == 0))
                ACT(sg[:], pg[:, 0:128], AF.Exp, scale=-1.0, R=[phg], W=[hsg])
                TS("dve", sg[:], sg[:], 1.0, None, ALU.add, R=[hsg], W=[hsg])
                RCP("dve", sg[:], sg[:], R=[hsg], W=[hsg])
                TT("dve", sg[:], sg[:], pg[:, 0:128], ALU.mult, R=[hsg, phg], W=[hsg])
                TT("dve", omixg[:, hd, :], t1[:], sg[:], ALU.mult, R=[ht1, hsg], **WJ(homixg, hd == 0))
        if cut <= 6:
            break
        for ch in range(2):
            pu, phu = rr()
            MM(pu[:, 0:256], kd[:, ch * 128:(ch + 1) * 128], vg[:, ch * 256:(ch + 1) * 256], R=[hkd, hvg], W=[phu])
            for hh in range(2):
                pp = slice(64 * hh, 64 * hh + 64)
                STT("dve", Sst[pp, ch, :], Sst[pp, ch, :], ecum[pp, ch * 128 + 127:ch * 128 + 128], pu[pp, hh * 128:(hh + 1) * 128],
                    ALU.mult, ALU.add, R=[hS, hecum, phu], W=[hS])
        CP("pool", Sbf[:], Sst[:], R=[hS], W=[hSbf])
        if not own:
            continue
        pq, phq = rr()
        for r in range(4):
            projF(pq[:, r * 128:(r + 1) * 128], r * 128, 128, lambda kc, r=r: WJ(phq, r == 0 and kc == 0))
        TS("dve", QT[:], pq[:, :], 0.125, None, ALU.mult, R=[phq], W=[hQT])
        pgt, phgt = rr()
        projF(pgt[0:24, 0:128], C_GT, 24, lambda kc: WJ(phgt, kc == 0))
        ACT(gate_sb[:], pgt[0:24, 0:128], AF.Exp, scale=-1.0, R=[phgt], W=[hgate])
        TS("dve", gate_sb[:], gate_sb[:], 1.0, None, ALU.add, R=[hgate], W=[hgate])
        RCP("dve", gate_sb[:], gate_sb[:], R=[hgate], W=[hgate])
        DMA("sp", gscr[jo % 2], gate_sb[:], key=hgate, R=[hgate], W=[H_gscr[jo % 2]])
        DMA("sp", grow[64:65, :], gscr[jo % 2].rearrange("(o a) b -> o (a b)", o=1), key=hgrow, R=[H_gscr[jo % 2]], W=[hgrow])
        nct = 2 if s >= 17 else 1
        for ct in range(nct):
            pvc, phvc = rr()
            MM(pvc[:, 0:128], geV[:, ct * 128:(ct + 1) * 128], w2_bf[:, 1, :], R=[hGeV, hW2], W=[phvc])
            TS("dve", Vc[:, ct, :, 0:64], pvc[:, 0:128].rearrange("p (g d) -> p g d", g=2), valc_f[:, ct:ct + 1], None, ALU.mult,
               R=[phvc, hValC], **WJ(hVc, ct == 0))
            for g in range(2):
                CP("pool", Vc[:, ct, g, 64:65], valc_f[:, ct:ct + 1], R=[hValC], J=[hVc])
        DMA("sp", selc[:], cSel[jo], key=hselc, W=[hselc])
        for g in range(2):
            gp = slice(64 * g, 64 * g + 64)
            hk_i = jo % 2
            bias_ct = None
            if s <= 15:
                bias_ct, sprime = 0, s
            elif s >= 17:
                bias_ct, sprime = 1, s - 16
            if bias_ct is not None:
                hankel(HKc[hk_i][:, g, :].rearrange("p (r q) -> p r q", r=4), g, OFFA + 128 * sprime - 2047, 16, 128, hHKc[hk_i][g])
            acc, hacc = ACC[acc_rot[0] % 2], hACC[acc_rot[0] % 2]
            acc_rot[0] += 1
            for ct in range(nct):
                bias = (HKc[hk_i][:, g, :], hHKc[hk_i][g]) if ct == bias_ct else None
                attn_tile(g, KcT[gp, ct * 128:(ct + 1) * 128], hKc, bias, None, Vc[:, ct, g, :], hVc, acc, hacc,
                          ct == 0, ct == nct - 1, keep=Pc[:, ct, :], hkeep=hPc[ct])
            TS("dve", crow[64:65, :], acc[64:65, :], 1e-30, None, ALU.max, R=[hacc], W=[hcrow])
            RCP("dve", crow[64:65, :], crow[64:65, :], R=[hcrow], W=[hcrow])
            pbc, phbc = rr()
            MM(pbc[:, :], ones_f[64:65, :], crow[64:65, :], R=[hOnesF, hcrow], W=[phbc])
            for ct in range(nct):
                TT("dve", pn_f[:], Pc[:, ct, :], pbc[:, :], ALU.mult, R=[hPc[ct], phbc], W=[hpn])
                def _red(e, ct=ct):
                    with nc.allow_low_precision("fp32 accumulate inside, bf16 store"):
                        return e.tensor_reduce(out=PsT[:, ct, :], in_=pn_f[:].rearrange("p (r q) -> p q r", r=4), axis=AX.X, op=ALU.add)
                S.add("dve", _red, reads=[hpn], **({"writes": [hPsT]} if ct == 0 else {"joins": [hPsT]}))
            pim, phim = rr()
            for ct in range(nct):
                MM(pim[:, 0:128], PsT[:, ct, :], A_bf[:, ct, :], start=(ct == 0), stop=(ct == nct - 1), R=[hPsT, hA], **WJ(phim, ct == 0))
            TT("dve", score[:], pim[:, 0:128], selc[:, 0, :], ALU.mult, R=[phim, hselc], W=[hscore])
            TT("dve", score[:], score[:], selc[:, 1, :], ALU.add, R=[hscore, hselc], W=[hscore])
            S.add("dve", lambda e: e.max(out=m8[:], in_=score[:, 0:72]), reads=[hscore], writes=[hm8])
            S.add("dve", lambda e: e.match_replace(out=sc2[:, 0:72], in_to_replace=m8[:], in_values=score[:, 0:72], imm_value=-1e30),
                  reads=[hscore, hm8], writes=[hsc2])
            S.add("dve", lambda e: e.max(out=m8b[:], in_=sc2[:, 0:72]), reads=[hsc2], writes=[hm8b])
            TS("dve", sc2[:], score[:], m8b[:, 7:8], None, ALU.is_ge, R=[hscore, hm8b], W=[hsc2])
            TS("dve", nsel[:], sc2[:], -1.0, BIG, ALU.add, ALU.mult, R=[hsc2], W=[hnsel])
            TR(PT[:, 0:128], nsel[:], ident_bf[:], R=[hnsel, hId], W=[H_PT])
            CP("act", nselT[:].rearrange("p (r q) -> p r q", r=4), PT[:, 0:128].unsqueeze(1).to_broadcast([128, 4, 128]), R=[H_PT], W=[hnselT])
            branch_finish(g, 0, acc, hacc, True)
            acc, hacc = ACC[acc_rot[0] % 2], hACC[acc_rot[0] % 2]
            acc_rot[0] += 1
            for ks in range(s + 1):
                dl = s - ks
                bias = (HK[:, dl, g, :], hHK[dl][g]) if dl <= 1 else None
                msk = None if (s <= 7 or ks == s) else E_bf[:, ks * 128:(ks + 1) * 128]
                attn_tile(g, KsT[gp, ks * 128:(ks + 1) * 128], hKs[ks], bias, msk,
                          Vs[:, ks, g, :], hVs[ks], acc, hacc, ks == 0, ks == s)
            branch_finish(g, 1, acc, hacc, False)
            acc, hacc = ACC[acc_rot[0] % 2], hACC[acc_rot[0] % 2]
            acc_rot[0] += 1
            k0 = max(0, s - 4)
            for ks in range(k0, s + 1):
                dl = s - ks
                bias = None
                if dl <= 1:
                    bias = (HK[:, dl, g, :], hHK[dl][g])
                elif dl == 4:
                    bias = (HK[:, 2, g, :], hHK[2][g])
                attn_tile(g, KwT[gp, (ks % 8) * 128:(ks % 8 + 1) * 128], hKw[ks % 8], bias, None,
                          Vw[:, ks % 8, g, :], hVw[ks % 8], acc, hacc, ks == k0, ks == s)
            branch_finish(g, 2, acc, hacc, False)
        CP("act", onsab[:], onsa[:], R=[honsa], W=[honsab])
        for half in range(2):
            pw, phw = rr()
            for c4 in range(4):
                dc = half * 4 + c4
                osl = pw[:, c4 * 128:(c4 + 1) * 128]
                n = 0
                for g in range(2):
                    for r in range(4):
                        MM(osl, wonsa_bf[:, g * 4 + r, dc * 128:(dc + 1) * 128], onsab[:, g, r * 128:(r + 1) * 128],
                           start=(n == 0), stop=False, R=[hWon, honsab], **WJ(phw, c4 == 0 and n == 0))
                        n += 1
                for hd in range(4):
                    MM(osl, wogla_bf[:, hd, dc * 128:(dc + 1) * 128], omixg[:, hd, :], start=False, stop=(hd == 3),
                       R=[hWog, homixg], J=[phw])
            TT("dve", h1t[:, half * 4:half * 4 + 4, :], xb[:, half * 4:half * 4 + 4, :],
               pw[:, :].rearrange("p (c q) -> p c q", c=4), ALU.add, R=[hx, phw], **WJ(hh1t, half == 0))
        DMA("sp", h1_scr[:, :, jo * 128:(jo + 1) * 128], h1t[:], key=hh1t, R=[hh1t], W=[H_h1[jo]])
        if dbg and s == 31:
            DBG["d_h1L"] = dout("d_h1L", [128, KC, 128])
            DMA("sp", DBG["d_h1L"], h1t[:], key=hh1t, R=[hh1t])
        if dbg and s == 1:
            DBG["d_h1"] = dout("d_h1", [128, KC, 128])
            DMA("sp", DBG["d_h1"], h1t[:], key=hh1t, R=[hh1t])
            for nm, tl, hh, shp in (("d_onsa", onsa, honsa, [64, 2, 512]), ("d_grow", grow, hgrow, [128, 3072]), ("d_score", score, hscore, [128, 128]),
                                    ("d_t1", t1, ht1, [128, 128]), ("d_sg", sg, hsg, [128, 128]), ("d_grs", grs, hgrs, [128, 128]),
                                    ("d_la", la, hla, [128, 256]), ("d_ecum", ecum, hecum, [128, 256]), ("d_S", Sst, hS, [128, 2, 128]),
                                    ("d_rstd", rstd, hrstd, [128, 128]), ("d_pn", pn_f, hpn, [128, 512]), ("d_num", numsb, hnum, [64, 512]), ("d_gx", gx, hgx, [128, 8])):
                DBG[nm] = dout(nm, shp)
                DMA("sp", DBG[nm], tl[:], key=H("k" + nm), R=[hh])
    DMA("sp", gla_p, Sst[:], key=hS, R=[hS])
    A_END = AR.top

    S.barrier()
    AR.top = PERS_END
    wg_bf = AR.alloc([128, KC, DFF], BF16); hWg = H("wg")
    wu_bf = AR.alloc([128, KC, DFF], BF16); hWu = H("wu")
    wd_bf = AR.alloc([128, NFC, D], BF16); hWd = H("wd")
    wpg_bf = AR.alloc([128, KC, D], BF16); hWpg = H("wpg")
    wple_bf = AR.alloc([128, 2, D], BF16); hWple = H("wple")

    def wload(dst, src, n1, ncol, h, step):
        first = True
        for i in range(n1):
            for a in range(0, ncol, step):
                DMA("pool", dst[:, i, a:a + step], src[:, i, a:a + step], key=h, **(dict(W=[h]) if first else dict(J=[h])))
                first = False
    if os.environ.get("SKIPB") == "1":
        ctx = dict(locals())
        return ctx
    wload(wg_bf, w_gate, KC, DFF, hWg, 1408)
    wload(wu_bf, w_up, KC, DFF, hWu, 1408)
    wload(wd_bf, w_down, NFC, D, hWd, 1024)
    wload(wpg_bf, w_pg, KC, D, hWpg, 1024)
    wload(wple_bf, w_ple, 2, D, hWple, 1024)
    TBM = 256
    hb = AR.alloc([128, KC, TBM], F32); hhb = H("hb")
    sqb = AR.alloc([128, KC, TBM], BF16); hsqb = H("sqb")
    rsb = AR.alloc([128, TBM], F32); hrsb = H("rsb")
    xnb = AR.alloc([128, KC, TBM], BF16); hxnb = H("xnb")
    actb = AR.alloc([128, NFC, TBM], BF16); hactb = H("actb")
    egbs = [AR.alloc([128, TBM], F32) for _ in range(2)]; hegbs = [H("egb0"), H("egb1")]
    pTf = AR.alloc([128, 2, TBM], F32); hpTf = H("pTf")
    pTb = AR.alloc([128, 2, TBM], BF16); hpTb = H("pTb")

    def rmsnorm_fm(src, hsrc, dst, hdst, nidx, TB, out_f32=False):
        ACT(sqb[:, :, 0:TB], src[:, :, 0:TB], AF.Square, R=[hsrc], W=[hsqb])
        pb, ph = rr()
        for kc in range(KC):
            MM(pb[:, 0:TB], ones_bf[:], sqb[:, kc, 0:TB], start=(kc == 0), stop=(kc == KC - 1), R=[hOnes, hsqb], **WJ(ph, kc == 0))
        ACT(rsb[:, 0:TB], pb[:, 0:TB], AF.Ln, bias=EPS, scale=1.0 / D, R=[ph], W=[hrsb])
        ACT(rsb[:, 0:TB], rsb[:, 0:TB], AF.Exp, scale=-0.5, R=[hrsb], W=[hrsb])
        for kc in range(KC):
            STT("dve", dst[:, kc, 0:TB], src[:, kc, 0:TB], norms_f[:, nidx, kc:kc + 1], rsb[:, 0:TB], ALU.mult, ALU.mult,
                R=[hsrc, hrsb, hNorms], **(WJ(hdst, kc == 0) if hdst is not hsrc else dict(W=[hdst])))

    def ffn_block(t0, TB, hsrc_dram):
        DMA("sp", hb[:, :, 0:TB], h1_scr[:, :, t0:t0 + TB], key=hhb, R=[hsrc_dram], W=[hhb])
        DMA("sp", pTf[:, :, 0:TB], pT[:, :, t0:t0 + TB], key=hpTf, W=[hpTf])
        CP("pool", pTb[:, :, 0:TB], pTf[:, :, 0:TB], R=[hpTf], W=[hpTb])
        rmsnorm_fm(hb, hhb, xnb, hxnb, 1, TB)
        for fc in range(NFC):
            pg_, phg_ = rr()
            for kc in range(KC):
                MM(pg_[:, 0:TB], wg_bf[:, kc, fc * 128:(fc + 1) * 128], xnb[:, kc, 0:TB], start=(kc == 0), stop=(kc == KC - 1),
                   R=[hWg, hxnb], **WJ(phg_, kc == 0))
            pu_, phu_ = rr()
            for kc in range(KC):
                MM(pu_[:, 0:TB], wu_bf[:, kc, fc * 128:(fc + 1) * 128], xnb[:, kc, 0:TB], start=(kc == 0), stop=(kc == KC - 1),
                   R=[hWu, hxnb], **WJ(phu_, kc == 0))
            eg_, heg_ = egbs[fc % 2], hegbs[fc % 2]
            ACT(eg_[:, 0:TB], pg_[:, 0:TB], AF.Silu, R=[phg_], W=[heg_])
            TT("dve", actb[:, fc, 0:TB], eg_[:, 0:TB], pu_[:, 0:TB], ALU.mult, R=[heg_, phu_], **WJ(hactb, fc == 0))
        for dc in range(KC):
            pd_, phd_ = rr()
            for fc in range(NFC):
                MM(pd_[:, 0:TB], wd_bf[:, fc, dc * 128:(dc + 1) * 128], actb[:, fc, 0:TB], start=(fc == 0), stop=(fc == NFC - 1),
                   R=[hWd, hactb], **WJ(phd_, fc == 0))
            TT("dve", hb[:, dc, 0:TB], hb[:, dc, 0:TB], pd_[:, 0:TB], ALU.add, R=[hhb, phd_], W=[hhb])
        rmsnorm_fm(hb, hhb, xnb, hxnb, 2, TB)
        for dc in range(KC):
            pg_, phg_ = rr()
            for kc in range(KC):
                MM(pg_[:, 0:TB], wpg_bf[:, kc, dc * 128:(dc + 1) * 128], xnb[:, kc, 0:TB], start=(kc == 0), stop=(kc == KC - 1),
                   R=[hWpg, hxnb], **WJ(phg_, kc == 0))
            pu_, phu_ = rr()
            for k2 in range(2):
                MM(pu_[:, 0:TB], wple_bf[:, k2, dc * 128:(dc + 1) * 128], pTb[:, k2, 0:TB], start=(k2 == 0), stop=(k2 == 1),
                   R=[hWple, hpTb], **WJ(phu_, k2 == 0))
            eg_, heg_ = egbs[dc % 2], hegbs[dc % 2]
            ACT(eg_[:, 0:TB], pg_[:, 0:TB], AF.Sigmoid, R=[phg_], W=[heg_])
            TT("dve", eg_[:, 0:TB], eg_[:, 0:TB], pu_[:, 0:TB], ALU.mult, R=[heg_, phu_], W=[heg_])
            TT("dve", hb[:, dc, 0:TB], hb[:, dc, 0:TB], eg_[:, 0:TB], ALU.add, R=[hhb, heg_], W=[hhb])
        rmsnorm_fm(hb, hhb, hb, hhb, 3, TB)
        DMA("sp", yT[:, :, t0:t0 + TB], hb[:, :, 0:TB], key=hhb, R=[hhb])

    nblk = (min(nslots, NSLOT) // 2) * 128 // TBM
    for bi in range(nblk):
        ffn_block(bi * TBM, TBM, H_h1[(bi * TBM) // 128 + 1] if False else H_h1[min(NOWN - 1, (bi * TBM + TBM - 1) // 128)])
    if do_sample and not (99.05 <= cut < 107):
        ffn_block(NT, NS, H_h1[NOWN])
    B_END = AR.top
    ctx = dict(locals())
    return ctx


_STATIC = None


def _kc(a):
    K = a.shape[0] // 128
    return np.ascontiguousarray(a.reshape(K, 128, -1).transpose(1, 0, 2))


def prep_shared(inp):
    global _STATIC
    if _STATIC is None:
        _STATIC = _static_consts()
    sh = dict(_STATIC)
    l = 0
    wi = np.array(inp["w_in"][l])
    wi[:, 0:512] = wi[:, 0:512].reshape(D, 2, 4, 64).transpose(0, 2, 1, 3).reshape(D, 512)
    sh["w_in"] = _kc(wi)
    wo = inp["w_o"][l]
    sh["w_o_nsa"] = np.ascontiguousarray(wo[:512].reshape(8, 64, D).transpose(1, 0, 2))
    sh["w_o_gla"] = _kc(wo[512:])
    sh["w_gate"] = _kc(inp["w_gate"][l])
    sh["w_up"] = _kc(inp["w_up"][l])
    sh["w_down"] = _kc(inp["w_down"][l])
    sh["w_ple"] = _kc(inp["w_ple"][l])
    sh["w_pg"] = _kc(inp["w_ple_gate"][l])
    nm = np.stack([inp["norm_mix"][l], inp["norm_ffn"][l], inp["norm_ple"][l], inp["norm_final"]], 0)
    sh["norms"] = np.ascontiguousarray(nm.reshape(4, KC, 128).transpose(2, 0, 1))
    w1 = inp["cmp_w1"][l].reshape(2, 32, 64, 64)
    w1bd = np.zeros((128, 2, 32, 128), np.float32)
    for g in range(2):
        w1bd[g * 64:(g + 1) * 64, :, :, g * 64:(g + 1) * 64] = w1.transpose(2, 0, 1, 3)
    sh["w1bd"] = w1bd
    pe = inp["cmp_pe"][l]
    sh["pecol"] = np.ascontiguousarray(np.concatenate([pe.transpose(2, 0, 1)] * 2, 0))
    w2 = inp["cmp_w2"][l]
    w2bd = np.zeros((128, 2, 128), np.float32)
    for g in range(2):
        w2bd[g * 64:(g + 1) * 64, :, g * 64:(g + 1) * 64] = w2.transpose(1, 0, 2)
    sh["w2bd"] = w2bd
    wg = np.zeros((33, 256), np.float32)
    wg[:16] = inp["w_gk"][l]
    wg[32] = inp["b_gk"][l]
    sh["wgk"] = wg
    sh["glan"] = np.ascontiguousarray(inp["gla_norm"][l].reshape(128, 1))
    rb = np.zeros((33, 8), np.float32)
    rb[:32] = inp["rel_bias"]
    rb[32] = -BIG
    sh["rb33"] = rb
    sh["cache"] = np.ascontiguousarray(inp["cache_nsa_kv"][l].reshape(-1, 128, 512))
    return sh


def prep_core(inp, c, sh):
    b, par = c // 2, c % 2
    shift = 1 - par
    l = 0
    m = dict(sh)
    m.update(_core_consts(par))
    x = inp["x_prompt"][b]
    xs = np.zeros((TOK, D), np.float32)
    xs[shift * 128:shift * 128 + SEQ] = x
    m["xT"] = np.ascontiguousarray(xs.T.reshape(KC, 128, TOK).transpose(1, 0, 2))
    own_tok = np.concatenate([np.arange(128) + (2 * j + par) * 128 for j in range(NOWN)])
    bs = slice(4 * c, 4 * c + 4)
    p = np.concatenate([inp["p_prompt"][l, b][own_tok], inp["p_sample"][l, bs].reshape(NS, 256)], 0)
    m["pT"] = np.ascontiguousarray(p.T.reshape(2, 128, NTS).transpose(1, 0, 2))
    xsm = inp["x_sample"][bs].reshape(NS, D)
    m["xsT"] = np.ascontiguousarray(xsm.T.reshape(KC, 128, NS).transpose(1, 0, 2))
    m["ptab"] = np.ascontiguousarray(inp["page_table"][bs].reshape(1, 4 * NPG).astype(np.int32))
    m["cwin"] = np.ascontiguousarray(inp["cache_win_kv"][l, bs].reshape(4, 512, 256))
    m["sgla"] = np.ascontiguousarray(inp["state_gla"][l, bs])
    return m


_PROG = None


def _get_prog():
    global _PROG
    if _PROG is None:
        c = build_program(dbg=False)
        c["S"].finish()
        c["S"].emit()
        _PROG = c
    return _PROG


def kernel(**inputs):
    inp = {k: np.asarray(v) for k, v in inputs.items()}
    c = _get_prog()
    sh = prep_shared(inp)
    maps = []
    for ci in range(8):
        m = prep_core(inp, ci, sh)
        maps.append({k: m[k] for k in c["IN"]})
    res = run_bass_kernel_spmd(c["nc"], maps, core_ids=list(range(8)))
    R = res.results
    B, T = 4, SEQ
    y_prompt = np.zeros((B, T, D), np.float32)
    y_sample = np.zeros((32, 4, D), np.float32)
    new_kv_prompt = np.zeros((1, B, T, 4, 2, 64), np.float32)
    new_kv_sample = np.zeros((1, 32, 4, 4, 2, 64), np.float32)
    new_win_prompt = np.zeros((1, B, 512, 2, 2, 64), np.float32)
    new_win_sample = np.zeros((1, 32, 512, 2, 2, 64), np.float32)
    new_gla_prompt = np.zeros((1, B, 4, 64, 128), np.float32)
    new_gla_sample = np.zeros((1, 32, 4, 64, 128), np.float32)
    kwin = np.zeros((B, T, 2, 64), np.float32)
    vwin = np.zeros((B, T, 2, 64), np.float32)
    for ci in range(8):
        b, par = ci // 2, ci % 2
        r = R[ci]
        own_tok = np.concatenate([np.arange(128) + (2 * j + par) * 128 for j in range(NOWN)])
        kvT = r["kvT_p"]
        vt = r["vtok_p"]
        yT = r["yT"]
        y_prompt[b, own_tok] = yT[:, :, :NT].transpose(2, 1, 0).reshape(NT, D)
        y_sample[4 * ci:4 * ci + 4] = yT[:, :, NT:].transpose(2, 1, 0).reshape(4, 4, D)
        for i, sl in enumerate((0, 1, 2)):
            new_kv_prompt[0, b, own_tok, sl] = kvT[:, i, :].T.reshape(NT, 2, 64)
        new_kv_prompt[0, b, own_tok, 3] = vt[:, 0, :].reshape(NT, 2, 64)
        kwin[b, own_tok] = kvT[:, 3, :].T.reshape(NT, 2, 64)
        vwin[b, own_tok] = vt[:, 1, :].reshape(NT, 2, 64)
        if par == 0:
            gp = r["gla_p"]
            for h in range(4):
                new_gla_prompt[0, b, h] = gp[(h % 2) * 64:(h % 2) * 64 + 64, h // 2, :]
        ks = r["kvT_s"]
        vs = r["vtok_s"]
        for i, sl in enumerate((0, 1, 2)):
            new_kv_sample[0, 4 * ci:4 * ci + 4, :, sl] = ks[:, i, :].T.reshape(4, 4, 2, 64)
        new_kv_sample[0, 4 * ci:4 * ci + 4, :, 3] = vs[:, 0, :].reshape(4, 4, 2, 64)
        ws = r["win_s"].reshape(4, 508, 2, 2, 64)
        new_win_sample[0, 4 * ci:4 * ci + 4, :508] = ws
        new_win_sample[0, 4 * ci:4 * ci + 4, 508:, 0] = ks[:, 3, :].T.reshape(4, 4, 2, 64)
        new_win_sample[0, 4 * ci:4 * ci + 4, 508:, 1] = vs[:, 1, :].reshape(4, 4, 2, 64)
        new_gla_sample[0, 4 * ci:4 * ci + 4] = r["gla_s"].reshape(4, 2, 64, 2, 128).transpose(0, 3, 1, 2, 4).reshape(4, 4, 64, 128)
    new_win_prompt[0, :, :, 0] = kwin[:, T - 512:]
    new_win_prompt[0, :, :, 1] = vwin[:, T - 512:]
    return (y_prompt, y_sample, new_kv_prompt, new_kv_sample, new_win_prompt, new_win_sample,
            new_gla_prompt, new_gla_sample)
```

```python
import math
import os
import contextlib
import numpy as np
import ml_dtypes
import concourse.bass as bass
import concourse.mybir as mybir
from concourse.bass_utils import run_bass_kernel_spmd

F32 = mybir.dt.float32
BF16 = mybir.dt.bfloat16
I32 = mybir.dt.int32
ALU = mybir.AluOpType
AF = mybir.ActivationFunctionType
AX = mybir.AxisListType

ENGS = ("pe", "act", "dve", "pool", "sp")
SAME_ENG_SYNC = True
RESCHED = True
SWDGE_DEPTH = 100000


class H:
    __slots__ = ("name", "writers", "readers", "gdeps", "excl")

    def __init__(self, name="", excl=False):
        self.name = name
        self.excl = excl
        self.writers = []
        self.readers = []
        self.gdeps = []


class Op:
    __slots__ = ("eng", "fn", "deps", "dma", "key", "ticket", "sig", "odeps", "dur", "idx")

    def __init__(self, eng, fn, dma, key):
        self.eng = eng
        self.fn = fn
        self.deps = []
        self.odeps = []
        self.dur = 0.3
        self.idx = 0
        self.dma = dma
        self.key = key
        self.ticket = None
        self.sig = False


class Sched:
    def __init__(self, nc):
        self.nc = nc
        self.ops = []
        self.q = {e: [] for e in ENGS}
        self.all_dma = []

    def add(self, eng, fn, reads=(), writes=(), joins=(), dma=False, key=None, dur=None):
        op = Op(eng, fn, dma, key)
        op.dur = dur if dur is not None else (2.5 if dma else 0.3)
        deps = []
        for h in reads:
            deps.extend(h.writers)
            if h.excl:
                deps.extend(r for r in h.readers if r.eng != eng)
        for h in writes:
            deps.extend(h.writers)
            deps.extend(h.readers)
        for h in joins:
            deps.extend(h.gdeps)
        for h in reads:
            h.readers.append(op)
        for h in writes:
            h.gdeps = list(h.writers) + list(h.readers)
            h.writers = [op]
            h.readers = []
        for h in joins:
            if h.excl and h.writers:
                op.odeps.append(h.writers[-1])
            h.writers.append(op)
        seen = set()
        for d in deps:
            if d is op or id(d) in seen:
                continue
            seen.add(id(d))
            op.odeps.append(d)
            if (not d.dma) and d.eng == eng and not dma:
                if eng == "pe" or not SAME_ENG_SYNC:
                    continue
            op.deps.append(d)
            d.sig = True
        if dma:
            assert key is not None
            self.all_dma.append(op)
            lk = self.q.setdefault("lastkey", {})
            if id(key) in lk:
                op.odeps.append(lk[id(key)])
            lk[id(key)] = op
            if eng == "pool":
                hist = self.q["pool_dma_hist"] if "pool_dma_hist" in self.q else self.q.setdefault("pool_dma_hist", [])
                if len(hist) >= SWDGE_DEPTH and hist[-SWDGE_DEPTH] not in op.deps and id(hist[-SWDGE_DEPTH].key) not in self.q.setdefault("group_keys", set()):
                    op.deps.append(hist[-SWDGE_DEPTH])
                    op.odeps.append(hist[-SWDGE_DEPTH])
                    hist[-SWDGE_DEPTH].sig = True
                if hist:
                    op.odeps.append(hist[-1])
                hist.append(op)
        self.ops.append(op)
        self.q[eng].append(op)
        return op

    def barrier(self):
        lasts = []
        for e in ENGS:
            for o in reversed(self.q[e]):
                if o.fn is not None and not o.dma:
                    lasts.append(o)
                    break
        dm = list(self.all_dma)
        for e in ENGS:
            op = Op(e, None, False, "barrier")
            for d in lasts + dm:
                if (not d.dma) and d.eng == e:
                    continue
                op.deps.append(d)
                d.sig = True
            self.ops.append(op)
            self.q[e].append(op)

    def finish(self, eng="sp"):
        op = Op(eng, None, False, None)
        for d in self.all_dma:
            op.deps.append(d)
            d.sig = True
        self.ops.append(op)
        self.q[eng].append(op)

    def reschedule(self):
        import heapq
        for i, op in enumerate(self.ops):
            op.idx = i
        segs, cur = [], []
        for op in self.ops:
            if op.fn is None:
                if cur:
                    segs.append(cur)
                    cur = []
                segs.append([op])
            else:
                cur.append(op)
        if cur:
            segs.append(cur)
        new_ops = []
        for seg in segs:
            if len(seg) == 1 and seg[0].fn is None:
                b = seg[0]
                if b.key == "barrier":
                    b.deps = []
                    for e2 in ENGS:
                        for o2 in reversed(new_ops):
                            if o2.eng == e2 and o2.fn is not None and not o2.dma:
                                if e2 != b.eng:
                                    b.deps.append(o2)
                                    o2.sig = True
                                break
                    for o2 in new_ops:
                        if o2.dma:
                            b.deps.append(o2)
                            o2.sig = True
                new_ops.append(b)
                continue
            inseg = {id(o) for o in seg}
            npred = {}
            succ = {}
            ready_t = {}
            for o in seg:
                ps = [d for d in o.odeps if id(d) in inseg]
                npred[id(o)] = len(ps)
                ready_t[id(o)] = 0.0
                for d in ps:
                    succ.setdefault(id(d), []).append(o)
            heaps = {e: [] for e in ENGS}
            for o in seg:
                if npred[id(o)] == 0:
                    heapq.heappush(heaps[o.eng], (0.0, o.idx, o))
            free = {e: 0.0 for e in ENGS}
            done = 0
            while done < len(seg):
                best = None
                for e in ENGS:
                    if heaps[e]:
                        rt, ix, o = heaps[e][0]
                        st = max(rt, free[e])
                        if best is None or (st, ix) < (best[0], best[1]):
                            best = (st, ix, e)
                st, ix, e = best
                rt, ix, o = heapq.heappop(heaps[e])
                issue = 0.05 if o.dma else o.dur
                free[e] = st + issue
                fin = st + o.dur + (0.25 if not o.dma else 0.0)
                new_ops.append(o)
                done += 1
                for sc in succ.get(id(o), ()):
                    ready_t[id(sc)] = max(ready_t[id(sc)], fin)
                    npred[id(sc)] -= 1
                    if npred[id(sc)] == 0:
                        heapq.heappush(heaps[sc.eng], (ready_t[id(sc)], sc.idx, sc))
        assert len(new_ops) == len(self.ops)
        self.ops = new_ops
        for e in ENGS:
            self.q[e] = [o for o in new_ops if o.eng == e]

    def emit(self):
        nc = self.nc
        if RESCHED:
            self.reschedule()
        cnt = {e: 0 for e in ENGS}
        dcnt = {}
        keys = []
        for op in self.ops:
            if op.dma:
                if op.key not in dcnt:
                    dcnt[op.key] = 0
                    keys.append(op.key)
                dcnt[op.key] += 16
                op.ticket = dcnt[op.key]
            elif op.sig:
                cnt[op.eng] += 1
                op.ticket = cnt[op.eng]
        with contextlib.ExitStack() as st:
            esem = {e: st.enter_context(nc.semaphore("s_" + e)) for e in ENGS if e != "sp"}
            dsem = {k: st.enter_context(nc.semaphore("d%d" % i)) for i, k in enumerate(keys)}
            block = st.enter_context(nc.Block())

            def run(engname):
                def body(e):
                    waited = {}
                    for op in self.q[engname]:
                        need = {}
                        for d in op.deps:
                            sem = dsem[d.key] if d.dma else esem[d.eng]
                            sid = id(sem)
                            if sid not in need or need[sid][1] < d.ticket:
                                need[sid] = (sem, d.ticket)
                        for sid, (sem, tk) in need.items():
                            if waited.get(sid, 0) >= tk:
                                continue
                            waited[sid] = tk
                            e.wait_ge(sem, tk)
                        if op.fn is None:
                            continue
                        ins = op.fn(e)
                        if op.dma:
                            ins.then_inc(dsem[op.key], 16)
                        elif op.sig:
                            ins.then_inc(esem[op.eng], 1)
                return body

            block.tensor(run("pe"))
            block.scalar(run("act"))
            block.vector(run("dve"))
            block.gpsimd(run("pool"))
            block.sync(run("sp"))
        return len(keys), cnt


D = 1024
KC = 8
SEQ = 4096
NSLOT = 33
TOK = NSLOT * 128
NOWN = 16
NT = 2048
NS = 16
NTS = NT + NS
DFF = 2816
NFC = 22
IN_DIM = 2856
C_Q, C_KV, C_GT, C_QG, C_KG, C_VG, C_LR, C_GG = 0, 512, 1280, 1304, 1560, 1816, 2328, 2344
EPS = 1e-6
BIG = 30000.0
LA = 4096
OFFA = 1936
LW = 768
OFFW = 128
LEXT = LA + LW
NPOOL = 2560
PAST = 8192
NPG = 64


def _bucket(n):
    n = np.asarray(n, np.int64)
    nf = np.maximum(n, 1).astype(np.float32)
    large = 16 + (np.log(nf / np.float32(16)) / np.float32(math.log(8.0)) * np.float32(16)).astype(np.int32)
    large = np.minimum(large, 31)
    return np.where(n < 16, n, large)


def _coef():
    c = np.zeros((33, LEXT), np.float32)
    m = np.arange(LA)
    n = m - OFFA
    for mi, ni in zip(m, n):
        if ni < 0:
            c[32, mi] = 1.0
        elif ni <= 112:
            c[_bucket(ni), mi] += 1.0
            c[31, mi] -= 1.0
    m = np.arange(LW)
    n = m - OFFW
    for mi, ni in zip(m, n):
        if ni < 0 or ni >= 512:
            c[32, LA + mi] = 1.0
        elif ni <= 112:
            c[_bucket(ni), LA + mi] += 1.0
            c[31, LA + mi] -= 1.0
    return c


def _static_consts():
    k = {}
    i = np.arange(128)
    k["cJ"] = (i[:, None] + i[None, :] == 127).astype(np.float32)
    k["cTri"] = (i[:, None] <= i[None, :]).astype(np.float32)
    k["cTriU"] = (i[:, None] > i[None, :]).astype(np.float32)
    k["cIdent"] = np.eye(128, dtype=np.float32)
    k["coef"] = _coef()
    E = np.zeros((128, TOK), np.float32)
    E[np.arange(TOK) // 64, np.arange(TOK)] = 1.0
    k["cE"] = E
    As = np.zeros((512, 128), np.float32)
    wts = {-1: 1.0, 0: 2.0, 1: 2.0, 2: 2.0, 3: 1.0}
    for j in range(128):
        for dd, w in wts.items():
            ii = 4 * j + dd
            if 0 <= ii <= 510:
                As[ii, j] = w
    k["cAs"] = As.reshape(4, 128, 128).transpose(1, 0, 2).copy()
    sel = np.zeros((4, 2, 128), np.float32)
    sel[:, 0, :] = 1.0
    sel[:, 0, 0] = 0.0
    sel[:, 0, 127] = 0.0
    sel[:, 1, 0] = 1e9
    sel[:, 1, 127] = 1e9
    k["cSels"] = sel
    blk = np.arange(128)
    k["cBm"] = (blk[:, None] // 4 == np.arange(64)[None, :] // 2).astype(np.float32)
    k["cHalf"] = ((blk[:, None] % 4) == (i[None, :] // 32)).astype(np.float32)
    t = np.arange(16)
    same = (t[:, None] // 4 == t[None, :] // 4)
    k["cTri16"] = (same & (t[:, None] <= t[None, :])).astype(np.float32)
    k["cTriU16"] = (same & (t[:, None] > t[None, :])).astype(np.float32)
    k["cSeqm"] = (t[:, None] // 4 == np.arange(4)[None, :]).astype(np.float32)
    vs_ = np.ones((128, 4), np.float32)
    vs_[127, 3] = 0.0
    k["cValS"] = vs_
    k["cPcol"] = (np.arange(128) % 64).astype(np.float32).reshape(128, 1)
    return k


def _core_consts(par):
    shift = 1 - par
    k = {}
    A = np.zeros((256, 128), np.float32)
    wts = {-1: 1.0, 0: 2.0, 1: 2.0, 2: 2.0, 3: 1.0}
    for bp in range(66):
        j = bp - 2 * shift
        if not (0 <= j <= 63):
            continue
        for dd, w in wts.items():
            ii = 4 * j + dd
            if 0 <= ii <= 254:
                c = ii + 8 * shift + 1
                if c < 256:
                    A[c, bp] = w
    k["cA"] = A.reshape(2, 128, 128).transpose(1, 0, 2).copy()
    sel = np.zeros((NOWN, 128, 2, 128), np.float32)
    sel[:, :, 1, :] = -2.0
    q = np.arange(128)
    for jo in range(NOWN):
        s = 2 * jo + 1
        pos = (s - shift) * 128 + q
        cur = pos // 64
        for bp in range(66):
            j = bp - 2 * shift
            if not (0 <= j <= 63):
                continue
            forced = (j == 0) | (j == cur) | (j == cur - 1)
            vis = (j * 64 <= pos)
            sel[jo, :, 0, bp] = np.where(vis & ~forced, 1.0, 0.0)
            sel[jo, :, 1, bp] = np.where(forced, 1e9, np.where(vis, 0.0, -1.0))
    k["cSel"] = sel
    valid = np.ones((128, NSLOT), np.float32)
    valid[:, 0 if shift == 1 else 32] = 0.0
    k["cValid"] = valid
    vc = np.zeros((256,), np.float32)
    for c in range(256):
        ii = c - 1 - 8 * shift
        vc[c] = 1.0 if 0 <= ii <= 254 else 0.0
    k["cValC"] = vc.reshape(2, 128).T.copy()
    return k


class Arena:
    def __init__(self, nc, base, end):
        self.nc, self.top, self.end = nc, base, end
        self.n = 0

    def alloc(self, shape, dt):
        sz = 4 if dt in (F32, I32) else 2
        nb = int(np.prod(shape[1:])) * sz
        nb = (nb + 63) // 64 * 64
        t = self.nc.alloc_sbuf_tensor_at("t%d" % self.n, list(shape), dt, offset=self.top)
        self.n += 1
        self.top += nb
        assert self.top <= self.end, ("SBUF overflow", self.top, self.end)
        return t


def build_program(dbg=False, nslots=NSLOT, cut=99, do_sample=True, nseq=4):
    nc = bass.Bass("TRN2", target_bir_lowering=False)
    S = Sched(nc)
    IN = {}
    OUT = {}

    def din(name, shape, dt=F32):
        IN[name] = (shape, dt)
        return nc.dram_tensor(name, list(shape), dt, kind="ExternalInput").ap()

    def dout(name, shape, dt=F32):
        OUT[name] = (shape, dt)
        return nc.dram_tensor(name, list(shape), dt, kind="ExternalOutput").ap()

    xT = din("xT", [128, KC, TOK])
    pT = din("pT", [128, 2, NTS])
    xsT = din("xsT", [128, KC, NS])
    w_in = din("w_in", [128, KC, IN_DIM])
    w_o_nsa = din("w_o_nsa", [64, 8, D])
    w_o_gla = din("w_o_gla", [128, 4, D])
    w_gate = din("w_gate", [128, KC, DFF])
    w_up = din("w_up", [128, KC, DFF])
    w_down = din("w_down", [128, NFC, D])
    w_ple = din("w_ple", [128, 2, D])
    w_pg = din("w_pg", [128, KC, D])
    norms = din("norms", [128, 4, KC])
    w1bd = din("w1bd", [128, 2, 32, 128])
    pecol = din("pecol", [128, 2, 32])
    w2bd = din("w2bd", [128, 2, 128])
    wgk = din("wgk", [33, 256])
    glan = din("glan", [128, 1])
    rb33 = din("rb33", [33, 8])
    coef = din("coef", [33, LEXT])
    cJ = din("cJ", [128, 128])
    cTri = din("cTri", [128, 128])
    cTriU = din("cTriU", [128, 128])
    cIdent = din("cIdent", [128, 128])
    cE = din("cE", [128, TOK])
    cA = din("cA", [128, 2, 128])
    cSel = din("cSel", [NOWN, 128, 2, 128])
    cValid = din("cValid", [128, NSLOT])
    cValC = din("cValC", [128, 2])
    cAs = din("cAs", [128, 4, 128])
    cSels = din("cSels", [4, 2, 128])
    cBm = din("cBm", [128, 64])
    cHalf = din("cHalf", [128, 128])
    cTri16 = din("cTri16", [16, 16])
    cTriU16 = din("cTriU16", [16, 16])
    cSeqm = din("cSeqm", [16, 4])
    cValS = din("cValS", [128, 4])
    cPcol = din("cPcol", [128, 1])
    cache = din("cache", [NPOOL, 128, 512])
    ptab = din("ptab", [1, 4 * NPG], I32)
    cwin = din("cwin", [4, 512, 256])
    sgla = din("sgla", [4, 4, 64, 128])

    yT = dout("yT", [128, KC, NTS])
    kvT_p = dout("kvT_p", [128, 4, NT])
    vtok_p = dout("vtok_p", [NT, 2, 128])
    gla_p = dout("gla_p", [128, 2, 128])
    kvT_s = dout("kvT_s", [128, 4, NS])
    vtok_s = dout("vtok_s", [NS, 2, 128])
    gla_s = dout("gla_s", [4, 128, 2, 128])
    win_s = dout("win_s", [4, 508, 256])
    DBG = {}

    ext_bf = nc.dram_tensor("ext_bf", [8, LEXT], BF16)
    h1_scr = nc.dram_tensor("h1_scr", [128, KC, NTS], F32).ap()
    H_ext = H("ext")
    gscr = [nc.dram_tensor("gscr%d" % i, [24, 128], F32).ap() for i in range(2)]
    gscr_s = nc.dram_tensor("gscr_s", [24, 16], F32).ap()
    H_gscr_s = H("gscr_s")
    H_gscr = [H("gscr0"), H("gscr1")]
    H_h1 = [H("h1s%d" % i) for i in range(NOWN + 1)]

    st = contextlib.ExitStack()
    arena = st.enter_context(nc.sbuf_tensor("arena", [128, 208000], mybir.dt.uint8))
    ABASE = 16512
    AEND = ABASE + 208000
    AR = Arena(nc, ABASE, AEND)

    PB = [st.enter_context(nc.psum_tensor("pb%d" % i, [128, 512], F32)) for i in range(7)]
    PBH = [H("pb%d" % i, excl=True) for i in range(7)]
    PT = st.enter_context(nc.psum_tensor("pbt", [128, 1024], BF16))
    H_PT = H("pbt", excl=True)
    rr_state = [0]

    def rr():
        i = 2 + (rr_state[0] % 5)
        rr_state[0] += 1
        return PB[i], PBH[i]

    def _fsz(ap):
        n = 1
        for d in ap.shape[1:]:
            n *= d
        return n

    def MM(out, lhsT, rhs, start=True, stop=True, R=(), W=(), J=()):
        S.add("pe", lambda e: e.matmul(out, lhsT=lhsT, rhs=rhs, start=start, stop=stop), reads=R, writes=W, joins=J,
              dur=max(_fsz(rhs), 128) / 2400.0 + 0.01)

    def TR(out, in_, ident, R=(), W=(), J=()):
        S.add("pe", lambda e: e.transpose(out=out, in_=in_, identity=ident), reads=R, writes=W, joins=J)

    def ACT(out, in_, func, bias=0.0, scale=1.0, R=(), W=(), J=()):
        S.add("act", lambda e: e.activation(out=out, in_=in_, func=func, bias=bias, scale=scale), reads=R, writes=W, joins=J,
              dur=_fsz(out) / 960.0 + 0.2)

    def CP(eng, out, in_, R=(), W=(), J=()):
        if eng == "act":
            S.add("act", lambda e: e.copy(out=out, in_=in_), reads=R, writes=W, joins=J)
        else:
            S.add(eng, lambda e: e.tensor_copy(out=out, in_=in_), reads=R, writes=W, joins=J)

    def TT(eng, out, in0, in1, op, R=(), W=(), J=()):
        S.add(eng, lambda e: e.tensor_tensor(out=out, in0=in0, in1=in1, op=op), reads=R, writes=W, joins=J, dur=_fsz(out) / 960.0 + 0.15)

    def TS(eng, out, in0, s1, s2, op0, op1=None, R=(), W=(), J=()):
        if op1 is None:
            S.add(eng, lambda e: e.tensor_scalar(out=out, in0=in0, scalar1=s1, scalar2=None, op0=op0), reads=R, writes=W, joins=J)
        else:
            S.add(eng, lambda e: e.tensor_scalar(out=out, in0=in0, scalar1=s1, scalar2=s2, op0=op0, op1=op1), reads=R, writes=W, joins=J)

    def STT(eng, out, in0, scalar, in1, op0, op1, R=(), W=(), J=()):
        S.add(eng, lambda e: e.scalar_tensor_tensor(out=out, in0=in0, scalar=scalar, in1=in1, op0=op0, op1=op1), reads=R, writes=W, joins=J)

    def MS(eng, ap, val, W=(), J=()):
        S.add(eng, lambda e: e.memset(ap, val), writes=W, joins=J)

    def RCP(eng, out, in_, R=(), W=(), J=()):
        S.add(eng, lambda e: e.reciprocal(out=out, in_=in_), reads=R, writes=W, joins=J, dur=_fsz(out) / 160.0 + 0.15)

    def DMA(eng, out, in_, key, R=(), W=(), J=(), **kw):
        S.add(eng, lambda e: e.dma_start(out=out, in_=in_, **kw), reads=R, writes=W, joins=J, dma=True, key=key)

    grp_started = set()
    cur_grp = [H("G0")]

    grp_qkeys = {}

    def gdma(eng, out, in_, grp, R=()):
        kobj = grp_qkeys.setdefault((id(grp), eng), H(grp.name + "_" + eng))
        S.q.setdefault("group_keys", set()).add(id(kobj))
        if id(grp) in grp_started:
            DMA(eng, out, in_, key=kobj, R=R, J=[grp])
        else:
            grp_started.add(id(grp))
            DMA(eng, out, in_, key=kobj, R=R, W=[grp])

    def load(shape, dt, src, eng=None, name=None, own=False):
        t = AR.alloc(shape, dt)
        if eng is None:
            eng = "pool" if dt == BF16 else "sp"
        if own or os.environ.get("OWNKEYS") == "1":
            h = H(name or "c")
            DMA(eng, t[:], src, key=h, W=[h])
            return t, h
        gdma(eng, t[:], src, cur_grp[0])
        return t, cur_grp[0]

    J_bf, hJ = load([128, 128], BF16, cJ)
    ident_bf, hId = load([128, 128], BF16, cIdent)
    tri_f, hTri = load([128, 128], F32, cTri)
    triu_f, hTriU = load([128, 128], F32, cTriU)
    norms_f, hNorms = load([128, 4, KC], F32, norms)
    glan_f, hGlan = load([128, 1], F32, glan)
    ones_bf = AR.alloc([128, 128], BF16); hOnes = H("ones")
    MS("dve", ones_bf[:], 1.0, W=[hOnes])
    ones_f = AR.alloc([128, 128], F32); hOnesF = H("onesf")
    MS("dve", ones_f[:], 1.0, W=[hOnesF])
    PERS_END = AR.top

    win_bf = AR.alloc([128, KC, 2880], BF16); hWin = H("win")
    first = True
    for kc in range(KC):
        for (a, b) in ((0, 1428), (1428, 2856)):
            DMA("pool", win_bf[:, kc, a:b], w_in[:, kc, a:b], key=hWin, W=[hWin] if first else (), J=() if first else [hWin])
            first = False
    wonsa_bf = AR.alloc([64, 8, D], BF16); hWon = H("wonsa")
    for hh in range(8):
        DMA("pool", wonsa_bf[:, hh, :], w_o_nsa[:, hh, :], key=hWon, W=[hWon] if hh == 0 else (), J=() if hh == 0 else [hWon])
    wogla_bf = AR.alloc([128, 4, D], BF16); hWog = H("wogla")
    for hh in range(4):
        DMA("pool", wogla_bf[:, hh, :], w_o_gla[:, hh, :], key=hWog, W=[hWog] if hh == 0 else (), J=() if hh == 0 else [hWog])
    w1_bf = AR.alloc([128, 2, 32, 128], BF16); hW1 = H("w1")
    for t in range(2):
        for s4 in range(2):
            DMA("pool", w1_bf[:, t, 16 * s4:16 * s4 + 16, :], w1bd[:, t, 16 * s4:16 * s4 + 16, :], key=hW1,
                W=[hW1] if (t == 0 and s4 == 0) else (), J=() if (t == 0 and s4 == 0) else [hW1])
    w2_bf, hW2 = load([128, 2, 128], BF16, w2bd)
    pe_bf, hPe = load([128, 2, 32], BF16, pecol, own=True)
    wgk_bf, hWgk = load([33, 256], BF16, wgk)
    As_bf, hAs = load([128, 4, 128], BF16, cAs)
    valid_f, hValid = load([128, NSLOT], F32, cValid)
    valc_f, hValC = load([128, 2], F32, cValC)
    bm_f, hBm = load([128, 64], F32, cBm)
    half_bf, hHalf = load([128, 128], BF16, cHalf)
    sels_f, hSels = load([4, 2, 128], F32, cSels)
    tri16_f, hTri16 = load([16, 16], F32, cTri16)
    triu16_f, hTriU16 = load([16, 16], F32, cTriU16)
    seqm_f, hSeqm = load([16, 4], F32, cSeqm)

    rb_f, hRb = load([33, 8], F32, rb33, own=True)
    _save_top = AR.top
    AR.top = AEND - 32768
    coef_f = AR.alloc([33, LEXT], F32); hCoef = H("coef")
    DMA("sp", coef_f[:], coef, key=hCoef, W=[hCoef])
    ext_sb = AR.alloc([8, LEXT], BF16); hExtSb = H("extsb")
    AR.top = _save_top
    nch = (LEXT + 511) // 512
    for i in range(nch):
        a = i * 512
        b = min(LEXT, a + 512)
        pb, ph = rr()
        MM(pb[0:8, 0:b - a], rb_f[:], coef_f[:, a:b], R=[hRb, hCoef], W=[ph])
        CP("act", ext_sb[:, a:b], pb[0:8, 0:b - a], R=[ph], W=[hExtSb] if i == 0 else (), )
        if i > 0:
            S.ops[-1]
    hExtSb.writers = [op for op in S.q["act"][-nch:]]
    DMA("sp", ext_bf.ap(), ext_sb[:], key=hExtSb, R=[hExtSb], W=[H_ext])

    def hankel(dst, g, base, pstride, nq, h, npart=128):
        src = bass.AP(ext_bf, 4 * g * LEXT + base, [[pstride, npart], [LEXT, 4], [1, nq]])
        if h in (GHS, GHSn, GHA):
            gdma("sp", dst, src, h, R=[H_ext])
        else:
            DMA("sp", dst, src, key=h, R=[H_ext], W=[h])

    GHS, GHSn, GHA = H("GHS"), H("GHSn"), H("GHA")

    HKS = AR.alloc([128, 4, 2, 16], BF16)
    hHKS = [[GHS for g in range(2)] for i in range(4)]
    MS("dve", HKS[:], 0.0, W=[GHS])
    HKC = AR.alloc([128, 2, 64], BF16)
    HKW = AR.alloc([128, 2, 64], BF16)
    HKS2 = AR.alloc([128, 2, 2, 16], BF16)
    MS("dve", HKC[:], 0.0, J=[GHS])
    MS("dve", HKW[:], 0.0, J=[GHS])
    for g in range(2):
        hankel(HKS[:, 0, g, :].rearrange("p (r q) -> p r q", r=4), g, OFFA - 15, 16, 4, hHKS[0][g])
        hankel(HKS[:, 1, g, :].rearrange("p (r q) -> p r q", r=4), g, OFFA + 1, 1, 4, hHKS[1][g])
        hankel(HKS[0:4, 2, g, :].rearrange("p (r q) -> p r q", r=4), g, OFFA - 3, 1, 4, hHKS[2][g], npart=4)
        hankel(HKS[:, 3, g, :].rearrange("p (r q) -> p r q", r=4), g, LA + OFFW + 385, 1, 4, hHKS[3][g])

    for g in range(2):
        for j in range(2):
            hankel(HKS2[:, j, g, :].rearrange("p (r q) -> p r q", r=4), g, OFFA + 2 - j, 2, 4, GHS)
        hankel(HKC[:, g, 48:64].rearrange("p (r q) -> p r q", r=4), g, OFFA - 15, 16, 4, GHS)
        hankel(HKW[:, g, 0:16].rearrange("p (r q) -> p r q", r=4), g, LA + OFFW + 385, 1, 4, GHS)
        hankel(HKW[:, g, 48:64].rearrange("p (r q) -> p r q", r=4), g, OFFA + 1, 1, 4, GHS)

    pew1 = AR.alloc([128, 2], F32); hPew1 = H("pew1")
    pb, ph = rr()
    for t in range(2):
        for sx in range(32):
            MM(pb[:, t * 8:t * 8 + 8], w1_bf[:, t, sx, :], pe_bf[:, t, sx:sx + 1].to_broadcast([128, 8]), start=(sx == 0), stop=(sx == 31),
               R=[hW1, hPe], **(dict(W=[ph]) if (t == 0 and sx == 0) else dict(J=[ph])))
    CP("dve", pew1[:], pb[:, 0:16:8], R=[ph], W=[hPew1])

    S.barrier()
    MIX_END = AR.top

    def WJ(h, first):
        return dict(W=[h]) if first else dict(J=[h])

    def phase_s():
        cur_grp[0] = H("GS")
        identf, hIdF = load([128, 128], F32, cIdent, name="identf")
        valS_f, hValS = load([128, 4], F32, cValS)
        xs_s = AR.alloc([128, KC, NS], F32); hxs_s = H("xs_s")
        DMA("sp", xs_s[:], xsT, key=hxs_s, W=[hxs_s])
        sq_s = AR.alloc([128, KC, NS], BF16); hsq_s = H("sq_s")
        rstd_s = AR.alloc([128, NS], F32); hrstd_s = H("rstd_s")
        xn_s = AR.alloc([128, KC, NS], BF16); hxn_s = H("xn_s")
        ACT(sq_s[:], xs_s[:], AF.Square, R=[hxs_s], W=[hsq_s])
        pb, ph = rr()
        for kc in range(KC):
            MM(pb[:, 0:NS], ones_bf[:], sq_s[:, kc, :], start=(kc == 0), stop=(kc == KC - 1), R=[hOnes, hsq_s], **WJ(ph, kc == 0))
        ACT(rstd_s[:], pb[:, 0:NS], AF.Ln, bias=EPS, scale=1.0 / D, R=[ph], W=[hrstd_s])
        ACT(rstd_s[:], rstd_s[:], AF.Exp, scale=-0.5, R=[hrstd_s], W=[hrstd_s])
        for kc in range(KC):
            STT("dve", xn_s[:, kc, :], xs_s[:, kc, :], norms_f[:, 0, kc:kc + 1], rstd_s[:], ALU.mult, ALU.mult,
                R=[hxs_s, hrstd_s, hNorms], **WJ(hxn_s, kc == 0))

        def pF(out, c0, M, fw):
            for kc in range(KC):
                MM(out, win_bf[:, kc, c0:c0 + M], xn_s[:, kc, :], start=(kc == 0), stop=(kc == KC - 1), R=[hWin, hxn_s], **fw(kc))

        def pTm(out, c0, N, fw):
            for kc in range(KC):
                MM(out, xn_s[:, kc, :], win_bf[:, kc, c0:c0 + N], start=(kc == 0), stop=(kc == KC - 1), R=[hWin, hxn_s], **fw(kc))

        if cut == 99.1:
            return
        QTs = AR.alloc([128, 64], BF16); hQTs = H("QTs")
        pq, phq = rr()
        for r in range(4):
            pF(pq[:, r * 16:(r + 1) * 16], r * 128, 128, lambda kc, r=r: WJ(phq, r == 0 and kc == 0))
        TS("dve", QTs[:].rearrange("p (b r q) -> p r b q", b=4, r=4), pq[:, 0:64].rearrange("p (r b q) -> p r b q", r=4, b=4),
           0.125, None, ALU.mult, R=[phq], W=[hQTs])
        kvo_s = AR.alloc([128, 4, NS], F32); hkvo_s = H("kvo_s")
        ksn = AR.alloc([128, NS], BF16); hksn = H("ksn")
        kwn = AR.alloc([128, NS], BF16); hkwn = H("kwn")
        pk, phk = rr()
        for i, c0 in enumerate((512, 640, 768, 1024)):
            pF(pk[:, i * 16:(i + 1) * 16], c0, 128, lambda kc, i=i: WJ(phk, i == 0 and kc == 0))
        CP("act", kvo_s[:].rearrange("p a b -> p (a b)"), pk[:, 0:64], R=[phk], W=[hkvo_s])
        CP("act", ksn[:], pk[:, 32:48], R=[phk], W=[hksn])
        CP("act", kwn[:], pk[:, 48:64], R=[phk], W=[hkwn])
        DMA("sp", kvT_s, kvo_s[:], key=hkvo_s, R=[hkvo_s])
        vto_s = AR.alloc([NS, 2, 128], F32); hvto_s = H("vto_s")
        Vsn = AR.alloc([NS, 2, 65], BF16); hVsn = H("Vsn")
        Vwn = AR.alloc([NS, 2, 65], BF16); hVwn = H("Vwn")
        Vsn_m = AR.alloc([NS, 4, 2, 65], BF16); hVsn_m = H("Vsn_m")
        Vwn_m = AR.alloc([NS, 4, 2, 65], BF16); hVwn_m = H("Vwn_m")
        pv, phv = rr()
        pTm(pv[0:NS, 0:128], 896, 128, lambda kc: WJ(phv, kc == 0))
        pTm(pv[0:NS, 128:256], 1152, 128, lambda kc: dict(J=[phv]))
        CP("act", vto_s[:].rearrange("p a b -> p (a b)"), pv[0:NS, 0:256], R=[phv], W=[hvto_s])
        DMA("sp", vtok_s, vto_s[:], key=hvto_s, R=[hvto_s])
        MS("pool", Vsn[:], 1.0, W=[hVsn])
        MS("pool", Vwn[:], 1.0, W=[hVwn])
        CP("act", Vsn[:, :, 0:64], pv[0:NS, 0:128].rearrange("p (g d) -> p g d", g=2), R=[phv], W=[hVsn])
        CP("act", Vwn[:, :, 0:64], pv[0:NS, 128:256].rearrange("p (g d) -> p g d", g=2), R=[phv], W=[hVwn])
        for bl in range(4):
            TS("dve", Vsn_m[:, bl, :, :], Vsn[:], seqm_f[:, bl:bl + 1], None, ALU.mult, R=[hVsn, hSeqm], **WJ(hVsn_m, bl == 0))
            TS("dve", Vwn_m[:, bl, :, :], Vwn[:], seqm_f[:, bl:bl + 1], None, ALU.mult, R=[hVwn, hSeqm], **WJ(hVwn_m, bl == 0))
        if cut == 99.2:
            return
        gate_s = AR.alloc([24, NS], F32); hgate_s = H("gate_s")
        grow_s = AR.alloc([128, 24 * NS], F32); hgrow_s = H("grow_s")
        pgt, phgt = rr()
        pF(pgt[0:24, 0:NS], C_GT, 24, lambda kc: WJ(phgt, kc == 0))
        ACT(gate_s[:], pgt[0:24, 0:NS], AF.Exp, scale=-1.0, R=[phgt], W=[hgate_s])
        TS("dve", gate_s[:], gate_s[:], 1.0, None, ALU.add, R=[hgate_s], W=[hgate_s])
        RCP("dve", gate_s[:], gate_s[:], R=[hgate_s], W=[hgate_s])
        DMA("sp", gscr_s, gate_s[:], key=hgate_s, R=[hgate_s], W=[H_gscr_s])
        DMA("sp", grow_s[64:65, :], gscr_s.rearrange("(o a) b -> o (a b)", o=1), key=hgrow_s, R=[H_gscr_s], W=[hgrow_s])
        if cut == 99.3:
            return
        HKSn = AR.alloc([NS, 4, 2, 16], BF16)
        hHKSn = [[GHSn for g in range(2)] for bl in range(4)]
        for bl in range(4):
            for g in range(2):
                hankel(HKSn[0:NS, bl, g, :].rearrange("p (r q) -> p r q", r=4), g, OFFA - 15 + 4 * bl, 1, 4, hHKSn[bl][g], npart=NS)
        if cut == 99.4:
            return
        lrT_s = AR.alloc([64, NS], BF16); hLr_s = H("lrT_s")
        MS("pool", lrT_s[:], 0.0, W=[hLr_s])
        MS("pool", lrT_s[32:33, :], 1.0, W=[hLr_s])
        la_s = AR.alloc([NS, 256], F32); hla_s = H("la_s")
        esuf_s = AR.alloc([NS, 256], F32); hesuf_s = H("esuf_s")
        ecum_s = AR.alloc([128, 2 * NS], F32); hecum_s = H("ecum_s")
        einv_s = AR.alloc([128, 2 * NS], F32); heinv_s = H("einv_s")
        keT_s = AR.alloc([128, 2 * NS], BF16); hke_s = H("keT_s")
        qeT_s = AR.alloc([128, 2 * NS], BF16); hqe_s = H("qeT_s")
        kd_s = AR.alloc([NS, 256], BF16); hkd_s = H("kd_s")
        KDM_off = [AR.top]
        kdm = AR.alloc([NS, 4, 256], BF16); hkdm = H("kdm")
        vg_s = AR.alloc([NS, 512], BF16); hvg_s = H("vg_s")
        attT_s = AR.alloc([NS, NS], BF16); hatt_s = H("attT_s")
        S0_off = [AR.top]
        S0 = AR.alloc([128, 4, 2, 128], F32); hS0 = H("S0")
        S0bf = AR.alloc([128, 4, 2, 128], BF16); hS0bf = H("S0bf")
        omixg_s = AR.alloc([128, 4, NS], BF16); homixg_s = H("omixg_s")
        osq_s = AR.alloc([128, NS], BF16); hosq_s = H("osq_s")
        grs_s = AR.alloc([128, NS], F32); hgrs_s = H("grs_s")
        sg_s = AR.alloc([128, NS], F32); hsg_s = H("sg_s")
        t1_s = AR.alloc([128, NS], F32); ht1_s = H("t1_s")
        first = True
        for bl in range(4):
            for hd in range(4):
                DMA("sp", S0[(hd % 2) * 64:(hd % 2) * 64 + 64, bl, hd // 2, :], sgla[bl, hd], key=hS0, **WJ(hS0, first))
                first = False
        CP("pool", S0bf[:], S0[:], R=[hS0], W=[hS0bf])
        pb, ph = rr()
        pF(pb[0:16, 0:NS], C_LR, 16, lambda kc: WJ(ph, kc == 0))
        CP("act", lrT_s[0:16, :], pb[0:16, 0:NS], R=[ph], W=[hLr_s])
        pz, phz = rr()
        MM(pz[0:NS, 0:256], lrT_s[0:33, :], wgk_bf[:], R=[hLr_s, hWgk], W=[phz])
        ACT(la_s[:], pz[0:NS, 0:256], AF.Exp, scale=-1.0, R=[phz], W=[hla_s])
        ACT(la_s[:], la_s[:], AF.Ln, bias=1.0, R=[hla_s], W=[hla_s])
        psf, phs = rr()
        MM(psf[0:NS, 0:256], triu16_f[:], la_s[:], R=[hTriU16, hla_s], W=[phs])
        ACT(esuf_s[:], psf[0:NS, 0:256], AF.Exp, scale=-1.0 / 16, R=[phs], W=[hesuf_s])
        pct, phc = rr()
        for ch in range(2):
            MM(pct[:, ch * NS:(ch + 1) * NS], la_s[:, ch * 128:(ch + 1) * 128], tri16_f[:], R=[hla_s, hTri16], **WJ(phc, ch == 0))
        ACT(ecum_s[:], pct[:, 0:2 * NS], AF.Exp, scale=-1.0 / 16, R=[phc], W=[hecum_s])
        ACT(einv_s[:], pct[:, 0:2 * NS], AF.Exp, scale=1.0 / 16, R=[phc], W=[heinv_s])
        pk, phk = rr()
        for ch in range(2):
            pF(pk[:, ch * NS:(ch + 1) * NS], C_KG + ch * 128, 128, lambda kc, ch=ch: WJ(phk, ch == 0 and kc == 0))
        TT("dve", keT_s[:], pk[:, 0:2 * NS], einv_s[:], ALU.mult, R=[phk, heinv_s], W=[hke_s])
        pq, phq = rr()
        for ch in range(2):
            pF(pq[:, ch * NS:(ch + 1) * NS], C_QG + ch * 128, 128, lambda kc, ch=ch: WJ(phq, ch == 0 and kc == 0))
        STT("dve", qeT_s[:], pq[:, 0:2 * NS], 0.125, ecum_s[:], ALU.mult, ALU.mult, R=[phq, hecum_s], W=[hqe_s])
        pkt, phkt = rr()
        pTm(pkt[0:NS, 0:256], C_KG, 256, lambda kc: WJ(phkt, kc == 0))
        TT("dve", kd_s[:], pkt[0:NS, 0:256], esuf_s[:], ALU.mult, R=[phkt, hesuf_s], W=[hkd_s])
        for bl in range(4):
            TS("dve", kdm[:, bl, :], kd_s[:], seqm_f[:, bl:bl + 1], None, ALU.mult, R=[hkd_s, hSeqm], **WJ(hkdm, bl == 0))
        pv2, phv2 = rr()
        pTm(pv2[0:NS, 0:512], C_VG, 512, lambda kc: WJ(phv2, kc == 0))
        CP("act", vg_s[:], pv2[0:NS, 0:512], R=[phv2], W=[hvg_s])
        if cut == 99.5:
            return
        for hd in range(4):
            ch = hd // 2
            pp = slice(64 * (hd % 2), 64 * (hd % 2) + 64)
            cs = slice(ch * NS, (ch + 1) * NS)
            pa, pha = rr()
            MM(pa[0:NS, 0:NS], keT_s[pp, cs], qeT_s[pp, cs], R=[hke_s, hqe_s], W=[pha])
            TT("dve", attT_s[:], pa[0:NS, 0:NS], tri16_f[:], ALU.mult, R=[pha, hTri16], W=[hatt_s])
            po, pho = rr()
            MM(po[:, 0:NS], vg_s[:, hd * 128:(hd + 1) * 128], attT_s[:], start=True, stop=False, R=[hvg_s, hatt_s], W=[pho])
            for bl in range(4):
                MM(po[:, bl * 4:bl * 4 + 4], S0bf[pp, bl, ch, :], qeT_s[pp, ch * NS + bl * 4:ch * NS + bl * 4 + 4],
                   start=False, stop=(bl == 3), R=[hS0bf, hqe_s], J=[pho])
            ACT(osq_s[:], po[:, 0:NS], AF.Square, R=[pho], W=[hosq_s])
            pn, phn = rr()
            MM(pn[:, 0:NS], ones_bf[:], osq_s[:], R=[hOnes, hosq_s], W=[phn])
            ACT(grs_s[:], pn[:, 0:NS], AF.Ln, bias=EPS, scale=1.0 / 128, R=[phn], W=[hgrs_s])
            ACT(grs_s[:], grs_s[:], AF.Exp, scale=-0.5, R=[hgrs_s], W=[hgrs_s])
            STT("dve", t1_s[:], po[:, 0:NS], glan_f[:, 0:1], grs_s[:], ALU.mult, ALU.mult, R=[pho, hGlan, hgrs_s], W=[ht1_s])
            pg, phg = rr()
            pF(pg[:, 0:NS], C_GG + hd * 128, 128, lambda kc: WJ(phg, kc == 0))
            ACT(sg_s[:], pg[:, 0:NS], AF.Exp, scale=-1.0, R=[phg], W=[hsg_s])
            TS("dve", sg_s[:], sg_s[:], 1.0, None, ALU.add, R=[hsg_s], W=[hsg_s])
            RCP("dve", sg_s[:], sg_s[:], R=[hsg_s], W=[hsg_s])
            TT("dve", sg_s[:], sg_s[:], pg[:, 0:NS], ALU.mult, R=[hsg_s, phg], W=[hsg_s])
            TT("dve", omixg_s[:, hd, :], t1_s[:], sg_s[:], ALU.mult, R=[ht1_s, hsg_s], **WJ(homixg_s, hd == 0))
        if cut == 99.6:
            return
        for bl in range(4):
            for ch in range(2):
                pu, phu = rr()
                MM(pu[:, 0:256], kdm[:, bl, ch * 128:(ch + 1) * 128], vg_s[:, ch * 256:(ch + 1) * 256], R=[hkdm, hvg_s], W=[phu])
                for hh in range(2):
                    pp = slice(64 * hh, 64 * hh + 64)
                    col = ch * NS + bl * 4 + 3
                    STT("dve", S0[pp, bl, ch, :], S0[pp, bl, ch, :], ecum_s[pp, col:col + 1], pu[pp, hh * 128:(hh + 1) * 128],
                        ALU.mult, ALU.add, R=[hS0, hecum_s, phu], W=[hS0])
        for bl in range(4):
            DMA("sp", gla_s[bl], S0[:, bl, :, :], key=hS0, R=[hS0])
        if cut == 99.7:
            return
        DMA("sp", win_s, cwin[:, 4:512, :], key=H("wincp"))

        if cut == 100:
            return
        KsT_s = AR.alloc([128, PAST], BF16); hKsT_s = H("KsT_s")
        Vs_s = AR.alloc([128, NPG, 2, 65], BF16); hVs_s = H("Vs_s")
        rawT_s = AR.alloc([128, 2, PAST], BF16); hraw_s = H("rawT_s")
        stg = [AR.alloc([128, 2, 512], F32)]; hstg = [H("stg0")]
        _s0base = KDM_off[0]
        assert S0_off[0] + 4096 + 2048 - _s0base >= 2 * 4096
        for i in range(2):
            stg.append(nc.alloc_sbuf_tensor_at("stgx%d" % i, [128, 2, 512], F32, offset=_s0base + i * 4096))
            hx_ = H("stgx%d" % i)
            for hd_ in (hkdm, hvg_s, hatt_s, hS0, hS0bf):
                hx_.writers += list(hd_.writers)
                hx_.readers += list(hd_.readers)
            hstg.append(hx_)
        NSTG = len(stg)
        KcT_s = AR.alloc([128, 512], BF16); hKcT_s = H("KcT_s")
        geV_s = AR.alloc([128, 512], BF16); hgeV_s = H("geV_s")
        ge_s = AR.alloc([128, 512], BF16); hge_s = H("ge_s")
        Vc_s = AR.alloc([128, 4, 2, 65], BF16); hVc_s = H("Vc_s")
        gx_s = AR.alloc([128, 512], F32); hgx_s = H("gx_s")
        gu_s = AR.alloc([128, 512], F32); hgu_s = H("gu_s")
        Pc_s = AR.alloc([128, 64], BF16); hPc_s = H("Pc_s")
        pn_s = AR.alloc([128, 64], F32); hpn_s = H("pn_s")
        PsT_s = AR.alloc([128, 4, 4], BF16); hPsT_s = H("PsT_s")
        P_s = AR.alloc([128, 1024], BF16); hP_s = H("P_s")
        R_s = AR.alloc([128, 1024], BF16); hR_s = H("R_s")
        KwT_s = AR.alloc([128, 512], BF16); hKwT_s = H("KwT_s")
        Vw_s = AR.alloc([128, 4, 2, 65], BF16); hVw_s = H("Vw_s")
        wstg = [AR.alloc([128, 256], F32) for _ in range(2)]; hwstg = [H("wstg0"), H("wstg1")]
        Pw_s = AR.alloc([128, 64], BF16); hPw_s = H("Pw_s")
        Pn16 = AR.alloc([NS, NS], BF16); hPn16 = H("Pn16")
        crow_s = AR.alloc([128, NS], F32); hcrow_s = H("crow_s")
        num_s = AR.alloc([64, NS], F32); hnum_s = H("num_s")
        onsa_s = AR.alloc([64, 2, NS], F32); honsa_s = H("onsa_s")
        onsab_s = AR.alloc([64, 2, 4, NS], BF16); honsab_s = H("onsab_s")
        score_s = AR.alloc([4, 128], F32); hscore_s = H("score_s")
        sc2_s = AR.alloc([4, 128], F32); hsc2_s = H("sc2_s")
        m8_s = AR.alloc([4, 8], F32); hm8_s = H("m8_s")
        m8b_s = AR.alloc([4, 8], F32); hm8b_s = H("m8b_s")
        nsel_s = AR.alloc([4, 128], BF16); hnsel_s = H("nsel_s")
        nselT_s = AR.alloc([128, 4], F32); hnselT_s = H("nselT_s")
        h1s = AR.alloc([128, KC, NS], F32); hh1s = H("h1s")
        MS("pool", Vs_s[:], 1.0, W=[hVs_s])
        MS("pool", Vw_s[:], 1.0, W=[hVw_s])
        MS("pool", KcT_s[:], 0.0, W=[hKcT_s])
        MS("pool", geV_s[:], 0.0, W=[hgeV_s])
        ACCs = [PB[0], PB[1]]
        hACCs = [PBH[0], PBH[1]]
        accs_rot = [0]

        def next_acc():
            i = accs_rot[0] % 2
            accs_rot[0] += 1
            return ACCs[i], hACCs[i]

        def finish_s(g, br, bl, acc, hacc, first_branch):
            TS("dve", crow_s[64:65, :], acc[64:65, 0:NS], 1e-30, None, ALU.max, R=[hacc], W=[hcrow_s])
            RCP("dve", crow_s[64:65, :], crow_s[64:65, :], R=[hcrow_s], W=[hcrow_s])
            off0 = (br * 8 + g * 4) * NS
            gv = grow_s[64:65, off0:off0 + 4 * NS].rearrange("o (r t) -> o r t", t=NS)[:, :, bl * 4:bl * 4 + 4]
            TT("dve", crow_s[64:65, :].rearrange("o (r q) -> o r q", r=4), crow_s[64:65, :].rearrange("o (r q) -> o r q", r=4), gv,
               ALU.mult, R=[hcrow_s, hgrow_s], W=[hcrow_s])
            pb, ph = rr()
            MM(pb[0:64, 0:NS], ones_f[64:65, 0:64], crow_s[64:65, :], R=[hOnesF, hcrow_s], W=[ph])
            CP("act", num_s[:], acc[0:64, 0:NS], R=[hacc], W=[hnum_s])
            if first_branch:
                TT("dve", onsa_s[:, g, :], num_s[:], pb[0:64, 0:NS], ALU.mult, R=[hnum_s, ph], W=[honsa_s])
            else:
                TT("dve", num_s[:], num_s[:], pb[0:64, 0:NS], ALU.mult, R=[hnum_s, ph], W=[hnum_s])
                TT("dve", onsa_s[:, g, :], onsa_s[:, g, :], num_s[:], ALU.add, R=[hnum_s, honsa_s], W=[honsa_s])
            if dbg and bl == 0:
                nm = "d_br%d%d" % (g, br)
                DBG[nm] = dout(nm, [64, NS])
                DMA("sp", DBG[nm], onsa_s[:, g, :], key=H("k" + nm), R=[honsa_s])

        def new_tile(g, bl, kn, hkn, Vm, hVm, acc, hacc):
            gp = slice(64 * g, 64 * g + 64)
            pn_, phn_ = rr()
            MM(pn_[0:NS, 0:NS], kn[gp, :], QTs[gp, bl * 16:(bl + 1) * 16], start=True, stop=False, R=[hkn, hQTs], W=[phn_])
            MM(pn_[0:NS, 0:NS], J_bf[0:NS, 128 - NS:128], HKSn[0:NS, bl, g, :], start=False, stop=True, R=[hJ, hHKSn[bl][g]], J=[phn_])
            ACT(Pn16[:], pn_[0:NS, 0:NS], AF.Exp, R=[phn_], W=[hPn16])
            if dbg and bl == 0 and g == 0 and kn is ksn:
                DBG["d_pn16"] = dout("d_pn16", [NS, NS])
                CP("act", crow_s[0:NS, 0:NS], pn_[0:NS, 0:NS], R=[phn_], W=[hcrow_s])
                DMA("sp", DBG["d_pn16"], crow_s[0:NS, 0:NS], key=H("kpn16"), R=[hcrow_s])
            MM(acc[0:65, 0:NS], Vm[:, bl, g, :], Pn16[:], start=False, stop=True, R=[hVm, hPn16], J=[hacc])

        pcol_f, hPcol = load([128, 1], F32, cPcol, own=True)
        ptab_i = AR.alloc([128, 4 * NPG], I32); hptab_i = H("ptab_i")
        DMA("sp", ptab_i[:], ptab.partition_broadcast(128), key=hptab_i, W=[hptab_i])
        idx_f = AR.alloc([128, 4 * NPG], F32); hidx_f = H("idx_f")
        idx_i = AR.alloc([128, 2 * NPG], I32); hidx_i = H("idx_i")
        CP("dve", idx_f[:], ptab_i[:], R=[hptab_i], W=[hidx_f])
        idx_f2 = AR.alloc([128, 2 * NPG], F32); hidx_f2 = H("idx_f2")
        CP("dve", idx_f2[0:64, :], idx_f[0:64, :].rearrange("p (a j) -> p a j", j=2)[:, :, 0], R=[hidx_f], W=[hidx_f2])
        CP("dve", idx_f2[64:128, :], idx_f[64:128, :].rearrange("p (a j) -> p a j", j=2)[:, :, 1], R=[hidx_f], J=[hidx_f2])
        TS("dve", idx_f2[:], idx_f2[:], 64.0, pcol_f[:, 0:1], ALU.mult, ALU.add, R=[hidx_f2, hPcol], W=[hidx_f2])
        CP("dve", idx_i[:, 0:2 * NPG], idx_f2[:], R=[hidx_f2], W=[hidx_i])
        cache_rows = cache.rearrange("n (p j) d -> (n p) (j d)", j=2)

        def gather_page(dst, idx):
            def f(e):
                return e.indirect_dma_start(out=dst, out_offset=None, in_=cache_rows,
                                            in_offset=bass.IndirectOffsetOnAxis(ap=idx_i[:, idx:idx + 1], axis=0))
            return f

        for bl in range(nseq):
            for pair in range(NPG // 2):
                sb_, hsb_ = stg[pair % NSTG], hstg[pair % NSTG]
                S.add("pool", gather_page(sb_[:].rearrange("p j d -> p (j d)"), bl * (NPG // 2) + pair), reads=[hidx_i], writes=[hsb_], dma=True, key=hsb_)
                for j in range(2):
                    tile_ = 2 * pair + j
                    ptx, phtx = rr()
                    for ci in range(3):
                        TR(ptx[:, ci * 128:(ci + 1) * 128], sb_[:, j, ci * 128:(ci + 1) * 128], identf[:], R=[hsb_, hIdF], **WJ(phtx, ci == 0))
                    CP("act", rawT_s[:, :, 256 * pair + j:256 * pair + j + 255:2], ptx[:, 0:256].rearrange("p (t n) -> p t n", t=2), R=[phtx],
                       **WJ(hraw_s, tile_ == 0))
                    CP("dve", KsT_s[:, tile_ * 128:(tile_ + 1) * 128], ptx[:, 256:384], R=[phtx], **WJ(hKsT_s, tile_ == 0))
                    CP("pool", Vs_s[:, tile_, :, 0:64], sb_[:, j, 384:512].rearrange("p (g d) -> p g d", g=2), R=[hsb_], **WJ(hVs_s, tile_ == 0))
                pg = 2 * pair + 1
                if (pg + 1) % 8 == 0:
                    gi = pg // 8
                    n0 = max(0, 64 * gi - 1)
                    cnt = 64 * gi + 62 - n0 + 1
                    for t in range(2):
                        pc, phc_ = rr()
                        for sx in range(32):
                            c0_ = 16 * n0 + sx
                            MM(pc[:, 0:cnt], w1_bf[:, t, sx, :], rawT_s[:, t, c0_:c0_ + 16 * (cnt - 1) + 1:16], start=(sx == 0), stop=(sx == 31),
                               R=[hW1, hraw_s], **WJ(phc_, sx == 0))
                        TS("dve", gx_s[:, 0:cnt], pc[:, 0:cnt], pew1[:, t:t + 1], None, ALU.add, R=[phc_, hPew1], W=[hgx_s])
                        TT("dve", gu_s[:, 0:cnt], gx_s[:, 0:cnt], gx_s[:, 0:cnt], ALU.mult, R=[hgx_s], W=[hgu_s])
                        TS("dve", gu_s[:, 0:cnt], gu_s[:, 0:cnt], 0.044715, 1.0, ALU.mult, ALU.add, R=[hgu_s], W=[hgu_s])
                        TT("dve", gu_s[:, 0:cnt], gu_s[:, 0:cnt], gx_s[:, 0:cnt], ALU.mult, R=[hgu_s, hgx_s], W=[hgu_s])
                        ACT(gu_s[:, 0:cnt], gu_s[:, 0:cnt], AF.Exp, scale=-1.5957691216057308, R=[hgu_s], W=[hgu_s])
                        TS("dve", gu_s[:, 0:cnt], gu_s[:, 0:cnt], 1.0, None, ALU.add, R=[hgu_s], W=[hgu_s])
                        RCP("dve", gu_s[:, 0:cnt], gu_s[:, 0:cnt], R=[hgu_s], W=[hgu_s])
                        if t == 0:
                            TT("dve", ge_s[:, 0:cnt], gx_s[:, 0:cnt], gu_s[:, 0:cnt], ALU.mult, R=[hgx_s, hgu_s], W=[hge_s])
                            pk2, phk2 = rr()
                            MM(pk2[:, 0:cnt], w2_bf[:, 0, :], ge_s[:, 0:cnt], R=[hW2, hge_s], W=[phk2])
                            CP("act", KcT_s[:, n0:n0 + cnt], pk2[:, 0:cnt], R=[phk2], **WJ(hKcT_s, gi == 0))
                        else:
                            TT("dve", geV_s[:, n0:n0 + cnt], gx_s[:, 0:cnt], gu_s[:, 0:cnt], ALU.mult, R=[hgx_s, hgu_s], **WJ(hgeV_s, gi == 0))
            if cut == 101:
                return
            for wt in range(4):
                wb_, hwb_ = wstg[wt % 2], hwstg[wt % 2]
                DMA("sp", wb_[:], cwin[bl, wt * 128:(wt + 1) * 128, :], key=hwb_, W=[hwb_])
                ptx, phtx = rr()
                TR(ptx[:, 0:128], wb_[:, 0:128], identf[:], R=[hwb_, hIdF], W=[phtx])
                CP("act", KwT_s[:, wt * 128:(wt + 1) * 128], ptx[:, 0:128], R=[phtx], **WJ(hKwT_s, wt == 0))
                CP("pool", Vw_s[:, wt, :, 0:64], wb_[:, 128:256].rearrange("p (g d) -> p g d", g=2), R=[hwb_], **WJ(hVw_s, wt == 0))
            for ct in range(4):
                pvc, phvc = rr()
                MM(pvc[:, 0:128], geV_s[:, ct * 128:(ct + 1) * 128], w2_bf[:, 1, :], R=[hgeV_s, hW2], W=[phvc])
                TS("dve", Vc_s[:, ct, :, 0:64], pvc[:, 0:128].rearrange("p (g d) -> p g d", g=2), valS_f[:, ct:ct + 1], None, ALU.mult,
                   R=[phvc, hValS], **WJ(hVc_s, ct == 0))
                for g in range(2):
                    CP("pool", Vc_s[:, ct, g, 64:65], valS_f[:, ct:ct + 1], R=[hValS], J=[hVc_s])
            if cut == 103:
                return
            for g in range(2):
                gp = slice(64 * g, 64 * g + 64)
                qs = QTs[gp, bl * 16:(bl + 1) * 16]
                pS, phS = rr()
                MM(pS[:, 0:64], J_bf[:], HKC[:, g, :], start=True, stop=False, R=[hJ, GHS], W=[phS])
                for ct in range(4):
                    MM(pS[:, ct * 16:(ct + 1) * 16], KcT_s[gp, ct * 128:(ct + 1) * 128], qs, start=False, stop=(ct == 3),
                       R=[hKcT_s, hQTs], J=[phS])
                ACT(Pc_s[:], pS[:, 0:64], AF.Exp, R=[phS], W=[hPc_s])
                acc, hacc = next_acc()
                for ct in range(4):
                    MM(acc[0:65, 0:NS], Vc_s[:, ct, g, :], Pc_s[:, ct * 16:(ct + 1) * 16], start=(ct == 0), stop=(ct == 3),
                       R=[hVc_s, hPc_s], **WJ(hacc, ct == 0))
                TS("dve", crow_s[64:65, :], acc[64:65, 0:NS], 1e-30, None, ALU.max, R=[hacc], W=[hcrow_s])
                RCP("dve", crow_s[64:65, :], crow_s[64:65, :], R=[hcrow_s], W=[hcrow_s])
                pbc, phbc = rr()
                MM(pbc[:, 0:NS], ones_f[64:65, :], crow_s[64:65, :], R=[hOnesF, hcrow_s], W=[phbc])
                CP("act", pn_s[:, 0:NS], pbc[:, 0:NS], R=[phbc], W=[hpn_s])
                for ct in range(4):
                    TT("dve", pn_s[:, 16 + 0:16 + NS] if False else gx_s[:, ct * 16:(ct + 1) * 16], Pc_s[:, ct * 16:(ct + 1) * 16], pn_s[:, 0:NS], ALU.mult,
                       R=[hPc_s, hpn_s], **WJ(hgx_s, ct == 0))

                for ct in range(4):
                    def _reds(e, ct=ct):
                        with nc.allow_low_precision("fp32 accumulate inside, bf16 store"):
                            return e.tensor_reduce(out=PsT_s[:, ct, :], in_=gx_s[:, ct * 16:(ct + 1) * 16].rearrange("p (r q) -> p q r", r=4),
                                                   axis=AX.X, op=ALU.add)
                    S.add("dve", _reds, reads=[hgx_s], **({"writes": [hPsT_s]} if ct == 0 else {"joins": [hPsT_s]}))
                pim, phim = rr()
                for ct in range(4):
                    MM(pim[0:4, 0:128], PsT_s[:, ct, :], As_bf[:, ct, :], start=(ct == 0), stop=(ct == 3), R=[hPsT_s, hAs], **WJ(phim, ct == 0))
                TT("dve", score_s[:], pim[0:4, 0:128], sels_f[:, 0, :], ALU.mult, R=[phim, hSels], W=[hscore_s])
                TT("dve", score_s[:], score_s[:], sels_f[:, 1, :], ALU.add, R=[hscore_s, hSels], W=[hscore_s])
                S.add("dve", lambda e: e.max(out=m8_s[:], in_=score_s[:]), reads=[hscore_s], writes=[hm8_s])
                S.add("dve", lambda e: e.match_replace(out=sc2_s[:], in_to_replace=m8_s[:], in_values=score_s[:], imm_value=-1e30),
                      reads=[hscore_s, hm8_s], writes=[hsc2_s])
                S.add("dve", lambda e: e.max(out=m8b_s[:], in_=sc2_s[:]), reads=[hsc2_s], writes=[hm8b_s])
                TS("dve", sc2_s[:], score_s[:], m8b_s[:, 6:7], None, ALU.is_ge, R=[hscore_s, hm8b_s], W=[hsc2_s])
                TS("dve", nsel_s[:], sc2_s[:], -1.0, BIG, ALU.add, ALU.mult, R=[hsc2_s], W=[hnsel_s])
                TR(PT[:, 0:4], nsel_s[:], ident_bf[0:4, 0:4], R=[hnsel_s, hId], W=[H_PT])
                CP("act", nselT_s[:], PT[:, 0:4], R=[H_PT], W=[hnselT_s])
                TT("dve", R_s[:].rearrange("p (k r q) -> p k r q", k=NPG, r=4),
                   nselT_s[:].unsqueeze(1).unsqueeze(1).to_broadcast([128, NPG, 4, 4]),
                   bm_f[:].unsqueeze(2).unsqueeze(3).to_broadcast([128, NPG, 4, 4]), ALU.mult,
                   R=[hnselT_s, hBm], W=[hR_s])
                finish_s(g, 0, bl, acc, hacc, True)
                if cut == 104:
                    return
                p1, ph1 = rr()
                p2, ph2 = rr()
                MM(p1[:, :], half_bf[:], R_s[:, 0:512], start=True, stop=False, R=[hHalf, hR_s], W=[ph1])
                MM(p2[:, :], half_bf[:], R_s[:, 512:1024], start=True, stop=False, R=[hHalf, hR_s], W=[ph2])
                for pg in range(NPG):
                    pp_, php_ = (p1, ph1) if pg < 32 else (p2, ph2)
                    c0 = (pg % 32) * 16
                    MM(pp_[:, c0:c0 + 16], KsT_s[gp, pg * 128:(pg + 1) * 128], qs, start=False, stop=(pg == 31),
                       R=[hKsT_s, hQTs], J=[php_])
                for j in range(2):
                    MM(p2[:, 480 + 16 * j:496 + 16 * j], J_bf[:], HKS2[:, j, g, :], start=False, stop=(j == 1), R=[hJ, GHS], J=[ph2])
                ACT(P_s[:, 0:512], p1[:, :], AF.Exp, R=[ph1], W=[hP_s])
                ACT(P_s[:, 512:1024], p2[:, :], AF.Exp, R=[ph2], J=[hP_s])
                if dbg and bl == 0 and g == 0:
                    DBG["d_S2"] = dout("d_S2", [128, 512])
                    CP("act", gx_s[:, 0:512], p2[:, :], R=[ph2], W=[hgx_s])
                    DMA("sp", DBG["d_S2"], gx_s[:, 0:512], key=H("kS2"), R=[hgx_s])
                    DBG["d_nselT"] = dout("d_nselT", [128, 4])
                    DMA("sp", DBG["d_nselT"], nselT_s[:], key=H("knselT"), R=[hnselT_s])
                acc, hacc = next_acc()
                for pg in range(NPG):
                    MM(acc[0:65, 0:NS], Vs_s[:, pg, g, :], P_s[:, pg * 16:(pg + 1) * 16], start=(pg == 0), stop=False,
                       R=[hVs_s, hP_s], **WJ(hacc, pg == 0))
                new_tile(g, bl, ksn, hksn, Vsn_m, hVsn_m, acc, hacc)
                finish_s(g, 1, bl, acc, hacc, False)
                if cut == 105:
                    return
                pW, phW = rr()
                MM(pW[:, 0:64], J_bf[:], HKW[:, g, :], start=True, stop=False, R=[hJ, GHS], W=[phW])
                for wt in range(4):
                    MM(pW[:, wt * 16:(wt + 1) * 16], KwT_s[gp, wt * 128:(wt + 1) * 128], qs, start=False, stop=(wt == 3),
                       R=[hKwT_s, hQTs], J=[phW])
                ACT(Pw_s[:], pW[:, 0:64], AF.Exp, R=[phW], W=[hPw_s])
                if dbg and bl == 0 and g == 0:
                    DBG["d_pW"] = dout("d_pW", [128, 64])
                    CP("act", gu_s[:, 0:64], pW[:, 0:64], R=[phW], W=[hgu_s])
                    DMA("sp", DBG["d_pW"], gu_s[:, 0:64], key=H("kpW"), R=[hgu_s])
                acc, hacc = next_acc()
                for wt in range(4):
                    MM(acc[0:65, 0:NS], Vw_s[:, wt, g, :], Pw_s[:, wt * 16:(wt + 1) * 16], start=(wt == 0), stop=False,
                       R=[hVw_s, hPw_s], **WJ(hacc, wt == 0))
                new_tile(g, bl, kwn, hkwn, Vwn_m, hVwn_m, acc, hacc)
                if dbg and bl == 0 and g == 0:
                    DBG["d_accW"] = dout("d_accW", [65, NS])
                    CP("act", gx_s[0:65, 0:NS], acc[0:65, 0:NS], R=[hacc], W=[hgx_s])
                    DMA("sp", DBG["d_accW"], gx_s[0:65, 0:NS], key=H("kaccW"), R=[hgx_s])
                finish_s(g, 2, bl, acc, hacc, False)
            CP("act", onsab_s[:, :, :, bl * 4:bl * 4 + 4], onsa_s[:].rearrange("p g (r q) -> p g r q", r=4), R=[honsa_s],
               **WJ(honsab_s, bl == 0))
        if nseq < 4:
            pass
        pw, phw = rr()
        for dc in range(KC):
            osl = pw[:, dc * NS:(dc + 1) * NS]
            n = 0
            for g in range(2):
                for r in range(4):
                    MM(osl, wonsa_bf[:, g * 4 + r, dc * 128:(dc + 1) * 128], onsab_s[:, g, r, :], start=(n == 0), stop=False,
                       R=[hWon, honsab_s], **WJ(phw, dc == 0 and n == 0))
                    n += 1
            for hd in range(4):
                MM(osl, wogla_bf[:, hd, dc * 128:(dc + 1) * 128], omixg_s[:, hd, :], start=False, stop=(hd == 3),
                   R=[hWog, homixg_s], J=[phw])
        TT("dve", h1s[:], xs_s[:], pw[:, 0:KC * NS].rearrange("p (c q) -> p c q", c=KC), ALU.add, R=[hxs_s, phw], W=[hh1s])
        DMA("sp", h1_scr[:, :, NT:NT + NS], h1s[:], key=hh1s, R=[hh1s], W=[H_h1[NOWN]])
        if dbg:
            DBG["d_h1s"] = dout("d_h1s", [128, KC, NS])
            DMA("sp", DBG["d_h1s"], h1s[:], key=H("kd_h1s"), R=[hh1s])
        DBG["S_END"] = AR.top

    if do_sample:
        phase_s()
        S.barrier()
        AR.top = MIX_END


    cur_grp[0] = H("GA")
    E_bf = AR.alloc([128, TOK], BF16); hE = H("E")
    for i in range(3):
        a, b = i * 1408, (i + 1) * 1408
        DMA("pool", E_bf[:, a:b], cE[:, a:b], key=hE, W=[hE] if i == 0 else (), J=() if i == 0 else [hE])
    A_bf, hA = load([128, 2, 128], BF16, cA)
    HK = AR.alloc([128, 3, 2, 512], BF16)
    hHK = [[GHA for g in range(2)] for i in range(3)]
    for g in range(2):
        hankel(HK[:, 0, g, :].rearrange("p (r q) -> p r q", r=4), g, OFFA - 127, 1, 128, hHK[0][g])
        hankel(HK[:, 1, g, :].rearrange("p (r q) -> p r q", r=4), g, OFFA + 1, 1, 128, hHK[1][g])
        hankel(HK[:, 2, g, :].rearrange("p (r q) -> p r q", r=4), g, LA + OFFW + 385, 1, 128, hHK[2][g])
    KsT = AR.alloc([128, TOK], BF16); hKs = [H("ks%d" % s) for s in range(NSLOT)]
    KwT = AR.alloc([128, 8 * 128], BF16); hKw = [H("kw%d" % s) for s in range(8)]
    Vs = AR.alloc([128, NSLOT, 2, 65], BF16); hVs = [H("vs%d" % s) for s in range(NSLOT)]
    Vw = AR.alloc([128, 8, 2, 65], BF16); hVw = [H("vw%d" % s) for s in range(8)]
    KcT = AR.alloc([128, 272], BF16); hKc = H("kc")
    geV = AR.alloc([128, 272], BF16); hGeV = H("gev")
    Vc = AR.alloc([128, 2, 2, 65], BF16); hVc = H("vc")
    rawT = AR.alloc([128, 2, 144], BF16); hRawP = H("rawp"); hRawC = H("rawc")
    Sst = AR.alloc([128, 2, 128], F32); hS = H("S")
    Sbf = AR.alloc([128, 2, 128], BF16); hSbf = H("Sbf")
    lrT = AR.alloc([64, 128], BF16); hLr = H("lrT")
    MS("pool", KsT[:], 0.0, W=hKs)
    MS("pool", KwT[:], 0.0, W=hKw)
    MS("pool", Vs[:], 0.0, W=hVs)
    MS("pool", Vw[:], 0.0, W=hVw)
    MS("pool", KcT[:], 0.0, W=[hKc])
    MS("pool", geV[:], 0.0, W=[hGeV])
    MS("pool", Vc[:], 0.0, W=[hVc])
    MS("pool", rawT[:], 0.0, W=[hRawP, hRawC])
    MS("pool", Sst[:], 0.0, W=[hS])
    MS("pool", Sbf[:], 0.0, W=[hSbf])
    MS("pool", lrT[:], 0.0, W=[hLr])
    MS("pool", lrT[32:33, :], 1.0, W=[hLr])

    xs = [AR.alloc([128, KC, 128], F32) for _ in range(2)]; hxs = [H("xs0"), H("xs1")]
    sq = AR.alloc([128, KC, 128], BF16); hsq = H("sq")
    rstd = AR.alloc([128, 128], F32); hrstd = H("rstd")
    xn = AR.alloc([128, KC, 128], BF16); hxn = H("xn")
    kvo = AR.alloc([128, 4, 128], F32); hkvo = H("kvo")
    vto = AR.alloc([128, 2, 128], F32); hvto = H("vto")
    la = AR.alloc([128, 256], F32); hla = H("la")
    esuf = AR.alloc([128, 256], F32); hesuf = H("esuf")
    ecum = AR.alloc([128, 256], F32); hecum = H("ecum")
    einv = AR.alloc([128, 256], F32); heinv = H("einv")
    keT = AR.alloc([128, 256], BF16); hke = H("keT")
    qeT = AR.alloc([128, 256], BF16); hqe = H("qeT")
    kd = AR.alloc([128, 256], BF16); hkd = H("kd")
    vg = AR.alloc([128, 512], BF16); hvg = H("vg")
    attT = AR.alloc([128, 128], BF16); hatt = H("attT")
    osq = AR.alloc([128, 128], BF16); hosq = H("osq")
    grs = AR.alloc([128, 128], F32); hgrs = H("grs")
    sg = AR.alloc([128, 128], F32); hsg = H("sg")
    t1 = AR.alloc([128, 128], F32); ht1 = H("t1")
    omixg = AR.alloc([128, 4, 128], BF16); homixg = H("omixg")
    gx = AR.alloc([128, 8], F32); hgx = H("gx")
    gu = AR.alloc([128, 8], F32); hgu = H("gu")
    ge = AR.alloc([128, 8], BF16); hge = H("ge")
    QT = AR.alloc([128, 512], BF16); hQT = H("QT")
    gate_sb = AR.alloc([24, 128], F32); hgate = H("gate")
    grow = AR.alloc([128, 24 * 128], F32); hgrow = H("grow")
    crow = AR.alloc([128, 512], F32); hcrow = H("crow")
    HKc = [AR.alloc([128, 2, 512], BF16) for _ in range(2)]; hHKc = [[H("hkc%d%d" % (i, g)) for g in range(2)] for i in range(2)]
    Pc = AR.alloc([128, 2, 512], BF16); hPc = [H("pc0"), H("pc1")]
    Pt = [AR.alloc([128, 512], BF16) for _ in range(6)]; hPt = [H("pt%d" % i) for i in range(6)]
    pn_f = AR.alloc([128, 512], F32); hpn = H("pn")
    PsT = AR.alloc([128, 2, 128], BF16); hPsT = H("PsT")
    selc = AR.alloc([128, 2, 128], F32); hselc = H("selc")
    score = AR.alloc([128, 128], F32); hscore = H("score")
    sc2 = AR.alloc([128, 128], F32); hsc2 = H("sc2")
    m8 = AR.alloc([128, 8], F32); hm8 = H("m8")
    m8b = AR.alloc([128, 8], F32); hm8b = H("m8b")
    nsel = AR.alloc([128, 128], BF16); hnsel = H("nsel")
    nselT = AR.alloc([128, 512], BF16); hnselT = H("nselT")
    numsb = AR.alloc([64, 512], F32); hnum = H("num")
    onsa = AR.alloc([64, 2, 512], F32); honsa = H("onsa")
    onsab = AR.alloc([64, 2, 512], BF16); honsab = H("onsab")
    h1t = AR.alloc([128, KC, 128], F32); hh1t = H("h1t")
    pt_rot = [0]

    def projF(out, c0, M, first_w):
        for kc in range(KC):
            MM(out, win_bf[:, kc, c0:c0 + M], xn[:, kc, :], start=(kc == 0), stop=(kc == KC - 1),
               R=[hWin, hxn], **first_w(kc))

    def projT(out, c0, N, first_w):
        for kc in range(KC):
            MM(out, xn[:, kc, :], win_bf[:, kc, c0:c0 + N], start=(kc == 0), stop=(kc == KC - 1),
               R=[hWin, hxn], **first_w(kc))

    def rms_rstd(src_ps, hsrc, n, scale):
        pass

    ACC = [PB[0], PB[1]]
    hACC = [PBH[0], PBH[1]]
    acc_rot = [0]

    def attn_tile(g, lhsK, hK, bias, mask, Vaug, hV, acc, hacc, first, last, keep=None, hkeep=None):
        gp = slice(64 * g, 64 * g + 64)
        pb, ph = rr()
        nmm = 1 + (bias is not None) + (mask is not None)
        k = 0
        MM(pb[:, :], lhsK, QT[gp, :], start=True, stop=(nmm == 1), R=[hK, hQT], W=[ph])
        k += 1
        if bias is not None:
            bap, bh = bias
            MM(pb[:, :], J_bf[:], bap, start=False, stop=(k == nmm - 1), R=[hJ, bh], J=[ph])
            k += 1
        if mask is not None:
            eap = mask
            MM(pb[:, :], eap, nselT[:, :], start=False, stop=True,
               R=[hE, hnselT], J=[ph])
        if keep is None:
            i = pt_rot[0] % 6
            pt_rot[0] += 1
            P, hP = Pt[i][:], hPt[i]
        else:
            P, hP = keep, hkeep
        ACT(P, pb[:, :], AF.Exp, R=[ph], W=[hP])
        MM(acc[0:65, :], Vaug, P, start=first, stop=last, R=[hV, hP], **(dict(W=[hacc]) if first else dict(J=[hacc])))

    def branch_finish(g, br, acc, hacc, first_branch):
        TS("dve", crow[64:65, :], acc[64:65, :], 1e-30, None, ALU.max, R=[hacc], W=[hcrow])
        RCP("dve", crow[64:65, :], crow[64:65, :], R=[hcrow], W=[hcrow])
        off = (br * 8 + g * 4) * 128
        TT("dve", crow[64:65, :], crow[64:65, :], grow[64:65, off:off + 512], ALU.mult, R=[hcrow, hgrow], W=[hcrow])
        pb, ph = rr()
        MM(pb[0:64, :], ones_f[64:65, 0:64], crow[64:65, :], R=[hOnesF, hcrow], W=[ph])
        CP("act", numsb[:, :], acc[0:64, :], R=[hacc], W=[hnum])
        if first_branch:
            TT("dve", onsa[:, g, :], numsb[:, :], pb[0:64, :], ALU.mult, R=[hnum, ph], W=[honsa])
        else:
            TT("dve", numsb[:, :], numsb[:, :], pb[0:64, :], ALU.mult, R=[hnum, ph], W=[hnum])
            TT("dve", onsa[:, g, :], onsa[:, g, :], numsb[:, :], ALU.add, R=[hnum, honsa], W=[honsa])

    for s in range(nslots):
        own = (s % 2 == 1)
        jo = s // 2
        xb, hx = xs[s % 2], hxs[s % 2]
        DMA("sp", xb[:], xT[:, :, s * 128:(s + 1) * 128], key=hx, W=[hx])
        ACT(sq[:], xb[:], AF.Square, R=[hx], W=[hsq])
        pb, ph = rr()
        for kc in range(KC):
            MM(pb[:, 0:128], ones_bf[:], sq[:, kc, :], start=(kc == 0), stop=(kc == KC - 1), R=[hOnes, hsq], **WJ(ph, kc == 0))
        ACT(rstd[:], pb[:, 0:128], AF.Ln, bias=EPS, scale=1.0 / D, R=[ph], W=[hrstd])
        ACT(rstd[:], rstd[:], AF.Exp, scale=-0.5, R=[hrstd], W=[hrstd])
        for kc in range(KC):
            STT("dve", xn[:, kc, :], xb[:, kc, :], norms_f[:, 0, kc:kc + 1], rstd[:], ALU.mult, ALU.mult,
                R=[hx, hrstd, hNorms], **WJ(hxn, kc == 0))
        if cut <= 1:
            break
        pb, ph = rr()
        for i, c0 in enumerate((512, 640, 768, 1024)):
            projF(pb[:, i * 128:(i + 1) * 128], c0, 128, lambda kc, i=i: WJ(ph, i == 0 and kc == 0))
        if cut == 1.1:
            break
        CP("act", rawT[:, :, 16:144], pb[:, 0:256].rearrange("p (t n) -> p t n", t=2), R=[ph], W=[hRawC])
        if cut == 1.2:
            break
        CP("act", KsT[:, s * 128:(s + 1) * 128], pb[:, 256:384], R=[ph], W=[hKs[s]])
        CP("act", KwT[:, (s % 8) * 128:(s % 8 + 1) * 128], pb[:, 384:512], R=[ph], W=[hKw[s % 8]])
        if own:
            CP("act", kvo[:].rearrange("p a b -> p (a b)"), pb[:, :], R=[ph], W=[hkvo])
            DMA("sp", kvT_p[:, :, jo * 128:(jo + 1) * 128], kvo[:], key=hkvo, R=[hkvo])
        if cut <= 2:
            break
        pb, ph = rr()
        projT(pb[:, 0:128], 896, 128, lambda kc: WJ(ph, kc == 0))
        projT(pb[:, 128:256], 1152, 128, lambda kc: dict(J=[ph]))
        CP("dve", Vs[:, s, :, 0:64], pb[:, 0:128].rearrange("p (g d) -> p g d", g=2), R=[ph], W=[hVs[s]])
        CP("dve", Vw[:, s % 8, :, 0:64], pb[:, 128:256].rearrange("p (g d) -> p g d", g=2), R=[ph], W=[hVw[s % 8]])
        for g in range(2):
            CP("pool", Vs[:, s, g, 64:65], valid_f[:, s:s + 1], R=[hValid], J=[hVs[s]])
            CP("pool", Vw[:, s % 8, g, 64:65], valid_f[:, s:s + 1], R=[hValid], J=[hVw[s % 8]])
        if own:
            CP("act", vto[:].rearrange("p a b -> p (a b)"), pb[:, 0:256], R=[ph], W=[hvto])
            DMA("sp", vtok_p[jo * 128:(jo + 1) * 128, :, :], vto[:], key=hvto, R=[hvto])
        if cut <= 3:
            break
        for t in range(2):
            pc, phc = rr()
            for sx in range(32):
                MM(pc[:, 0:8], w1_bf[:, t, sx, :], rawT[:, t, sx:sx + 113:16], start=(sx == 0), stop=(sx == 31),
                   R=[hW1, hRawP, hRawC], **WJ(phc, sx == 0))
            TS("dve", gx[:], pc[:, 0:8], pew1[:, t:t + 1], None, ALU.add, R=[phc, hPew1], W=[hgx])
            TT("dve", gu[:], gx[:], gx[:], ALU.mult, R=[hgx], W=[hgu])
            TS("dve", gu[:], gu[:], 0.044715, 1.0, ALU.mult, ALU.add, R=[hgu], W=[hgu])
            TT("dve", gu[:], gu[:], gx[:], ALU.mult, R=[hgu, hgx], W=[hgu])
            ACT(gu[:], gu[:], AF.Exp, scale=-1.5957691216057308, R=[hgu], W=[hgu])
            TS("dve", gu[:], gu[:], 1.0, None, ALU.add, R=[hgu], W=[hgu])
            RCP("dve", gu[:], gu[:], R=[hgu], W=[hgu])
            if t == 0:
                TT("dve", ge[:], gx[:], gu[:], ALU.mult, R=[hgx, hgu], W=[hge])
                pk2, phk2 = rr()
                MM(pk2[:, 0:8], w2_bf[:, 0, :], ge[:], R=[hW2, hge], W=[phk2])
                CP("act", KcT[:, 8 * s:8 * s + 8], pk2[:, 0:8], R=[phk2], W=[hKc])
            else:
                TT("dve", geV[:, 8 * s:8 * s + 8], gx[:], gu[:], ALU.mult, R=[hgx, hgu], W=[hGeV])
        CP("pool", rawT[:, :, 0:16], rawT[:, :, 128:144], R=[hRawC], W=[hRawP])
        if cut <= 4:
            break
        pb, ph = rr()
        projF(pb[0:16, 0:128], C_LR, 16, lambda kc: WJ(ph, kc == 0))
        CP("act", lrT[0:16, :], pb[0:16, 0:128], R=[ph], W=[hLr])
        pz, phz = rr()
        MM(pz[:, 0:256], lrT[0:33, :], wgk_bf[:], R=[hLr, hWgk], W=[phz])
        ACT(la[:], pz[:, 0:256], AF.Exp, scale=-1.0, R=[phz], W=[hla])
        ACT(la[:], la[:], AF.Ln, bias=1.0, R=[hla], W=[hla])
        psf, phs = rr()
        MM(psf[:, 0:256], triu_f[:], la[:], R=[hTriU, hla], W=[phs])
        ACT(esuf[:], psf[:, 0:256], AF.Exp, scale=-1.0 / 16, R=[phs], W=[hesuf])
        pct, phc = rr()
        for ch in range(2):
            MM(pct[:, ch * 128:(ch + 1) * 128], la[:, ch * 128:(ch + 1) * 128], tri_f[:], R=[hla, hTri], **WJ(phc, ch == 0))
        ACT(ecum[:], pct[:, 0:256], AF.Exp, scale=-1.0 / 16, R=[phc], W=[hecum])
        ACT(einv[:], pct[:, 0:256], AF.Exp, scale=1.0 / 16, R=[phc], W=[heinv])
        if cut <= 5:
            break
        pk, phk = rr()
        for ch in range(2):
            projF(pk[:, ch * 128:(ch + 1) * 128], C_KG + ch * 128, 128, lambda kc, ch=ch: WJ(phk, ch == 0 and kc == 0))
        TT("dve", keT[:], pk[:, 0:256], einv[:], ALU.mult, R=[phk, heinv], W=[hke])
        pkt, phkt = rr()
        projT(pkt[:, 0:256], C_KG, 256, lambda kc: WJ(phkt, kc == 0))
        TT("dve", kd[:], pkt[:, 0:256], esuf[:], ALU.mult, R=[phkt, hesuf], W=[hkd])
        pv, phv = rr()
        projT(pv[:, 0:512], C_VG, 512, lambda kc: WJ(phv, kc == 0))
        CP("act", vg[:], pv[:, 0:512], R=[phv], W=[hvg])
        if own:
            pq, phq = rr()
            for ch in range(2):
                projF(pq[:, ch * 128:(ch + 1) * 128], C_QG + ch * 128, 128, lambda kc, ch=ch: WJ(phq, ch == 0 and kc == 0))
            STT("dve", qeT[:], pq[:, 0:256], 0.125, ecum[:], ALU.mult, ALU.mult, R=[phq, hecum], W=[hqe])
            for hd in range(4):
                ch = hd // 2
                pp = slice(64 * (hd % 2), 64 * (hd % 2) + 64)
                cs = slice(ch * 128, (ch + 1) * 128)
                pa, pha = rr()
                MM(pa[:, 0:128], keT[pp, cs], qeT[pp, cs], R=[hke, hqe], W=[pha])
                TT("dve", attT[:], pa[:, 0:128], tri_f[:], ALU.mult, R=[pha, hTri], W=[hatt])
                po, pho = rr()
                MM(po[:, 0:128], vg[:, hd * 128:(hd + 1) * 128], attT[:], start=True, stop=False, R=[hvg, hatt], W=[pho])
                MM(po[:, 0:128], Sbf[pp, ch, :], qeT[pp, cs], start=False, stop=True, R=[hSbf, hqe], J=[pho])
                ACT(osq[:], po[:, 0:128], AF.Square, R=[pho], W=[hosq])
                pn, phn = rr()
                MM(pn[:, 0:128], ones_bf[:], osq[:], R=[hOnes, hosq], W=[phn])
                ACT(grs[:], pn[:, 0:128], AF.Ln, bias=EPS, scale=1.0 / 128, R=[phn], W=[hgrs])
                ACT(grs[:], grs[:], AF.Exp, scale=-0.5, R=[hgrs], W=[hgrs])
                STT("dve", t1[:], po[:, 0:128], glan_f[:, 0:1], grs[:], ALU.mult, ALU.mult, R=[pho, hGlan, hgrs], W=[ht1])
                pg, phg = rr()
                projF(pg[:, 0:128], C_GG + hd * 128, 128, lambda kc: WJ(phg, kc == 0))
                ACT(sg[:], pg[:, 0:128], AF.Exp, scale=-1.0, R=[phg], W=[hsg])
                TS("dve", sg[:], sg[:], 1.0, None, ALU.add, R=[hsg], W=[hsg])
                RCP("dve", sg[:], sg[:], R=[hsg], W=[hsg])
                TT("dve", sg[:], sg[:], pg[:, 0:128], ALU.mult, R=[hsg, phg], W=[hsg])
                TT("dve", omixg[:, hd, :], t1[:], sg[:], ALU.mult, R=[ht1, hsg], **WJ(homixg, hd == 0))
        if cut <= 6:
            break
        for ch in range(2):
            pu, phu = rr()
            MM(pu[:, 0:256], kd[:, ch * 128:(ch + 1) * 128], vg[:, ch * 256:(ch + 1) * 256], R=[hkd, hvg], W=[phu])
            for hh in range(2):
                pp = slice(64 * hh, 64 * hh + 64)
                STT("dve", Sst[pp, ch, :], Sst[pp, ch, :], ecum[pp, ch * 128 + 127:ch * 128 + 128], pu[pp, hh * 128:(hh + 1) * 128],
                    ALU.mult, ALU.add, R=[hS, hecum, phu], W=[hS])
        CP("pool", Sbf[:], Sst[:], R=[hS], W=[hSbf])
        if not own:
            continue
        pq, phq = rr()
        for r in range(4):
            projF(pq[:, r * 128:(r + 1) * 128], r * 128, 128, lambda kc, r=r: WJ(phq, r == 0 and kc == 0))
        TS("dve", QT[:], pq[:, :], 0.125, None, ALU.mult, R=[phq], W=[hQT])
        pgt, phgt = rr()
        projF(pgt[0:24, 0:128], C_GT, 24, lambda kc: WJ(phgt, kc == 0))
        ACT(gate_sb[:], pgt[0:24, 0:128], AF.Exp, scale=-1.0, R=[phgt], W=[hgate])
        TS("dve", gate_sb[:], gate_sb[:], 1.0, None, ALU.add, R=[hgate], W=[hgate])
        RCP("dve", gate_sb[:], gate_sb[:], R=[hgate], W=[hgate])
        DMA("sp", gscr[jo % 2], gate_sb[:], key=hgate, R=[hgate], W=[H_gscr[jo % 2]])
        DMA("sp", grow[64:65, :], gscr[jo % 2].rearrange("(o a) b -> o (a b)", o=1), key=hgrow, R=[H_gscr[jo % 2]], W=[hgrow])
        nct = 2 if s >= 17 else 1
        for ct in range(nct):
            pvc, phvc = rr()
            MM(pvc[:, 0:128], geV[:, ct * 128:(ct + 1) * 128], w2_bf[:, 1, :], R=[hGeV, hW2], W=[phvc])
            TS("dve", Vc[:, ct, :, 0:64], pvc[:, 0:128].rearrange("p (g d) -> p g d", g=2), valc_f[:, ct:ct + 1], None, ALU.mult,
               R=[phvc, hValC], **WJ(hVc, ct == 0))
            for g in range(2):
                CP("pool", Vc[:, ct, g, 64:65], valc_f[:, ct:ct + 1], R=[hValC], J=[hVc])
        DMA("sp", selc[:], cSel[jo], key=hselc, W=[hselc])
        for g in range(2):
            gp = slice(64 * g, 64 * g + 64)
            hk_i = jo % 2
            bias_ct = None
            if s <= 15:
                bias_ct, sprime = 0, s
            elif s >= 17:
                bias_ct, sprime = 1, s - 16
            if bias_ct is not None:
                hankel(HKc[hk_i][:, g, :].rearrange("p (r q) -> p r q", r=4), g, OFFA + 128 * sprime - 2047, 16, 128, hHKc[hk_i][g])
            acc, hacc = ACC[acc_rot[0] % 2], hACC[acc_rot[0] % 2]
            acc_rot[0] += 1
            for ct in range(nct):
                bias = (HKc[hk_i][:, g, :], hHKc[hk_i][g]) if ct == bias_ct else None
                attn_tile(g, KcT[gp, ct * 128:(ct + 1) * 128], hKc, bias, None, Vc[:, ct, g, :], hVc, acc, hacc,
                          ct == 0, ct == nct - 1, keep=Pc[:, ct, :], hkeep=hPc[ct])
            TS("dve", crow[64:65, :], acc[64:65, :], 1e-30, None, ALU.max, R=[hacc], W=[hcrow])
            RCP("dve", crow[64:65, :], crow[64:65, :], R=[hcrow], W=[hcrow])
            pbc, phbc = rr()
            MM(pbc[:, :], ones_f[64:65, :], crow[64:65, :], R=[hOnesF, hcrow], W=[phbc])
            for ct in range(nct):
                TT("dve", pn_f[:], Pc[:, ct, :], pbc[:, :], ALU.mult, R=[hPc[ct], phbc], W=[hpn])
                def _red(e, ct=ct):
                    with nc.allow_low_precision("fp32 accumulate inside, bf16 store"):
                        return e.tensor_reduce(out=PsT[:, ct, :], in_=pn_f[:].rearrange("p (r q) -> p q r", r=4), axis=AX.X, op=ALU.add)
                S.add("dve", _red, reads=[hpn], **({"writes": [hPsT]} if ct == 0 else {"joins": [hPsT]}))
            pim, phim = rr()
            for ct in range(nct):
                MM(pim[:, 0:128], PsT[:, ct, :], A_bf[:, ct, :], start=(ct == 0), stop=(ct == nct - 1), R=[hPsT, hA], **WJ(phim, ct == 0))
            TT("dve", score[:], pim[:, 0:128], selc[:, 0, :], ALU.mult, R=[phim, hselc], W=[hscore])
            TT("dve", score[:], score[:], selc[:, 1, :], ALU.add, R=[hscore, hselc], W=[hscore])
            S.add("dve", lambda e: e.max(out=m8[:], in_=score[:, 0:72]), reads=[hscore], writes=[hm8])
            S.add("dve", lambda e: e.match_replace(out=sc2[:, 0:72], in_to_replace=m8[:], in_values=score[:, 0:72], imm_value=-1e30),
                  reads=[hscore, hm8], writes=[hsc2])
            S.add("dve", lambda e: e.max(out=m8b[:], in_=sc2[:, 0:72]), reads=[hsc2], writes=[hm8b])
            TS("dve", sc2[:], score[:], m8b[:, 7:8], None, ALU.is_ge, R=[hscore, hm8b], W=[hsc2])
            TS("dve", nsel[:], sc2[:], -1.0, BIG, ALU.add, ALU.mult, R=[hsc2], W=[hnsel])
            TR(PT[:, 0:128], nsel[:], ident_bf[:], R=[hnsel, hId], W=[H_PT])
            CP("act", nselT[:].rearrange("p (r q) -> p r q", r=4), PT[:, 0:128].unsqueeze(1).to_broadcast([128, 4, 128]), R=[H_PT], W=[hnselT])
            branch_finish(g, 0, acc, hacc, True)
            acc, hacc = ACC[acc_rot[0] % 2], hACC[acc_rot[0] % 2]
            acc_rot[0] += 1
            for ks in range(s + 1):
                dl = s - ks
                bias = (HK[:, dl, g, :], hHK[dl][g]) if dl <= 1 else None
                msk = None if (s <= 7 or ks == s) else E_bf[:, ks * 128:(ks + 1) * 128]
                attn_tile(g, KsT[gp, ks * 128:(ks + 1) * 128], hKs[ks], bias, msk,
                          Vs[:, ks, g, :], hVs[ks], acc, hacc, ks == 0, ks == s)
            branch_finish(g, 1, acc, hacc, False)
            acc, hacc = ACC[acc_rot[0] % 2], hACC[acc_rot[0] % 2]
            acc_rot[0] += 1
            k0 = max(0, s - 4)
            for ks in range(k0, s + 1):
                dl = s - ks
                bias = None
                if dl <= 1:
                    bias = (HK[:, dl, g, :], hHK[dl][g])
                elif dl == 4:
                    bias = (HK[:, 2, g, :], hHK[2][g])
                attn_tile(g, KwT[gp, (ks % 8) * 128:(ks % 8 + 1) * 128], hKw[ks % 8], bias, None,
                          Vw[:, ks % 8, g, :], hVw[ks % 8], acc, hacc, ks == k0, ks == s)
            branch_finish(g, 2, acc, hacc, False)
        CP("act", onsab[:], onsa[:], R=[honsa], W=[honsab])
        for half in range(2):
            pw, phw = rr()
            for c4 in range(4):
                dc = half * 4 + c4
                osl = pw[:, c4 * 128:(c4 + 1) * 128]
                n = 0
                for g in range(2):
                    for r in range(4):
                        MM(osl, wonsa_bf[:, g * 4 + r, dc * 128:(dc + 1) * 128], onsab[:, g, r * 128:(r + 1) * 128],
                           start=(n == 0), stop=False, R=[hWon, honsab], **WJ(phw, c4 == 0 and n == 0))
                        n += 1
                for hd in range(4):
                    MM(osl, wogla_bf[:, hd, dc * 128:(dc + 1) * 128], omixg[:, hd, :], start=False, stop=(hd == 3),
                       R=[hWog, homixg], J=[phw])
            TT("dve", h1t[:, half * 4:half * 4 + 4, :], xb[:, half * 4:half * 4 + 4, :],
               pw[:, :].rearrange("p (c q) -> p c q", c=4), ALU.add, R=[hx, phw], **WJ(hh1t, half == 0))
        DMA("sp", h1_scr[:, :, jo * 128:(jo + 1) * 128], h1t[:], key=hh1t, R=[hh1t], W=[H_h1[jo]])
        if dbg and s == 31:
            DBG["d_h1L"] = dout("d_h1L", [128, KC, 128])
            DMA("sp", DBG["d_h1L"], h1t[:], key=hh1t, R=[hh1t])
        if dbg and s == 1:
            DBG["d_h1"] = dout("d_h1", [128, KC, 128])
            DMA("sp", DBG["d_h1"], h1t[:], key=hh1t, R=[hh1t])
            for nm, tl, hh, shp in (("d_onsa", onsa, honsa, [64, 2, 512]), ("d_grow", grow, hgrow, [128, 3072]), ("d_score", score, hscore, [128, 128]),
                                    ("d_t1", t1, ht1, [128, 128]), ("d_sg", sg, hsg, [128, 128]), ("d_grs", grs, hgrs, [128, 128]),
                                    ("d_la", la, hla, [128, 256]), ("d_ecum", ecum, hecum, [128, 256]), ("d_S", Sst, hS, [128, 2, 128]),
                                    ("d_rstd", rstd, hrstd, [128, 128]), ("d_pn", pn_f, hpn, [128, 512]), ("d_num", numsb, hnum, [64, 512]), ("d_gx", gx, hgx, [128, 8])):
                DBG[nm] = dout(nm, shp)
                DMA("sp", DBG[nm], tl[:], key=H("k" + nm), R=[hh])
    DMA("sp", gla_p, Sst[:], key=hS, R=[hS])
    A_END = AR.top

    S.barrier()
    AR.top = PERS_END
    wg_bf = AR.alloc([128, KC, DFF], BF16); hWg = H("wg")
    wu_bf = AR.alloc([128, KC, DFF], BF16); hWu = H("wu")
    wd_bf = AR.alloc([128, NFC, D], BF16); hWd = H("wd")
    wpg_bf = AR.alloc([128, KC, D], BF16); hWpg = H("wpg")
    wple_bf = AR.alloc([128, 2, D], BF16); hWple = H("wple")

    def wload(dst, src, n1, ncol, h, step):
        first = True
        for i in range(n1):
            for a in range(0, ncol, step):
                DMA("pool", dst[:, i, a:a + step], src[:, i, a:a + step], key=h, **(dict(W=[h]) if first else dict(J=[h])))
                first = False
    if os.environ.get("SKIPB") == "1":
        ctx = dict(locals())
        return ctx
    wload(wg_bf, w_gate, KC, DFF, hWg, 1408)
    wload(wu_bf, w_up, KC, DFF, hWu, 1408)
    wload(wd_bf, w_down, NFC, D, hWd, 1024)
    wload(wpg_bf, w_pg, KC, D, hWpg, 1024)
    wload(wple_bf, w_ple, 2, D, hWple, 1024)
    TBM = 256
    hbs = [AR.alloc([128, KC, TBM], F32) for _ in range(2)]; hhbs = [H("hb0"), H("hb1")]
    sqb = AR.alloc([128, KC, TBM], BF16); hsqb = H("sqb")
    rsb = AR.alloc([128, TBM], F32); hrsb = H("rsb")
    xnb = AR.alloc([128, KC, TBM], BF16); hxnb = H("xnb")
    actb = AR.alloc([128, NFC, TBM], BF16); hactb = H("actb")
    egbs = [AR.alloc([128, TBM], F32) for _ in range(2)]; hegbs = [H("egb0"), H("egb1")]
    pTf = AR.alloc([128, 2, TBM], F32); hpTf = H("pTf")
    pTb = AR.alloc([128, 2, TBM], BF16); hpTb = H("pTb")

    def rmsnorm_fm(src, hsrc, dst, hdst, nidx, TB, out_f32=False):
        ACT(sqb[:, :, 0:TB], src[:, :, 0:TB], AF.Square, R=[hsrc], W=[hsqb])
        pb, ph = rr()
        for kc in range(KC):
            MM(pb[:, 0:TB], ones_bf[:], sqb[:, kc, 0:TB], start=(kc == 0), stop=(kc == KC - 1), R=[hOnes, hsqb], **WJ(ph, kc == 0))
        ACT(rsb[:, 0:TB], pb[:, 0:TB], AF.Ln, bias=EPS, scale=1.0 / D, R=[ph], W=[hrsb])
        ACT(rsb[:, 0:TB], rsb[:, 0:TB], AF.Exp, scale=-0.5, R=[hrsb], W=[hrsb])
        for kc in range(KC):
            STT("dve", dst[:, kc, 0:TB], src[:, kc, 0:TB], norms_f[:, nidx, kc:kc + 1], rsb[:, 0:TB], ALU.mult, ALU.mult,
                R=[hsrc, hrsb, hNorms], **(WJ(hdst, kc == 0) if hdst is not hsrc else dict(W=[hdst])))

    def ffn_block(t0, TB, hsrc_dram, bidx=0):
        hb, hhb = hbs[bidx % 2], hhbs[bidx % 2]
        DMA("sp", hb[:, :, 0:TB], h1_scr[:, :, t0:t0 + TB], key=hhb, R=[hsrc_dram], W=[hhb])
        DMA("sp", pTf[:, :, 0:TB], pT[:, :, t0:t0 + TB], key=hpTf, W=[hpTf])
        CP("pool", pTb[:, :, 0:TB], pTf[:, :, 0:TB], R=[hpTf], W=[hpTb])
        rmsnorm_fm(hb, hhb, xnb, hxnb, 1, TB)
        for fc in range(NFC):
            pg_, phg_ = rr()
            for kc in range(KC):
                MM(pg_[:, 0:TB], wg_bf[:, kc, fc * 128:(fc + 1) * 128], xnb[:, kc, 0:TB], start=(kc == 0), stop=(kc == KC - 1),
                   R=[hWg, hxnb], **WJ(phg_, kc == 0))
            pu_, phu_ = rr()
            for kc in range(KC):
                MM(pu_[:, 0:TB], wu_bf[:, kc, fc * 128:(fc + 1) * 128], xnb[:, kc, 0:TB], start=(kc == 0), stop=(kc == KC - 1),
                   R=[hWu, hxnb], **WJ(phu_, kc == 0))
            eg_, heg_ = egbs[fc % 2], hegbs[fc % 2]
            ACT(eg_[:, 0:TB], pg_[:, 0:TB], AF.Silu, R=[phg_], W=[heg_])
            TT("dve", actb[:, fc, 0:TB], eg_[:, 0:TB], pu_[:, 0:TB], ALU.mult, R=[heg_, phu_], **WJ(hactb, fc == 0))
        for dc in range(KC):
            pd_, phd_ = rr()
            for fc in range(NFC):
                MM(pd_[:, 0:TB], wd_bf[:, fc, dc * 128:(dc + 1) * 128], actb[:, fc, 0:TB], start=(fc == 0), stop=(fc == NFC - 1),
                   R=[hWd, hactb], **WJ(phd_, fc == 0))
            TT("dve", hb[:, dc, 0:TB], hb[:, dc, 0:TB], pd_[:, 0:TB], ALU.add, R=[hhb, phd_], W=[hhb])
        rmsnorm_fm(hb, hhb, xnb, hxnb, 2, TB)
        for dc in range(KC):
            pg_, phg_ = rr()
            for kc in range(KC):
                MM(pg_[:, 0:TB], wpg_bf[:, kc, dc * 128:(dc + 1) * 128], xnb[:, kc, 0:TB], start=(kc == 0), stop=(kc == KC - 1),
                   R=[hWpg, hxnb], **WJ(phg_, kc == 0))
            pu_, phu_ = rr()
            for k2 in range(2):
                MM(pu_[:, 0:TB], wple_bf[:, k2, dc * 128:(dc + 1) * 128], pTb[:, k2, 0:TB], start=(k2 == 0), stop=(k2 == 1),
                   R=[hWple, hpTb], **WJ(phu_, k2 == 0))
            eg_, heg_ = egbs[dc % 2], hegbs[dc % 2]
            ACT(eg_[:, 0:TB], pg_[:, 0:TB], AF.Sigmoid, R=[phg_], W=[heg_])
            TT("dve", eg_[:, 0:TB], eg_[:, 0:TB], pu_[:, 0:TB], ALU.mult, R=[heg_, phu_], W=[heg_])
            TT("dve", hb[:, dc, 0:TB], hb[:, dc, 0:TB], eg_[:, 0:TB], ALU.add, R=[hhb, heg_], W=[hhb])
        rmsnorm_fm(hb, hhb, hb, hhb, 3, TB)
        DMA("sp", yT[:, :, t0:t0 + TB], hb[:, :, 0:TB], key=hhb, R=[hhb])

    nblk = (min(nslots, NSLOT) // 2) * 128 // TBM
    for bi in range(nblk):
        ffn_block(bi * TBM, TBM, H_h1[min(NOWN - 1, (bi * TBM + TBM - 1) // 128)], bi)
    if do_sample and not (99.05 <= cut < 107):
        ffn_block(NT, NS, H_h1[NOWN], nblk)
    B_END = AR.top
    ctx = dict(locals())
    return ctx


_STATIC = None


def _kc(a):
    K = a.shape[0] // 128
    return np.ascontiguousarray(a.reshape(K, 128, -1).transpose(1, 0, 2))


def prep_shared(inp):
    global _STATIC
    if _STATIC is None:
        _STATIC = _static_consts()
    sh = dict(_STATIC)
    l = 0
    wi = np.array(inp["w_in"][l])
    wi[:, 0:512] = wi[:, 0:512].reshape(D, 2, 4, 64).transpose(0, 2, 1, 3).reshape(D, 512)
    sh["w_in"] = _kc(wi)
    wo = inp["w_o"][l]
    sh["w_o_nsa"] = np.ascontiguousarray(wo[:512].reshape(8, 64, D).transpose(1, 0, 2))
    sh["w_o_gla"] = _kc(wo[512:])
    sh["w_gate"] = _kc(inp["w_gate"][l])
    sh["w_up"] = _kc(inp["w_up"][l])
    sh["w_down"] = _kc(inp["w_down"][l])
    sh["w_ple"] = _kc(inp["w_ple"][l])
    sh["w_pg"] = _kc(inp["w_ple_gate"][l])
    nm = np.stack([inp["norm_mix"][l], inp["norm_ffn"][l], inp["norm_ple"][l], inp["norm_final"]], 0)
    sh["norms"] = np.ascontiguousarray(nm.reshape(4, KC, 128).transpose(2, 0, 1))
    w1 = inp["cmp_w1"][l].reshape(2, 32, 64, 64)
    w1bd = np.zeros((128, 2, 32, 128), np.float32)
    for g in range(2):
        w1bd[g * 64:(g + 1) * 64, :, :, g * 64:(g + 1) * 64] = w1.transpose(2, 0, 1, 3)
    sh["w1bd"] = w1bd
    pe = inp["cmp_pe"][l]
    sh["pecol"] = np.ascontiguousarray(np.concatenate([pe.transpose(2, 0, 1)] * 2, 0))
    w2 = inp["cmp_w2"][l]
    w2bd = np.zeros((128, 2, 128), np.float32)
    for g in range(2):
        w2bd[g * 64:(g + 1) * 64, :, g * 64:(g + 1) * 64] = w2.transpose(1, 0, 2)
    sh["w2bd"] = w2bd
    wg = np.zeros((33, 256), np.float32)
    wg[:16] = inp["w_gk"][l]
    wg[32] = inp["b_gk"][l]
    sh["wgk"] = wg
    sh["glan"] = np.ascontiguousarray(inp["gla_norm"][l].reshape(128, 1))
    rb = np.zeros((33, 8), np.float32)
    rb[:32] = inp["rel_bias"]
    rb[32] = -BIG
    sh["rb33"] = rb
    sh["cache"] = np.ascontiguousarray(inp["cache_nsa_kv"][l].reshape(-1, 128, 512))
    return sh


def prep_core(inp, c, sh):
    b, par = c // 2, c % 2
    shift = 1 - par
    l = 0
    m = dict(sh)
    m.update(_core_consts(par))
    x = inp["x_prompt"][b]
    xs = np.zeros((TOK, D), np.float32)
    xs[shift * 128:shift * 128 + SEQ] = x
    m["xT"] = np.ascontiguousarray(xs.T.reshape(KC, 128, TOK).transpose(1, 0, 2))
    own_tok = np.concatenate([np.arange(128) + (2 * j + par) * 128 for j in range(NOWN)])
    bs = slice(4 * c, 4 * c + 4)
    p = np.concatenate([inp["p_prompt"][l, b][own_tok], inp["p_sample"][l, bs].reshape(NS, 256)], 0)
    m["pT"] = np.ascontiguousarray(p.T.reshape(2, 128, NTS).transpose(1, 0, 2))
    xsm = inp["x_sample"][bs].reshape(NS, D)
    m["xsT"] = np.ascontiguousarray(xsm.T.reshape(KC, 128, NS).transpose(1, 0, 2))
    m["ptab"] = np.ascontiguousarray(inp["page_table"][bs].reshape(1, 4 * NPG).astype(np.int32))
    m["cwin"] = np.ascontiguousarray(inp["cache_win_kv"][l, bs].reshape(4, 512, 256))
    m["sgla"] = np.ascontiguousarray(inp["state_gla"][l, bs])
    return m


_PROG = None


def _get_prog():
    global _PROG
    if _PROG is None:
        c = build_program(dbg=False)
        c["S"].finish()
        c["S"].emit()
        _PROG = c
    return _PROG


def kernel(**inputs):
    inp = {k: np.asarray(v) for k, v in inputs.items()}
    c = _get_prog()
    sh = prep_shared(inp)
    maps = []
    for ci in range(8):
        m = prep_core(inp, ci, sh)
        maps.append({k: m[k] for k in c["IN"]})
    res = run_bass_kernel_spmd(c["nc"], maps, core_ids=list(range(8)))
    R = res.results
    B, T = 4, SEQ
    y_prompt = np.zeros((B, T, D), np.float32)
    y_sample = np.zeros((32, 4, D), np.float32)
    new_kv_prompt = np.zeros((1, B, T, 4, 2, 64), np.float32)
    new_kv_sample = np.zeros((1, 32, 4, 4, 2, 64), np.float32)
    new_win_prompt = np.zeros((1, B, 512, 2, 2, 64), np.float32)
    new_win_sample = np.zeros((1, 32, 512, 2, 2, 64), np.float32)
    new_gla_prompt = np.zeros((1, B, 4, 64, 128), np.float32)
    new_gla_sample = np.zeros((1, 32, 4, 64, 128), np.float32)
    kwin = np.zeros((B, T, 2, 64), np.float32)
    vwin = np.zeros((B, T, 2, 64), np.float32)
    for ci in range(8):
        b, par = ci // 2, ci % 2
        r = R[ci]
        own_tok = np.concatenate([np.arange(128) + (2 * j + par) * 128 for j in range(NOWN)])
        kvT = r["kvT_p"]
        vt = r["vtok_p"]
        yT = r["yT"]
        y_prompt[b, own_tok] = yT[:, :, :NT].transpose(2, 1, 0).reshape(NT, D)
        y_sample[4 * ci:4 * ci + 4] = yT[:, :, NT:].transpose(2, 1, 0).reshape(4, 4, D)
        for i, sl in enumerate((0, 1, 2)):
            new_kv_prompt[0, b, own_tok, sl] = kvT[:, i, :].T.reshape(NT, 2, 64)
        new_kv_prompt[0, b, own_tok, 3] = vt[:, 0, :].reshape(NT, 2, 64)
        kwin[b, own_tok] = kvT[:, 3, :].T.reshape(NT, 2, 64)
        vwin[b, own_tok] = vt[:, 1, :].reshape(NT, 2, 64)
        if par == 0:
            gp = r["gla_p"]
            for h in range(4):
                new_gla_prompt[0, b, h] = gp[(h % 2) * 64:(h % 2) * 64 + 64, h // 2, :]
        ks = r["kvT_s"]
        vs = r["vtok_s"]
        for i, sl in enumerate((0, 1, 2)):
            new_kv_sample[0, 4 * ci:4 * ci + 4, :, sl] = ks[:, i, :].T.reshape(4, 4, 2, 64)
        new_kv_sample[0, 4 * ci:4 * ci + 4, :, 3] = vs[:, 0, :].reshape(4, 4, 2, 64)
        ws = r["win_s"].reshape(4, 508, 2, 2, 64)
        new_win_sample[0, 4 * ci:4 * ci + 4, :508] = ws
        new_win_sample[0, 4 * ci:4 * ci + 4, 508:, 0] = ks[:, 3, :].T.reshape(4, 4, 2, 64)
        new_win_sample[0, 4 * ci:4 * ci + 4, 508:, 1] = vs[:, 1, :].reshape(4, 4, 2, 64)
        new_gla_sample[0, 4 * ci:4 * ci + 4] = r["gla_s"].reshape(4, 2, 64, 2, 128).transpose(0, 3, 1, 2, 4).reshape(4, 4, 64, 128)
    new_win_prompt[0, :, :, 0] = kwin[:, T - 512:]
    new_win_prompt[0, :, :, 1] = vwin[:, T - 512:]
    return (y_prompt, y_sample, new_kv_prompt, new_kv_sample, new_win_prompt, new_win_sample,
            new_gla_prompt, new_gla_sample)
```
